# Optimizing a Trainium2 kernel written in Bass

```python
import math
import jax, jax.numpy as jnp
from jax import lax
import numpy as np


D_MODEL = 2048
BATCH = 1
SEQ = 8192
DEPTH = 1

CHUNK = 64
S5_WIDTH = 1024
S5_GROUP_WIDTH = 16
S5_GROUPS = S5_WIDTH // S5_GROUP_WIDTH
S5_STATE = 64
SGU_WIDTH = 1024
SGU_HEADS = 8
SGU_HEAD_DIM = SGU_WIDTH // SGU_HEADS
MLP_CHUNK = 128
MIX_IN = S5_WIDTH + 2 * SGU_WIDTH
D_FF = 5632
NORM_EPS = 1e-6
DT_MIN = 1e-3
DT_MAX = 1e-1

kernel_name = 'hybrid_s5_sgu_macaron_block'


def rms_norm(x, g):
    xf = x.astype(jnp.float32)
    y = xf * lax.rsqrt(jnp.mean(xf * xf, axis=-1, keepdims=True) + NORM_EPS)
    return (y * g.astype(jnp.float32)).astype(x.dtype)


def layer_norm(x, g, b):
    xf = x.astype(jnp.float32)
    mu = jnp.mean(xf, axis=-1, keepdims=True)
    var = jnp.mean(jnp.square(xf - mu), axis=-1, keepdims=True)
    y = (xf - mu) * lax.rsqrt(var + NORM_EPS)
    return (y * g.astype(jnp.float32) + b.astype(jnp.float32)).astype(x.dtype)


def swiglu_ffn(h, w_gate, w_up, w_down):
    return (jax.nn.silu(h @ w_gate) * (h @ w_up)) @ w_down


def _complex_affine_combine(e1, e2):
    a1r, a1i, b1r, b1i = e1
    a2r, a2i, b2r, b2i = e2
    ar = a2r * a1r - a2i * a1i
    ai = a2r * a1i + a2i * a1r
    br = a2r * b1r - a2i * b1i + b2r
    bi = a2r * b1i + a2i * b1r + b2i
    return (ar, ai, br, bi)


def s5_mixer(u, a_re, a_im, log_dt, b_re, b_im, c_re, c_im, d_skip, w_glu, b_glu):
    bsz, seq, _ = u.shape
    uf = u.astype(jnp.float32).reshape(bsz, seq, S5_GROUPS, S5_GROUP_WIDTH)
    lam_re = a_re.astype(jnp.float32)
    lam_im = a_im.astype(jnp.float32)
    dt = jnp.exp(log_dt.astype(jnp.float32))[:, None]
    decay = jnp.exp(lam_re * dt)
    abar_re = decay * jnp.cos(lam_im * dt)
    abar_im = decay * jnp.sin(lam_im * dt)
    denom = lam_re * lam_re + lam_im * lam_im
    num_re = abar_re - 1.0
    num_im = abar_im
    k_re = (num_re * lam_re + num_im * lam_im) / denom
    k_im = (num_im * lam_re - num_re * lam_im) / denom
    bu_re = jnp.einsum('blgc,gpc->blgp', uf, b_re.astype(jnp.float32))
    bu_im = jnp.einsum('blgc,gpc->blgp', uf, b_im.astype(jnp.float32))
    in_re = k_re * bu_re - k_im * bu_im
    in_im = k_re * bu_im + k_im * bu_re
    a_r = jnp.broadcast_to(abar_re, in_re.shape)
    a_i = jnp.broadcast_to(abar_im, in_re.shape)
    _, _, x_re, x_im = lax.associative_scan(
        _complex_affine_combine, (a_r, a_i, in_re, in_im), axis=1)
    y = (jnp.einsum('blgp,gcp->blgc', x_re, c_re.astype(jnp.float32))
         - jnp.einsum('blgp,gcp->blgc', x_im, c_im.astype(jnp.float32))
         + d_skip.astype(jnp.float32) * uf)
    y = jax.nn.gelu(y.reshape(bsz, seq, S5_WIDTH)).astype(u.dtype)
    return y * jax.nn.sigmoid(y @ w_glu + b_glu)


def sgu_mixer(uv, ln_g, ln_b, w_s, b_s):
    u, v = jnp.split(jax.nn.gelu(uv), 2, axis=-1)
    v = layer_norm(v, ln_g, ln_b)
    bsz, seq, _ = v.shape
    n_chunks = seq // MLP_CHUNK
    v = v.reshape(bsz, n_chunks, MLP_CHUNK, SGU_HEADS, SGU_HEAD_DIM)
    blk = jnp.arange(MLP_CHUNK) // CHUNK
    mask = blk[:, None] >= blk[None, :]
    ws = jnp.where(mask[None], w_s, jnp.zeros((), w_s.dtype))
    mixed = jnp.einsum('hts,bnshc->bnthc', ws, v) + jnp.transpose(b_s)[:, :, None]
    return u * mixed.reshape(bsz, seq, SGU_WIDTH)


def setup_inputs(seed: int = 0) -> dict:
    key = jax.random.key(seed)
    keys = iter(jax.random.split(key, 40))
    L = DEPTH
    D = D_MODEL

    def nrm(shape, scale):
        return scale * jax.random.normal(next(keys), shape, jnp.float32)

    x = nrm((BATCH, SEQ, D), 1.0)
    ffn1_norm = 1.0 + nrm((L, D), 0.02)
    ffn1_w_gate = nrm((L, D, D_FF), D ** -0.5)
    ffn1_w_up = nrm((L, D, D_FF), D ** -0.5)
    ffn1_w_down = nrm((L, D_FF, D), D_FF ** -0.5)
    mix_norm = 1.0 + nrm((L, D), 0.02)
    w_in = nrm((L, D, MIX_IN), D ** -0.5)
    n_idx = jnp.arange(S5_STATE, dtype=jnp.float32)
    s5_a_re = -0.5 + nrm((L, S5_GROUPS, S5_STATE), 0.01)
    s5_a_im = math.pi * n_idx + nrm((L, S5_GROUPS, S5_STATE), 0.01)
    s5_log_dt = math.log(DT_MIN) + jax.random.uniform(
        next(keys), (L, S5_GROUPS), jnp.float32) * (math.log(DT_MAX) - math.log(DT_MIN))
    s5_b_re = nrm((L, S5_GROUPS, S5_STATE, S5_GROUP_WIDTH), (2 * S5_GROUP_WIDTH) ** -0.5)
    s5_b_im = nrm((L, S5_GROUPS, S5_STATE, S5_GROUP_WIDTH), (2 * S5_GROUP_WIDTH) ** -0.5)
    s5_c_re = nrm((L, S5_GROUPS, S5_GROUP_WIDTH, S5_STATE), S5_STATE ** -0.5)
    s5_c_im = nrm((L, S5_GROUPS, S5_GROUP_WIDTH, S5_STATE), S5_STATE ** -0.5)
    s5_d = nrm((L, S5_GROUPS, S5_GROUP_WIDTH), 1.0)
    s5_w_glu = nrm((L, S5_WIDTH, S5_WIDTH), S5_WIDTH ** -0.5)
    s5_b_glu = nrm((L, S5_WIDTH), 0.01)
    sgu_ln_g = 1.0 + nrm((L, SGU_WIDTH), 0.02)
    sgu_ln_b = nrm((L, SGU_WIDTH), 0.01)
    sgu_w_s = nrm((L, SGU_HEADS, MLP_CHUNK, MLP_CHUNK), 0.05)
    sgu_b_s = 1.0 + nrm((L, SGU_HEADS, MLP_CHUNK), 0.05)
    w_branch_a = nrm((L, S5_WIDTH, D), S5_WIDTH ** -0.5)
    w_branch_b = nrm((L, SGU_WIDTH, D), SGU_WIDTH ** -0.5)
    w_gate = nrm((L, D, 2 * D), D ** -0.5)
    b_gate = nrm((L, 2 * D), 0.01)
    w_out = nrm((L, D, D), D ** -0.5)
    ffn2_norm = 1.0 + nrm((L, D), 0.02)
    ffn2_w_gate = nrm((L, D, D_FF), D ** -0.5)
    ffn2_w_up = nrm((L, D, D_FF), D ** -0.5)
    ffn2_w_down = nrm((L, D_FF, D), D_FF ** -0.5)
    final_norm = 1.0 + nrm((D,), 0.02)
    return {
        'x': x,
        'ffn1_norm': ffn1_norm, 'ffn1_w_gate': ffn1_w_gate, 'ffn1_w_up': ffn1_w_up,
        'ffn1_w_down': ffn1_w_down,
        'mix_norm': mix_norm, 'w_in': w_in,
        's5_a_re': s5_a_re, 's5_a_im': s5_a_im, 's5_log_dt': s5_log_dt,
        's5_b_re': s5_b_re, 's5_b_im': s5_b_im, 's5_c_re': s5_c_re, 's5_c_im': s5_c_im,
        's5_d': s5_d, 's5_w_glu': s5_w_glu, 's5_b_glu': s5_b_glu,
        'sgu_ln_g': sgu_ln_g, 'sgu_ln_b': sgu_ln_b, 'sgu_w_s': sgu_w_s, 'sgu_b_s': sgu_b_s,
        'w_branch_a': w_branch_a, 'w_branch_b': w_branch_b,
        'w_gate': w_gate, 'b_gate': b_gate, 'w_out': w_out,
        'ffn2_norm': ffn2_norm, 'ffn2_w_gate': ffn2_w_gate, 'ffn2_w_up': ffn2_w_up,
        'ffn2_w_down': ffn2_w_down,
        'final_norm': final_norm,
    }


def reference(x, ffn1_norm, ffn1_w_gate, ffn1_w_up, ffn1_w_down, mix_norm, w_in,
              s5_a_re, s5_a_im, s5_log_dt, s5_b_re, s5_b_im, s5_c_re, s5_c_im, s5_d,
              s5_w_glu, s5_b_glu, sgu_ln_g, sgu_ln_b, sgu_w_s, sgu_b_s,
              w_branch_a, w_branch_b, w_gate, b_gate, w_out,
              ffn2_norm, ffn2_w_gate, ffn2_w_up, ffn2_w_down, final_norm):
    for i in range(DEPTH):
        h = rms_norm(x, ffn1_norm[i])
        x = x + 0.5 * swiglu_ffn(h, ffn1_w_gate[i], ffn1_w_up[i], ffn1_w_down[i])
        h = rms_norm(x, mix_norm[i])
        proj = h @ w_in[i]
        u_a = proj[..., :S5_WIDTH]
        uv_b = proj[..., S5_WIDTH:]
        y_a = s5_mixer(u_a, s5_a_re[i], s5_a_im[i], s5_log_dt[i], s5_b_re[i], s5_b_im[i],
                       s5_c_re[i], s5_c_im[i], s5_d[i], s5_w_glu[i], s5_b_glu[i])
        y_b = sgu_mixer(uv_b, sgu_ln_g[i], sgu_ln_b[i], sgu_w_s[i], sgu_b_s[i])
        gates = jax.nn.sigmoid(h @ w_gate[i] + b_gate[i])
        g_a, g_b = jnp.split(gates, 2, axis=-1)
        merged = g_a * (y_a @ w_branch_a[i]) + g_b * (y_b @ w_branch_b[i])
        x = x + merged @ w_out[i]
        h = rms_norm(x, ffn2_norm[i])
        x = x + 0.5 * swiglu_ffn(h, ffn2_w_gate[i], ffn2_w_up[i], ffn2_w_down[i])
    return rms_norm(x, final_norm)
```

```python
import contextlib
import numpy as np
import concourse.bass as bass
import concourse.mybir as mybir
from concourse.bass_utils import run_bass_kernel_spmd

ENGINES = ("tensor", "vector", "scalar", "gpsimd", "sync")


class Sched:
    def __init__(self, self_edges=True):
        self.ops = []
        self.last_w = {}
        self.readers = {}
        self.self_edges = self_edges

    def _add(self, eng, emit, reads, writes, dma_key=None):
        i = len(self.ops)
        deps = set()
        for r in reads:
            if r in self.last_w:
                deps.add(self.last_w[r])
        for w in writes:
            if w in self.last_w:
                deps.add(self.last_w[w])
            deps.update(self.readers.get(w, ()))
        deps.discard(i)
        self.ops.append(dict(eng=eng, emit=emit, deps=sorted(deps), dma_key=dma_key))
        for r in reads:
            self.readers.setdefault(r, []).append(i)
        for w in writes:
            self.last_w[w] = i
            self.readers[w] = []
        return i

    def op(self, eng, emit, reads=(), writes=()):
        return self._add(eng, emit, list(reads), list(writes))

    def dma(self, eng, emit, reads=(), writes=(), sem_key=None):
        assert sem_key is not None
        return self._add(eng, emit, list(reads), list(writes), dma_key=sem_key)

    def emit(self, nc, final_wait_ops=()):
        ops = self.ops
        need_inc = [False] * len(ops)
        for i, o in enumerate(ops):
            for d in o["deps"]:
                p = ops[d]
                if p["dma_key"] is not None:
                    continue
                if p["eng"] == o["eng"] and (p["eng"] == "tensor" or not self.self_edges):
                    continue
                need_inc[d] = True
        eng_cnt = {e: 0 for e in ENGINES}
        inc_val = [None] * len(ops)
        dma_cnt = {}
        for i, o in enumerate(ops):
            if o["dma_key"] is not None:
                k = o["dma_key"]
                dma_cnt[k] = dma_cnt.get(k, 0) + 16
                inc_val[i] = dma_cnt[k]
            elif need_inc[i]:
                eng_cnt[o["eng"]] += 1
                inc_val[i] = eng_cnt[o["eng"]]
        import contextlib
        with contextlib.ExitStack() as st:
            esem = {e: st.enter_context(nc.semaphore("e_" + e)) for e in ENGINES}
            dsem = {k: st.enter_context(nc.semaphore("d_%d" % j)) for j, k in enumerate(dma_cnt)}
            block = st.enter_context(nc.Block())
            per_eng = {e: [i for i, o in enumerate(ops) if o["eng"] == e] for e in ENGINES}

            def make(e):
                def body(eng):
                    waited = {}
                    for i in per_eng[e]:
                        o = ops[i]
                        for d in o["deps"]:
                            p = ops[d]
                            if p["dma_key"] is not None:
                                key = ("d", p["dma_key"])
                                sem = dsem[p["dma_key"]]
                            else:
                                if p["eng"] == e and (e == "tensor" or not self.self_edges):
                                    continue
                                key = ("e", p["eng"])
                                sem = esem[p["eng"]]
                            v = inc_val[d]
                            if waited.get(key, 0) >= v:
                                continue
                            eng.wait_ge(sem, v)
                            waited[key] = v
                        ins = o["emit"](eng)
                        if o["dma_key"] is not None:
                            ins.then_inc(dsem[o["dma_key"]], 16)
                        elif need_inc[i]:
                            ins.then_inc(esem[e], 1)
                    if e == "sync":
                        for i in final_wait_ops:
                            p = ops[i]
                            sem = dsem[p["dma_key"]] if p["dma_key"] is not None else esem[p["eng"]]
                            eng.wait_ge(sem, inc_val[i])
                return body
            for e in ENGINES:
                if per_eng[e] or e == "sync":
                    getattr(block, e)(make(e))
        return nc


F32 = mybir.dt.float32
BF16 = mybir.dt.bfloat16
AF = mybir.ActivationFunctionType
ALU = mybir.AluOpType

D = 2048
KC = D // 128
TOK = 1024
NT = TOK // 512
DFF = 5632
NFB = DFF // 512
EPS = 1e-6


class Ctx:
    pass


def alloc_common(nc, st, c):
    c.x = st.enter_context(nc.sbuf_tensor("x", [128, KC, TOK], F32))
    c.h = st.enter_context(nc.sbuf_tensor("h", [128, KC, TOK], BF16))
    c.wring = st.enter_context(nc.sbuf_tensor("wring", [128, 4, 8192], BF16))
    c.hid = st.enter_context(nc.sbuf_tensor("hid", [128, 2, 4, TOK], BF16))
    c.sq = st.enter_context(nc.sbuf_tensor("sq", [128, 2, 512], BF16))
    c.rstd = st.enter_context(nc.sbuf_tensor("rstd", [128, TOK], F32))
    c.silu = st.enter_context(nc.sbuf_tensor("silu", [128, 2, 512], F32))
    c.ones = st.enter_context(nc.sbuf_tensor("ones", [128, 128], BF16))
    c.gains = st.enter_context(nc.sbuf_tensor("gains", [128, 4, KC], F32))
    c.epsb = st.enter_context(nc.sbuf_tensor("epsb", [128, 1], F32))
    c.ps = st.enter_context(nc.psum_tensor("ps", [128, 8, 512], F32))
    c.wslot = 0
    c.sqslot = 0
    c.silslot = 0


def emit_consts(c):
    s = c.s
    s.op("gpsimd", lambda e: e.memset(c.ones[:], 1.0), writes=[("ones",)])
    s.op("gpsimd", lambda e: e.memset(c.epsb[:], EPS), writes=[("epsb",)])


def load_gain(c, idx, g_ap):
    c.s.dma("sync", lambda e: e.dma_start(out=c.gains[:, idx, :], in_=g_ap.rearrange("(k p) -> p k", p=128),
                                          allow_slow_non_contiguous=True),
            writes=[("gain", idx)], sem_key=("gain", idx))


def emit_rmsnorm(c, gidx, out_key="h"):
    s = c.s
    for t in range(NT):
        tsl = slice(t * 512, (t + 1) * 512)
        bank = 7
        for k in range(KC):
            sl = c.sqslot; c.sqslot ^= 1
            s.op("scalar", lambda e, k=k, sl=sl, tsl=tsl: e.activation(out=c.sq[:, sl, :], in_=c.x[:, k, tsl], func=AF.Square),
                 reads=[("x", k, t)], writes=[("sq", sl)])
            s.op("tensor", lambda e, k=k, sl=sl, bank=bank: e.matmul(c.ps[:, bank, :], c.ones[:], c.sq[:, sl, :],
                                                          start=(k == 0), stop=(k == KC - 1)),
                 reads=[("sq", sl), ("ones",)], writes=[("ps", bank)])
        s.op("scalar", lambda e, tsl=tsl, bank=bank: e.activation(out=c.rstd[:, tsl], in_=c.ps[:, bank, :], func=AF.Sqrt,
                                              bias=c.epsb[:, 0:1], scale=1.0 / D),
             reads=[("ps", bank), ("epsb",)], writes=[("rstd", t)])
        s.op("vector", lambda e, tsl=tsl: e.reciprocal(out=c.rstd[:, tsl], in_=c.rstd[:, tsl]),
             reads=[("rstd", t)], writes=[("rstd", t)])
        for k in range(KC):
            s.op("vector", lambda e, k=k, tsl=tsl: e.scalar_tensor_tensor(
                out=c.h[:, k, tsl], in0=c.x[:, k, tsl], scalar=c.gains[:, gidx, k:k + 1],
                in1=c.rstd[:, tsl], op0=ALU.mult, op1=ALU.mult),
                 reads=[("x", k, t), ("gain", gidx), ("rstd", t)], writes=[(out_key, k, t)])


def wload(c, view_shape, src_ap, tag):
    slot = c.wslot; c.wslot = (c.wslot + 1) % 4
    n = 1
    for d in view_shape[1:]:
        n *= d
    assert n <= 8192
    flat = c.wring[:, slot, 0:n]
    if len(view_shape) == 3:
        view = flat.rearrange("p (a b) -> p a b", a=view_shape[1])
    else:
        view = flat
    c.s.dma("gpsimd", lambda e: e.dma_start(out=view, in_=src_ap), writes=[("w", slot)], sem_key=("w", slot))
    return slot, view


def emit_ffn(c, gidx, wg, wu, wd):
    s = c.s
    emit_rmsnorm(c, gidx)
    wg_v = wg.rearrange("(k p) f -> p k f", p=128)
    wu_v = wu.rearrange("(k p) f -> p k f", p=128)
    wd_v = wd.rearrange("(m p) d -> p m d", p=128)
    for b in range(NFB):
        hb = b % 2
        gs, gv = wload(c, [128, KC, 512], wg_v[:, :, b * 512:(b + 1) * 512], "g")
        us, uv = wload(c, [128, KC, 512], wu_v[:, :, b * 512:(b + 1) * 512], "u")
        ds_, dv = wload(c, [128, 4, D], wd_v[:, b * 4:(b + 1) * 4, :], "d")
        for m in range(4):
            for kind, (ws, wv) in enumerate(((gs, gv), (us, uv))):
                for t in range(NT):
                    bank = kind * 2 + t
                    for k in range(KC):
                        s.op("tensor", lambda e, wv=wv, k=k, m=m, t=t, bank=bank: e.matmul(
                            c.ps[:, bank, :], wv[:, k, m * 128:(m + 1) * 128], c.h[:, k, t * 512:(t + 1) * 512],
                            start=(k == 0), stop=(k == KC - 1)),
                            reads=[("w", ws), ("h", k, t)], writes=[("ps", bank)])
            for t in range(NT):
                sl = c.silslot; c.silslot ^= 1
                s.op("scalar", lambda e, t=t, sl=sl: e.activation(out=c.silu[:, sl, :], in_=c.ps[:, t, :], func=AF.Silu),
                     reads=[("ps", t)], writes=[("silu", sl)])
                s.op("vector", lambda e, t=t, sl=sl, m=m, hb=hb: e.tensor_tensor(
                    out=c.hid[:, hb, m, t * 512:(t + 1) * 512], in0=c.ps[:, 2 + t, :], in1=c.silu[:, sl, :], op=ALU.mult),
                     reads=[("ps", 2 + t), ("silu", sl)], writes=[("hid", hb, m, t)])
        gi = 0
        for n in range(KC):
            for t in range(NT):
                bank = 4 + (gi % 4); gi += 1
                for m in range(4):
                    s.op("tensor", lambda e, n=n, t=t, m=m, bank=bank, hb=hb, dv=dv: e.matmul(
                        c.ps[:, bank, :], dv[:, m, n * 128:(n + 1) * 128], c.hid[:, hb, m, t * 512:(t + 1) * 512],
                        start=(m == 0), stop=(m == 3)),
                        reads=[("w", ds_), ("hid", hb, m, t)], writes=[("ps", bank)])
                s.op("vector", lambda e, n=n, t=t, bank=bank: e.scalar_tensor_tensor(
                    out=c.x[:, n, t * 512:(t + 1) * 512], in0=c.ps[:, bank, :], scalar=0.5,
                    in1=c.x[:, n, t * 512:(t + 1) * 512], op0=ALU.mult, op1=ALU.add),
                     reads=[("ps", bank), ("x", n, t)], writes=[("x", n, t)])


def load_x(c, x_ap):
    raise NotImplementedError


def load_xT(c, xT_ap):
    v = xT_ap.rearrange("(k p) t -> p k t", p=128)
    for k in range(KC):
        c.s.dma("sync", lambda e, k=k: e.dma_start(out=c.x[:, k, :], in_=v[:, k, :]),
                writes=[("x", k, t) for t in range(NT)], sem_key=("xld", k))


def store_T(c, src, key, outT_ap):
    v = outT_ap.rearrange("(k p) t -> p k t", p=128)
    ids = []
    for k in range(KC):
        ids.append(c.s.dma("sync", lambda e, k=k: e.dma_start(out=v[:, k, :], in_=src[:, k, :]),
                           reads=[(key, k, t) for t in range(NT)], sem_key=("st", k)))
    return ids


GELU_C1 = 0.044715
GELU_C2 = 1.5957691216057308


def emit_gelu_from_psum(c, bank, out_ap, tmp_a, tmp_b, reads, writes, tmpkeys):
    s = c.s
    ka, kb = tmpkeys
    s.op("scalar", lambda e: e.activation(out=tmp_a, in_=c.ps[:, bank, :], func=AF.Square),
         reads=reads, writes=[ka])
    s.op("vector", lambda e: e.tensor_scalar(out=tmp_a, in0=tmp_a, scalar1=GELU_C1, scalar2=1.0, op0=ALU.mult, op1=ALU.add),
         reads=[ka], writes=[ka])
    s.op("vector", lambda e: e.tensor_tensor(out=tmp_a, in0=c.ps[:, bank, :], in1=tmp_a, op=ALU.mult),
         reads=reads + [ka], writes=[ka])
    s.op("scalar", lambda e: e.activation(out=tmp_b, in_=tmp_a, func=AF.Sigmoid, scale=GELU_C2),
         reads=[ka], writes=[kb])
    s.op("vector", lambda e: e.tensor_tensor(out=out_ap, in0=c.ps[:, bank, :], in1=tmp_b, op=ALU.mult),
         reads=reads + [kb], writes=writes)


MIXIN = 3072


def emit_inproj(c, w_in, projT_ap, stage, tmpa, tmpb):
    s = c.s
    wv = w_in.rearrange("(k p) f -> p k f", p=128)
    ids = []
    gi = 0
    for u in range(MIXIN // 512):
        ws, wview = wload(c, [128, KC, 512], wv[:, :, u * 512:(u + 1) * 512], "in")
        for m in range(4):
            cc = u * 4 + m
            for t in range(NT):
                bank = gi % 4
                sl = gi % 2
                gi += 1
                for k in range(KC):
                    s.op("tensor", lambda e, wview=wview, k=k, m=m, t=t, bank=bank: e.matmul(
                        c.ps[:, bank, :], wview[:, k, m * 128:(m + 1) * 128], c.h[:, k, t * 512:(t + 1) * 512],
                        start=(k == 0), stop=(k == KC - 1)),
                        reads=[("w", ws), ("h", k, t)], writes=[("ps", bank)])
                if cc < 8:
                    s.op("scalar", lambda e, bank=bank, sl=sl: e.activation(out=stage[:, sl, :], in_=c.ps[:, bank, :], func=AF.Copy),
                         reads=[("ps", bank)], writes=[("stage", sl)])
                else:
                    emit_gelu_from_psum(c, bank, stage[:, sl, :], tmpa[:, sl, :], tmpb[:, sl, :],
                                        reads=[("ps", bank)], writes=[("stage", sl)], tmpkeys=(("tmpa", sl), ("tmpb", sl)))
                ids.append(s.dma("sync", lambda e, cc=cc, t=t, sl=sl: e.dma_start(
                    out=projT_ap[cc * 128:(cc + 1) * 128, t * 512:(t + 1) * 512], in_=stage[:, sl, :]),
                    reads=[("stage", sl)], sem_key=("stg", sl)))
    return ids


TWO_PI = 6.283185
INV_2PI = 0.15915494309189535
I32 = mybir.dt.int32


def bc_last(ap, n):
    shp = list(ap.shape)
    shp[-1] = n
    return ap.to_broadcast(shp)


def emit_sincos(c, f_ap, sin_ap, cos_ap, scr, keyp, reads):
    s = c.s
    K = lambda n: (keyp, n)
    s.op("vector", lambda e: e.tensor_copy(out=scr["i32"], in_=f_ap), reads=reads, writes=[K("i32")])
    s.op("vector", lambda e: e.tensor_copy(out=scr["a"], in_=scr["i32"]), reads=[K("i32")], writes=[K("a")])
    s.op("vector", lambda e: e.tensor_tensor(out=scr["a"], in0=f_ap, in1=scr["a"], op=ALU.subtract), reads=reads + [K("a")], writes=[K("a")])
    for which, out_ap, off in (("s", sin_ap, 0.0), ("c", cos_ap, 0.25)):
        if off != 0.0:
            s.op("vector", lambda e, off=off: e.tensor_scalar(out=scr["b"], in0=scr["a"], scalar1=off, scalar2=None, op0=ALU.add),
                 reads=[K("a")], writes=[K("b")])
            src = scr["b"]; srck = K("b")
        else:
            src = scr["a"]; srck = K("a")
        s.op("vector", lambda e, src=src: e.scalar_tensor_tensor(out=scr["b"], in0=src, scalar=0.5, in1=src, op0=ALU.is_gt, op1=ALU.subtract),
             reads=[srck], writes=[K("b")])
        s.op("vector", lambda e: e.scalar_tensor_tensor(out=scr["b"], in0=scr["b"], scalar=0.5, in1=scr["b"], op0=ALU.is_gt, op1=ALU.subtract),
             reads=[K("b")], writes=[K("b")])
        s.op("scalar", lambda e, out_ap=out_ap: e.activation(out=out_ap, in_=scr["b"], func=AF.Sin, scale=TWO_PI),
             reads=[K("b")], writes=[K(which)])


NP = 32
TC = 128


def emit_gelu_src(c, src, src_keys, out_ap, tmp_a, tmp_b, writes, tmpkeys):
    s = c.s
    ka, kb = tmpkeys
    s.op("scalar", lambda e: e.activation(out=tmp_a, in_=src, func=AF.Square), reads=src_keys, writes=[ka])
    s.op("vector", lambda e: e.tensor_scalar(out=tmp_a, in0=tmp_a, scalar1=GELU_C1, scalar2=1.0, op0=ALU.mult, op1=ALU.add),
         reads=[ka], writes=[ka])
    s.op("vector", lambda e: e.tensor_tensor(out=tmp_a, in0=src, in1=tmp_a, op=ALU.mult), reads=src_keys + [ka], writes=[ka])
    s.op("scalar", lambda e: e.activation(out=tmp_b, in_=tmp_a, func=AF.Sigmoid, scale=GELU_C2), reads=[ka], writes=[kb])
    s.op("vector", lambda e: e.tensor_tensor(out=out_ap, in0=src, in1=tmp_b, op=ALU.mult), reads=src_keys + [kb], writes=writes)


def s5_alloc(nc, st, c, full):
    T = lambda name, shp, dt=F32: st.enter_context(nc.sbuf_tensor(name, shp, dt))
    c.ua = T("ua", [128, 8, TOK], BF16)
    c.BtT = T("BtT", [128, 2, NP, 128], BF16)
    c.aq = T("aq", [128, 3, NP])
    c.qs = T("qs", [128, 12, NP])
    c.qi = T("qi", [128, NP], I32)
    c.E = T("E", [128, 2, NP, TC])
    c.Gp = T("Gp", [128, 2, 2, NP])
    c.G128 = T("G128", [128, 2, NP])
    c.etmp = T("etmp", [128, 2, NP, 64])
    c.arow = T("arow", [128, 3, 1024])
    c.btp = T("btp", [128, 2, 1024])
    c.rs = T("rs", [128, 9, 1024])
    c.ri = T("ri", [128, 1024], I32)
    c.z = T("z", [128, 2, 2, 512])
    c.mt = T("mt", [128, 2, 512])
    c.w = T("w", [128, 2, 2, 512])
    c.ini = T("ini", [128, NP, 2])
    c.itmp = T("itmp", [128, 2])
    c.ps = st.enter_context(nc.psum_tensor("ps", [128, 8, 512], F32))
    c.xend = T("xend", [128, 2, NP])
    if full:
        c.Cz = T("Cz", [128, 2, NP, 128], BF16)
        c.dq = T("dq", [128, 8])
        c.xs = T("xs", [128, 2, 2, 512], BF16)
        c.ypre = T("ypre", [128, 2, 512])
        c.ga = T("ga", [128, 2, 512]); c.gb = T("gb", [128, 2, 512])
        c.yst = T("yst", [128, 2, 512])
        c.xall = T("xall", [128, 8, 2, NP])
        c.oneh = T("oneh", [128, 8])
        c.X = T("X", [128, 2, 2, NP])
        c.A = T("A", [128, 2, NP])
        c.xinit = T("xinit", [128, 2, NP])


def s5_setup(c, aq_ap, arow_ap, BT_ap, full):
    s = c.s
    s.dma("sync", lambda e: e.dma_start(out=c.aq[:], in_=aq_ap), writes=["aq"], sem_key="aq")
    q = lambda i: c.qs[:, i, :]
    s.op("scalar", lambda e: e.activation(out=q(0), in_=c.aq[:, 2, :], func=AF.Exp), reads=["aq"], writes=["q0"])
    s.op("vector", lambda e: e.tensor_tensor(out=q(1), in0=c.aq[:, 0, :], in1=q(0), op=ALU.mult), reads=["aq", "q0"], writes=["q1"])
    s.op("vector", lambda e: e.scalar_tensor_tensor(out=q(2), in0=c.aq[:, 1, :], scalar=INV_2PI, in1=q(0), op0=ALU.mult, op1=ALU.mult),
         reads=["aq", "q0"], writes=["q2"])
    s.op("scalar", lambda e: e.activation(out=q(3), in_=q(1), func=AF.Exp), reads=["q1"], writes=["q3"])
    emit_sincos(c, q(2), q(4), q(5), {"i32": c.qi[:], "a": q(6), "b": q(7)}, "qsc", reads=["q2"])
    QS, QC = ("qsc", "s"), ("qsc", "c")
    s.op("vector", lambda e: e.memset(c.E[:, 0, :, 0:1], 1.0), writes=["E"])
    s.op("vector", lambda e: e.memset(c.E[:, 1, :, 0:1], 0.0), reads=["E"], writes=["E"])
    s.op("vector", lambda e: e.tensor_copy(out=c.Gp[:, 0, 0, :], in_=q(5)), reads=[QC], writes=["G"])
    s.op("vector", lambda e: e.tensor_copy(out=c.Gp[:, 0, 1, :], in_=q(4)), reads=[QS, "G"], writes=["G"])
    cur = 0
    k = 1
    t0 = c.etmp[:, 0]; t1 = c.etmp[:, 1]

    def square(src, dst, dst_is_pp=True):
        sr, si = src
        dr, di = dst
        s.op("vector", lambda e: e.tensor_tensor(out=t0[:, :, 0], in0=si, in1=si, op=ALU.mult), reads=["G"], writes=["t0"])
        s.op("vector", lambda e: e.tensor_tensor(out=t1[:, :, 0], in0=sr, in1=sr, op=ALU.mult), reads=["G"], writes=["t1"])
        s.op("vector", lambda e: e.scalar_tensor_tensor(out=di, in0=sr, scalar=2.0, in1=si, op0=ALU.mult, op1=ALU.mult), reads=["G"], writes=["G"])
        s.op("vector", lambda e: e.tensor_tensor(out=dr, in0=t1[:, :, 0], in1=t0[:, :, 0], op=ALU.subtract), reads=["t0", "t1", "G"], writes=["G"])

    while k < TC:
        gr = c.Gp[:, cur, 0, :]; gi = c.Gp[:, cur, 1, :]
        grb = bc_last(gr.unsqueeze(2), k); gib = bc_last(gi.unsqueeze(2), k)

        def mk(k=k, grb=grb, gib=gib):
            s.op("vector", lambda e: e.tensor_tensor(out=t0[:, :, 0:k], in0=c.E[:, 1, :, 0:k], in1=gib, op=ALU.mult), reads=["E", "G"], writes=["t0"])
            s.op("vector", lambda e: e.tensor_tensor(out=t1[:, :, 0:k], in0=c.E[:, 0, :, 0:k], in1=grb, op=ALU.mult), reads=["E", "G"], writes=["t1"])
            s.op("vector", lambda e: e.tensor_tensor(out=c.E[:, 0, :, k:2 * k], in0=t1[:, :, 0:k], in1=t0[:, :, 0:k], op=ALU.subtract), reads=["t0", "t1", "E"], writes=["E"])
            s.op("vector", lambda e: e.tensor_tensor(out=t0[:, :, 0:k], in0=c.E[:, 0, :, 0:k], in1=gib, op=ALU.mult), reads=["E", "G"], writes=["t0"])
            s.op("vector", lambda e: e.tensor_tensor(out=t1[:, :, 0:k], in0=c.E[:, 1, :, 0:k], in1=grb, op=ALU.mult), reads=["E", "G"], writes=["t1"])
            s.op("vector", lambda e: e.tensor_tensor(out=c.E[:, 1, :, k:2 * k], in0=t1[:, :, 0:k], in1=t0[:, :, 0:k], op=ALU.add), reads=["t0", "t1", "E"], writes=["E"])
        mk()
        nxt = 1 - cur
        square((gr, gi), (c.Gp[:, nxt, 0, :], c.Gp[:, nxt, 1, :]))
        cur = nxt
        k *= 2
    s.op("vector", lambda e, cur=cur: e.tensor_copy(out=c.G128[:], in_=c.Gp[:, cur]), reads=["G"], writes=["G128"])
    if full:
        for _ in range(3):
            nxt = 1 - cur
            square((c.Gp[:, cur, 0, :], c.Gp[:, cur, 1, :]), (c.Gp[:, nxt, 0, :], c.Gp[:, nxt, 1, :]))
            cur = nxt
        s.op("scalar", lambda e: e.activation(out=q(8), in_=q(1), func=AF.Exp, scale=1024.0), reads=["q1"], writes=["q8"])
        s.op("vector", lambda e, cur=cur: e.tensor_tensor(out=c.A[:, 0, :], in0=c.Gp[:, cur, 0, :], in1=q(8), op=ALU.mult), reads=["G", "q8"], writes=["A"])
        s.op("vector", lambda e, cur=cur: e.tensor_tensor(out=c.A[:, 1, :], in0=c.Gp[:, cur, 1, :], in1=q(8), op=ALU.mult), reads=["G", "q8", "A"], writes=["A"])
    R = lambda i: c.rs[:, i, :]
    for pc in range(4):
        sl = slice(pc * 1024, (pc + 1) * 1024)
        s.dma("sync", lambda e, sl=sl: e.dma_start(out=c.arow[:], in_=arow_ap[:, :, sl]), writes=["arow"], sem_key="arow")
        s.dma("sync", lambda e, sl=sl: e.dma_start(out=c.btp[:], in_=BT_ap[:, :, sl]), writes=["btp"], sem_key="btp")
        are = c.arow[:, 0, :]; aim = c.arow[:, 1, :]
        s.op("scalar", lambda e: e.activation(out=R(0), in_=c.arow[:, 2, :], func=AF.Exp), reads=["arow"], writes=["r0"])
        s.op("vector", lambda e: e.tensor_tensor(out=R(1), in0=are, in1=R(0), op=ALU.mult), reads=["arow", "r0"], writes=["r1"])
        s.op("vector", lambda e: e.scalar_tensor_tensor(out=R(2), in0=aim, scalar=INV_2PI, in1=R(0), op0=ALU.mult, op1=ALU.mult),
             reads=["arow", "r0"], writes=["r2"])
        s.op("scalar", lambda e: e.activation(out=R(3), in_=R(1), func=AF.Exp), reads=["r1"], writes=["r3"])
        emit_sincos(c, R(2), R(4), R(5), {"i32": c.ri[:], "a": R(6), "b": R(7)}, "rsc", reads=["r2"])
        RS, RC = ("rsc", "s"), ("rsc", "c")
        s.op("vector", lambda e: e.tensor_tensor(out=R(5), in0=R(5), in1=R(3), op=ALU.mult), reads=[RC, "r3"], writes=[RC])
        s.op("vector", lambda e: e.tensor_scalar(out=R(5), in0=R(5), scalar1=-1.0, scalar2=None, op0=ALU.add), reads=[RC], writes=[RC])
        s.op("vector", lambda e: e.tensor_tensor(out=R(4), in0=R(4), in1=R(3), op=ALU.mult), reads=[RS, "r3"], writes=[RS])
        s.op("vector", lambda e: e.tensor_tensor(out=R(0), in0=are, in1=are, op=ALU.mult), reads=["arow", "r1", "r2"], writes=["r0"])
        s.op("vector", lambda e: e.tensor_tensor(out=R(1), in0=aim, in1=aim, op=ALU.mult), reads=["arow", "r3"], writes=["r1"])
        s.op("vector", lambda e: e.tensor_tensor(out=R(0), in0=R(0), in1=R(1), op=ALU.add), reads=["r0", "r1"], writes=["r0"])
        s.op("vector", lambda e: e.reciprocal(out=R(0), in_=R(0)), reads=["r0"], writes=["r0"])
        s.op("vector", lambda e: e.tensor_tensor(out=R(1), in0=R(5), in1=are, op=ALU.mult), reads=[RC, "arow", "r1"], writes=["r1"])
        s.op("vector", lambda e: e.tensor_tensor(out=R(2), in0=R(4), in1=aim, op=ALU.mult), reads=[RS, "arow", "r2", ("rsc", "a"), ("rsc", "b")], writes=["r2"])
        s.op("vector", lambda e: e.tensor_tensor(out=R(1), in0=R(1), in1=R(2), op=ALU.add), reads=["r1", "r2"], writes=["r1"])
        s.op("vector", lambda e: e.tensor_tensor(out=R(6), in0=R(1), in1=R(0), op=ALU.mult), reads=["r1", "r0", ("rsc", "a"), ("rsc", "b")], writes=["r6"])
        s.op("vector", lambda e: e.tensor_tensor(out=R(1), in0=R(4), in1=are, op=ALU.mult), reads=[RS, "arow", "r1", "r6"], writes=["r1"])
        s.op("vector", lambda e: e.tensor_tensor(out=R(2), in0=R(5), in1=aim, op=ALU.mult), reads=[RC, "arow", "r2"], writes=["r2"])
        s.op("vector", lambda e: e.tensor_tensor(out=R(1), in0=R(1), in1=R(2), op=ALU.subtract), reads=["r1", "r2"], writes=["r1"])
        s.op("vector", lambda e: e.tensor_tensor(out=R(7), in0=R(1), in1=R(0), op=ALU.mult), reads=["r1", "r0", "r6"], writes=["r7"])
        bre = c.btp[:, 0, :]; bim = c.btp[:, 1, :]
        ore = c.BtT[:, 0, pc * 8:(pc + 1) * 8, :].rearrange("p a b -> p (a b)")
        oim = c.BtT[:, 1, pc * 8:(pc + 1) * 8, :].rearrange("p a b -> p (a b)")
        s.op("vector", lambda e: e.tensor_tensor(out=R(1), in0=bre, in1=R(6), op=ALU.mult), reads=["btp", "r6", "r1"], writes=["r1"])
        s.op("vector", lambda e: e.tensor_tensor(out=R(2), in0=bim, in1=R(7), op=ALU.mult), reads=["btp", "r7", "r2"], writes=["r2"])
        s.op("vector", lambda e, ore=ore: e.tensor_tensor(out=ore, in0=R(1), in1=R(2), op=ALU.subtract), reads=["r1", "r2"], writes=[("BtT", pc)])
        s.op("vector", lambda e: e.tensor_tensor(out=R(1), in0=bre, in1=R(7), op=ALU.mult), reads=["btp", "r7", "r1", ("BtT", pc)], writes=["r1"])
        s.op("vector", lambda e: e.tensor_tensor(out=R(2), in0=bim, in1=R(6), op=ALU.mult), reads=["btp", "r6", "r2", ("BtT", pc)], writes=["r2"])
        s.op("vector", lambda e, oim=oim: e.tensor_tensor(out=oim, in0=R(1), in1=R(2), op=ALU.add), reads=["r1", "r2", ("BtT", pc)], writes=[("BtT", pc)])


def s5_main(c, full, yag_out=None):
    s = c.s
    ids = []
    rq = lambda p: c.qs[:, 3, p:p + 1]
    QC, QS = ("qsc", "c"), ("qsc", "s")
    if full:
        cq = c.qs[:, 5, :]; sq_ = c.qs[:, 4, :]
        a0 = c.qs[:, 9, :]; a1 = c.qs[:, 10, :]
        s.op("vector", lambda e: e.tensor_tensor(out=a0, in0=sq_, in1=c.xinit[:, 1, :], op=ALU.mult), reads=[QS, "xinit"], writes=["a0"])
        s.op("vector", lambda e: e.tensor_tensor(out=a1, in0=cq, in1=c.xinit[:, 0, :], op=ALU.mult), reads=[QC, "xinit"], writes=["a1"])
        s.op("vector", lambda e: e.tensor_tensor(out=c.ini[:, :, 0], in0=a1, in1=a0, op=ALU.subtract), reads=["a0", "a1"], writes=["ini_all"])
        s.op("vector", lambda e: e.tensor_tensor(out=a0, in0=sq_, in1=c.xinit[:, 0, :], op=ALU.mult), reads=[QS, "xinit", "ini_all"], writes=["a0"])
        s.op("vector", lambda e: e.tensor_tensor(out=a1, in0=cq, in1=c.xinit[:, 1, :], op=ALU.mult), reads=[QC, "xinit", "ini_all"], writes=["a1"])
        s.op("vector", lambda e: e.tensor_tensor(out=c.ini[:, :, 1], in0=a1, in1=a0, op=ALU.add), reads=["a0", "a1", "ini_all"], writes=["ini_all"])
    else:
        s.op("vector", lambda e: e.memset(c.ini[:], 0.0), writes=["ini_all"])
    gi = 0
    for t in range(NT):
        tsl = slice(t * 512, (t + 1) * 512)
        for cc in range(8):
            ybank = 4 + (cc % 2)
            for pp in range(4):
                p = cc * 4 + pp
                sl = gi % 2; gi += 1
                b_re, b_im = 2 * sl, 2 * sl + 1
                for ri, bank in ((0, b_re), (1, b_im)):
                    s.op("tensor", lambda e, ri=ri, bank=bank, p=p, cc=cc, tsl=tsl: e.matmul(
                        c.ps[:, bank, :], c.BtT[:, ri, p, :], c.ua[:, cc, tsl], start=True, stop=True),
                        reads=[("BtT", p // 8), ("ua", cc)], writes=[("ps", bank)])
                Cb = c.E[:, 0, p, :].unsqueeze(1).to_broadcast([128, 4, TC])
                Sb = c.E[:, 1, p, :].unsqueeze(1).to_broadcast([128, 4, TC])
                v4 = lambda ap: ap.rearrange("p (a b) -> p a b", a=4)
                pre = v4(c.ps[:, b_re, :]); pim = v4(c.ps[:, b_im, :])
                zre = c.z[:, sl, 0, :]; zim = c.z[:, sl, 1, :]
                m0 = c.mt[:, 0, :]; m1 = c.mt[:, 1, :]
                Zr, Zi, M0, M1 = ("z", sl, 0), ("z", sl, 1), "m0", "m1"
                s.op("vector", lambda e, pre=pre, Cb=Cb, m0=m0: e.tensor_tensor(out=v4(m0), in0=pre, in1=Cb, op=ALU.mult), reads=[("ps", b_re), "E"], writes=[M0])
                s.op("vector", lambda e, pim=pim, Sb=Sb, m1=m1: e.tensor_tensor(out=v4(m1), in0=pim, in1=Sb, op=ALU.mult), reads=[("ps", b_im), "E"], writes=[M1])
                s.op("vector", lambda e, zre=zre, m0=m0, m1=m1: e.tensor_tensor(out=zre, in0=m0, in1=m1, op=ALU.add), reads=[M0, M1], writes=[Zr])
                s.op("vector", lambda e, pim=pim, Cb=Cb, m0=m0: e.tensor_tensor(out=v4(m0), in0=pim, in1=Cb, op=ALU.mult), reads=[("ps", b_im), "E", Zr], writes=[M0])
                s.op("vector", lambda e, pre=pre, Sb=Sb, m1=m1: e.tensor_tensor(out=v4(m1), in0=pre, in1=Sb, op=ALU.mult), reads=[("ps", b_re), "E", Zr], writes=[M1])
                s.op("vector", lambda e, zim=zim, m0=m0, m1=m1: e.tensor_tensor(out=zim, in0=m0, in1=m1, op=ALU.subtract), reads=[M0, M1], writes=[Zi])
                wre = c.w[:, sl, 0, :]; wim = c.w[:, sl, 1, :]
                Wr, Wi, INI = ("w", sl, 0), ("w", sl, 1), ("ini", p)
                rb = rq(p).to_broadcast([128, TC])
                g_re = c.G128[:, 0, p:p + 1]; g_im = c.G128[:, 1, p:p + 1]
                for j in range(4):
                    js = slice(j * TC, (j + 1) * TC)
                    s.op("vector", lambda e, js=js, wre=wre, zre=zre, rb=rb, p=p: e.tensor_tensor_scan(
                        out=wre[:, js], data0=rb, data1=zre[:, js], initial=c.ini[:, p, 0:1], op0=ALU.mult, op1=ALU.add),
                        reads=[Zr, INI, "ini_all", "q3"], writes=[Wr])
                    s.op("vector", lambda e, js=js, wim=wim, zim=zim, rb=rb, p=p: e.tensor_tensor_scan(
                        out=wim[:, js], data0=rb, data1=zim[:, js], initial=c.ini[:, p, 1:2], op0=ALU.mult, op1=ALU.add),
                        reads=[Zi, INI, "ini_all", "q3"], writes=[Wi])
                    er = wre[:, j * TC + TC - 1:j * TC + TC]; ei = wim[:, j * TC + TC - 1:j * TC + TC]
                    s.op("vector", lambda e, ei=ei, g_im=g_im: e.tensor_scalar(out=c.itmp[:, 0:1], in0=ei, scalar1=g_im, scalar2=None, op0=ALU.mult),
                         reads=[Wi, "G128"], writes=["it0"])
                    s.op("vector", lambda e, ei=ei, g_re=g_re: e.tensor_scalar(out=c.itmp[:, 1:2], in0=ei, scalar1=g_re, scalar2=None, op0=ALU.mult),
                         reads=[Wi, "G128"], writes=["it1"])
                    s.op("vector", lambda e, er=er, g_re=g_re, p=p: e.scalar_tensor_tensor(out=c.ini[:, p, 0:1], in0=er, scalar=g_re, in1=c.itmp[:, 0:1],
                                                                                       op0=ALU.mult, op1=ALU.subtract),
                         reads=[Wr, "it0", "G128"], writes=[INI])
                    s.op("vector", lambda e, er=er, g_im=g_im, p=p: e.scalar_tensor_tensor(out=c.ini[:, p, 1:2], in0=er, scalar=g_im, in1=c.itmp[:, 1:2],
                                                                                       op0=ALU.mult, op1=ALU.add),
                         reads=[Wr, "it1", "G128", INI], writes=[INI])
                if full:
                    xre = c.xs[:, sl, 0, :]; nxi = c.xs[:, sl, 1, :]
                    Xr, Xi = ("xs", sl, 0), ("xs", sl, 1)
                    s.op("vector", lambda e, wre=wre, Cb=Cb, m0=m0: e.tensor_tensor(out=v4(m0), in0=v4(wre), in1=Cb, op=ALU.mult), reads=[Wr, "E", Zi], writes=[M0])
                    s.op("vector", lambda e, wim=wim, Sb=Sb, m1=m1: e.tensor_tensor(out=v4(m1), in0=v4(wim), in1=Sb, op=ALU.mult), reads=[Wi, "E", Zi], writes=[M1])
                    s.op("vector", lambda e, xre=xre, m0=m0, m1=m1: e.tensor_tensor(out=xre, in0=m0, in1=m1, op=ALU.subtract), reads=[M0, M1], writes=[Xr])
                    s.op("vector", lambda e, wre=wre, Sb=Sb, m0=m0: e.tensor_tensor(out=v4(m0), in0=v4(wre), in1=Sb, op=ALU.mult), reads=[Wr, "E", Xr], writes=[M0])
                    s.op("vector", lambda e, wim=wim, Cb=Cb, m1=m1: e.tensor_tensor(out=v4(m1), in0=v4(wim), in1=Cb, op=ALU.mult), reads=[Wi, "E", Xr], writes=[M1])
                    s.op("vector", lambda e, nxi=nxi, m0=m0, m1=m1: e.scalar_tensor_tensor(out=nxi, in0=m0, scalar=-1.0, in1=m1, op0=ALU.mult, op1=ALU.subtract),
                         reads=[M0, M1], writes=[Xi])
                    s.op("tensor", lambda e, p=p, xre=xre, ybank=ybank, pp=pp: e.matmul(c.ps[:, ybank, :], c.Cz[:, 0, p, :], xre, start=(pp == 0), stop=False),
                         reads=[Xr, "Cz"], writes=[("ps", ybank)])
                    s.op("tensor", lambda e, p=p, nxi=nxi, ybank=ybank, pp=pp: e.matmul(c.ps[:, ybank, :], c.Cz[:, 1, p, :], nxi, start=False, stop=(pp == 3)),
                         reads=[Xi, "Cz"], writes=[("ps", ybank)])
            if full:
                ysl = cc % 2
                s.op("vector", lambda e, cc=cc, tsl=tsl, ybank=ybank, ysl=ysl: e.scalar_tensor_tensor(
                    out=c.ypre[:, ysl, :], in0=c.ua[:, cc, tsl], scalar=c.dq[:, cc:cc + 1], in1=c.ps[:, ybank, :], op0=ALU.mult, op1=ALU.add),
                    reads=[("ua", cc), "dq", ("ps", ybank)], writes=[("ypre", ysl)])
                emit_gelu_src(c, c.ypre[:, ysl, :], [("ypre", ysl)], c.yst[:, ysl, :], c.ga[:, ysl, :], c.gb[:, ysl, :],
                              writes=[("yst", ysl)], tmpkeys=(("ga", ysl), ("gb", ysl)))
                ids.append(s.dma("sync", lambda e, cc=cc, tsl=tsl, ysl=ysl: e.dma_start(out=yag_out[cc * 128:(cc + 1) * 128, tsl], in_=c.yst[:, ysl, :]),
                                 reads=[("yst", ysl)], sem_key=("yst", ysl)))
    if not full:
        cq = c.qs[:, 5, :]; sq_ = c.qs[:, 4, :]
        a0 = c.qs[:, 9, :]; a1 = c.qs[:, 10, :]
        allini = [("ini", p) for p in range(NP)] + ["ini_all"]
        s.op("vector", lambda e: e.tensor_tensor(out=a0, in0=sq_, in1=c.ini[:, :, 1], op=ALU.mult), reads=[QS] + allini, writes=["a0"])
        s.op("vector", lambda e: e.tensor_tensor(out=a1, in0=cq, in1=c.ini[:, :, 0], op=ALU.mult), reads=[QC] + allini, writes=["a1"])
        s.op("vector", lambda e: e.tensor_tensor(out=c.xend[:, 0, :], in0=a1, in1=a0, op=ALU.add), reads=["a0", "a1"], writes=["xend"])
        s.op("vector", lambda e: e.tensor_tensor(out=a0, in0=sq_, in1=c.ini[:, :, 0], op=ALU.mult), reads=[QS, "xend"] + allini, writes=["a0"])
        s.op("vector", lambda e: e.tensor_tensor(out=a1, in0=cq, in1=c.ini[:, :, 1], op=ALU.mult), reads=[QC, "xend"] + allini, writes=["a1"])
        s.op("vector", lambda e: e.tensor_tensor(out=c.xend[:, 1, :], in0=a1, in1=a0, op=ALU.subtract), reads=["a0", "a1", "xend"], writes=["xend"])
    return ids


def s5_combine(c, xall_ap, oneh_ap):
    s = c.s
    s.dma("sync", lambda e: e.dma_start(out=c.xall[:], in_=xall_ap), writes=["xall"], sem_key="xall")
    s.dma("sync", lambda e: e.dma_start(out=c.oneh[:], in_=oneh_ap), writes=["oneh"], sem_key="oneh")
    s.op("vector", lambda e: e.memset(c.X[:], 0.0), writes=["X"])
    s.op("vector", lambda e: e.memset(c.xinit[:], 0.0), writes=["xinit"])
    a0 = c.qs[:, 9, :]; a1 = c.qs[:, 10, :]
    cur = 0
    for cidx in range(1, 8):
        nxt = 1 - cur
        xr = c.X[:, cur, 0, :]; xi = c.X[:, cur, 1, :]
        nr = c.X[:, nxt, 0, :]; ni = c.X[:, nxt, 1, :]
        er = c.xall[:, cidx - 1, 0, :]; ei = c.xall[:, cidx - 1, 1, :]
        Ar = c.A[:, 0, :]; Ai = c.A[:, 1, :]
        s.op("vector", lambda e, xr=xr, Ar=Ar: e.tensor_tensor(out=a0, in0=xr, in1=Ar, op=ALU.mult), reads=["X", "A"], writes=["a0"])
        s.op("vector", lambda e, xi=xi, Ai=Ai: e.tensor_tensor(out=a1, in0=xi, in1=Ai, op=ALU.mult), reads=["X", "A"], writes=["a1"])
        s.op("vector", lambda e: e.tensor_tensor(out=a0, in0=a0, in1=a1, op=ALU.subtract), reads=["a0", "a1"], writes=["a0"])
        s.op("vector", lambda e, nr=nr, er=er: e.tensor_tensor(out=nr, in0=a0, in1=er, op=ALU.add), reads=["a0", "xall", "X"], writes=["X"])
        s.op("vector", lambda e, xr=xr, Ai=Ai: e.tensor_tensor(out=a0, in0=xr, in1=Ai, op=ALU.mult), reads=["X", "A"], writes=["a0"])
        s.op("vector", lambda e, xi=xi, Ar=Ar: e.tensor_tensor(out=a1, in0=xi, in1=Ar, op=ALU.mult), reads=["X", "A"], writes=["a1"])
        s.op("vector", lambda e: e.tensor_tensor(out=a0, in0=a0, in1=a1, op=ALU.add), reads=["a0", "a1"], writes=["a0"])
        s.op("vector", lambda e, ni=ni, ei=ei: e.tensor_tensor(out=ni, in0=a0, in1=ei, op=ALU.add), reads=["a0", "xall", "X"], writes=["X"])
        for ri, src in ((0, nr), (1, ni)):
            s.op("vector", lambda e, ri=ri, src=src, cidx=cidx: e.scalar_tensor_tensor(
                out=c.xinit[:, ri, :], in0=src, scalar=c.oneh[:, cidx:cidx + 1], in1=c.xinit[:, ri, :], op0=ALU.mult, op1=ALU.add),
                reads=["X", "oneh", "xinit"], writes=["xinit"])
        cur = nxt


def wload_half(c, view_shape, src_ap, half):
    raise NotImplementedError


def emit_glu_sgu(c, nc, st, yagT, projT, w_glu, b_glu, ln_g, ln_b, wsT, b_s, ident, yaT, ybT):
    s = c.s
    T = lambda name, shp, dt=F32: st.enter_context(nc.sbuf_tensor(name, shp, dt))
    yag = T("yag", [128, 8, TOK], BF16)
    ug = T("ug", [128, 8, TOK], BF16)
    vg = T("vg", [128, 8, TOK], BF16)
    vn = T("vn", [128, 8, TOK], BF16)
    c.wring = T("wring", [128, 4, 8192], BF16); c.wslot = 0
    sq = T("sq", [128, 2, 512], BF16)
    ones = T("ones", [128, 128], BF16)
    idb = T("idb", [128, 128], BF16)
    par = T("par", [128, 3, 8])
    epsb = T("epsb", [128, 1])
    mean = T("mean", [128, 512]); msq = T("msq", [128, 512]); rstd = T("rstd", [128, 512]); t1 = T("t1", [128, 2, 512])
    sig = T("sig", [128, 2, 512]); stg = T("stg", [128, 2, 512])
    wsb = T("wsb", [128, 8, 128], BF16)
    bsf = T("bsf", [1, 1024]); bsh = T("bsh", [1, 1024], BF16); bsl = T("bsl", [1, 1024], BF16); bst = T("bst", [1, 1024])
    vT = T("vT", [128, 2, 128], BF16)
    ps = st.enter_context(nc.psum_tensor("ps", [128, 7, 512], F32))
    psT = st.enter_context(nc.psum_tensor("psT", [128, 2, 128], BF16))
    ids = []
    s.op("gpsimd", lambda e: e.memset(ones[:], 1.0), writes=["ones"])
    s.op("gpsimd", lambda e: e.memset(epsb[:], EPS), writes=["epsb"])
    for i, ap in enumerate((b_glu, ln_g, ln_b)):
        s.dma("sync", lambda e, i=i, ap=ap: e.dma_start(out=par[:, i, :], in_=ap.rearrange("(k p) -> p k", p=128), allow_slow_non_contiguous=True),
              writes=[("par", i)], sem_key=("par", i))
    s.dma("gpsimd", lambda e: e.dma_start(out=idb[:], in_=ident), writes=["idb"], sem_key="idb")
    s.dma("gpsimd", lambda e: e.dma_start(out=wsb[:], in_=wsT.rearrange("h s t -> s h t")), writes=["wsb"], sem_key="wsb")
    s.op("vector", lambda e: e.memset(wsb[64:128, :, 0:64], 0.0), reads=["wsb"], writes=["wsb"])
    s.dma("sync", lambda e: e.dma_start(out=bsf[:], in_=b_s.rearrange("(o h) t -> o (h t)", o=1)), writes=["bsf"], sem_key="bsf")
    s.op("vector", lambda e: e.tensor_copy(out=bsh[:], in_=bsf[:]), reads=["bsf"], writes=["bsh"])
    s.op("vector", lambda e: e.tensor_copy(out=bst[:], in_=bsh[:]), reads=["bsh"], writes=["bst"])
    s.op("vector", lambda e: e.tensor_tensor(out=bsl[:], in0=bsf[:], in1=bst[:], op=ALU.subtract), reads=["bsf", "bst"], writes=["bsl"])
    for cc in range(8):
        s.dma("gpsimd", lambda e, cc=cc: e.dma_start(out=yag[:, cc, :], in_=yagT[cc * 128:(cc + 1) * 128, :]), writes=[("yag", cc)], sem_key=("yag", cc))
    for cc in range(8):
        s.dma("gpsimd", lambda e, cc=cc: e.dma_start(out=ug[:, cc, :], in_=projT[1024 + cc * 128:1024 + (cc + 1) * 128, :]), writes=[("ug", cc)], sem_key=("ug", cc))
        s.dma("gpsimd", lambda e, cc=cc: e.dma_start(out=vg[:, cc, :], in_=projT[2048 + cc * 128:2048 + (cc + 1) * 128, :]), writes=[("vg", cc)], sem_key=("vg", cc))
    wv = w_glu.rearrange("(k p) f -> p k f", p=128)
    gi = 0
    for u in range(2):
        ws, wview = wload(c, [128, 8, 512], wv[:, :, u * 512:(u + 1) * 512], "glu")
        for m in range(4):
            mc = u * 4 + m
            for t in range(NT):
                tsl = slice(t * 512, (t + 1) * 512)
                bank = gi % 4; sl = gi % 2; gi += 1
                for k in range(8):
                    s.op("tensor", lambda e, wview=wview, k=k, m=m, tsl=tsl, bank=bank: e.matmul(
                        ps[:, bank, :], wview[:, k, m * 128:(m + 1) * 128], yag[:, k, tsl], start=(k == 0), stop=(k == 7)),
                        reads=[("w", ws), ("yag", k)], writes=[("ps", bank)])
                s.op("scalar", lambda e, bank=bank, sl=sl, mc=mc: e.activation(out=sig[:, sl, :], in_=ps[:, bank, :], func=AF.Sigmoid,
                                                                            bias=par[:, 0, mc:mc + 1], scale=1.0),
                     reads=[("ps", bank), ("par", 0)], writes=[("sig", sl)])
                s.op("vector", lambda e, sl=sl, mc=mc, tsl=tsl: e.tensor_tensor(out=stg[:, sl, :], in0=yag[:, mc, tsl], in1=sig[:, sl, :], op=ALU.mult),
                     reads=[("yag", mc), ("sig", sl)], writes=[("stg", sl)])
                ids.append(s.dma("sync", lambda e, mc=mc, tsl=tsl, sl=sl: e.dma_start(out=yaT[mc * 128:(mc + 1) * 128, tsl], in_=stg[:, sl, :]),
                                 reads=[("stg", sl)], sem_key=("stg", sl)))
    sqs = 0
    for t in range(NT):
        tsl = slice(t * 512, (t + 1) * 512)
        for cc in range(8):
            s.op("tensor", lambda e, cc=cc, tsl=tsl: e.matmul(ps[:, 4, :], ones[:], vg[:, cc, tsl], start=(cc == 0), stop=(cc == 7)),
                 reads=["ones", ("vg", cc)], writes=[("ps", 4)])
        for cc in range(8):
            sl = sqs; sqs ^= 1
            s.op("scalar", lambda e, cc=cc, tsl=tsl, sl=sl: e.activation(out=sq[:, sl, :], in_=vg[:, cc, tsl], func=AF.Square),
                 reads=[("vg", cc)], writes=[("sq", sl)])
            s.op("tensor", lambda e, cc=cc, sl=sl: e.matmul(ps[:, 5, :], ones[:], sq[:, sl, :], start=(cc == 0), stop=(cc == 7)),
                 reads=["ones", ("sq", sl)], writes=[("ps", 5)])
        s.op("scalar", lambda e: e.activation(out=mean[:], in_=ps[:, 4, :], func=AF.Copy, scale=1.0 / 1024), reads=[("ps", 4)], writes=["mean"])
        s.op("vector", lambda e: e.tensor_tensor(out=msq[:], in0=mean[:], in1=mean[:], op=ALU.mult), reads=["mean"], writes=["msq"])
        s.op("vector", lambda e: e.scalar_tensor_tensor(out=msq[:], in0=ps[:, 5, :], scalar=1.0 / 1024, in1=msq[:], op0=ALU.mult, op1=ALU.subtract),
             reads=[("ps", 5), "msq"], writes=["msq"])
        s.op("scalar", lambda e: e.activation(out=rstd[:], in_=msq[:], func=AF.Sqrt, bias=epsb[:, 0:1], scale=1.0), reads=["msq", "epsb"], writes=["rstd"])
        s.op("vector", lambda e: e.reciprocal(out=rstd[:], in_=rstd[:]), reads=["rstd"], writes=["rstd"])
        for cc in range(8):
            sl = cc % 2
            s.op("vector", lambda e, cc=cc, tsl=tsl, sl=sl: e.tensor_tensor(out=t1[:, sl, :], in0=vg[:, cc, tsl], in1=mean[:], op=ALU.subtract),
                 reads=[("vg", cc), "mean"], writes=[("t1", sl)])
            s.op("vector", lambda e, sl=sl: e.tensor_tensor(out=t1[:, sl, :], in0=t1[:, sl, :], in1=rstd[:], op=ALU.mult),
                 reads=[("t1", sl), "rstd"], writes=[("t1", sl)])
            s.op("vector", lambda e, cc=cc, tsl=tsl, sl=sl: e.tensor_scalar(out=vn[:, cc, tsl], in0=t1[:, sl, :], scalar1=par[:, 1, cc:cc + 1],
                                                                           scalar2=par[:, 2, cc:cc + 1], op0=ALU.mult, op1=ALU.add),
                 reads=[("t1", sl), ("par", 1), ("par", 2)], writes=[("vn", cc, t)])
        for hh in range(8):
            bank = 4 + 2 + (hh % 1)
            bank = 6
            for j in range(4):
                tok = slice(t * 512 + j * TC, t * 512 + (j + 1) * TC)
                vs = (hh * 4 + j) % 2
                s.op("tensor", lambda e, hh=hh, tok=tok, vs=vs: e.transpose(psT[:, vs, :], vn[:, hh, tok], idb[:]),
                     reads=[("vn", hh, t), "idb"], writes=[("psT", vs)])
                s.op("scalar", lambda e, vs=vs: e.activation(out=vT[:, vs, :], in_=psT[:, vs, :], func=AF.Copy), reads=[("psT", vs)], writes=[("vT", vs)])
                osl = slice(j * TC, (j + 1) * TC)
                s.op("tensor", lambda e, hh=hh, vs=vs, osl=osl: e.matmul(ps[:, 6, osl], vT[:, vs, :], wsb[:, hh, :], start=True, stop=False, skip_group_check=True),
                     reads=[("vT", vs), "wsb"], writes=[("ps", 6)])
                s.op("tensor", lambda e, hh=hh, osl=osl: e.matmul(ps[:, 6, osl], ones[0:1, :], bsh[0:1, hh * 128:(hh + 1) * 128], start=False, stop=False, skip_group_check=True),
                     reads=["ones", "bsh"], writes=[("ps", 6)])
                s.op("tensor", lambda e, hh=hh, osl=osl: e.matmul(ps[:, 6, osl], ones[0:1, :], bsl[0:1, hh * 128:(hh + 1) * 128], start=False, stop=True, skip_group_check=True),
                     reads=["ones", "bsl"], writes=[("ps", 6)])
            sl = hh % 2
            s.op("vector", lambda e, hh=hh, tsl=tsl, sl=sl: e.tensor_tensor(out=stg[:, sl, :], in0=ps[:, 6, :], in1=ug[:, hh, tsl], op=ALU.mult),
                 reads=[("ps", 6), ("ug", hh)], writes=[("stg", sl)])
            ids.append(s.dma("sync", lambda e, hh=hh, tsl=tsl, sl=sl: e.dma_start(out=ybT[hh * 128:(hh + 1) * 128, tsl], in_=stg[:, sl, :]),
                             reads=[("stg", sl)], sem_key=("stg", sl)))
    return ids


def emit_merge(c, nc, st, x1T, yaT, ybT, g_mix, w_a, w_b, w_gate, b_gate, w_out, x2T):
    s = c.s
    T = lambda name, shp, dt=F32: st.enter_context(nc.sbuf_tensor(name, shp, dt))
    c.h = T("h", [128, KC, TOK], BF16)
    c.wring = T("wring", [128, 4, 8192], BF16); c.wslot = 0
    ya = T("ya", [128, 8, TOK], BF16); yb = T("yb", [128, 8, TOK], BF16)
    mg = T("mg", [128, KC, TOK], BF16)
    xst = T("xst", [128, 4, 512]); sq = T("sq", [128, 2, 512], BF16)
    ones = T("ones", [128, 128], BF16); epsb = T("epsb", [128, 1]); rstd = T("rstd", [128, TOK])
    gain = T("gain", [128, KC]); bg = T("bg", [128, 2 * KC])
    sga = T("sga", [128, 2, 512]); tmp = T("tmp", [128, 2, 512]); ost = T("ost", [128, 2, 512])
    ps = st.enter_context(nc.psum_tensor("ps", [128, 8, 512], F32))
    ids = []
    s.op("gpsimd", lambda e: e.memset(ones[:], 1.0), writes=["ones"])
    s.op("gpsimd", lambda e: e.memset(epsb[:], EPS), writes=["epsb"])
    s.dma("sync", lambda e: e.dma_start(out=gain[:], in_=g_mix.rearrange("(k p) -> p k", p=128), allow_slow_non_contiguous=True), writes=["gain"], sem_key="gain")
    s.dma("sync", lambda e: e.dma_start(out=bg[:], in_=b_gate.rearrange("(k p) -> p k", p=128), allow_slow_non_contiguous=True), writes=["bg"], sem_key="bg")
    for cc in range(8):
        s.dma("gpsimd", lambda e, cc=cc: e.dma_start(out=ya[:, cc, :], in_=yaT[cc * 128:(cc + 1) * 128, :]), writes=[("ya", cc)], sem_key=("ya", cc))
        s.dma("gpsimd", lambda e, cc=cc: e.dma_start(out=yb[:, cc, :], in_=ybT[cc * 128:(cc + 1) * 128, :]), writes=[("yb", cc)], sem_key=("yb", cc))
    xv = x1T.rearrange("(k p) t -> p k t", p=128)
    xs = 0
    for t in range(NT):
        tsl = slice(t * 512, (t + 1) * 512)
        for k in range(KC):
            sl = xs % 4; xs += 1
            s.dma("sync", lambda e, k=k, tsl=tsl, sl=sl: e.dma_start(out=xst[:, sl, :], in_=xv[:, k, tsl]), writes=[("xst", sl)], sem_key=("xst", sl))
            s.op("scalar", lambda e, sl=sl: e.activation(out=sq[:, sl % 2, :], in_=xst[:, sl, :], func=AF.Square), reads=[("xst", sl)], writes=[("sq", sl % 2)])
            s.op("tensor", lambda e, sl=sl, k=k: e.matmul(ps[:, 7, :], ones[:], sq[:, sl % 2, :], start=(k == 0), stop=(k == KC - 1)),
                 reads=["ones", ("sq", sl % 2)], writes=[("ps", 7)])
        s.op("scalar", lambda e, tsl=tsl: e.activation(out=rstd[:, tsl], in_=ps[:, 7, :], func=AF.Sqrt, bias=epsb[:, 0:1], scale=1.0 / D),
             reads=[("ps", 7), "epsb"], writes=[("rstd", t)])
        s.op("vector", lambda e, tsl=tsl: e.reciprocal(out=rstd[:, tsl], in_=rstd[:, tsl]), reads=[("rstd", t)], writes=[("rstd", t)])
        for k in range(KC):
            sl = xs % 4; xs += 1
            s.dma("sync", lambda e, k=k, tsl=tsl, sl=sl: e.dma_start(out=xst[:, sl, :], in_=xv[:, k, tsl]), writes=[("xst", sl)], sem_key=("xst", sl))
            s.op("vector", lambda e, k=k, tsl=tsl, sl=sl: e.scalar_tensor_tensor(out=c.h[:, k, tsl], in0=xst[:, sl, :], scalar=gain[:, k:k + 1], in1=rstd[:, tsl],
                                                                                 op0=ALU.mult, op1=ALU.mult),
                 reads=[("xst", sl), "gain", ("rstd", t)], writes=[("h", k, t)])
    wa_v = w_a.rearrange("(k p) f -> p k f", p=128); wb_v = w_b.rearrange("(k p) f -> p k f", p=128)
    wg_v = w_gate.rearrange("(k p) f -> p k f", p=128)
    gi = 0
    for n4 in range(4):
        cs = slice(n4 * 512, (n4 + 1) * 512)
        sa, va = wload(c, [128, 8, 512], wa_v[:, :, cs], "a")
        sga_s, vga = wload(c, [128, KC, 512], wg_v[:, :, cs], "ga")
        sb, vb = wload(c, [128, 8, 512], wb_v[:, :, cs], "b")
        sgb_s, vgb = wload(c, [128, KC, 512], wg_v[:, :, D + n4 * 512:D + (n4 + 1) * 512], "gb")
        for m in range(4):
            n = n4 * 4 + m
            msl = slice(m * 128, (m + 1) * 128)
            for t in range(NT):
                tsl = slice(t * 512, (t + 1) * 512)
                pb = (gi % 2) * 4; sl = gi % 2; gi += 1
                for (bank, wsl, wvw, src, nk, skey) in ((pb, sa, va, ya, 8, "ya"), (pb + 1, sga_s, vga, c.h, KC, "h"), (pb + 2, sb, vb, yb, 8, "yb"), (pb + 3, sgb_s, vgb, c.h, KC, "h")):
                    for k in range(nk):
                        rk = (skey, k, t) if skey == "h" else (skey, k)
                        s.op("tensor", lambda e, bank=bank, wvw=wvw, src=src, k=k, nk=nk, msl=msl, tsl=tsl: e.matmul(
                            ps[:, bank, :], wvw[:, k, msl], src[:, k, tsl], start=(k == 0), stop=(k == nk - 1)),
                            reads=[("w", wsl), rk], writes=[("ps", bank)])
                s.op("scalar", lambda e, pb=pb, sl=sl, n=n: e.activation(out=sga[:, sl, :], in_=ps[:, pb + 1, :], func=AF.Sigmoid, bias=bg[:, n:n + 1], scale=1.0),
                     reads=[("ps", pb + 1), "bg"], writes=[("sga", sl)])
                s.op("vector", lambda e, pb=pb, sl=sl: e.tensor_tensor(out=tmp[:, sl, :], in0=ps[:, pb, :], in1=sga[:, sl, :], op=ALU.mult),
                     reads=[("ps", pb), ("sga", sl)], writes=[("tmp", sl)])
                s.op("scalar", lambda e, pb=pb, sl=sl, n=n: e.activation(out=sga[:, sl, :], in_=ps[:, pb + 3, :], func=AF.Sigmoid, bias=bg[:, KC + n:KC + n + 1], scale=1.0),
                     reads=[("ps", pb + 3), "bg", ("tmp", sl)], writes=[("sga", sl)])
                s.op("vector", lambda e, pb=pb, sl=sl: e.tensor_tensor(out=sga[:, sl, :], in0=ps[:, pb + 2, :], in1=sga[:, sl, :], op=ALU.mult),
                     reads=[("ps", pb + 2), ("sga", sl)], writes=[("sga", sl)])
                s.op("vector", lambda e, sl=sl, n=n, tsl=tsl: e.tensor_tensor(out=mg[:, n, tsl], in0=tmp[:, sl, :], in1=sga[:, sl, :], op=ALU.add),
                     reads=[("tmp", sl), ("sga", sl)], writes=[("mg", n, t)])
    wo_v = w_out.rearrange("(k p) f -> p k f", p=128)
    gi = 0
    for n4 in range(4):
        so, vo = wload(c, [128, KC, 512], wo_v[:, :, n4 * 512:(n4 + 1) * 512], "o")
        for m in range(4):
            n = n4 * 4 + m
            msl = slice(m * 128, (m + 1) * 128)
            for t in range(NT):
                tsl = slice(t * 512, (t + 1) * 512)
                bank = gi % 4; sl = gi % 2; xsl = gi % 4; gi += 1
                for k in range(KC):
                    s.op("tensor", lambda e, bank=bank, vo=vo, k=k, msl=msl, tsl=tsl: e.matmul(ps[:, bank, :], vo[:, k, msl], mg[:, k, tsl], start=(k == 0), stop=(k == KC - 1)),
                         reads=[("w", so), ("mg", k, t)], writes=[("ps", bank)])
                s.dma("sync", lambda e, n=n, tsl=tsl, xsl=xsl: e.dma_start(out=xst[:, xsl, :], in_=xv[:, n, tsl]), writes=[("xst", xsl)], sem_key=("xst", xsl))
                s.op("vector", lambda e, bank=bank, sl=sl, xsl=xsl: e.tensor_tensor(out=ost[:, sl, :], in0=ps[:, bank, :], in1=xst[:, xsl, :], op=ALU.add),
                     reads=[("ps", bank), ("xst", xsl)], writes=[("ost", sl)])
                ids.append(s.dma("sync", lambda e, n=n, tsl=tsl, sl=sl: e.dma_start(out=x2T[n * 128:(n + 1) * 128, tsl], in_=ost[:, sl, :]),
                                 reads=[("ost", sl)], sem_key=("ost", sl)))
    return ids


def emit_final_norm(c, gidx, outT, stage):
    s = c.s
    ids = []
    gi = 0
    for t in range(NT):
        tsl = slice(t * 512, (t + 1) * 512)
        bank = 7
        for k in range(KC):
            sl = c.sqslot; c.sqslot ^= 1
            s.op("scalar", lambda e, k=k, sl=sl, tsl=tsl: e.activation(out=c.sq[:, sl, :], in_=c.x[:, k, tsl], func=AF.Square),
                 reads=[("x", k, t)], writes=[("sq", sl)])
            s.op("tensor", lambda e, k=k, sl=sl, bank=bank: e.matmul(c.ps[:, bank, :], c.ones[:], c.sq[:, sl, :], start=(k == 0), stop=(k == KC - 1)),
                 reads=[("sq", sl), ("ones",)], writes=[("ps", bank)])
        s.op("scalar", lambda e, tsl=tsl, bank=bank: e.activation(out=c.rstd[:, tsl], in_=c.ps[:, bank, :], func=AF.Sqrt, bias=c.epsb[:, 0:1], scale=1.0 / D),
             reads=[("ps", bank), ("epsb",)], writes=[("rstd", t)])
        s.op("vector", lambda e, tsl=tsl: e.reciprocal(out=c.rstd[:, tsl], in_=c.rstd[:, tsl]), reads=[("rstd", t)], writes=[("rstd", t)])
        for k in range(KC):
            sl = gi % 2; gi += 1
            s.op("vector", lambda e, k=k, tsl=tsl, sl=sl: e.scalar_tensor_tensor(out=stage[:, sl, :], in0=c.x[:, k, tsl], scalar=c.gains[:, gidx, k:k + 1],
                                                                                 in1=c.rstd[:, tsl], op0=ALU.mult, op1=ALU.mult),
                 reads=[("x", k, t), ("gain", gidx), ("rstd", t)], writes=[("fstage", sl)])
            ids.append(s.dma("sync", lambda e, k=k, tsl=tsl, sl=sl: e.dma_start(out=outT[k * 128:(k + 1) * 128, tsl], in_=stage[:, sl, :]),
                             reads=[("fstage", sl)], sem_key=("fstage", sl)))
    return ids


def s5_layouts(a_re, a_im, log_dt, b_re, b_im, c_re, c_im, d_skip):
    NP = 32
    def qlay(v):
        return v.reshape(NP, 2, 64).transpose(1, 2, 0).reshape(128, NP)
    ldt2 = np.repeat(log_dt[:, None], 64, axis=1)
    aq = np.stack([qlay(a_re), qlay(a_im), qlay(ldt2)], axis=1).astype(np.float32)
    def rlay(v):
        return v.reshape(NP, 128).reshape(-1)
    arow1 = np.stack([rlay(a_re), rlay(a_im), rlay(ldt2)], axis=0)
    arow = np.ascontiguousarray(np.broadcast_to(arow1[None], (128, 3, NP * 128))).astype(np.float32)
    BT = np.zeros((128, 2, NP, 128), np.float32)
    Cz = np.zeros((128, 2, NP, 128), np.float32)
    for p in range(NP):
        for g2 in range(2):
            g = 2 * p + g2
            r0 = 32 * (p % 4) + 16 * g2
            for ri, (bm, cm) in enumerate(((b_re, c_re), (b_im, c_im))):
                BT[r0:r0 + 16, ri, p, g2 * 64:(g2 + 1) * 64] = bm[g].T
                Cz[g2 * 64:(g2 + 1) * 64, ri, p, r0:r0 + 16] = cm[g].T
    dq = np.ascontiguousarray(d_skip.reshape(8, 128).T).astype(np.float32)
    return dict(aq=aq, arow=arow, BT=BT.reshape(128, 2, NP * 128), Cz=Cz, dq=dq)


def _dram_in(nc, name, shape):
    return nc.dram_tensor(name, list(shape), F32, kind="ExternalInput").ap()


def _dram_out(nc, name, shape):
    return nc.dram_tensor(name, list(shape), F32, kind="ExternalOutput").ap()


def build_L1():
    nc = bass.Bass("TRN2", target_bir_lowering=False)
    xT = _dram_in(nc, "xT", [D, TOK]); g1 = _dram_in(nc, "g1", [D]); g2 = _dram_in(nc, "g2", [D])
    wg = _dram_in(nc, "wg", [D, DFF]); wu = _dram_in(nc, "wu", [D, DFF]); wd = _dram_in(nc, "wd", [DFF, D])
    win = _dram_in(nc, "win", [D, MIXIN])
    x1T = _dram_out(nc, "x1T", [D, TOK]); projT = _dram_out(nc, "projT", [MIXIN, TOK])
    c = Ctx(); c.nc = nc; c.s = Sched()
    with contextlib.ExitStack() as st:
        alloc_common(nc, st, c)
        stage = st.enter_context(nc.sbuf_tensor("stage", [128, 2, 512], F32))
        tmpa = st.enter_context(nc.sbuf_tensor("tmpa", [128, 2, 512], F32))
        tmpb = st.enter_context(nc.sbuf_tensor("tmpb", [128, 2, 512], F32))
        emit_consts(c)
        load_gain(c, 0, g1); load_gain(c, 1, g2)
        load_xT(c, xT)
        emit_ffn(c, 0, wg, wu, wd)
        ids = store_T(c, c.x, "x", x1T)
        emit_rmsnorm(c, 1)
        ids += emit_inproj(c, win, projT, stage, tmpa, tmpb)
        c.s.emit(nc, final_wait_ops=ids)
    return nc


def build_S5(full):
    nc = bass.Bass("TRN2", target_bir_lowering=False)
    uaT = _dram_in(nc, "uaT", [1024, TOK]); aq = _dram_in(nc, "aq_in", [128, 3, NP])
    arow = _dram_in(nc, "arow_in", [128, 3, NP * 128]); BT = _dram_in(nc, "BT_in", [128, 2, NP * 128])
    c = Ctx(); c.nc = nc; c.s = Sched(); s = c.s
    with contextlib.ExitStack() as st:
        s5_alloc(nc, st, c, full)
        for cc in range(8):
            s.dma("gpsimd", lambda e, cc=cc: e.dma_start(out=c.ua[:, cc, :], in_=uaT[cc * 128:(cc + 1) * 128, :]), writes=[("ua", cc)], sem_key=("ua", cc))
        s5_setup(c, aq, arow, BT, full)
        if full:
            Cz = _dram_in(nc, "Cz_in", [128, 2, NP, 128]); dq = _dram_in(nc, "dq_in", [128, 8])
            xall = _dram_in(nc, "xall_in", [128, 8, 2, NP]); oneh = _dram_in(nc, "oneh_in", [128, 8])
            yag = _dram_out(nc, "yagT", [1024, TOK])
            s.dma("gpsimd", lambda e: e.dma_start(out=c.Cz[:], in_=Cz), writes=["Cz"], sem_key="Cz")
            s.dma("sync", lambda e: e.dma_start(out=c.dq[:], in_=dq), writes=["dq"], sem_key="dq")
            s5_combine(c, xall, oneh)
            ids = s5_main(c, True, yag)
        else:
            xend = _dram_out(nc, "xend_out", [128, 2, NP])
            s5_main(c, False)
            ids = [s.dma("sync", lambda e: e.dma_start(out=xend, in_=c.xend[:]), reads=["xend"], sem_key="xe")]
        s.emit(nc, final_wait_ops=ids)
    return nc


def build_LB1():
    nc = bass.Bass("TRN2", target_bir_lowering=False)
    yagT = _dram_in(nc, "yagT_in", [1024, TOK]); projT = _dram_in(nc, "projT_in", [MIXIN, TOK])
    w_glu = _dram_in(nc, "w_glu", [1024, 1024]); b_glu = _dram_in(nc, "b_glu", [1024])
    ln_g = _dram_in(nc, "ln_g", [1024]); ln_b = _dram_in(nc, "ln_b", [1024])
    wsT = _dram_in(nc, "wsT", [8, 128, 128]); b_s = _dram_in(nc, "b_s", [8, 128]); ident = _dram_in(nc, "ident", [128, 128])
    yaT = _dram_out(nc, "yaT", [1024, TOK]); ybT = _dram_out(nc, "ybT", [1024, TOK])
    c = Ctx(); c.nc = nc; c.s = Sched()
    with contextlib.ExitStack() as st:
        ids = emit_glu_sgu(c, nc, st, yagT, projT, w_glu, b_glu, ln_g, ln_b, wsT, b_s, ident, yaT, ybT)
        c.s.emit(nc, final_wait_ops=ids)
    return nc


def build_LB2():
    nc = bass.Bass("TRN2", target_bir_lowering=False)
    x1T = _dram_in(nc, "x1T_in", [D, TOK]); yaT = _dram_in(nc, "yaT_in", [1024, TOK]); ybT = _dram_in(nc, "ybT_in", [1024, TOK])
    g_mix = _dram_in(nc, "g_mix", [D]); w_a = _dram_in(nc, "w_a", [1024, D]); w_b = _dram_in(nc, "w_b", [1024, D])
    w_gate = _dram_in(nc, "w_gate", [D, 2 * D]); b_gate = _dram_in(nc, "b_gate", [2 * D]); w_out = _dram_in(nc, "w_out", [D, D])
    x2T = _dram_out(nc, "x2T", [D, TOK])
    c = Ctx(); c.nc = nc; c.s = Sched()
    with contextlib.ExitStack() as st:
        ids = emit_merge(c, nc, st, x1T, yaT, ybT, g_mix, w_a, w_b, w_gate, b_gate, w_out, x2T)
        c.s.emit(nc, final_wait_ops=ids)
    return nc


def build_L3():
    nc = bass.Bass("TRN2", target_bir_lowering=False)
    xT = _dram_in(nc, "xT", [D, TOK]); g1 = _dram_in(nc, "g1", [D]); g2 = _dram_in(nc, "g2", [D])
    wg = _dram_in(nc, "wg", [D, DFF]); wu = _dram_in(nc, "wu", [D, DFF]); wd = _dram_in(nc, "wd", [DFF, D])
    outT = _dram_out(nc, "outT", [D, TOK])
    c = Ctx(); c.nc = nc; c.s = Sched()
    with contextlib.ExitStack() as st:
        alloc_common(nc, st, c)
        stage = st.enter_context(nc.sbuf_tensor("stage", [128, 2, 512], F32))
        emit_consts(c)
        load_gain(c, 0, g1); load_gain(c, 1, g2)
        load_xT(c, xT)
        emit_ffn(c, 0, wg, wu, wd)
        ids = emit_final_norm(c, 1, outT, stage)
        c.s.emit(nc, final_wait_ops=ids)
    return nc


NCORES = 8


def _run(nc, maps):
    return run_bass_kernel_spmd(nc, maps, core_ids=list(range(NCORES))).results


def kernel(x, ffn1_norm, ffn1_w_gate, ffn1_w_up, ffn1_w_down, mix_norm, w_in,
           s5_a_re, s5_a_im, s5_log_dt, s5_b_re, s5_b_im, s5_c_re, s5_c_im, s5_d,
           s5_w_glu, s5_b_glu, sgu_ln_g, sgu_ln_b, sgu_w_s, sgu_b_s,
           w_branch_a, w_branch_b, w_gate, b_gate, w_out,
           ffn2_norm, ffn2_w_gate, ffn2_w_up, ffn2_w_down, final_norm):
    f = lambda a: np.ascontiguousarray(np.asarray(a, dtype=np.float32))
    x = f(x)[0]
    n = NCORES
    xTs = [np.ascontiguousarray(x[i * TOK:(i + 1) * TOK].T) for i in range(n)]
    w1 = dict(g1=f(ffn1_norm)[0], g2=f(mix_norm)[0], wg=f(ffn1_w_gate)[0], wu=f(ffn1_w_up)[0], wd=f(ffn1_w_down)[0], win=f(w_in)[0])
    r1 = _run(build_L1(), [dict(w1, xT=xTs[i]) for i in range(n)])
    x1T = [r1[i]["x1T"] for i in range(n)]; projT = [r1[i]["projT"] for i in range(n)]
    lay = s5_layouts(f(s5_a_re)[0], f(s5_a_im)[0], f(s5_log_dt)[0], f(s5_b_re)[0], f(s5_b_im)[0], f(s5_c_re)[0], f(s5_c_im)[0], f(s5_d)[0])
    base = [dict(uaT=np.ascontiguousarray(projT[i][0:1024]), aq_in=lay["aq"], arow_in=lay["arow"], BT_in=lay["BT"]) for i in range(n)]
    r2 = _run(build_S5(False), base)
    xall = np.ascontiguousarray(np.stack([r2[i]["xend_out"] for i in range(n)], axis=1))
    maps = []
    for i in range(n):
        oh = np.zeros((128, 8), np.float32); oh[:, i] = 1.0
        maps.append(dict(base[i], Cz_in=lay["Cz"], dq_in=lay["dq"], xall_in=xall, oneh_in=oh))
    r3 = _run(build_S5(True), maps)
    wsT = np.ascontiguousarray(np.transpose(f(sgu_w_s)[0], (0, 2, 1)))
    wl = dict(w_glu=f(s5_w_glu)[0], b_glu=f(s5_b_glu)[0], ln_g=f(sgu_ln_g)[0], ln_b=f(sgu_ln_b)[0], wsT=wsT, b_s=f(sgu_b_s)[0],
              ident=np.eye(128, dtype=np.float32))
    r4 = _run(build_LB1(), [dict(wl, yagT_in=r3[i]["yagT"], projT_in=projT[i]) for i in range(n)])
    wm = dict(g_mix=f(mix_norm)[0], w_a=f(w_branch_a)[0], w_b=f(w_branch_b)[0], w_gate=f(w_gate)[0], b_gate=f(b_gate)[0], w_out=f(w_out)[0])
    r5 = _run(build_LB2(), [dict(wm, x1T_in=x1T[i], yaT_in=r4[i]["yaT"], ybT_in=r4[i]["ybT"]) for i in range(n)])
    w3 = dict(g1=f(ffn2_norm)[0], g2=f(final_norm), wg=f(ffn2_w_gate)[0], wu=f(ffn2_w_up)[0], wd=f(ffn2_w_down)[0])
    r6 = _run(build_L3(), [dict(w3, xT=r5[i]["x2T"]) for i in range(n)])
    out = np.concatenate([r6[i]["outT"].T for i in range(n)], axis=0)
    return np.ascontiguousarray(out[None].astype(np.float32))
```

```python
import contextlib
import numpy as np
import concourse.bass as bass
import concourse.mybir as mybir
from concourse.bass_utils import run_bass_kernel_spmd

ENGINES = ("tensor", "vector", "scalar", "gpsimd", "sync")
RELAX_BULK = False


class Sched:
    def __init__(self, self_edges=True):
        self.ops = []
        self.last_w = {}
        self.readers = {}
        self.self_edges = self_edges

    def _add(self, eng, emit, reads, writes, dma_key=None, size=0, strict=False):
        i = len(self.ops)
        deps = set()
        for r in reads:
            if r in self.last_w:
                deps.add(self.last_w[r])
        for w in writes:
            if w in self.last_w:
                deps.add(self.last_w[w])
            deps.update(self.readers.get(w, ()))
        deps.discard(i)
        self.ops.append(dict(eng=eng, emit=emit, deps=sorted(deps), dma_key=dma_key, size=size, strict=strict))
        for r in reads:
            self.readers.setdefault(r, []).append(i)
        for w in writes:
            self.last_w[w] = i
            self.readers[w] = []
        return i

    def op(self, eng, emit, reads=(), writes=(), size=0, strict=False):
        return self._add(eng, emit, list(reads), list(writes), size=size, strict=strict)

    def dma(self, eng, emit, reads=(), writes=(), sem_key=None):
        assert sem_key is not None
        return self._add(eng, emit, list(reads), list(writes), dma_key=sem_key)

    def _self_skip(self, p, o):
        if p["eng"] != o["eng"]:
            return False
        if p["eng"] == "tensor" or not self.self_edges:
            return True
        return RELAX_BULK and p["size"] >= 256 and not o["strict"]

    def emit(self, nc, final_wait_ops=()):
        ops = self.ops
        need_inc = [False] * len(ops)
        for i, o in enumerate(ops):
            for d in o["deps"]:
                p = ops[d]
                if p["dma_key"] is not None:
                    continue
                if self._self_skip(p, o):
                    continue
                need_inc[d] = True
        eng_cnt = {e: 0 for e in ENGINES}
        inc_val = [None] * len(ops)
        dma_cnt = {}
        for i, o in enumerate(ops):
            if o["dma_key"] is not None:
                k = o["dma_key"]
                dma_cnt[k] = dma_cnt.get(k, 0) + 16
                inc_val[i] = dma_cnt[k]
            elif need_inc[i]:
                eng_cnt[o["eng"]] += 1
                inc_val[i] = eng_cnt[o["eng"]]
        import contextlib
        with contextlib.ExitStack() as st:
            esem = {e: st.enter_context(nc.semaphore("e_" + e)) for e in ENGINES}
            dsem = {k: st.enter_context(nc.semaphore("d_%d" % j)) for j, k in enumerate(dma_cnt)}
            block = st.enter_context(nc.Block())
            per_eng = {e: [i for i, o in enumerate(ops) if o["eng"] == e] for e in ENGINES}

            def make(e):
                def body(eng):
                    waited = {}
                    for i in per_eng[e]:
                        o = ops[i]
                        for d in o["deps"]:
                            p = ops[d]
                            if p["dma_key"] is not None:
                                key = ("d", p["dma_key"])
                                sem = dsem[p["dma_key"]]
                            else:
                                if self._self_skip(p, o):
                                    continue
                                key = ("e", p["eng"])
                                sem = esem[p["eng"]]
                            v = inc_val[d]
                            if waited.get(key, 0) >= v:
                                continue
                            eng.wait_ge(sem, v)
                            waited[key] = v
                        ins = o["emit"](eng)
                        if o["dma_key"] is not None:
                            ins.then_inc(dsem[o["dma_key"]], 16)
                        elif need_inc[i]:
                            ins.then_inc(esem[e], 1)
                    if e == "sync":
                        for i in final_wait_ops:
                            p = ops[i]
                            sem = dsem[p["dma_key"]] if p["dma_key"] is not None else esem[p["eng"]]
                            eng.wait_ge(sem, inc_val[i])
                return body
            for e in ENGINES:
                if per_eng[e] or e == "sync":
                    getattr(block, e)(make(e))
        return nc


F32 = mybir.dt.float32
BF16 = mybir.dt.bfloat16
AF = mybir.ActivationFunctionType
ALU = mybir.AluOpType

D = 2048
KC = D // 128
TOK = 1024
NT = TOK // 512
DFF = 5632
NFB = DFF // 512
EPS = 1e-6


class Ctx:
    pass


def alloc_common(nc, st, c):
    c.x = st.enter_context(nc.sbuf_tensor("x", [128, KC, TOK], F32))
    c.h = st.enter_context(nc.sbuf_tensor("h", [128, KC, TOK], BF16))
    c.wring = st.enter_context(nc.sbuf_tensor("wring", [128, 4, 8192], BF16))
    c.hid = st.enter_context(nc.sbuf_tensor("hid", [128, 2, 4, TOK], BF16))
    c.sq = st.enter_context(nc.sbuf_tensor("sq", [128, 2, 512], BF16))
    c.rstd = st.enter_context(nc.sbuf_tensor("rstd", [128, TOK], F32))
    c.silu = st.enter_context(nc.sbuf_tensor("silu", [128, 2, 512], F32))
    c.ones = st.enter_context(nc.sbuf_tensor("ones", [128, 128], BF16))
    c.gains = st.enter_context(nc.sbuf_tensor("gains", [128, 4, KC], F32))
    c.epsb = st.enter_context(nc.sbuf_tensor("epsb", [128, 1], F32))
    c.ps = st.enter_context(nc.psum_tensor("ps", [128, 8, 512], F32))
    c.wslot = 0
    c.sqslot = 0
    c.silslot = 0


def emit_consts(c):
    s = c.s
    s.op("gpsimd", lambda e: e.memset(c.ones[:], 1.0), writes=[("ones",)])
    s.op("gpsimd", lambda e: e.memset(c.epsb[:], EPS), writes=[("epsb",)])


def load_gain(c, idx, g_ap):
    c.s.dma("sync", lambda e: e.dma_start(out=c.gains[:, idx, :], in_=g_ap.rearrange("(k p) -> p k", p=128),
                                          allow_slow_non_contiguous=True),
            writes=[("gain", idx)], sem_key=("gain", idx))


def emit_rmsnorm(c, gidx, out_key="h"):
    s = c.s
    for t in range(NT):
        tsl = slice(t * 512, (t + 1) * 512)
        bank = 7
        for k in range(KC):
            sl = c.sqslot; c.sqslot ^= 1
            s.op("scalar", lambda e, k=k, sl=sl, tsl=tsl: e.activation(out=c.sq[:, sl, :], in_=c.x[:, k, tsl], func=AF.Square),
                 reads=[("x", k, t)], writes=[("sq", sl)])
            s.op("tensor", lambda e, k=k, sl=sl, bank=bank: e.matmul(c.ps[:, bank, :], c.ones[:], c.sq[:, sl, :],
                                                          start=(k == 0), stop=(k == KC - 1)),
                 reads=[("sq", sl), ("ones",)], writes=[("ps", bank)])
        s.op("scalar", lambda e, tsl=tsl, bank=bank: e.activation(out=c.rstd[:, tsl], in_=c.ps[:, bank, :], func=AF.Sqrt,
                                              bias=c.epsb[:, 0:1], scale=1.0 / D),
             reads=[("ps", bank), ("epsb",)], writes=[("rstd", t)])
        s.op("vector", lambda e, tsl=tsl: e.reciprocal(out=c.rstd[:, tsl], in_=c.rstd[:, tsl]),
             reads=[("rstd", t)], writes=[("rstd", t)])
        for k in range(KC):
            s.op("vector", lambda e, k=k, tsl=tsl: e.scalar_tensor_tensor(
                out=c.h[:, k, tsl], in0=c.x[:, k, tsl], scalar=c.gains[:, gidx, k:k + 1],
                in1=c.rstd[:, tsl], op0=ALU.mult, op1=ALU.mult),
                 reads=[("x", k, t), ("gain", gidx), ("rstd", t)], writes=[(out_key, k, t)])


def wload(c, view_shape, src_ap, tag):
    slot = c.wslot; c.wslot = (c.wslot + 1) % 4
    n = 1
    for d in view_shape[1:]:
        n *= d
    assert n <= 8192
    flat = c.wring[:, slot, 0:n]
    if len(view_shape) == 3:
        view = flat.rearrange("p (a b) -> p a b", a=view_shape[1])
    else:
        view = flat
    c.s.dma("gpsimd", lambda e: e.dma_start(out=view, in_=src_ap), writes=[("w", slot)], sem_key=("w", slot))
    return slot, view


def emit_ffn(c, gidx, wg, wu, wd):
    s = c.s
    emit_rmsnorm(c, gidx)
    wg_v = wg.rearrange("(k p) f -> p k f", p=128)
    wu_v = wu.rearrange("(k p) f -> p k f", p=128)
    wd_v = wd.rearrange("(m p) d -> p m d", p=128)
    for b in range(NFB):
        hb = b % 2
        gs, gv = wload(c, [128, KC, 512], wg_v[:, :, b * 512:(b + 1) * 512], "g")
        us, uv = wload(c, [128, KC, 512], wu_v[:, :, b * 512:(b + 1) * 512], "u")
        ds_, dv = wload(c, [128, 4, D], wd_v[:, b * 4:(b + 1) * 4, :], "d")
        for m in range(4):
            for kind, (ws, wv) in enumerate(((gs, gv), (us, uv))):
                for t in range(NT):
                    bank = kind * 2 + t
                    for k in range(KC):
                        s.op("tensor", lambda e, wv=wv, k=k, m=m, t=t, bank=bank: e.matmul(
                            c.ps[:, bank, :], wv[:, k, m * 128:(m + 1) * 128], c.h[:, k, t * 512:(t + 1) * 512],
                            start=(k == 0), stop=(k == KC - 1)),
                            reads=[("w", ws), ("h", k, t)], writes=[("ps", bank)])
            for t in range(NT):
                sl = c.silslot; c.silslot ^= 1
                s.op("scalar", lambda e, t=t, sl=sl: e.activation(out=c.silu[:, sl, :], in_=c.ps[:, t, :], func=AF.Silu),
                     reads=[("ps", t)], writes=[("silu", sl)])
                s.op("vector", lambda e, t=t, sl=sl, m=m, hb=hb: e.tensor_tensor(
                    out=c.hid[:, hb, m, t * 512:(t + 1) * 512], in0=c.ps[:, 2 + t, :], in1=c.silu[:, sl, :], op=ALU.mult),
                     reads=[("ps", 2 + t), ("silu", sl)], writes=[("hid", hb, m, t)])
        gi = 0
        for n in range(KC):
            for t in range(NT):
                bank = 4 + (gi % 4); gi += 1
                for m in range(4):
                    s.op("tensor", lambda e, n=n, t=t, m=m, bank=bank, hb=hb, dv=dv: e.matmul(
                        c.ps[:, bank, :], dv[:, m, n * 128:(n + 1) * 128], c.hid[:, hb, m, t * 512:(t + 1) * 512],
                        start=(m == 0), stop=(m == 3)),
                        reads=[("w", ds_), ("hid", hb, m, t)], writes=[("ps", bank)])
                s.op("vector", lambda e, n=n, t=t, bank=bank: e.scalar_tensor_tensor(
                    out=c.x[:, n, t * 512:(t + 1) * 512], in0=c.ps[:, bank, :], scalar=0.5,
                    in1=c.x[:, n, t * 512:(t + 1) * 512], op0=ALU.mult, op1=ALU.add),
                     reads=[("ps", bank), ("x", n, t)], writes=[("x", n, t)])


def load_x(c, x_ap):
    raise NotImplementedError


def load_xT(c, xT_ap):
    v = xT_ap.rearrange("(k p) t -> p k t", p=128)
    for k in range(KC):
        c.s.dma("sync", lambda e, k=k: e.dma_start(out=c.x[:, k, :], in_=v[:, k, :]),
                writes=[("x", k, t) for t in range(NT)], sem_key=("xld", k))


def store_T(c, src, key, outT_ap):
    v = outT_ap.rearrange("(k p) t -> p k t", p=128)
    ids = []
    for k in range(KC):
        ids.append(c.s.dma("sync", lambda e, k=k: e.dma_start(out=v[:, k, :], in_=src[:, k, :]),
                           reads=[(key, k, t) for t in range(NT)], sem_key=("st", k)))
    return ids


GELU_C1 = 0.044715
GELU_C2 = 1.5957691216057308


def emit_gelu_from_psum(c, bank, out_ap, tmp_a, tmp_b, reads, writes, tmpkeys):
    s = c.s
    ka, kb = tmpkeys
    s.op("scalar", lambda e: e.activation(out=tmp_a, in_=c.ps[:, bank, :], func=AF.Square),
         reads=reads, writes=[ka])
    s.op("vector", lambda e: e.tensor_scalar(out=tmp_a, in0=tmp_a, scalar1=GELU_C1, scalar2=1.0, op0=ALU.mult, op1=ALU.add),
         reads=[ka], writes=[ka])
    s.op("vector", lambda e: e.tensor_tensor(out=tmp_a, in0=c.ps[:, bank, :], in1=tmp_a, op=ALU.mult),
         reads=reads + [ka], writes=[ka])
    s.op("scalar", lambda e: e.activation(out=tmp_b, in_=tmp_a, func=AF.Sigmoid, scale=GELU_C2),
         reads=[ka], writes=[kb])
    s.op("vector", lambda e: e.tensor_tensor(out=out_ap, in0=c.ps[:, bank, :], in1=tmp_b, op=ALU.mult),
         reads=reads + [kb], writes=writes)


MIXIN = 3072


def emit_inproj(c, w_in, projT_ap, stage, tmpa, tmpb):
    s = c.s
    wv = w_in.rearrange("(k p) f -> p k f", p=128)
    ids = []
    gi = 0
    for u in range(MIXIN // 512):
        ws, wview = wload(c, [128, KC, 512], wv[:, :, u * 512:(u + 1) * 512], "in")
        for m in range(4):
            cc = u * 4 + m
            for t in range(NT):
                bank = gi % 4
                sl = gi % 2
                gi += 1
                for k in range(KC):
                    s.op("tensor", lambda e, wview=wview, k=k, m=m, t=t, bank=bank: e.matmul(
                        c.ps[:, bank, :], wview[:, k, m * 128:(m + 1) * 128], c.h[:, k, t * 512:(t + 1) * 512],
                        start=(k == 0), stop=(k == KC - 1)),
                        reads=[("w", ws), ("h", k, t)], writes=[("ps", bank)])
                if cc < 8:
                    s.op("scalar", lambda e, bank=bank, sl=sl: e.activation(out=stage[:, sl, :], in_=c.ps[:, bank, :], func=AF.Copy),
                         reads=[("ps", bank)], writes=[("stage", sl)])
                else:
                    emit_gelu_from_psum(c, bank, stage[:, sl, :], tmpa[:, sl, :], tmpb[:, sl, :],
                                        reads=[("ps", bank)], writes=[("stage", sl)], tmpkeys=(("tmpa", sl), ("tmpb", sl)))
                ids.append(s.dma("sync", lambda e, cc=cc, t=t, sl=sl: e.dma_start(
                    out=projT_ap[cc * 128:(cc + 1) * 128, t * 512:(t + 1) * 512], in_=stage[:, sl, :]),
                    reads=[("stage", sl)], sem_key=("stg", sl)))
    return ids


TWO_PI = 6.283185
INV_2PI = 0.15915494309189535
I32 = mybir.dt.int32


def bc_last(ap, n):
    shp = list(ap.shape)
    shp[-1] = n
    return ap.to_broadcast(shp)


def emit_sincos(c, f_ap, sin_ap, cos_ap, scr, keyp, reads):
    s = c.s
    K = lambda n: (keyp, n)
    s.op("vector", lambda e: e.tensor_copy(out=scr["i32"], in_=f_ap), reads=reads, writes=[K("i32")])
    s.op("vector", lambda e: e.tensor_copy(out=scr["a"], in_=scr["i32"]), reads=[K("i32")], writes=[K("a")])
    s.op("vector", lambda e: e.tensor_tensor(out=scr["a"], in0=f_ap, in1=scr["a"], op=ALU.subtract), reads=reads + [K("a")], writes=[K("a")])
    for which, out_ap, off in (("s", sin_ap, 0.0), ("c", cos_ap, 0.25)):
        if off != 0.0:
            s.op("vector", lambda e, off=off: e.tensor_scalar(out=scr["b"], in0=scr["a"], scalar1=off, scalar2=None, op0=ALU.add),
                 reads=[K("a")], writes=[K("b")])
            src = scr["b"]; srck = K("b")
        else:
            src = scr["a"]; srck = K("a")
        s.op("vector", lambda e, src=src: e.scalar_tensor_tensor(out=scr["b"], in0=src, scalar=0.5, in1=src, op0=ALU.is_gt, op1=ALU.subtract),
             reads=[srck], writes=[K("b")])
        s.op("vector", lambda e: e.scalar_tensor_tensor(out=scr["b"], in0=scr["b"], scalar=0.5, in1=scr["b"], op0=ALU.is_gt, op1=ALU.subtract),
             reads=[K("b")], writes=[K("b")])
        s.op("scalar", lambda e, out_ap=out_ap: e.activation(out=out_ap, in_=scr["b"], func=AF.Sin, scale=TWO_PI),
             reads=[K("b")], writes=[K(which)])


NP = 32
TC = 128


def emit_gelu_src(c, src, src_keys, out_ap, tmp_a, tmp_b, writes, tmpkeys):
    s = c.s
    ka, kb = tmpkeys
    s.op("scalar", lambda e: e.activation(out=tmp_a, in_=src, func=AF.Square), reads=src_keys, writes=[ka])
    s.op("vector", lambda e: e.tensor_scalar(out=tmp_a, in0=tmp_a, scalar1=GELU_C1, scalar2=1.0, op0=ALU.mult, op1=ALU.add),
         reads=[ka], writes=[ka])
    s.op("vector", lambda e: e.tensor_tensor(out=tmp_a, in0=src, in1=tmp_a, op=ALU.mult), reads=src_keys + [ka], writes=[ka])
    s.op("scalar", lambda e: e.activation(out=tmp_b, in_=tmp_a, func=AF.Sigmoid, scale=GELU_C2), reads=[ka], writes=[kb])
    s.op("vector", lambda e: e.tensor_tensor(out=out_ap, in0=src, in1=tmp_b, op=ALU.mult), reads=src_keys + [kb], writes=writes)


def s5_alloc(nc, st, c, full):
    T = lambda name, shp, dt=F32: st.enter_context(nc.sbuf_tensor(name, shp, dt))
    c.ua = T("ua", [128, 8, TOK], BF16)
    c.BtT = T("BtT", [128, 2, NP, 128], BF16)
    c.aq = T("aq", [128, 3, NP])
    c.qs = T("qs", [128, 12, NP])
    c.qi = T("qi", [128, NP], I32)
    c.E = T("E", [128, 2, NP, TC])
    c.Gp = T("Gp", [128, 2, 2, NP])
    c.G128 = T("G128", [128, 2, NP])
    c.etmp = T("etmp", [128, 2, NP, 64])
    c.arow = T("arow", [128, 3, 1024])
    c.btp = T("btp", [128, 2, 1024])
    c.rs = T("rs", [128, 9, 1024])
    c.ri = T("ri", [128, 1024], I32)
    c.z = T("z", [128, 2, 2, 512])
    c.mt = T("mt", [128, 2, 512])
    c.w = T("w", [128, 2, 2, 512])
    c.ini = T("ini", [128, NP, 2])
    c.itmp = T("itmp", [128, 2])
    c.ps = st.enter_context(nc.psum_tensor("ps", [128, 8, 512], F32))
    c.xend = T("xend", [128, 2, NP])
    if full:
        c.Cz = T("Cz", [128, 2, NP, 128], BF16)
        c.dq = T("dq", [128, 8])
        c.xs = T("xs", [128, 2, 2, 512], BF16)
        c.ypre = T("ypre", [128, 2, 512])
        c.ga = T("ga", [128, 2, 512]); c.gb = T("gb", [128, 2, 512])
        c.yst = T("yst", [128, 2, 512])
        c.xall = T("xall", [128, 8, 2, NP])
        c.oneh = T("oneh", [128, 8])
        c.X = T("X", [128, 2, 2, NP])
        c.A = T("A", [128, 2, NP])
        c.xinit = T("xinit", [128, 2, NP])


def s5_setup(c, aq_ap, arow_ap, BT_ap, full):
    s = c.s
    s.dma("sync", lambda e: e.dma_start(out=c.aq[:], in_=aq_ap), writes=["aq"], sem_key="aq")
    q = lambda i: c.qs[:, i, :]
    s.op("scalar", lambda e: e.activation(out=q(0), in_=c.aq[:, 2, :], func=AF.Exp), reads=["aq"], writes=["q0"])
    s.op("vector", lambda e: e.tensor_tensor(out=q(1), in0=c.aq[:, 0, :], in1=q(0), op=ALU.mult), reads=["aq", "q0"], writes=["q1"])
    s.op("vector", lambda e: e.scalar_tensor_tensor(out=q(2), in0=c.aq[:, 1, :], scalar=INV_2PI, in1=q(0), op0=ALU.mult, op1=ALU.mult),
         reads=["aq", "q0"], writes=["q2"])
    s.op("scalar", lambda e: e.activation(out=q(3), in_=q(1), func=AF.Exp), reads=["q1"], writes=["q3"])
    emit_sincos(c, q(2), q(4), q(5), {"i32": c.qi[:], "a": q(6), "b": q(7)}, "qsc", reads=["q2"])
    QS, QC = ("qsc", "s"), ("qsc", "c")
    s.op("vector", lambda e: e.memset(c.E[:, 0, :, 0:1], 1.0), writes=["E"])
    s.op("vector", lambda e: e.memset(c.E[:, 1, :, 0:1], 0.0), reads=["E"], writes=["E"])
    s.op("vector", lambda e: e.tensor_copy(out=c.Gp[:, 0, 0, :], in_=q(5)), reads=[QC], writes=["G"])
    s.op("vector", lambda e: e.tensor_copy(out=c.Gp[:, 0, 1, :], in_=q(4)), reads=[QS, "G"], writes=["G"])
    cur = 0
    k = 1
    t0 = c.etmp[:, 0]; t1 = c.etmp[:, 1]

    def square(src, dst, dst_is_pp=True):
        sr, si = src
        dr, di = dst
        s.op("vector", lambda e: e.tensor_tensor(out=t0[:, :, 0], in0=si, in1=si, op=ALU.mult), reads=["G"], writes=["t0"])
        s.op("vector", lambda e: e.tensor_tensor(out=t1[:, :, 0], in0=sr, in1=sr, op=ALU.mult), reads=["G"], writes=["t1"])
        s.op("vector", lambda e: e.scalar_tensor_tensor(out=di, in0=sr, scalar=2.0, in1=si, op0=ALU.mult, op1=ALU.mult), reads=["G"], writes=["G"])
        s.op("vector", lambda e: e.tensor_tensor(out=dr, in0=t1[:, :, 0], in1=t0[:, :, 0], op=ALU.subtract), reads=["t0", "t1", "G"], writes=["G"])

    while k < TC:
        gr = c.Gp[:, cur, 0, :]; gi = c.Gp[:, cur, 1, :]
        grb = bc_last(gr.unsqueeze(2), k); gib = bc_last(gi.unsqueeze(2), k)

        def mk(k=k, grb=grb, gib=gib):
            s.op("vector", lambda e: e.tensor_tensor(out=t0[:, :, 0:k], in0=c.E[:, 1, :, 0:k], in1=gib, op=ALU.mult), reads=["E", "G"], writes=["t0"])
            s.op("vector", lambda e: e.tensor_tensor(out=t1[:, :, 0:k], in0=c.E[:, 0, :, 0:k], in1=grb, op=ALU.mult), reads=["E", "G"], writes=["t1"])
            s.op("vector", lambda e: e.tensor_tensor(out=c.E[:, 0, :, k:2 * k], in0=t1[:, :, 0:k], in1=t0[:, :, 0:k], op=ALU.subtract), reads=["t0", "t1", "E"], writes=["E"])
            s.op("vector", lambda e: e.tensor_tensor(out=t0[:, :, 0:k], in0=c.E[:, 0, :, 0:k], in1=gib, op=ALU.mult), reads=["E", "G"], writes=["t0"])
            s.op("vector", lambda e: e.tensor_tensor(out=t1[:, :, 0:k], in0=c.E[:, 1, :, 0:k], in1=grb, op=ALU.mult), reads=["E", "G"], writes=["t1"])
            s.op("vector", lambda e: e.tensor_tensor(out=c.E[:, 1, :, k:2 * k], in0=t1[:, :, 0:k], in1=t0[:, :, 0:k], op=ALU.add), reads=["t0", "t1", "E"], writes=["E"])
        mk()
        nxt = 1 - cur
        square((gr, gi), (c.Gp[:, nxt, 0, :], c.Gp[:, nxt, 1, :]))
        cur = nxt
        k *= 2
    s.op("vector", lambda e, cur=cur: e.tensor_copy(out=c.G128[:], in_=c.Gp[:, cur]), reads=["G"], writes=["G128"])
    if full:
        for _ in range(3):
            nxt = 1 - cur
            square((c.Gp[:, cur, 0, :], c.Gp[:, cur, 1, :]), (c.Gp[:, nxt, 0, :], c.Gp[:, nxt, 1, :]))
            cur = nxt
        s.op("scalar", lambda e: e.activation(out=q(8), in_=q(1), func=AF.Exp, scale=1024.0), reads=["q1"], writes=["q8"])
        s.op("vector", lambda e, cur=cur: e.tensor_tensor(out=c.A[:, 0, :], in0=c.Gp[:, cur, 0, :], in1=q(8), op=ALU.mult), reads=["G", "q8"], writes=["A"])
        s.op("vector", lambda e, cur=cur: e.tensor_tensor(out=c.A[:, 1, :], in0=c.Gp[:, cur, 1, :], in1=q(8), op=ALU.mult), reads=["G", "q8", "A"], writes=["A"])
    R = lambda i: c.rs[:, i, :]
    for pc in range(4):
        sl = slice(pc * 1024, (pc + 1) * 1024)
        s.dma("sync", lambda e, sl=sl: e.dma_start(out=c.arow[:], in_=arow_ap[:, :, sl]), writes=["arow"], sem_key="arow")
        s.dma("sync", lambda e, sl=sl: e.dma_start(out=c.btp[:], in_=BT_ap[:, :, sl]), writes=["btp"], sem_key="btp")
        are = c.arow[:, 0, :]; aim = c.arow[:, 1, :]
        s.op("scalar", lambda e: e.activation(out=R(0), in_=c.arow[:, 2, :], func=AF.Exp), reads=["arow"], writes=["r0"])
        s.op("vector", lambda e: e.tensor_tensor(out=R(1), in0=are, in1=R(0), op=ALU.mult), reads=["arow", "r0"], writes=["r1"])
        s.op("vector", lambda e: e.scalar_tensor_tensor(out=R(2), in0=aim, scalar=INV_2PI, in1=R(0), op0=ALU.mult, op1=ALU.mult),
             reads=["arow", "r0"], writes=["r2"])
        s.op("scalar", lambda e: e.activation(out=R(3), in_=R(1), func=AF.Exp), reads=["r1"], writes=["r3"])
        emit_sincos(c, R(2), R(4), R(5), {"i32": c.ri[:], "a": R(6), "b": R(7)}, "rsc", reads=["r2"])
        RS, RC = ("rsc", "s"), ("rsc", "c")
        s.op("vector", lambda e: e.tensor_tensor(out=R(5), in0=R(5), in1=R(3), op=ALU.mult), reads=[RC, "r3"], writes=[RC])
        s.op("vector", lambda e: e.tensor_scalar(out=R(5), in0=R(5), scalar1=-1.0, scalar2=None, op0=ALU.add), reads=[RC], writes=[RC])
        s.op("vector", lambda e: e.tensor_tensor(out=R(4), in0=R(4), in1=R(3), op=ALU.mult), reads=[RS, "r3"], writes=[RS])
        s.op("vector", lambda e: e.tensor_tensor(out=R(0), in0=are, in1=are, op=ALU.mult), reads=["arow", "r1", "r2"], writes=["r0"])
        s.op("vector", lambda e: e.tensor_tensor(out=R(1), in0=aim, in1=aim, op=ALU.mult), reads=["arow", "r3"], writes=["r1"])
        s.op("vector", lambda e: e.tensor_tensor(out=R(0), in0=R(0), in1=R(1), op=ALU.add), reads=["r0", "r1"], writes=["r0"])
        s.op("vector", lambda e: e.reciprocal(out=R(0), in_=R(0)), reads=["r0"], writes=["r0"])
        s.op("vector", lambda e: e.tensor_tensor(out=R(1), in0=R(5), in1=are, op=ALU.mult), reads=[RC, "arow", "r1"], writes=["r1"])
        s.op("vector", lambda e: e.tensor_tensor(out=R(2), in0=R(4), in1=aim, op=ALU.mult), reads=[RS, "arow", "r2", ("rsc", "a"), ("rsc", "b")], writes=["r2"])
        s.op("vector", lambda e: e.tensor_tensor(out=R(1), in0=R(1), in1=R(2), op=ALU.add), reads=["r1", "r2"], writes=["r1"])
        s.op("vector", lambda e: e.tensor_tensor(out=R(6), in0=R(1), in1=R(0), op=ALU.mult), reads=["r1", "r0", ("rsc", "a"), ("rsc", "b")], writes=["r6"])
        s.op("vector", lambda e: e.tensor_tensor(out=R(1), in0=R(4), in1=are, op=ALU.mult), reads=[RS, "arow", "r1", "r6"], writes=["r1"])
        s.op("vector", lambda e: e.tensor_tensor(out=R(2), in0=R(5), in1=aim, op=ALU.mult), reads=[RC, "arow", "r2"], writes=["r2"])
        s.op("vector", lambda e: e.tensor_tensor(out=R(1), in0=R(1), in1=R(2), op=ALU.subtract), reads=["r1", "r2"], writes=["r1"])
        s.op("vector", lambda e: e.tensor_tensor(out=R(7), in0=R(1), in1=R(0), op=ALU.mult), reads=["r1", "r0", "r6"], writes=["r7"])
        bre = c.btp[:, 0, :]; bim = c.btp[:, 1, :]
        ore = c.BtT[:, 0, pc * 8:(pc + 1) * 8, :].rearrange("p a b -> p (a b)")
        oim = c.BtT[:, 1, pc * 8:(pc + 1) * 8, :].rearrange("p a b -> p (a b)")
        s.op("vector", lambda e: e.tensor_tensor(out=R(1), in0=bre, in1=R(6), op=ALU.mult), reads=["btp", "r6", "r1"], writes=["r1"])
        s.op("vector", lambda e: e.tensor_tensor(out=R(2), in0=bim, in1=R(7), op=ALU.mult), reads=["btp", "r7", "r2"], writes=["r2"])
        s.op("vector", lambda e, ore=ore: e.tensor_tensor(out=ore, in0=R(1), in1=R(2), op=ALU.subtract), reads=["r1", "r2"], writes=[("BtT", pc)])
        s.op("vector", lambda e: e.tensor_tensor(out=R(1), in0=bre, in1=R(7), op=ALU.mult), reads=["btp", "r7", "r1", ("BtT", pc)], writes=["r1"])
        s.op("vector", lambda e: e.tensor_tensor(out=R(2), in0=bim, in1=R(6), op=ALU.mult), reads=["btp", "r6", "r2", ("BtT", pc)], writes=["r2"])
        s.op("vector", lambda e, oim=oim: e.tensor_tensor(out=oim, in0=R(1), in1=R(2), op=ALU.add), reads=["r1", "r2", ("BtT", pc)], writes=[("BtT", pc)])


def s5_main(c, full, yag_out=None, zero_init=False, pregelu=False):
    s = c.s
    ids = []
    rq = lambda p: c.qs[:, 3, p:p + 1]
    QC, QS = ("qsc", "c"), ("qsc", "s")
    if full and not zero_init:
        cq = c.qs[:, 5, :]; sq_ = c.qs[:, 4, :]
        a0 = c.qs[:, 9, :]; a1 = c.qs[:, 10, :]
        s.op("vector", lambda e: e.tensor_tensor(out=a0, in0=sq_, in1=c.xinit[:, 1, :], op=ALU.mult), reads=[QS, "xinit"], writes=["a0"])
        s.op("vector", lambda e: e.tensor_tensor(out=a1, in0=cq, in1=c.xinit[:, 0, :], op=ALU.mult), reads=[QC, "xinit"], writes=["a1"])
        s.op("vector", lambda e: e.tensor_tensor(out=c.ini[:, :, 0], in0=a1, in1=a0, op=ALU.subtract), reads=["a0", "a1"], writes=["ini_all"])
        s.op("vector", lambda e: e.tensor_tensor(out=a0, in0=sq_, in1=c.xinit[:, 0, :], op=ALU.mult), reads=[QS, "xinit", "ini_all"], writes=["a0"])
        s.op("vector", lambda e: e.tensor_tensor(out=a1, in0=cq, in1=c.xinit[:, 1, :], op=ALU.mult), reads=[QC, "xinit", "ini_all"], writes=["a1"])
        s.op("vector", lambda e: e.tensor_tensor(out=c.ini[:, :, 1], in0=a1, in1=a0, op=ALU.add), reads=["a0", "a1", "ini_all"], writes=["ini_all"])
    else:
        s.op("vector", lambda e: e.memset(c.ini[:], 0.0), writes=["ini_all"])
    gi = 0
    for t in range(NT):
        tsl = slice(t * 512, (t + 1) * 512)
        for cc in range(8):
            ybank = 4 + (cc % 2)
            for pp in range(4):
                p = cc * 4 + pp
                sl = gi % 2; gi += 1
                b_re, b_im = 2 * sl, 2 * sl + 1
                for ri, bank in ((0, b_re), (1, b_im)):
                    s.op("tensor", lambda e, ri=ri, bank=bank, p=p, cc=cc, tsl=tsl: e.matmul(
                        c.ps[:, bank, :], c.BtT[:, ri, p, :], c.ua[:, cc, tsl], start=True, stop=True),
                        reads=[("BtT", p // 8), ("ua", cc)], writes=[("ps", bank)])
                Cb = c.E[:, 0, p, :].unsqueeze(1).to_broadcast([128, 4, TC])
                Sb = c.E[:, 1, p, :].unsqueeze(1).to_broadcast([128, 4, TC])
                v4 = lambda ap: ap.rearrange("p (a b) -> p a b", a=4)
                pre = v4(c.ps[:, b_re, :]); pim = v4(c.ps[:, b_im, :])
                zre = c.z[:, sl, 0, :]; zim = c.z[:, sl, 1, :]
                m0 = c.mt[:, 0, :]; m1 = c.mt[:, 1, :]
                Zr, Zi, M0, M1 = ("z", sl, 0), ("z", sl, 1), "m0", "m1"
                s.op("vector", lambda e, pre=pre, Cb=Cb, m0=m0: e.tensor_tensor(out=v4(m0), in0=pre, in1=Cb, op=ALU.mult), reads=[("ps", b_re), "E"], writes=[M0], size=512)
                s.op("vector", lambda e, pim=pim, Sb=Sb, m1=m1: e.tensor_tensor(out=v4(m1), in0=pim, in1=Sb, op=ALU.mult), reads=[("ps", b_im), "E"], writes=[M1], size=512)
                s.op("vector", lambda e, zre=zre, m0=m0, m1=m1: e.tensor_tensor(out=zre, in0=m0, in1=m1, op=ALU.add), reads=[M0, M1], writes=[Zr], size=512)
                s.op("vector", lambda e, pim=pim, Cb=Cb, m0=m0: e.tensor_tensor(out=v4(m0), in0=pim, in1=Cb, op=ALU.mult), reads=[("ps", b_im), "E", Zr], writes=[M0], size=512)
                s.op("vector", lambda e, pre=pre, Sb=Sb, m1=m1: e.tensor_tensor(out=v4(m1), in0=pre, in1=Sb, op=ALU.mult), reads=[("ps", b_re), "E", Zr], writes=[M1], size=512)
                s.op("vector", lambda e, zim=zim, m0=m0, m1=m1: e.tensor_tensor(out=zim, in0=m0, in1=m1, op=ALU.subtract), reads=[M0, M1], writes=[Zi], size=512)
                wre = c.w[:, sl, 0, :]; wim = c.w[:, sl, 1, :]
                Wr, Wi, INI = ("w", sl, 0), ("w", sl, 1), ("ini", p)
                rb = rq(p).to_broadcast([128, TC])
                g_re = c.G128[:, 0, p:p + 1]; g_im = c.G128[:, 1, p:p + 1]
                for j in range(4):
                    js = slice(j * TC, (j + 1) * TC)
                    s.op("vector", lambda e, js=js, wre=wre, zre=zre, rb=rb, p=p: e.tensor_tensor_scan(
                        out=wre[:, js], data0=rb, data1=zre[:, js], initial=c.ini[:, p, 0:1], op0=ALU.mult, op1=ALU.add),
                        reads=[Zr, INI, "ini_all", "q3"], writes=[Wr], strict=True)
                    s.op("vector", lambda e, js=js, wim=wim, zim=zim, rb=rb, p=p: e.tensor_tensor_scan(
                        out=wim[:, js], data0=rb, data1=zim[:, js], initial=c.ini[:, p, 1:2], op0=ALU.mult, op1=ALU.add),
                        reads=[Zi, INI, "ini_all", "q3"], writes=[Wi], strict=True)
                    er = wre[:, j * TC + TC - 1:j * TC + TC]; ei = wim[:, j * TC + TC - 1:j * TC + TC]
                    s.op("vector", lambda e, ei=ei, g_im=g_im: e.tensor_scalar(out=c.itmp[:, 0:1], in0=ei, scalar1=g_im, scalar2=None, op0=ALU.mult),
                         reads=[Wi, "G128"], writes=["it0"])
                    s.op("vector", lambda e, ei=ei, g_re=g_re: e.tensor_scalar(out=c.itmp[:, 1:2], in0=ei, scalar1=g_re, scalar2=None, op0=ALU.mult),
                         reads=[Wi, "G128"], writes=["it1"])
                    s.op("vector", lambda e, er=er, g_re=g_re, p=p: e.scalar_tensor_tensor(out=c.ini[:, p, 0:1], in0=er, scalar=g_re, in1=c.itmp[:, 0:1],
                                                                                       op0=ALU.mult, op1=ALU.subtract),
                         reads=[Wr, "it0", "G128"], writes=[INI])
                    s.op("vector", lambda e, er=er, g_im=g_im, p=p: e.scalar_tensor_tensor(out=c.ini[:, p, 1:2], in0=er, scalar=g_im, in1=c.itmp[:, 1:2],
                                                                                       op0=ALU.mult, op1=ALU.add),
                         reads=[Wr, "it1", "G128", INI], writes=[INI])
                if full:
                    xre = c.xs[:, sl, 0, :]; nxi = c.xs[:, sl, 1, :]
                    Xr, Xi = ("xs", sl, 0), ("xs", sl, 1)
                    s.op("vector", lambda e, wre=wre, Cb=Cb, m0=m0: e.tensor_tensor(out=v4(m0), in0=v4(wre), in1=Cb, op=ALU.mult), reads=[Wr, "E", Zi], writes=[M0], size=512)
                    s.op("vector", lambda e, wim=wim, Sb=Sb, m1=m1: e.tensor_tensor(out=v4(m1), in0=v4(wim), in1=Sb, op=ALU.mult), reads=[Wi, "E", Zi], writes=[M1], size=512)
                    s.op("vector", lambda e, xre=xre, m0=m0, m1=m1: e.tensor_tensor(out=xre, in0=m0, in1=m1, op=ALU.subtract), reads=[M0, M1], writes=[Xr], size=512)
                    s.op("vector", lambda e, wre=wre, Sb=Sb, m0=m0: e.tensor_tensor(out=v4(m0), in0=v4(wre), in1=Sb, op=ALU.mult), reads=[Wr, "E", Xr], writes=[M0], size=512)
                    s.op("vector", lambda e, wim=wim, Cb=Cb, m1=m1: e.tensor_tensor(out=v4(m1), in0=v4(wim), in1=Cb, op=ALU.mult), reads=[Wi, "E", Xr], writes=[M1], size=512)
                    s.op("vector", lambda e, nxi=nxi, m0=m0, m1=m1: e.scalar_tensor_tensor(out=nxi, in0=m0, scalar=-1.0, in1=m1, op0=ALU.mult, op1=ALU.subtract),
                         reads=[M0, M1], writes=[Xi], size=512)
                    s.op("tensor", lambda e, p=p, xre=xre, ybank=ybank, pp=pp: e.matmul(c.ps[:, ybank, :], c.Cz[:, 0, p, :], xre, start=(pp == 0), stop=False),
                         reads=[Xr, "Cz"], writes=[("ps", ybank)])
                    s.op("tensor", lambda e, p=p, nxi=nxi, ybank=ybank, pp=pp: e.matmul(c.ps[:, ybank, :], c.Cz[:, 1, p, :], nxi, start=False, stop=(pp == 3)),
                         reads=[Xi, "Cz"], writes=[("ps", ybank)])
            if full:
                ysl = cc % 2
                s.op("vector", lambda e, cc=cc, tsl=tsl, ybank=ybank, ysl=ysl: e.scalar_tensor_tensor(
                    out=c.ypre[:, ysl, :], in0=c.ua[:, cc, tsl], scalar=c.dq[:, cc:cc + 1], in1=c.ps[:, ybank, :], op0=ALU.mult, op1=ALU.add),
                    reads=[("ua", cc), "dq", ("ps", ybank)], writes=[("ypre", ysl)])
                if pregelu:
                    ids.append(s.dma("sync", lambda e, cc=cc, tsl=tsl, ysl=ysl: e.dma_start(out=yag_out[cc * 128:(cc + 1) * 128, tsl], in_=c.ypre[:, ysl, :]),
                                     reads=[("ypre", ysl)], sem_key=("ypre", ysl)))
                else:
                    emit_gelu_src(c, c.ypre[:, ysl, :], [("ypre", ysl)], c.yst[:, ysl, :], c.ga[:, ysl, :], c.gb[:, ysl, :],
                                  writes=[("yst", ysl)], tmpkeys=(("ga", ysl), ("gb", ysl)))
                    ids.append(s.dma("sync", lambda e, cc=cc, tsl=tsl, ysl=ysl: e.dma_start(out=yag_out[cc * 128:(cc + 1) * 128, tsl], in_=c.yst[:, ysl, :]),
                                     reads=[("yst", ysl)], sem_key=("yst", ysl)))
    if (not full) or zero_init:
        cq = c.qs[:, 5, :]; sq_ = c.qs[:, 4, :]
        a0 = c.qs[:, 9, :]; a1 = c.qs[:, 10, :]
        allini = [("ini", p) for p in range(NP)] + ["ini_all"]
        s.op("vector", lambda e: e.tensor_tensor(out=a0, in0=sq_, in1=c.ini[:, :, 1], op=ALU.mult), reads=[QS] + allini, writes=["a0"])
        s.op("vector", lambda e: e.tensor_tensor(out=a1, in0=cq, in1=c.ini[:, :, 0], op=ALU.mult), reads=[QC] + allini, writes=["a1"])
        s.op("vector", lambda e: e.tensor_tensor(out=c.xend[:, 0, :], in0=a1, in1=a0, op=ALU.add), reads=["a0", "a1"], writes=["xend"])
        s.op("vector", lambda e: e.tensor_tensor(out=a0, in0=sq_, in1=c.ini[:, :, 0], op=ALU.mult), reads=[QS, "xend"] + allini, writes=["a0"])
        s.op("vector", lambda e: e.tensor_tensor(out=a1, in0=cq, in1=c.ini[:, :, 1], op=ALU.mult), reads=[QC, "xend"] + allini, writes=["a1"])
        s.op("vector", lambda e: e.tensor_tensor(out=c.xend[:, 1, :], in0=a1, in1=a0, op=ALU.subtract), reads=["a0", "a1", "xend"], writes=["xend"])
    return ids


def s5_combine(c, xall_ap, oneh_ap):
    s = c.s
    s.dma("sync", lambda e: e.dma_start(out=c.xall[:], in_=xall_ap), writes=["xall"], sem_key="xall")
    s.dma("sync", lambda e: e.dma_start(out=c.oneh[:], in_=oneh_ap), writes=["oneh"], sem_key="oneh")
    s.op("vector", lambda e: e.memset(c.X[:], 0.0), writes=["X"])
    s.op("vector", lambda e: e.memset(c.xinit[:], 0.0), writes=["xinit"])
    a0 = c.qs[:, 9, :]; a1 = c.qs[:, 10, :]
    cur = 0
    for cidx in range(1, 8):
        nxt = 1 - cur
        xr = c.X[:, cur, 0, :]; xi = c.X[:, cur, 1, :]
        nr = c.X[:, nxt, 0, :]; ni = c.X[:, nxt, 1, :]
        er = c.xall[:, cidx - 1, 0, :]; ei = c.xall[:, cidx - 1, 1, :]
        Ar = c.A[:, 0, :]; Ai = c.A[:, 1, :]
        s.op("vector", lambda e, xr=xr, Ar=Ar: e.tensor_tensor(out=a0, in0=xr, in1=Ar, op=ALU.mult), reads=["X", "A"], writes=["a0"])
        s.op("vector", lambda e, xi=xi, Ai=Ai: e.tensor_tensor(out=a1, in0=xi, in1=Ai, op=ALU.mult), reads=["X", "A"], writes=["a1"])
        s.op("vector", lambda e: e.tensor_tensor(out=a0, in0=a0, in1=a1, op=ALU.subtract), reads=["a0", "a1"], writes=["a0"])
        s.op("vector", lambda e, nr=nr, er=er: e.tensor_tensor(out=nr, in0=a0, in1=er, op=ALU.add), reads=["a0", "xall", "X"], writes=["X"])
        s.op("vector", lambda e, xr=xr, Ai=Ai: e.tensor_tensor(out=a0, in0=xr, in1=Ai, op=ALU.mult), reads=["X", "A"], writes=["a0"])
        s.op("vector", lambda e, xi=xi, Ar=Ar: e.tensor_tensor(out=a1, in0=xi, in1=Ar, op=ALU.mult), reads=["X", "A"], writes=["a1"])
        s.op("vector", lambda e: e.tensor_tensor(out=a0, in0=a0, in1=a1, op=ALU.add), reads=["a0", "a1"], writes=["a0"])
        s.op("vector", lambda e, ni=ni, ei=ei: e.tensor_tensor(out=ni, in0=a0, in1=ei, op=ALU.add), reads=["a0", "xall", "X"], writes=["X"])
        for ri, src in ((0, nr), (1, ni)):
            s.op("vector", lambda e, ri=ri, src=src, cidx=cidx: e.scalar_tensor_tensor(
                out=c.xinit[:, ri, :], in0=src, scalar=c.oneh[:, cidx:cidx + 1], in1=c.xinit[:, ri, :], op0=ALU.mult, op1=ALU.add),
                reads=["X", "oneh", "xinit"], writes=["xinit"])
        cur = nxt


def wload_half(c, view_shape, src_ap, half):
    raise NotImplementedError


def emit_glu_sgu(c, nc, st, yagT, projT, w_glu, b_glu, ln_g, ln_b, wsT, b_s, ident, yaT, ybT):
    s = c.s
    T = lambda name, shp, dt=F32: st.enter_context(nc.sbuf_tensor(name, shp, dt))
    yag = T("yag", [128, 8, TOK], BF16)
    ug = T("ug", [128, 8, TOK], BF16)
    vg = T("vg", [128, 8, TOK], BF16)
    vn = T("vn", [128, 8, TOK], BF16)
    c.wring = T("wring", [128, 4, 8192], BF16); c.wslot = 0
    sq = T("sq", [128, 2, 512], BF16)
    ones = T("ones", [128, 128], BF16)
    idb = T("idb", [128, 128], BF16)
    par = T("par", [128, 3, 8])
    epsb = T("epsb", [128, 1])
    mean = T("mean", [128, 512]); msq = T("msq", [128, 512]); rstd = T("rstd", [128, 512]); t1 = T("t1", [128, 2, 512])
    sig = T("sig", [128, 2, 512]); stg = T("stg", [128, 2, 512])
    wsb = T("wsb", [128, 8, 128], BF16)
    bsf = T("bsf", [1, 1024]); bsh = T("bsh", [1, 1024], BF16); bsl = T("bsl", [1, 1024], BF16); bst = T("bst", [1, 1024])
    vT = T("vT", [128, 2, 128], BF16)
    ps = st.enter_context(nc.psum_tensor("ps", [128, 7, 512], F32))
    psT = st.enter_context(nc.psum_tensor("psT", [128, 2, 128], BF16))
    ids = []
    s.op("gpsimd", lambda e: e.memset(ones[:], 1.0), writes=["ones"])
    s.op("gpsimd", lambda e: e.memset(epsb[:], EPS), writes=["epsb"])
    for i, ap in enumerate((b_glu, ln_g, ln_b)):
        s.dma("sync", lambda e, i=i, ap=ap: e.dma_start(out=par[:, i, :], in_=ap.rearrange("(k p) -> p k", p=128), allow_slow_non_contiguous=True),
              writes=[("par", i)], sem_key=("par", i))
    s.dma("gpsimd", lambda e: e.dma_start(out=idb[:], in_=ident), writes=["idb"], sem_key="idb")
    s.dma("gpsimd", lambda e: e.dma_start(out=wsb[:], in_=wsT.rearrange("h s t -> s h t")), writes=["wsb"], sem_key="wsb")
    s.op("vector", lambda e: e.memset(wsb[64:128, :, 0:64], 0.0), reads=["wsb"], writes=["wsb"])
    s.dma("sync", lambda e: e.dma_start(out=bsf[:], in_=b_s.rearrange("(o h) t -> o (h t)", o=1)), writes=["bsf"], sem_key="bsf")
    s.op("vector", lambda e: e.tensor_copy(out=bsh[:], in_=bsf[:]), reads=["bsf"], writes=["bsh"])
    s.op("vector", lambda e: e.tensor_copy(out=bst[:], in_=bsh[:]), reads=["bsh"], writes=["bst"])
    s.op("vector", lambda e: e.tensor_tensor(out=bsl[:], in0=bsf[:], in1=bst[:], op=ALU.subtract), reads=["bsf", "bst"], writes=["bsl"])
    for cc in range(8):
        s.dma("gpsimd", lambda e, cc=cc: e.dma_start(out=yag[:, cc, :], in_=yagT[cc * 128:(cc + 1) * 128, :]), writes=[("yag", cc)], sem_key=("yag", cc))
    for cc in range(8):
        s.dma("gpsimd", lambda e, cc=cc: e.dma_start(out=ug[:, cc, :], in_=projT[1024 + cc * 128:1024 + (cc + 1) * 128, :]), writes=[("ug", cc)], sem_key=("ug", cc))
        s.dma("gpsimd", lambda e, cc=cc: e.dma_start(out=vg[:, cc, :], in_=projT[2048 + cc * 128:2048 + (cc + 1) * 128, :]), writes=[("vg", cc)], sem_key=("vg", cc))
    wv = w_glu.rearrange("(k p) f -> p k f", p=128)
    gi = 0
    for u in range(2):
        ws, wview = wload(c, [128, 8, 512], wv[:, :, u * 512:(u + 1) * 512], "glu")
        for m in range(4):
            mc = u * 4 + m
            for t in range(NT):
                tsl = slice(t * 512, (t + 1) * 512)
                bank = gi % 4; sl = gi % 2; gi += 1
                for k in range(8):
                    s.op("tensor", lambda e, wview=wview, k=k, m=m, tsl=tsl, bank=bank: e.matmul(
                        ps[:, bank, :], wview[:, k, m * 128:(m + 1) * 128], yag[:, k, tsl], start=(k == 0), stop=(k == 7)),
                        reads=[("w", ws), ("yag", k)], writes=[("ps", bank)])
                s.op("scalar", lambda e, bank=bank, sl=sl, mc=mc: e.activation(out=sig[:, sl, :], in_=ps[:, bank, :], func=AF.Sigmoid,
                                                                            bias=par[:, 0, mc:mc + 1], scale=1.0),
                     reads=[("ps", bank), ("par", 0)], writes=[("sig", sl)])
                s.op("vector", lambda e, sl=sl, mc=mc, tsl=tsl: e.tensor_tensor(out=stg[:, sl, :], in0=yag[:, mc, tsl], in1=sig[:, sl, :], op=ALU.mult),
                     reads=[("yag", mc), ("sig", sl)], writes=[("stg", sl)])
                ids.append(s.dma("sync", lambda e, mc=mc, tsl=tsl, sl=sl: e.dma_start(out=yaT[mc * 128:(mc + 1) * 128, tsl], in_=stg[:, sl, :]),
                                 reads=[("stg", sl)], sem_key=("stg", sl)))
    sqs = 0
    for t in range(NT):
        tsl = slice(t * 512, (t + 1) * 512)
        for cc in range(8):
            s.op("tensor", lambda e, cc=cc, tsl=tsl: e.matmul(ps[:, 4, :], ones[:], vg[:, cc, tsl], start=(cc == 0), stop=(cc == 7)),
                 reads=["ones", ("vg", cc)], writes=[("ps", 4)])
        for cc in range(8):
            sl = sqs; sqs ^= 1
            s.op("scalar", lambda e, cc=cc, tsl=tsl, sl=sl: e.activation(out=sq[:, sl, :], in_=vg[:, cc, tsl], func=AF.Square),
                 reads=[("vg", cc)], writes=[("sq", sl)])
            s.op("tensor", lambda e, cc=cc, sl=sl: e.matmul(ps[:, 5, :], ones[:], sq[:, sl, :], start=(cc == 0), stop=(cc == 7)),
                 reads=["ones", ("sq", sl)], writes=[("ps", 5)])
        s.op("scalar", lambda e: e.activation(out=mean[:], in_=ps[:, 4, :], func=AF.Copy, scale=1.0 / 1024), reads=[("ps", 4)], writes=["mean"])
        s.op("vector", lambda e: e.tensor_tensor(out=msq[:], in0=mean[:], in1=mean[:], op=ALU.mult), reads=["mean"], writes=["msq"])
        s.op("vector", lambda e: e.scalar_tensor_tensor(out=msq[:], in0=ps[:, 5, :], scalar=1.0 / 1024, in1=msq[:], op0=ALU.mult, op1=ALU.subtract),
             reads=[("ps", 5), "msq"], writes=["msq"])
        s.op("scalar", lambda e: e.activation(out=rstd[:], in_=msq[:], func=AF.Sqrt, bias=epsb[:, 0:1], scale=1.0), reads=["msq", "epsb"], writes=["rstd"])
        s.op("vector", lambda e: e.reciprocal(out=rstd[:], in_=rstd[:]), reads=["rstd"], writes=["rstd"])
        for cc in range(8):
            sl = cc % 2
            s.op("vector", lambda e, cc=cc, tsl=tsl, sl=sl: e.tensor_tensor(out=t1[:, sl, :], in0=vg[:, cc, tsl], in1=mean[:], op=ALU.subtract),
                 reads=[("vg", cc), "mean"], writes=[("t1", sl)])
            s.op("vector", lambda e, sl=sl: e.tensor_tensor(out=t1[:, sl, :], in0=t1[:, sl, :], in1=rstd[:], op=ALU.mult),
                 reads=[("t1", sl), "rstd"], writes=[("t1", sl)])
            s.op("vector", lambda e, cc=cc, tsl=tsl, sl=sl: e.tensor_scalar(out=vn[:, cc, tsl], in0=t1[:, sl, :], scalar1=par[:, 1, cc:cc + 1],
                                                                           scalar2=par[:, 2, cc:cc + 1], op0=ALU.mult, op1=ALU.add),
                 reads=[("t1", sl), ("par", 1), ("par", 2)], writes=[("vn", cc, t)])
        for hh in range(8):
            bank = 4 + 2 + (hh % 1)
            bank = 6
            for j in range(4):
                tok = slice(t * 512 + j * TC, t * 512 + (j + 1) * TC)
                vs = (hh * 4 + j) % 2
                s.op("tensor", lambda e, hh=hh, tok=tok, vs=vs: e.transpose(psT[:, vs, :], vn[:, hh, tok], idb[:]),
                     reads=[("vn", hh, t), "idb"], writes=[("psT", vs)])
                s.op("scalar", lambda e, vs=vs: e.activation(out=vT[:, vs, :], in_=psT[:, vs, :], func=AF.Copy), reads=[("psT", vs)], writes=[("vT", vs)])
                osl = slice(j * TC, (j + 1) * TC)
                s.op("tensor", lambda e, hh=hh, vs=vs, osl=osl: e.matmul(ps[:, 6, osl], vT[:, vs, :], wsb[:, hh, :], start=True, stop=False, skip_group_check=True),
                     reads=[("vT", vs), "wsb"], writes=[("ps", 6)])
                s.op("tensor", lambda e, hh=hh, osl=osl: e.matmul(ps[:, 6, osl], ones[0:1, :], bsh[0:1, hh * 128:(hh + 1) * 128], start=False, stop=False, skip_group_check=True),
                     reads=["ones", "bsh"], writes=[("ps", 6)])
                s.op("tensor", lambda e, hh=hh, osl=osl: e.matmul(ps[:, 6, osl], ones[0:1, :], bsl[0:1, hh * 128:(hh + 1) * 128], start=False, stop=True, skip_group_check=True),
                     reads=["ones", "bsl"], writes=[("ps", 6)])
            sl = hh % 2
            s.op("vector", lambda e, hh=hh, tsl=tsl, sl=sl: e.tensor_tensor(out=stg[:, sl, :], in0=ps[:, 6, :], in1=ug[:, hh, tsl], op=ALU.mult),
                 reads=[("ps", 6), ("ug", hh)], writes=[("stg", sl)])
            ids.append(s.dma("sync", lambda e, hh=hh, tsl=tsl, sl=sl: e.dma_start(out=ybT[hh * 128:(hh + 1) * 128, tsl], in_=stg[:, sl, :]),
                             reads=[("stg", sl)], sem_key=("stg", sl)))
    return ids


def emit_merge(c, nc, st, x1T, yaT, ybT, g_mix, w_a, w_b, w_gate, b_gate, w_out, x2T):
    s = c.s
    T = lambda name, shp, dt=F32: st.enter_context(nc.sbuf_tensor(name, shp, dt))
    c.h = T("h", [128, KC, TOK], BF16)
    c.wring = T("wring", [128, 4, 8192], BF16); c.wslot = 0
    ya = T("ya", [128, 8, TOK], BF16); yb = T("yb", [128, 8, TOK], BF16)
    mg = T("mg", [128, KC, TOK], BF16)
    xst = T("xst", [128, 4, 512]); sq = T("sq", [128, 2, 512], BF16)
    ones = T("ones", [128, 128], BF16); epsb = T("epsb", [128, 1]); rstd = T("rstd", [128, TOK])
    gain = T("gain", [128, KC]); bg = T("bg", [128, 2 * KC])
    sga = T("sga", [128, 2, 512]); tmp = T("tmp", [128, 2, 512]); ost = T("ost", [128, 2, 512])
    ps = st.enter_context(nc.psum_tensor("ps", [128, 8, 512], F32))
    ids = []
    s.op("gpsimd", lambda e: e.memset(ones[:], 1.0), writes=["ones"])
    s.op("gpsimd", lambda e: e.memset(epsb[:], EPS), writes=["epsb"])
    s.dma("sync", lambda e: e.dma_start(out=gain[:], in_=g_mix.rearrange("(k p) -> p k", p=128), allow_slow_non_contiguous=True), writes=["gain"], sem_key="gain")
    s.dma("sync", lambda e: e.dma_start(out=bg[:], in_=b_gate.rearrange("(k p) -> p k", p=128), allow_slow_non_contiguous=True), writes=["bg"], sem_key="bg")
    for cc in range(8):
        s.dma("gpsimd", lambda e, cc=cc: e.dma_start(out=ya[:, cc, :], in_=yaT[cc * 128:(cc + 1) * 128, :]), writes=[("ya", cc)], sem_key=("ya", cc))
        s.dma("gpsimd", lambda e, cc=cc: e.dma_start(out=yb[:, cc, :], in_=ybT[cc * 128:(cc + 1) * 128, :]), writes=[("yb", cc)], sem_key=("yb", cc))
    xv = x1T.rearrange("(k p) t -> p k t", p=128)
    xs = 0
    for t in range(NT):
        tsl = slice(t * 512, (t + 1) * 512)
        for k in range(KC):
            sl = xs % 4; xs += 1
            s.dma("sync", lambda e, k=k, tsl=tsl, sl=sl: e.dma_start(out=xst[:, sl, :], in_=xv[:, k, tsl]), writes=[("xst", sl)], sem_key=("xst", sl))
            s.op("scalar", lambda e, sl=sl: e.activation(out=sq[:, sl % 2, :], in_=xst[:, sl, :], func=AF.Square), reads=[("xst", sl)], writes=[("sq", sl % 2)])
            s.op("tensor", lambda e, sl=sl, k=k: e.matmul(ps[:, 7, :], ones[:], sq[:, sl % 2, :], start=(k == 0), stop=(k == KC - 1)),
                 reads=["ones", ("sq", sl % 2)], writes=[("ps", 7)])
        s.op("scalar", lambda e, tsl=tsl: e.activation(out=rstd[:, tsl], in_=ps[:, 7, :], func=AF.Sqrt, bias=epsb[:, 0:1], scale=1.0 / D),
             reads=[("ps", 7), "epsb"], writes=[("rstd", t)])
        s.op("vector", lambda e, tsl=tsl: e.reciprocal(out=rstd[:, tsl], in_=rstd[:, tsl]), reads=[("rstd", t)], writes=[("rstd", t)])
        for k in range(KC):
            sl = xs % 4; xs += 1
            s.dma("sync", lambda e, k=k, tsl=tsl, sl=sl: e.dma_start(out=xst[:, sl, :], in_=xv[:, k, tsl]), writes=[("xst", sl)], sem_key=("xst", sl))
            s.op("vector", lambda e, k=k, tsl=tsl, sl=sl: e.scalar_tensor_tensor(out=c.h[:, k, tsl], in0=xst[:, sl, :], scalar=gain[:, k:k + 1], in1=rstd[:, tsl],
                                                                                 op0=ALU.mult, op1=ALU.mult),
                 reads=[("xst", sl), "gain", ("rstd", t)], writes=[("h", k, t)])
    wa_v = w_a.rearrange("(k p) f -> p k f", p=128); wb_v = w_b.rearrange("(k p) f -> p k f", p=128)
    wg_v = w_gate.rearrange("(k p) f -> p k f", p=128)
    gi = 0
    for n4 in range(4):
        cs = slice(n4 * 512, (n4 + 1) * 512)
        sa, va = wload(c, [128, 8, 512], wa_v[:, :, cs], "a")
        sga_s, vga = wload(c, [128, KC, 512], wg_v[:, :, cs], "ga")
        sb, vb = wload(c, [128, 8, 512], wb_v[:, :, cs], "b")
        sgb_s, vgb = wload(c, [128, KC, 512], wg_v[:, :, D + n4 * 512:D + (n4 + 1) * 512], "gb")
        for m in range(4):
            n = n4 * 4 + m
            msl = slice(m * 128, (m + 1) * 128)
            for t in range(NT):
                tsl = slice(t * 512, (t + 1) * 512)
                pb = (gi % 2) * 4; sl = gi % 2; gi += 1
                for (bank, wsl, wvw, src, nk, skey) in ((pb, sa, va, ya, 8, "ya"), (pb + 1, sga_s, vga, c.h, KC, "h"), (pb + 2, sb, vb, yb, 8, "yb"), (pb + 3, sgb_s, vgb, c.h, KC, "h")):
                    for k in range(nk):
                        rk = (skey, k, t) if skey == "h" else (skey, k)
                        s.op("tensor", lambda e, bank=bank, wvw=wvw, src=src, k=k, nk=nk, msl=msl, tsl=tsl: e.matmul(
                            ps[:, bank, :], wvw[:, k, msl], src[:, k, tsl], start=(k == 0), stop=(k == nk - 1)),
                            reads=[("w", wsl), rk], writes=[("ps", bank)])
                s.op("scalar", lambda e, pb=pb, sl=sl, n=n: e.activation(out=sga[:, sl, :], in_=ps[:, pb + 1, :], func=AF.Sigmoid, bias=bg[:, n:n + 1], scale=1.0),
                     reads=[("ps", pb + 1), "bg"], writes=[("sga", sl)])
                s.op("vector", lambda e, pb=pb, sl=sl: e.tensor_tensor(out=tmp[:, sl, :], in0=ps[:, pb, :], in1=sga[:, sl, :], op=ALU.mult),
                     reads=[("ps", pb), ("sga", sl)], writes=[("tmp", sl)])
                s.op("scalar", lambda e, pb=pb, sl=sl, n=n: e.activation(out=sga[:, sl, :], in_=ps[:, pb + 3, :], func=AF.Sigmoid, bias=bg[:, KC + n:KC + n + 1], scale=1.0),
                     reads=[("ps", pb + 3), "bg", ("tmp", sl)], writes=[("sga", sl)])
                s.op("vector", lambda e, pb=pb, sl=sl: e.tensor_tensor(out=sga[:, sl, :], in0=ps[:, pb + 2, :], in1=sga[:, sl, :], op=ALU.mult),
                     reads=[("ps", pb + 2), ("sga", sl)], writes=[("sga", sl)])
                s.op("vector", lambda e, sl=sl, n=n, tsl=tsl: e.tensor_tensor(out=mg[:, n, tsl], in0=tmp[:, sl, :], in1=sga[:, sl, :], op=ALU.add),
                     reads=[("tmp", sl), ("sga", sl)], writes=[("mg", n, t)])
    wo_v = w_out.rearrange("(k p) f -> p k f", p=128)
    gi = 0
    for n4 in range(4):
        so, vo = wload(c, [128, KC, 512], wo_v[:, :, n4 * 512:(n4 + 1) * 512], "o")
        for m in range(4):
            n = n4 * 4 + m
            msl = slice(m * 128, (m + 1) * 128)
            for t in range(NT):
                tsl = slice(t * 512, (t + 1) * 512)
                bank = gi % 4; sl = gi % 2; xsl = gi % 4; gi += 1
                for k in range(KC):
                    s.op("tensor", lambda e, bank=bank, vo=vo, k=k, msl=msl, tsl=tsl: e.matmul(ps[:, bank, :], vo[:, k, msl], mg[:, k, tsl], start=(k == 0), stop=(k == KC - 1)),
                         reads=[("w", so), ("mg", k, t)], writes=[("ps", bank)])
                s.dma("sync", lambda e, n=n, tsl=tsl, xsl=xsl: e.dma_start(out=xst[:, xsl, :], in_=xv[:, n, tsl]), writes=[("xst", xsl)], sem_key=("xst", xsl))
                s.op("vector", lambda e, bank=bank, sl=sl, xsl=xsl: e.tensor_tensor(out=ost[:, sl, :], in0=ps[:, bank, :], in1=xst[:, xsl, :], op=ALU.add),
                     reads=[("ps", bank), ("xst", xsl)], writes=[("ost", sl)])
                ids.append(s.dma("sync", lambda e, n=n, tsl=tsl, sl=sl: e.dma_start(out=x2T[n * 128:(n + 1) * 128, tsl], in_=ost[:, sl, :]),
                                 reads=[("ost", sl)], sem_key=("ost", sl)))
    return ids


def emit_final_norm(c, gidx, outT, stage):
    s = c.s
    ids = []
    gi = 0
    for t in range(NT):
        tsl = slice(t * 512, (t + 1) * 512)
        bank = 7
        for k in range(KC):
            sl = c.sqslot; c.sqslot ^= 1
            s.op("scalar", lambda e, k=k, sl=sl, tsl=tsl: e.activation(out=c.sq[:, sl, :], in_=c.x[:, k, tsl], func=AF.Square),
                 reads=[("x", k, t)], writes=[("sq", sl)])
            s.op("tensor", lambda e, k=k, sl=sl, bank=bank: e.matmul(c.ps[:, bank, :], c.ones[:], c.sq[:, sl, :], start=(k == 0), stop=(k == KC - 1)),
                 reads=[("sq", sl), ("ones",)], writes=[("ps", bank)])
        s.op("scalar", lambda e, tsl=tsl, bank=bank: e.activation(out=c.rstd[:, tsl], in_=c.ps[:, bank, :], func=AF.Sqrt, bias=c.epsb[:, 0:1], scale=1.0 / D),
             reads=[("ps", bank), ("epsb",)], writes=[("rstd", t)])
        s.op("vector", lambda e, tsl=tsl: e.reciprocal(out=c.rstd[:, tsl], in_=c.rstd[:, tsl]), reads=[("rstd", t)], writes=[("rstd", t)])
        for k in range(KC):
            sl = gi % 2; gi += 1
            s.op("vector", lambda e, k=k, tsl=tsl, sl=sl: e.scalar_tensor_tensor(out=stage[:, sl, :], in0=c.x[:, k, tsl], scalar=c.gains[:, gidx, k:k + 1],
                                                                                 in1=c.rstd[:, tsl], op0=ALU.mult, op1=ALU.mult),
                 reads=[("x", k, t), ("gain", gidx), ("rstd", t)], writes=[("fstage", sl)])
            ids.append(s.dma("sync", lambda e, k=k, tsl=tsl, sl=sl: e.dma_start(out=outT[k * 128:(k + 1) * 128, tsl], in_=stage[:, sl, :]),
                             reads=[("fstage", sl)], sem_key=("fstage", sl)))
    return ids


def emit_cpow(s, E, Gp, etmp, base_keys, k_end, first_is_base):
    t0 = etmp[:, 0]; t1 = etmp[:, 1]
    cur = 0; k = 1
    while k < k_end:
        gr = Gp[:, cur, 0, :]; gi = Gp[:, cur, 1, :]
        grb = bc_last(gr.unsqueeze(2), k); gib = bc_last(gi.unsqueeze(2), k)

        def mk(k=k, grb=grb, gib=gib):
            s.op("vector", lambda e: e.tensor_tensor(out=t0[:, :, 0:k], in0=E[:, 1, :, 0:k], in1=gib, op=ALU.mult), reads=["E", "G"], writes=["t0"])
            s.op("vector", lambda e: e.tensor_tensor(out=t1[:, :, 0:k], in0=E[:, 0, :, 0:k], in1=grb, op=ALU.mult), reads=["E", "G"], writes=["t1"])
            s.op("vector", lambda e: e.tensor_tensor(out=E[:, 0, :, k:2 * k], in0=t1[:, :, 0:k], in1=t0[:, :, 0:k], op=ALU.subtract), reads=["t0", "t1", "E"], writes=["E"])
            s.op("vector", lambda e: e.tensor_tensor(out=t0[:, :, 0:k], in0=E[:, 0, :, 0:k], in1=gib, op=ALU.mult), reads=["E", "G"], writes=["t0"])
            s.op("vector", lambda e: e.tensor_tensor(out=t1[:, :, 0:k], in0=E[:, 1, :, 0:k], in1=grb, op=ALU.mult), reads=["E", "G"], writes=["t1"])
            s.op("vector", lambda e: e.tensor_tensor(out=E[:, 1, :, k:2 * k], in0=t1[:, :, 0:k], in1=t0[:, :, 0:k], op=ALU.add), reads=["t0", "t1", "E"], writes=["E"])
        mk()
        nxt = 1 - cur
        sr, si = gr, gi
        dr, di = Gp[:, nxt, 0, :], Gp[:, nxt, 1, :]

        def sqr(sr=sr, si=si, dr=dr, di=di):
            s.op("vector", lambda e: e.tensor_tensor(out=t0[:, :, 0], in0=si, in1=si, op=ALU.mult), reads=["G"], writes=["t0"])
            s.op("vector", lambda e: e.tensor_tensor(out=t1[:, :, 0], in0=sr, in1=sr, op=ALU.mult), reads=["G"], writes=["t1"])
            s.op("vector", lambda e: e.scalar_tensor_tensor(out=di, in0=sr, scalar=2.0, in1=si, op0=ALU.mult, op1=ALU.mult), reads=["G"], writes=["G"])
            s.op("vector", lambda e: e.tensor_tensor(out=dr, in0=t1[:, :, 0], in1=t0[:, :, 0], op=ALU.subtract), reads=["t0", "t1", "G"], writes=["G"])
        sqr()
        cur = nxt; k *= 2
    return cur


def emit_s5_correct(c, nc, st, aq_ap, Cz_ap, xall_ap, oneh_ap, ypreT, yagT):
    s = c.s
    T = lambda name, shp, dt=F32: st.enter_context(nc.sbuf_tensor(name, shp, dt))
    c.aq = T("aq", [128, 3, NP]); c.qs = T("qs", [128, 12, NP]); c.qi = T("qi", [128, NP], I32)
    P = T("P", [128, 2, NP, TC])
    Pb = T("Pb", [128, 2, NP, TC], BF16)
    Gp = T("Gp", [128, 2, 2, NP]); etmp = T("etmp", [128, 2, NP, 64])
    c.A = T("A", [128, 2, NP]); c.X = T("X", [128, 2, 2, NP]); c.xinit = T("xinit", [128, 2, NP])
    c.xall = T("xall", [128, 8, 2, NP]); c.oneh = T("oneh", [128, 8])
    Czf = T("Czf", [128, 2, NP, 32])
    V = T("V", [128, 2, 2, NP])
    Wt = T("Wt", [128, 2, 2, NP, 128], BF16)
    wa = T("wa", [128, NP, 32]); wb = T("wb", [128, NP, 32])
    yl = T("yl", [128, 2, 512]); ga = T("ga", [128, 2, 512]); gb = T("gb", [128, 2, 512]); yst = T("yst", [128, 2, 512])
    ps = st.enter_context(nc.psum_tensor("ps", [128, 8, 512], F32))
    ids = []
    q = lambda i: c.qs[:, i, :]
    s.dma("sync", lambda e: e.dma_start(out=c.aq[:], in_=aq_ap), writes=["aq"], sem_key="aq")
    Cz4 = Cz_ap.rearrange("q r (a b) c -> q r a b c", b=4)
    for b in range(4):
        s.dma("sync", lambda e, b=b: e.dma_start(out=Czf[:].rearrange("q r (a b) c -> q r a b c", b=4)[:, :, :, b, :],
                                                 in_=Cz4[:, :, :, b, 32 * b:32 * b + 32]), writes=[("Czf", b)], sem_key=("Czf", b))
    CZ = [("Czf", b) for b in range(4)]
    s.op("scalar", lambda e: e.activation(out=q(0), in_=c.aq[:, 2, :], func=AF.Exp), reads=["aq"], writes=["q0"])
    s.op("vector", lambda e: e.tensor_tensor(out=q(1), in0=c.aq[:, 0, :], in1=q(0), op=ALU.mult), reads=["aq", "q0"], writes=["q1"])
    s.op("vector", lambda e: e.scalar_tensor_tensor(out=q(2), in0=c.aq[:, 1, :], scalar=INV_2PI, in1=q(0), op0=ALU.mult, op1=ALU.mult),
         reads=["aq", "q0"], writes=["q2"])
    s.op("scalar", lambda e: e.activation(out=q(3), in_=q(1), func=AF.Exp), reads=["q1"], writes=["q3"])
    emit_sincos(c, q(2), q(4), q(5), {"i32": c.qi[:], "a": q(6), "b": q(7)}, "qsc", reads=["q2"])
    s.op("vector", lambda e: e.tensor_tensor(out=Gp[:, 0, 0, :], in0=q(5), in1=q(3), op=ALU.mult), reads=[("qsc", "c"), "q3"], writes=["G"])
    s.op("vector", lambda e: e.tensor_tensor(out=Gp[:, 0, 1, :], in0=q(4), in1=q(3), op=ALU.mult), reads=[("qsc", "s"), "q3", "G"], writes=["G"])
    s.op("vector", lambda e: e.tensor_copy(out=P[:, 0, :, 0], in_=Gp[:, 0, 0, :]), reads=["G"], writes=["E"])
    s.op("vector", lambda e: e.tensor_copy(out=P[:, 1, :, 0], in_=Gp[:, 0, 1, :]), reads=["G", "E"], writes=["E"])
    cur = emit_cpow(s, P, Gp, etmp, None, TC, True)
    G128 = T("G128", [128, 2, NP])
    s.op("vector", lambda e, cur=cur: e.tensor_copy(out=G128[:], in_=Gp[:, cur]), reads=["G"], writes=["G128"])
    s.op("vector", lambda e: e.tensor_copy(out=Pb[:], in_=P[:]), reads=["E"], writes=["Pb"])
    t0 = etmp[:, 0]; t1 = etmp[:, 1]
    for _ in range(3):
        nxt = 1 - cur
        sr, si = Gp[:, cur, 0, :], Gp[:, cur, 1, :]
        dr, di = Gp[:, nxt, 0, :], Gp[:, nxt, 1, :]
        s.op("vector", lambda e, si=si: e.tensor_tensor(out=t0[:, :, 0], in0=si, in1=si, op=ALU.mult), reads=["G"], writes=["t0"])
        s.op("vector", lambda e, sr=sr: e.tensor_tensor(out=t1[:, :, 0], in0=sr, in1=sr, op=ALU.mult), reads=["G"], writes=["t1"])
        s.op("vector", lambda e, sr=sr, si=si, di=di: e.scalar_tensor_tensor(out=di, in0=sr, scalar=2.0, in1=si, op0=ALU.mult, op1=ALU.mult), reads=["G"], writes=["G"])
        s.op("vector", lambda e, dr=dr: e.tensor_tensor(out=dr, in0=t1[:, :, 0], in1=t0[:, :, 0], op=ALU.subtract), reads=["t0", "t1", "G"], writes=["G"])
        cur = nxt
    s.op("vector", lambda e, cur=cur: e.tensor_copy(out=c.A[:], in_=Gp[:, cur]), reads=["G"], writes=["A"])
    s5_combine(c, xall_ap, oneh_ap)
    s.op("vector", lambda e: e.tensor_copy(out=V[:, 0], in_=c.xinit[:]), reads=["xinit"], writes=["V"])
    s.op("gpsimd", lambda e: e.memset(Wt[:], 0.0), writes=[("Wt", 0), ("Wt", 1)])
    vcur = 0
    gi = 0
    for j in range(8):
        ws_ = j % 2
        vr = bc_last(V[:, vcur, 0, :].unsqueeze(2), 32); vi = bc_last(V[:, vcur, 1, :].unsqueeze(2), 32)
        cre = Czf[:, 0]; cim = Czf[:, 1]
        def blk(ri, ws_=ws_):
            return [Wt[:, ws_, ri].rearrange("q (a b) c -> q a b c", b=4)[:, :, b, 32 * b:32 * b + 32] for b in range(4)]
        s.op("vector", lambda e, vr=vr: e.tensor_tensor(out=wa[:], in0=cre, in1=vr, op=ALU.mult), reads=CZ + ["V"], writes=["wa"])
        s.op("vector", lambda e, vi=vi: e.tensor_tensor(out=wb[:], in0=cim, in1=vi, op=ALU.mult), reads=CZ + ["V"], writes=["wb"])
        s.op("vector", lambda e: e.tensor_tensor(out=wa[:], in0=wa[:], in1=wb[:], op=ALU.subtract), reads=["wa", "wb"], writes=["wa"])
        for b, dst in enumerate(blk(0)):
            s.op("vector", lambda e, b=b, dst=dst: e.tensor_copy(out=dst, in_=wa[:].rearrange("q (a b) c -> q a b c", b=4)[:, :, b, :]),
                 reads=["wa"], writes=[("Wt", ws_)])
        s.op("vector", lambda e, vi=vi: e.tensor_tensor(out=wa[:], in0=cre, in1=vi, op=ALU.mult), reads=CZ + ["V", ("Wt", ws_)], writes=["wa"])
        s.op("vector", lambda e, vr=vr: e.tensor_tensor(out=wb[:], in0=cim, in1=vr, op=ALU.mult), reads=CZ + ["V"], writes=["wb"])
        s.op("vector", lambda e: e.scalar_tensor_tensor(out=wa[:], in0=wa[:], scalar=-1.0, in1=wb[:], op0=ALU.mult, op1=ALU.subtract), reads=["wa", "wb"], writes=["wa"])
        for b, dst in enumerate(blk(1)):
            s.op("vector", lambda e, b=b, dst=dst: e.tensor_copy(out=dst, in_=wa[:].rearrange("q (a b) c -> q a b c", b=4)[:, :, b, :]),
                 reads=["wa"], writes=[("Wt", ws_)])
        osl = slice((j % 4) * TC, (j % 4 + 1) * TC)
        for cc in range(8):
            for pp in range(4):
                p = cc * 4 + pp
                s.op("tensor", lambda e, cc=cc, p=p, pp=pp, ws_=ws_, osl=osl: e.matmul(ps[:, cc, osl], Wt[:, ws_, 0, p, :], Pb[:, 0, p, :],
                                                                                    start=(pp == 0), stop=False, skip_group_check=True),
                     reads=[("Wt", ws_), "Pb"], writes=[("ps", cc)])
                s.op("tensor", lambda e, cc=cc, p=p, pp=pp, ws_=ws_, osl=osl: e.matmul(ps[:, cc, osl], Wt[:, ws_, 1, p, :], Pb[:, 1, p, :],
                                                                                    start=False, stop=(pp == 3), skip_group_check=True),
                     reads=[("Wt", ws_), "Pb"], writes=[("ps", cc)])
        nv = 1 - vcur
        a0 = c.qs[:, 9, :]; a1 = c.qs[:, 10, :]
        s.op("vector", lambda e, vcur=vcur: e.tensor_tensor(out=a0, in0=V[:, vcur, 0, :], in1=G128[:, 0, :], op=ALU.mult), reads=["V", "G128"], writes=["a0"])
        s.op("vector", lambda e, vcur=vcur: e.tensor_tensor(out=a1, in0=V[:, vcur, 1, :], in1=G128[:, 1, :], op=ALU.mult), reads=["V", "G128"], writes=["a1"])
        s.op("vector", lambda e, nv=nv: e.tensor_tensor(out=V[:, nv, 0, :], in0=a0, in1=a1, op=ALU.subtract), reads=["a0", "a1", "V"], writes=["V"])
        s.op("vector", lambda e, vcur=vcur: e.tensor_tensor(out=a0, in0=V[:, vcur, 0, :], in1=G128[:, 1, :], op=ALU.mult), reads=["V", "G128"], writes=["a0"])
        s.op("vector", lambda e, vcur=vcur: e.tensor_tensor(out=a1, in0=V[:, vcur, 1, :], in1=G128[:, 0, :], op=ALU.mult), reads=["V", "G128"], writes=["a1"])
        s.op("vector", lambda e, nv=nv: e.tensor_tensor(out=V[:, nv, 1, :], in0=a0, in1=a1, op=ALU.add), reads=["a0", "a1", "V"], writes=["V"])
        vcur = nv
        if j % 4 == 3:
            t = j // 4
            tsl = slice(t * 512, (t + 1) * 512)
            for cc in range(8):
                sl = gi % 2; gi += 1
                s.dma("sync", lambda e, cc=cc, tsl=tsl, sl=sl: e.dma_start(out=yl[:, sl, :], in_=ypreT[cc * 128:(cc + 1) * 128, tsl]), writes=[("yl", sl)], sem_key=("yl", sl))
                s.op("vector", lambda e, cc=cc, sl=sl: e.tensor_tensor(out=yl[:, sl, :], in0=ps[:, cc, :], in1=yl[:, sl, :], op=ALU.add),
                     reads=[("ps", cc), ("yl", sl)], writes=[("yl", sl)])
                emit_gelu_src(c, yl[:, sl, :], [("yl", sl)], yst[:, sl, :], ga[:, sl, :], gb[:, sl, :], writes=[("yst", sl)], tmpkeys=(("ga", sl), ("gb", sl)))
                ids.append(s.dma("sync", lambda e, cc=cc, tsl=tsl, sl=sl: e.dma_start(out=yagT[cc * 128:(cc + 1) * 128, tsl], in_=yst[:, sl, :]),
                                 reads=[("yst", sl)], sem_key=("yst", sl)))
    return ids


def s5_layouts(a_re, a_im, log_dt, b_re, b_im, c_re, c_im, d_skip):
    NP = 32
    def qlay(v):
        return v.reshape(NP, 2, 64).transpose(1, 2, 0).reshape(128, NP)
    ldt2 = np.repeat(log_dt[:, None], 64, axis=1)
    aq = np.stack([qlay(a_re), qlay(a_im), qlay(ldt2)], axis=1).astype(np.float32)
    def rlay(v):
        return v.reshape(NP, 128).reshape(-1)
    arow1 = np.stack([rlay(a_re), rlay(a_im), rlay(ldt2)], axis=0)
    arow = np.ascontiguousarray(np.broadcast_to(arow1[None], (128, 3, NP * 128))).astype(np.float32)
    BT = np.zeros((128, 2, NP, 128), np.float32)
    Cz = np.zeros((128, 2, NP, 128), np.float32)
    for p in range(NP):
        for g2 in range(2):
            g = 2 * p + g2
            r0 = 32 * (p % 4) + 16 * g2
            for ri, (bm, cm) in enumerate(((b_re, c_re), (b_im, c_im))):
                BT[r0:r0 + 16, ri, p, g2 * 64:(g2 + 1) * 64] = bm[g].T
                Cz[g2 * 64:(g2 + 1) * 64, ri, p, r0:r0 + 16] = cm[g].T
    dq = np.ascontiguousarray(d_skip.reshape(8, 128).T).astype(np.float32)
    return dict(aq=aq, arow=arow, BT=BT.reshape(128, 2, NP * 128), Cz=Cz, dq=dq)


def _dram_in(nc, name, shape):
    return nc.dram_tensor(name, list(shape), F32, kind="ExternalInput").ap()


def _dram_out(nc, name, shape):
    return nc.dram_tensor(name, list(shape), F32, kind="ExternalOutput").ap()


def build_L1():
    nc = bass.Bass("TRN2", target_bir_lowering=False)
    xT = _dram_in(nc, "xT", [D, TOK]); g1 = _dram_in(nc, "g1", [D]); g2 = _dram_in(nc, "g2", [D])
    wg = _dram_in(nc, "wg", [D, DFF]); wu = _dram_in(nc, "wu", [D, DFF]); wd = _dram_in(nc, "wd", [DFF, D])
    win = _dram_in(nc, "win", [D, MIXIN])
    x1T = _dram_out(nc, "x1T", [D, TOK]); projT = _dram_out(nc, "projT", [MIXIN, TOK])
    c = Ctx(); c.nc = nc; c.s = Sched()
    with contextlib.ExitStack() as st:
        alloc_common(nc, st, c)
        stage = st.enter_context(nc.sbuf_tensor("stage", [128, 2, 512], F32))
        tmpa = st.enter_context(nc.sbuf_tensor("tmpa", [128, 2, 512], F32))
        tmpb = st.enter_context(nc.sbuf_tensor("tmpb", [128, 2, 512], F32))
        emit_consts(c)
        load_gain(c, 0, g1); load_gain(c, 1, g2)
        load_xT(c, xT)
        emit_ffn(c, 0, wg, wu, wd)
        ids = store_T(c, c.x, "x", x1T)
        emit_rmsnorm(c, 1)
        ids += emit_inproj(c, win, projT, stage, tmpa, tmpb)
        c.s.emit(nc, final_wait_ops=ids)
    return nc


def build_S5L():
    nc = bass.Bass("TRN2", target_bir_lowering=False)
    uaT = _dram_in(nc, "uaT", [1024, TOK]); aq = _dram_in(nc, "aq_in", [128, 3, NP])
    arow = _dram_in(nc, "arow_in", [128, 3, NP * 128]); BT = _dram_in(nc, "BT_in", [128, 2, NP * 128])
    Cz = _dram_in(nc, "Cz_in", [128, 2, NP, 128]); dq = _dram_in(nc, "dq_in", [128, 8])
    ypre = _dram_out(nc, "ypreT", [1024, TOK]); xend = _dram_out(nc, "xend_out", [128, 2, NP])
    c = Ctx(); c.nc = nc; c.s = Sched(); s = c.s
    with contextlib.ExitStack() as st:
        s5_alloc(nc, st, c, True)
        for cc in range(8):
            s.dma("gpsimd", lambda e, cc=cc: e.dma_start(out=c.ua[:, cc, :], in_=uaT[cc * 128:(cc + 1) * 128, :]), writes=[("ua", cc)], sem_key=("ua", cc))
        s5_setup(c, aq, arow, BT, False)
        s.dma("gpsimd", lambda e: e.dma_start(out=c.Cz[:], in_=Cz), writes=["Cz"], sem_key="Cz")
        s.dma("sync", lambda e: e.dma_start(out=c.dq[:], in_=dq), writes=["dq"], sem_key="dq")
        ids = s5_main(c, True, ypre, zero_init=True, pregelu=True)
        ids.append(s.dma("sync", lambda e: e.dma_start(out=xend, in_=c.xend[:]), reads=["xend"], sem_key="xe"))
        s.emit(nc, final_wait_ops=ids)
    return nc


def build_S5C():
    nc = bass.Bass("TRN2", target_bir_lowering=False)
    aq = _dram_in(nc, "aq_in", [128, 3, NP]); Cz = _dram_in(nc, "Cz_in", [128, 2, NP, 128])
    xall = _dram_in(nc, "xall_in", [128, 8, 2, NP]); oneh = _dram_in(nc, "oneh_in", [128, 8])
    ypre = _dram_in(nc, "ypreT_in", [1024, TOK]); yag = _dram_out(nc, "yagT", [1024, TOK])
    c = Ctx(); c.nc = nc; c.s = Sched()
    with contextlib.ExitStack() as st:
        ids = emit_s5_correct(c, nc, st, aq, Cz, xall, oneh, ypre, yag)
        c.s.emit(nc, final_wait_ops=ids)
    return nc


def build_LB1():
    nc = bass.Bass("TRN2", target_bir_lowering=False)
    yagT = _dram_in(nc, "yagT_in", [1024, TOK]); projT = _dram_in(nc, "projT_in", [MIXIN, TOK])
    w_glu = _dram_in(nc, "w_glu", [1024, 1024]); b_glu = _dram_in(nc, "b_glu", [1024])
    ln_g = _dram_in(nc, "ln_g", [1024]); ln_b = _dram_in(nc, "ln_b", [1024])
    wsT = _dram_in(nc, "wsT", [8, 128, 128]); b_s = _dram_in(nc, "b_s", [8, 128]); ident = _dram_in(nc, "ident", [128, 128])
    yaT = _dram_out(nc, "yaT", [1024, TOK]); ybT = _dram_out(nc, "ybT", [1024, TOK])
    c = Ctx(); c.nc = nc; c.s = Sched()
    with contextlib.ExitStack() as st:
        ids = emit_glu_sgu(c, nc, st, yagT, projT, w_glu, b_glu, ln_g, ln_b, wsT, b_s, ident, yaT, ybT)
        c.s.emit(nc, final_wait_ops=ids)
    return nc


def build_LB2():
    nc = bass.Bass("TRN2", target_bir_lowering=False)
    x1T = _dram_in(nc, "x1T_in", [D, TOK]); yaT = _dram_in(nc, "yaT_in", [1024, TOK]); ybT = _dram_in(nc, "ybT_in", [1024, TOK])
    g_mix = _dram_in(nc, "g_mix", [D]); w_a = _dram_in(nc, "w_a", [1024, D]); w_b = _dram_in(nc, "w_b", [1024, D])
    w_gate = _dram_in(nc, "w_gate", [D, 2 * D]); b_gate = _dram_in(nc, "b_gate", [2 * D]); w_out = _dram_in(nc, "w_out", [D, D])
    x2T = _dram_out(nc, "x2T", [D, TOK])
    c = Ctx(); c.nc = nc; c.s = Sched()
    with contextlib.ExitStack() as st:
        ids = emit_merge(c, nc, st, x1T, yaT, ybT, g_mix, w_a, w_b, w_gate, b_gate, w_out, x2T)
        c.s.emit(nc, final_wait_ops=ids)
    return nc


def build_L3():
    nc = bass.Bass("TRN2", target_bir_lowering=False)
    xT = _dram_in(nc, "xT", [D, TOK]); g1 = _dram_in(nc, "g1", [D]); g2 = _dram_in(nc, "g2", [D])
    wg = _dram_in(nc, "wg", [D, DFF]); wu = _dram_in(nc, "wu", [D, DFF]); wd = _dram_in(nc, "wd", [DFF, D])
    outT = _dram_out(nc, "outT", [D, TOK])
    c = Ctx(); c.nc = nc; c.s = Sched()
    with contextlib.ExitStack() as st:
        alloc_common(nc, st, c)
        stage = st.enter_context(nc.sbuf_tensor("stage", [128, 2, 512], F32))
        emit_consts(c)
        load_gain(c, 0, g1); load_gain(c, 1, g2)
        load_xT(c, xT)
        emit_ffn(c, 0, wg, wu, wd)
        ids = emit_final_norm(c, 1, outT, stage)
        c.s.emit(nc, final_wait_ops=ids)
    return nc


NCORES = 8


def _run(nc, maps):
    return run_bass_kernel_spmd(nc, maps, core_ids=list(range(NCORES))).results


def kernel(x, ffn1_norm, ffn1_w_gate, ffn1_w_up, ffn1_w_down, mix_norm, w_in,
           s5_a_re, s5_a_im, s5_log_dt, s5_b_re, s5_b_im, s5_c_re, s5_c_im, s5_d,
           s5_w_glu, s5_b_glu, sgu_ln_g, sgu_ln_b, sgu_w_s, sgu_b_s,
           w_branch_a, w_branch_b, w_gate, b_gate, w_out,
           ffn2_norm, ffn2_w_gate, ffn2_w_up, ffn2_w_down, final_norm):
    f = lambda a: np.ascontiguousarray(np.asarray(a, dtype=np.float32))
    x = f(x)[0]
    n = NCORES
    xTs = [np.ascontiguousarray(x[i * TOK:(i + 1) * TOK].T) for i in range(n)]
    w1 = dict(g1=f(ffn1_norm)[0], g2=f(mix_norm)[0], wg=f(ffn1_w_gate)[0], wu=f(ffn1_w_up)[0], wd=f(ffn1_w_down)[0], win=f(w_in)[0])
    r1 = _run(build_L1(), [dict(w1, xT=xTs[i]) for i in range(n)])
    x1T = [r1[i]["x1T"] for i in range(n)]; projT = [r1[i]["projT"] for i in range(n)]
    lay = s5_layouts(f(s5_a_re)[0], f(s5_a_im)[0], f(s5_log_dt)[0], f(s5_b_re)[0], f(s5_b_im)[0], f(s5_c_re)[0], f(s5_c_im)[0], f(s5_d)[0])
    r2 = _run(build_S5L(), [dict(uaT=np.ascontiguousarray(projT[i][0:1024]), aq_in=lay["aq"], arow_in=lay["arow"], BT_in=lay["BT"],
                                 Cz_in=lay["Cz"], dq_in=lay["dq"]) for i in range(n)])
    xall = np.ascontiguousarray(np.stack([r2[i]["xend_out"] for i in range(n)], axis=1))
    maps = []
    for i in range(n):
        oh = np.zeros((128, 8), np.float32); oh[:, i] = 1.0
        maps.append(dict(aq_in=lay["aq"], Cz_in=lay["Cz"], xall_in=xall, oneh_in=oh, ypreT_in=r2[i]["ypreT"]))
    r3 = _run(build_S5C(), maps)
    wsT = np.ascontiguousarray(np.transpose(f(sgu_w_s)[0], (0, 2, 1)))
    wl = dict(w_glu=f(s5_w_glu)[0], b_glu=f(s5_b_glu)[0], ln_g=f(sgu_ln_g)[0], ln_b=f(sgu_ln_b)[0], wsT=wsT, b_s=f(sgu_b_s)[0],
              ident=np.eye(128, dtype=np.float32))
    r4 = _run(build_LB1(), [dict(wl, yagT_in=r3[i]["yagT"], projT_in=projT[i]) for i in range(n)])
    wm = dict(g_mix=f(mix_norm)[0], w_a=f(w_branch_a)[0], w_b=f(w_branch_b)[0], w_gate=f(w_gate)[0], b_gate=f(b_gate)[0], w_out=f(w_out)[0])
    r5 = _run(build_LB2(), [dict(wm, x1T_in=x1T[i], yaT_in=r4[i]["yaT"], ybT_in=r4[i]["ybT"]) for i in range(n)])
    w3 = dict(g1=f(ffn2_norm)[0], g2=f(final_norm), wg=f(ffn2_w_gate)[0], wu=f(ffn2_w_up)[0], wd=f(ffn2_w_down)[0])
    r6 = _run(build_L3(), [dict(w3, xT=r5[i]["x2T"]) for i in range(n)])
    out = np.concatenate([r6[i]["outT"].T for i in range(n)], axis=0)
    return np.ascontiguousarray(out[None].astype(np.float32))
```

```python
import contextlib
import numpy as np
import concourse.bass as bass
import concourse.mybir as mybir
from concourse.bass_utils import run_bass_kernel_spmd

ENGINES = ("tensor", "vector", "scalar", "gpsimd", "sync")
RELAX_BULK = False


class Sched:
    def __init__(self, self_edges=True):
        self.ops = []
        self.last_w = {}
        self.readers = {}
        self.self_edges = self_edges
        self.fence = []
        self.fenced = set()

    def _add(self, eng, emit, reads, writes, dma_key=None, size=0, strict=False):
        i = len(self.ops)
        deps = set()
        for r in reads:
            if r in self.last_w:
                deps.add(self.last_w[r])
        for w in writes:
            if w in self.last_w:
                deps.add(self.last_w[w])
            deps.update(self.readers.get(w, ()))
        if self.fence and eng not in self.fenced:
            deps.update(self.fence)
            self.fenced.add(eng)
        deps.discard(i)
        self.ops.append(dict(eng=eng, emit=emit, deps=sorted(deps), dma_key=dma_key, size=size, strict=strict))
        for r in reads:
            self.readers.setdefault(r, []).append(i)
        for w in writes:
            self.last_w[w] = i
            self.readers[w] = []
        return i

    def barrier(self):
        last = {}
        for i, o in enumerate(self.ops):
            k = ("d", o["dma_key"]) if o["dma_key"] is not None else ("e", o["eng"])
            last[k] = i
        self.fence = sorted(last.values())
        self.fenced = set()

    def op(self, eng, emit, reads=(), writes=(), size=0, strict=False):
        return self._add(eng, emit, list(reads), list(writes), size=size, strict=strict)

    def dma(self, eng, emit, reads=(), writes=(), sem_key=None):
        assert sem_key is not None
        return self._add(eng, emit, list(reads), list(writes), dma_key=sem_key)

    def _self_skip(self, p, o):
        if p["eng"] != o["eng"]:
            return False
        if p["eng"] == "tensor" or not self.self_edges:
            return True
        return RELAX_BULK and p["size"] >= 256 and not o["strict"]

    def emit(self, nc, final_wait_ops=()):
        ops = self.ops
        need_inc = [False] * len(ops)
        for i, o in enumerate(ops):
            for d in o["deps"]:
                p = ops[d]
                if p["dma_key"] is not None:
                    continue
                if self._self_skip(p, o):
                    continue
                need_inc[d] = True
        eng_cnt = {e: 0 for e in ENGINES}
        inc_val = [None] * len(ops)
        dma_cnt = {}
        for i, o in enumerate(ops):
            if o["dma_key"] is not None:
                k = o["dma_key"]
                dma_cnt[k] = dma_cnt.get(k, 0) + 16
                inc_val[i] = dma_cnt[k]
            elif need_inc[i]:
                eng_cnt[o["eng"]] += 1
                inc_val[i] = eng_cnt[o["eng"]]
        import contextlib
        with contextlib.ExitStack() as st:
            esem = {e: st.enter_context(nc.semaphore("e_" + e)) for e in ENGINES}
            dsem = {k: st.enter_context(nc.semaphore("d_%d" % j)) for j, k in enumerate(dma_cnt)}
            block = st.enter_context(nc.Block())
            per_eng = {e: [i for i, o in enumerate(ops) if o["eng"] == e] for e in ENGINES}

            def make(e):
                def body(eng):
                    waited = {}
                    for i in per_eng[e]:
                        o = ops[i]
                        for d in o["deps"]:
                            p = ops[d]
                            if p["dma_key"] is not None:
                                key = ("d", p["dma_key"])
                                sem = dsem[p["dma_key"]]
                            else:
                                if self._self_skip(p, o):
                                    continue
                                key = ("e", p["eng"])
                                sem = esem[p["eng"]]
                            v = inc_val[d]
                            if waited.get(key, 0) >= v:
                                continue
                            eng.wait_ge(sem, v)
                            waited[key] = v
                        ins = o["emit"](eng)
                        if o["dma_key"] is not None:
                            ins.then_inc(dsem[o["dma_key"]], 16)
                        elif need_inc[i]:
                            ins.then_inc(esem[e], 1)
                    if e == "sync":
                        for i in final_wait_ops:
                            p = ops[i]
                            sem = dsem[p["dma_key"]] if p["dma_key"] is not None else esem[p["eng"]]
                            eng.wait_ge(sem, inc_val[i])
                return body
            for e in ENGINES:
                if per_eng[e] or e == "sync":
                    getattr(block, e)(make(e))
        return nc


F32 = mybir.dt.float32
BF16 = mybir.dt.bfloat16
AF = mybir.ActivationFunctionType
ALU = mybir.AluOpType

D = 2048
KC = D // 128
TOK = 1024
NT = TOK // 512
DFF = 5632
NFB = DFF // 512
EPS = 1e-6


class Ctx:
    pass


_DSZ = {F32: 4, BF16: 2}


class Arena:
    def __init__(self, nc, st, kb):
        self.words = kb * 256
        self.t = st.enter_context(nc.sbuf_tensor("arena", [128, self.words], F32))

    def view(self, off_kb, shape, dt=F32, parts=128):
        n = 1
        for d in shape[1:]:
            n *= d
        esz = 2 if dt == BF16 else 4
        words = (n * esz + 3) // 4
        lo = int(round(off_kb * 256))
        assert lo + words <= self.words, (off_kb, shape)
        ap = self.t[0:parts, lo:lo + words]
        if dt != F32:
            ap = ap.bitcast(dt)
        ap = ap[:, 0:n]
        if len(shape) == 2:
            return ap
        names = " ".join("d%d" % i for i in range(len(shape) - 1))
        kw = {"d%d" % i: shape[i + 1] for i in range(len(shape) - 2)}
        return ap.rearrange("p (%s) -> p %s" % (names, names), **kw)


def alloc_ffn(c, V):
    c.x = V("x", [128, KC, TOK], F32)
    c.h = V("h", [128, KC, TOK], BF16)
    c.hid = V("hid", [128, 2, 4, TOK], BF16)
    c.sq = V("sq", [128, 2, 512], BF16)
    c.rstd = V("rstd", [128, TOK], F32)
    c.silu = V("silu", [128, 2, 512], F32)
    c.ones = V("ones", [128, 128], BF16)
    c.gains = V("gains", [128, 4, KC], F32)
    c.epsb = V("epsb", [128, 1], F32)
    c.sqslot = 0
    c.silslot = 0


def emit_consts(c):
    s = c.s
    s.op("gpsimd", lambda e: e.memset(c.ones[:], 1.0), writes=[("ones",)])
    s.op("gpsimd", lambda e: e.memset(c.epsb[:], EPS), writes=[("epsb",)])


def load_gain(c, idx, g_ap):
    c.s.dma("sync", lambda e: e.dma_start(out=c.gains[:, idx, :], in_=g_ap.rearrange("(k p) -> p k", p=128),
                                          allow_slow_non_contiguous=True),
            writes=[("gain", idx)], sem_key=("gain", idx))


def emit_rmsnorm(c, gidx, out_key="h"):
    s = c.s
    for t in range(NT):
        tsl = slice(t * 512, (t + 1) * 512)
        bank = 7
        for k in range(KC):
            sl = c.sqslot; c.sqslot ^= 1
            s.op("scalar", lambda e, k=k, sl=sl, tsl=tsl: e.activation(out=c.sq[:, sl, :], in_=c.x[:, k, tsl], func=AF.Square),
                 reads=[("x", k, t)], writes=[("sq", sl)])
            s.op("tensor", lambda e, k=k, sl=sl, bank=bank: e.matmul(c.ps[:, bank, :], c.ones[:], c.sq[:, sl, :],
                                                          start=(k == 0), stop=(k == KC - 1)),
                 reads=[("sq", sl), ("ones",)], writes=[("ps", bank)])
        s.op("scalar", lambda e, tsl=tsl, bank=bank: e.activation(out=c.rstd[:, tsl], in_=c.ps[:, bank, :], func=AF.Sqrt,
                                              bias=c.epsb[:, 0:1], scale=1.0 / D),
             reads=[("ps", bank), ("epsb",)], writes=[("rstd", t)])
        s.op("vector", lambda e, tsl=tsl: e.reciprocal(out=c.rstd[:, tsl], in_=c.rstd[:, tsl]),
             reads=[("rstd", t)], writes=[("rstd", t)])
        for k in range(KC):
            s.op("vector", lambda e, k=k, tsl=tsl: e.scalar_tensor_tensor(
                out=c.h[:, k, tsl], in0=c.x[:, k, tsl], scalar=c.gains[:, gidx, k:k + 1],
                in1=c.rstd[:, tsl], op0=ALU.mult, op1=ALU.mult),
                 reads=[("x", k, t), ("gain", gidx), ("rstd", t)], writes=[(out_key, k, t)])


def wload(c, view_shape, src_ap, tag):
    slot = c.wslot; c.wslot = (c.wslot + 1) % 4
    n = 1
    for d in view_shape[1:]:
        n *= d
    assert n <= 8192
    flat = c.wring[:, slot, 0:n]
    if len(view_shape) == 3:
        view = flat.rearrange("p (a b) -> p a b", a=view_shape[1])
    else:
        view = flat
    c.s.dma("gpsimd", lambda e: e.dma_start(out=view, in_=src_ap), writes=[("w", slot)], sem_key=("w", slot))
    return slot, view


def emit_ffn(c, gidx, wg, wu, wd):
    s = c.s
    emit_rmsnorm(c, gidx)
    wg_v = wg.rearrange("(k p) f -> p k f", p=128)
    wu_v = wu.rearrange("(k p) f -> p k f", p=128)
    wd_v = wd.rearrange("(m p) d -> p m d", p=128)
    for b in range(NFB):
        hb = b % 2
        gs, gv = wload(c, [128, KC, 512], wg_v[:, :, b * 512:(b + 1) * 512], "g")
        us, uv = wload(c, [128, KC, 512], wu_v[:, :, b * 512:(b + 1) * 512], "u")
        ds_, dv = wload(c, [128, 4, D], wd_v[:, b * 4:(b + 1) * 4, :], "d")
        for m in range(4):
            for kind, (ws, wv) in enumerate(((gs, gv), (us, uv))):
                for t in range(NT):
                    bank = kind * 2 + t
                    for k in range(KC):
                        s.op("tensor", lambda e, wv=wv, k=k, m=m, t=t, bank=bank: e.matmul(
                            c.ps[:, bank, :], wv[:, k, m * 128:(m + 1) * 128], c.h[:, k, t * 512:(t + 1) * 512],
                            start=(k == 0), stop=(k == KC - 1)),
                            reads=[("w", ws), ("h", k, t)], writes=[("ps", bank)])
            for t in range(NT):
                sl = c.silslot; c.silslot ^= 1
                s.op("scalar", lambda e, t=t, sl=sl: e.activation(out=c.silu[:, sl, :], in_=c.ps[:, t, :], func=AF.Silu),
                     reads=[("ps", t)], writes=[("silu", sl)])
                s.op("vector", lambda e, t=t, sl=sl, m=m, hb=hb: e.tensor_tensor(
                    out=c.hid[:, hb, m, t * 512:(t + 1) * 512], in0=c.ps[:, 2 + t, :], in1=c.silu[:, sl, :], op=ALU.mult),
                     reads=[("ps", 2 + t), ("silu", sl)], writes=[("hid", hb, m, t)])
        gi = 0
        for n in range(KC):
            for t in range(NT):
                bank = 4 + (gi % 4); gi += 1
                for m in range(4):
                    s.op("tensor", lambda e, n=n, t=t, m=m, bank=bank, hb=hb, dv=dv: e.matmul(
                        c.ps[:, bank, :], dv[:, m, n * 128:(n + 1) * 128], c.hid[:, hb, m, t * 512:(t + 1) * 512],
                        start=(m == 0), stop=(m == 3)),
                        reads=[("w", ds_), ("hid", hb, m, t)], writes=[("ps", bank)])
                s.op("vector", lambda e, n=n, t=t, bank=bank: e.scalar_tensor_tensor(
                    out=c.x[:, n, t * 512:(t + 1) * 512], in0=c.ps[:, bank, :], scalar=0.5,
                    in1=c.x[:, n, t * 512:(t + 1) * 512], op0=ALU.mult, op1=ALU.add),
                     reads=[("ps", bank), ("x", n, t)], writes=[("x", n, t)])


def load_x(c, x_ap):
    raise NotImplementedError


def load_xT(c, xT_ap):
    v = xT_ap.rearrange("(k p) t -> p k t", p=128)
    for k in range(KC):
        c.s.dma("sync", lambda e, k=k: e.dma_start(out=c.x[:, k, :], in_=v[:, k, :]),
                writes=[("x", k, t) for t in range(NT)], sem_key=("xld", k))


def store_T(c, src, key, outT_ap):
    v = outT_ap.rearrange("(k p) t -> p k t", p=128)
    ids = []
    for k in range(KC):
        ids.append(c.s.dma("sync", lambda e, k=k: e.dma_start(out=v[:, k, :], in_=src[:, k, :]),
                           reads=[(key, k, t) for t in range(NT)], sem_key=("st", k)))
    return ids


GELU_C1 = 0.044715
GELU_C2 = 1.5957691216057308


def emit_gelu_from_psum(c, bank, out_ap, tmp_a, tmp_b, reads, writes, tmpkeys):
    s = c.s
    ka, kb = tmpkeys
    s.op("scalar", lambda e: e.activation(out=tmp_a, in_=c.ps[:, bank, :], func=AF.Square),
         reads=reads, writes=[ka])
    s.op("vector", lambda e: e.tensor_scalar(out=tmp_a, in0=tmp_a, scalar1=GELU_C1, scalar2=1.0, op0=ALU.mult, op1=ALU.add),
         reads=[ka], writes=[ka])
    s.op("vector", lambda e: e.tensor_tensor(out=tmp_a, in0=c.ps[:, bank, :], in1=tmp_a, op=ALU.mult),
         reads=reads + [ka], writes=[ka])
    s.op("scalar", lambda e: e.activation(out=tmp_b, in_=tmp_a, func=AF.Sigmoid, scale=GELU_C2),
         reads=[ka], writes=[kb])
    s.op("vector", lambda e: e.tensor_tensor(out=out_ap, in0=c.ps[:, bank, :], in1=tmp_b, op=ALU.mult),
         reads=reads + [kb], writes=writes)


MIXIN = 3072


def emit_inproj(c, w_in, projT_ap, stage, tmpa, tmpb):
    s = c.s
    wv = w_in.rearrange("(k p) f -> p k f", p=128)
    ids = []
    gi = 0
    for u in range(MIXIN // 512):
        ws, wview = wload(c, [128, KC, 512], wv[:, :, u * 512:(u + 1) * 512], "in")
        for m in range(4):
            cc = u * 4 + m
            for t in range(NT):
                bank = gi % 4
                sl = gi % 2
                gi += 1
                for k in range(KC):
                    s.op("tensor", lambda e, wview=wview, k=k, m=m, t=t, bank=bank: e.matmul(
                        c.ps[:, bank, :], wview[:, k, m * 128:(m + 1) * 128], c.h[:, k, t * 512:(t + 1) * 512],
                        start=(k == 0), stop=(k == KC - 1)),
                        reads=[("w", ws), ("h", k, t)], writes=[("ps", bank)])
                if cc < 8:
                    s.op("scalar", lambda e, bank=bank, cc=cc, t=t: e.activation(out=c.ua[:, cc, t * 512:(t + 1) * 512], in_=c.ps[:, bank, :], func=AF.Copy),
                         reads=[("ps", bank)], writes=[("ua", cc)])
                    continue
                else:
                    emit_gelu_from_psum(c, bank, stage[:, sl, :], tmpa[:, sl, :], tmpb[:, sl, :],
                                        reads=[("ps", bank)], writes=[("stage", sl)], tmpkeys=(("tmpa", sl), ("tmpb", sl)))
                ids.append(s.dma("sync", lambda e, cc=cc, t=t, sl=sl: e.dma_start(
                    out=projT_ap[(cc - 8) * 128:(cc - 7) * 128, t * 512:(t + 1) * 512], in_=stage[:, sl, :]),
                    reads=[("stage", sl)], sem_key=("stg", sl)))
    return ids


TWO_PI = 6.283185
INV_2PI = 0.15915494309189535
I32 = mybir.dt.int32


def bc_last(ap, n):
    shp = list(ap.shape)
    shp[-1] = n
    return ap.to_broadcast(shp)


def emit_sincos(c, f_ap, sin_ap, cos_ap, scr, keyp, reads):
    s = c.s
    K = lambda n: (keyp, n)
    s.op("vector", lambda e: e.tensor_copy(out=scr["i32"], in_=f_ap), reads=reads, writes=[K("i32")])
    s.op("vector", lambda e: e.tensor_copy(out=scr["a"], in_=scr["i32"]), reads=[K("i32")], writes=[K("a")])
    s.op("vector", lambda e: e.tensor_tensor(out=scr["a"], in0=f_ap, in1=scr["a"], op=ALU.subtract), reads=reads + [K("a")], writes=[K("a")])
    for which, out_ap, off in (("s", sin_ap, 0.0), ("c", cos_ap, 0.25)):
        if off != 0.0:
            s.op("vector", lambda e, off=off: e.tensor_scalar(out=scr["b"], in0=scr["a"], scalar1=off, scalar2=None, op0=ALU.add),
                 reads=[K("a")], writes=[K("b")])
            src = scr["b"]; srck = K("b")
        else:
            src = scr["a"]; srck = K("a")
        s.op("vector", lambda e, src=src: e.scalar_tensor_tensor(out=scr["b"], in0=src, scalar=0.5, in1=src, op0=ALU.is_gt, op1=ALU.subtract),
             reads=[srck], writes=[K("b")])
        s.op("vector", lambda e: e.scalar_tensor_tensor(out=scr["b"], in0=scr["b"], scalar=0.5, in1=scr["b"], op0=ALU.is_gt, op1=ALU.subtract),
             reads=[K("b")], writes=[K("b")])
        s.op("scalar", lambda e, out_ap=out_ap: e.activation(out=out_ap, in_=scr["b"], func=AF.Sin, scale=TWO_PI),
             reads=[K("b")], writes=[K(which)])


NP = 32
TC = 128


def emit_gelu_src(c, src, src_keys, out_ap, tmp_a, tmp_b, writes, tmpkeys):
    s = c.s
    ka, kb = tmpkeys
    s.op("scalar", lambda e: e.activation(out=tmp_a, in_=src, func=AF.Square), reads=src_keys, writes=[ka])
    s.op("vector", lambda e: e.tensor_scalar(out=tmp_a, in0=tmp_a, scalar1=GELU_C1, scalar2=1.0, op0=ALU.mult, op1=ALU.add),
         reads=[ka], writes=[ka])
    s.op("vector", lambda e: e.tensor_tensor(out=tmp_a, in0=src, in1=tmp_a, op=ALU.mult), reads=src_keys + [ka], writes=[ka])
    s.op("scalar", lambda e: e.activation(out=tmp_b, in_=tmp_a, func=AF.Sigmoid, scale=GELU_C2), reads=[ka], writes=[kb])
    s.op("vector", lambda e: e.tensor_tensor(out=out_ap, in0=src, in1=tmp_b, op=ALU.mult), reads=src_keys + [kb], writes=writes)


def s5_alloc_v(c, V):
    T = V
    c.BtT = T("BtT", [128, 2, NP, 128], BF16)
    c.Cz = T("Cz", [128, 2, NP, 128], BF16)
    c.E = T("E", [128, 2, NP, TC])
    c.z = T("z", [128, 2, 2, 512])
    c.mt = T("mt", [128, 2, 512])
    c.w = T("w", [128, 2, 2, 512])
    c.xs = T("xs", [128, 2, 2, 512], BF16)
    c.ypre = T("ypre", [128, 2, 512])
    c.etmp = T("etmp", [128, 2, NP, 64])
    c.arow = T("arow", [128, 3, 1024])
    c.btp = T("btp", [128, 2, 1024])
    c.rs = T("rs", [128, 8, 1024])
    c.ri = T("ri", [128, 1024], I32)
    c.aq = T("aq", [128, 3, NP])
    c.qs = T("qs", [128, 12, NP])
    c.qi = T("qi", [128, NP], I32)
    c.Gp = T("Gp", [128, 2, 2, NP])
    c.G128 = T("G128", [128, 2, NP])
    c.ini = T("ini", [128, NP, 2])
    c.itmp = T("itmp", [128, 2])
    c.xend = T("xend", [128, 2, NP])
    c.dq = T("dq", [128, 8])


def s5_setup(c, aq_ap, arow_ap, BT_ap, full):
    s = c.s
    s.dma("sync", lambda e: e.dma_start(out=c.aq[:], in_=aq_ap), writes=["aq"], sem_key="aq")
    q = lambda i: c.qs[:, i, :]
    s.op("scalar", lambda e: e.activation(out=q(0), in_=c.aq[:, 2, :], func=AF.Exp), reads=["aq"], writes=["q0"])
    s.op("vector", lambda e: e.tensor_tensor(out=q(1), in0=c.aq[:, 0, :], in1=q(0), op=ALU.mult), reads=["aq", "q0"], writes=["q1"])
    s.op("vector", lambda e: e.scalar_tensor_tensor(out=q(2), in0=c.aq[:, 1, :], scalar=INV_2PI, in1=q(0), op0=ALU.mult, op1=ALU.mult),
         reads=["aq", "q0"], writes=["q2"])
    s.op("scalar", lambda e: e.activation(out=q(3), in_=q(1), func=AF.Exp), reads=["q1"], writes=["q3"])
    emit_sincos(c, q(2), q(4), q(5), {"i32": c.qi[:], "a": q(6), "b": q(7)}, "qsc", reads=["q2"])
    QS, QC = ("qsc", "s"), ("qsc", "c")
    s.op("vector", lambda e: e.memset(c.E[:, 0, :, 0:1], 1.0), writes=["E"])
    s.op("vector", lambda e: e.memset(c.E[:, 1, :, 0:1], 0.0), reads=["E"], writes=["E"])
    s.op("vector", lambda e: e.tensor_copy(out=c.Gp[:, 0, 0, :], in_=q(5)), reads=[QC], writes=["G"])
    s.op("vector", lambda e: e.tensor_copy(out=c.Gp[:, 0, 1, :], in_=q(4)), reads=[QS, "G"], writes=["G"])
    cur = 0
    k = 1
    t0 = c.etmp[:, 0]; t1 = c.etmp[:, 1]

    def square(src, dst, dst_is_pp=True):
        sr, si = src
        dr, di = dst
        s.op("vector", lambda e: e.tensor_tensor(out=t0[:, :, 0], in0=si, in1=si, op=ALU.mult), reads=["G"], writes=["t0"])
        s.op("vector", lambda e: e.tensor_tensor(out=t1[:, :, 0], in0=sr, in1=sr, op=ALU.mult), reads=["G"], writes=["t1"])
        s.op("vector", lambda e: e.scalar_tensor_tensor(out=di, in0=sr, scalar=2.0, in1=si, op0=ALU.mult, op1=ALU.mult), reads=["G"], writes=["G"])
        s.op("vector", lambda e: e.tensor_tensor(out=dr, in0=t1[:, :, 0], in1=t0[:, :, 0], op=ALU.subtract), reads=["t0", "t1", "G"], writes=["G"])

    while k < TC:
        gr = c.Gp[:, cur, 0, :]; gi = c.Gp[:, cur, 1, :]
        grb = bc_last(gr.unsqueeze(2), k); gib = bc_last(gi.unsqueeze(2), k)

        def mk(k=k, grb=grb, gib=gib):
            s.op("vector", lambda e: e.tensor_tensor(out=t0[:, :, 0:k], in0=c.E[:, 1, :, 0:k], in1=gib, op=ALU.mult), reads=["E", "G"], writes=["t0"])
            s.op("vector", lambda e: e.tensor_tensor(out=t1[:, :, 0:k], in0=c.E[:, 0, :, 0:k], in1=grb, op=ALU.mult), reads=["E", "G"], writes=["t1"])
            s.op("vector", lambda e: e.tensor_tensor(out=c.E[:, 0, :, k:2 * k], in0=t1[:, :, 0:k], in1=t0[:, :, 0:k], op=ALU.subtract), reads=["t0", "t1", "E"], writes=["E"])
            s.op("vector", lambda e: e.tensor_tensor(out=t0[:, :, 0:k], in0=c.E[:, 0, :, 0:k], in1=gib, op=ALU.mult), reads=["E", "G"], writes=["t0"])
            s.op("vector", lambda e: e.tensor_tensor(out=t1[:, :, 0:k], in0=c.E[:, 1, :, 0:k], in1=grb, op=ALU.mult), reads=["E", "G"], writes=["t1"])
            s.op("vector", lambda e: e.tensor_tensor(out=c.E[:, 1, :, k:2 * k], in0=t1[:, :, 0:k], in1=t0[:, :, 0:k], op=ALU.add), reads=["t0", "t1", "E"], writes=["E"])
        mk()
        nxt = 1 - cur
        square((gr, gi), (c.Gp[:, nxt, 0, :], c.Gp[:, nxt, 1, :]))
        cur = nxt
        k *= 2
    s.op("vector", lambda e, cur=cur: e.tensor_copy(out=c.G128[:], in_=c.Gp[:, cur]), reads=["G"], writes=["G128"])
    if full:
        for _ in range(3):
            nxt = 1 - cur
            square((c.Gp[:, cur, 0, :], c.Gp[:, cur, 1, :]), (c.Gp[:, nxt, 0, :], c.Gp[:, nxt, 1, :]))
            cur = nxt
        s.op("scalar", lambda e: e.activation(out=q(8), in_=q(1), func=AF.Exp, scale=1024.0), reads=["q1"], writes=["q8"])
        s.op("vector", lambda e, cur=cur: e.tensor_tensor(out=c.A[:, 0, :], in0=c.Gp[:, cur, 0, :], in1=q(8), op=ALU.mult), reads=["G", "q8"], writes=["A"])
        s.op("vector", lambda e, cur=cur: e.tensor_tensor(out=c.A[:, 1, :], in0=c.Gp[:, cur, 1, :], in1=q(8), op=ALU.mult), reads=["G", "q8", "A"], writes=["A"])
    R = lambda i: c.rs[:, i, :]
    for pc in range(4):
        sl = slice(pc * 1024, (pc + 1) * 1024)
        s.dma("sync", lambda e, sl=sl: e.dma_start(out=c.arow[:], in_=arow_ap[:, :, sl]), writes=["arow"], sem_key="arow")
        s.dma("sync", lambda e, sl=sl: e.dma_start(out=c.btp[:], in_=BT_ap[:, :, sl]), writes=["btp"], sem_key="btp")
        are = c.arow[:, 0, :]; aim = c.arow[:, 1, :]
        s.op("scalar", lambda e: e.activation(out=R(0), in_=c.arow[:, 2, :], func=AF.Exp), reads=["arow"], writes=["r0"])
        s.op("vector", lambda e: e.tensor_tensor(out=R(1), in0=are, in1=R(0), op=ALU.mult), reads=["arow", "r0"], writes=["r1"])
        s.op("vector", lambda e: e.scalar_tensor_tensor(out=R(2), in0=aim, scalar=INV_2PI, in1=R(0), op0=ALU.mult, op1=ALU.mult),
             reads=["arow", "r0"], writes=["r2"])
        s.op("scalar", lambda e: e.activation(out=R(3), in_=R(1), func=AF.Exp), reads=["r1"], writes=["r3"])
        emit_sincos(c, R(2), R(4), R(5), {"i32": c.ri[:], "a": R(6), "b": R(7)}, "rsc", reads=["r2"])
        RS, RC = ("rsc", "s"), ("rsc", "c")
        s.op("vector", lambda e: e.tensor_tensor(out=R(5), in0=R(5), in1=R(3), op=ALU.mult), reads=[RC, "r3"], writes=[RC])
        s.op("vector", lambda e: e.tensor_scalar(out=R(5), in0=R(5), scalar1=-1.0, scalar2=None, op0=ALU.add), reads=[RC], writes=[RC])
        s.op("vector", lambda e: e.tensor_tensor(out=R(4), in0=R(4), in1=R(3), op=ALU.mult), reads=[RS, "r3"], writes=[RS])
        s.op("vector", lambda e: e.tensor_tensor(out=R(0), in0=are, in1=are, op=ALU.mult), reads=["arow", "r1", "r2"], writes=["r0"])
        s.op("vector", lambda e: e.tensor_tensor(out=R(1), in0=aim, in1=aim, op=ALU.mult), reads=["arow", "r3"], writes=["r1"])
        s.op("vector", lambda e: e.tensor_tensor(out=R(0), in0=R(0), in1=R(1), op=ALU.add), reads=["r0", "r1"], writes=["r0"])
        s.op("vector", lambda e: e.reciprocal(out=R(0), in_=R(0)), reads=["r0"], writes=["r0"])
        s.op("vector", lambda e: e.tensor_tensor(out=R(1), in0=R(5), in1=are, op=ALU.mult), reads=[RC, "arow", "r1"], writes=["r1"])
        s.op("vector", lambda e: e.tensor_tensor(out=R(2), in0=R(4), in1=aim, op=ALU.mult), reads=[RS, "arow", "r2", ("rsc", "a"), ("rsc", "b")], writes=["r2"])
        s.op("vector", lambda e: e.tensor_tensor(out=R(1), in0=R(1), in1=R(2), op=ALU.add), reads=["r1", "r2"], writes=["r1"])
        s.op("vector", lambda e: e.tensor_tensor(out=R(6), in0=R(1), in1=R(0), op=ALU.mult), reads=["r1", "r0", ("rsc", "a"), ("rsc", "b")], writes=["r6"])
        s.op("vector", lambda e: e.tensor_tensor(out=R(1), in0=R(4), in1=are, op=ALU.mult), reads=[RS, "arow", "r1", "r6"], writes=["r1"])
        s.op("vector", lambda e: e.tensor_tensor(out=R(2), in0=R(5), in1=aim, op=ALU.mult), reads=[RC, "arow", "r2"], writes=["r2"])
        s.op("vector", lambda e: e.tensor_tensor(out=R(1), in0=R(1), in1=R(2), op=ALU.subtract), reads=["r1", "r2"], writes=["r1"])
        s.op("vector", lambda e: e.tensor_tensor(out=R(7), in0=R(1), in1=R(0), op=ALU.mult), reads=["r1", "r0", "r6"], writes=["r7"])
        bre = c.btp[:, 0, :]; bim = c.btp[:, 1, :]
        ore = c.BtT[:, 0, pc * 8:(pc + 1) * 8, :].rearrange("p a b -> p (a b)")
        oim = c.BtT[:, 1, pc * 8:(pc + 1) * 8, :].rearrange("p a b -> p (a b)")
        s.op("vector", lambda e: e.tensor_tensor(out=R(1), in0=bre, in1=R(6), op=ALU.mult), reads=["btp", "r6", "r1"], writes=["r1"])
        s.op("vector", lambda e: e.tensor_tensor(out=R(2), in0=bim, in1=R(7), op=ALU.mult), reads=["btp", "r7", "r2"], writes=["r2"])
        s.op("vector", lambda e, ore=ore: e.tensor_tensor(out=ore, in0=R(1), in1=R(2), op=ALU.subtract), reads=["r1", "r2"], writes=[("BtT", pc)])
        s.op("vector", lambda e: e.tensor_tensor(out=R(1), in0=bre, in1=R(7), op=ALU.mult), reads=["btp", "r7", "r1", ("BtT", pc)], writes=["r1"])
        s.op("vector", lambda e: e.tensor_tensor(out=R(2), in0=bim, in1=R(6), op=ALU.mult), reads=["btp", "r6", "r2", ("BtT", pc)], writes=["r2"])
        s.op("vector", lambda e, oim=oim: e.tensor_tensor(out=oim, in0=R(1), in1=R(2), op=ALU.add), reads=["r1", "r2", ("BtT", pc)], writes=[("BtT", pc)])


def s5_main(c, full, yag_out=None, zero_init=False, pregelu=False):
    s = c.s
    ids = []
    rq = lambda p: c.qs[:, 3, p:p + 1]
    QC, QS = ("qsc", "c"), ("qsc", "s")
    if full and not zero_init:
        cq = c.qs[:, 5, :]; sq_ = c.qs[:, 4, :]
        a0 = c.qs[:, 9, :]; a1 = c.qs[:, 10, :]
        s.op("vector", lambda e: e.tensor_tensor(out=a0, in0=sq_, in1=c.xinit[:, 1, :], op=ALU.mult), reads=[QS, "xinit"], writes=["a0"])
        s.op("vector", lambda e: e.tensor_tensor(out=a1, in0=cq, in1=c.xinit[:, 0, :], op=ALU.mult), reads=[QC, "xinit"], writes=["a1"])
        s.op("vector", lambda e: e.tensor_tensor(out=c.ini[:, :, 0], in0=a1, in1=a0, op=ALU.subtract), reads=["a0", "a1"], writes=["ini_all"])
        s.op("vector", lambda e: e.tensor_tensor(out=a0, in0=sq_, in1=c.xinit[:, 0, :], op=ALU.mult), reads=[QS, "xinit", "ini_all"], writes=["a0"])
        s.op("vector", lambda e: e.tensor_tensor(out=a1, in0=cq, in1=c.xinit[:, 1, :], op=ALU.mult), reads=[QC, "xinit", "ini_all"], writes=["a1"])
        s.op("vector", lambda e: e.tensor_tensor(out=c.ini[:, :, 1], in0=a1, in1=a0, op=ALU.add), reads=["a0", "a1", "ini_all"], writes=["ini_all"])
    else:
        s.op("vector", lambda e: e.memset(c.ini[:], 0.0), writes=["ini_all"])
    gi = 0
    for t in range(NT):
        tsl = slice(t * 512, (t + 1) * 512)
        for cc in range(8):
            ybank = 4 + (cc % 2)
            for pp in range(4):
                p = cc * 4 + pp
                sl = gi % 2; gi += 1
                b_re, b_im = 2 * sl, 2 * sl + 1
                for ri, bank in ((0, b_re), (1, b_im)):
                    s.op("tensor", lambda e, ri=ri, bank=bank, p=p, cc=cc, tsl=tsl: e.matmul(
                        c.ps[:, bank, :], c.BtT[:, ri, p, :], c.ua[:, cc, tsl], start=True, stop=True),
                        reads=[("BtT", p // 8), ("ua", cc)], writes=[("ps", bank)])
                Cb = c.E[:, 0, p, :].unsqueeze(1).to_broadcast([128, 4, TC])
                Sb = c.E[:, 1, p, :].unsqueeze(1).to_broadcast([128, 4, TC])
                v4 = lambda ap: ap.rearrange("p (a b) -> p a b", a=4)
                pre = v4(c.ps[:, b_re, :]); pim = v4(c.ps[:, b_im, :])
                zre = c.z[:, sl, 0, :]; zim = c.z[:, sl, 1, :]
                m0 = c.mt[:, 0, :]; m1 = c.mt[:, 1, :]
                Zr, Zi, M0, M1 = ("z", sl, 0), ("z", sl, 1), "m0", "m1"
                s.op("vector", lambda e, pre=pre, Cb=Cb, m0=m0: e.tensor_tensor(out=v4(m0), in0=pre, in1=Cb, op=ALU.mult), reads=[("ps", b_re), "E"], writes=[M0], size=512)
                s.op("vector", lambda e, pim=pim, Sb=Sb, m1=m1: e.tensor_tensor(out=v4(m1), in0=pim, in1=Sb, op=ALU.mult), reads=[("ps", b_im), "E"], writes=[M1], size=512)
                s.op("vector", lambda e, zre=zre, m0=m0, m1=m1: e.tensor_tensor(out=zre, in0=m0, in1=m1, op=ALU.add), reads=[M0, M1], writes=[Zr], size=512)
                s.op("vector", lambda e, pim=pim, Cb=Cb, m0=m0: e.tensor_tensor(out=v4(m0), in0=pim, in1=Cb, op=ALU.mult), reads=[("ps", b_im), "E", Zr], writes=[M0], size=512)
                s.op("vector", lambda e, pre=pre, Sb=Sb, m1=m1: e.tensor_tensor(out=v4(m1), in0=pre, in1=Sb, op=ALU.mult), reads=[("ps", b_re), "E", Zr], writes=[M1], size=512)
                s.op("vector", lambda e, zim=zim, m0=m0, m1=m1: e.tensor_tensor(out=zim, in0=m0, in1=m1, op=ALU.subtract), reads=[M0, M1], writes=[Zi], size=512)
                wre = c.w[:, sl, 0, :]; wim = c.w[:, sl, 1, :]
                Wr, Wi, INI = ("w", sl, 0), ("w", sl, 1), ("ini", p)
                rb = rq(p).to_broadcast([128, TC])
                g_re = c.G128[:, 0, p:p + 1]; g_im = c.G128[:, 1, p:p + 1]
                for j in range(4):
                    js = slice(j * TC, (j + 1) * TC)
                    s.op("vector", lambda e, js=js, wre=wre, zre=zre, rb=rb, p=p: e.tensor_tensor_scan(
                        out=wre[:, js], data0=rb, data1=zre[:, js], initial=c.ini[:, p, 0:1], op0=ALU.mult, op1=ALU.add),
                        reads=[Zr, INI, "ini_all", "q3"], writes=[Wr], strict=True)
                    s.op("vector", lambda e, js=js, wim=wim, zim=zim, rb=rb, p=p: e.tensor_tensor_scan(
                        out=wim[:, js], data0=rb, data1=zim[:, js], initial=c.ini[:, p, 1:2], op0=ALU.mult, op1=ALU.add),
                        reads=[Zi, INI, "ini_all", "q3"], writes=[Wi], strict=True)
                    er = wre[:, j * TC + TC - 1:j * TC + TC]; ei = wim[:, j * TC + TC - 1:j * TC + TC]
                    s.op("vector", lambda e, ei=ei, g_im=g_im: e.tensor_scalar(out=c.itmp[:, 0:1], in0=ei, scalar1=g_im, scalar2=None, op0=ALU.mult),
                         reads=[Wi, "G128"], writes=["it0"])
                    s.op("vector", lambda e, ei=ei, g_re=g_re: e.tensor_scalar(out=c.itmp[:, 1:2], in0=ei, scalar1=g_re, scalar2=None, op0=ALU.mult),
                         reads=[Wi, "G128"], writes=["it1"])
                    s.op("vector", lambda e, er=er, g_re=g_re, p=p: e.scalar_tensor_tensor(out=c.ini[:, p, 0:1], in0=er, scalar=g_re, in1=c.itmp[:, 0:1],
                                                                                       op0=ALU.mult, op1=ALU.subtract),
                         reads=[Wr, "it0", "G128"], writes=[INI])
                    s.op("vector", lambda e, er=er, g_im=g_im, p=p: e.scalar_tensor_tensor(out=c.ini[:, p, 1:2], in0=er, scalar=g_im, in1=c.itmp[:, 1:2],
                                                                                       op0=ALU.mult, op1=ALU.add),
                         reads=[Wr, "it1", "G128", INI], writes=[INI])
                if full:
                    xre = c.xs[:, sl, 0, :]; nxi = c.xs[:, sl, 1, :]
                    Xr, Xi = ("xs", sl, 0), ("xs", sl, 1)
                    s.op("vector", lambda e, wre=wre, Cb=Cb, m0=m0: e.tensor_tensor(out=v4(m0), in0=v4(wre), in1=Cb, op=ALU.mult), reads=[Wr, "E", Zi], writes=[M0], size=512)
                    s.op("vector", lambda e, wim=wim, Sb=Sb, m1=m1: e.tensor_tensor(out=v4(m1), in0=v4(wim), in1=Sb, op=ALU.mult), reads=[Wi, "E", Zi], writes=[M1], size=512)
                    s.op("vector", lambda e, xre=xre, m0=m0, m1=m1: e.tensor_tensor(out=xre, in0=m0, in1=m1, op=ALU.subtract), reads=[M0, M1], writes=[Xr], size=512)
                    s.op("vector", lambda e, wre=wre, Sb=Sb, m0=m0: e.tensor_tensor(out=v4(m0), in0=v4(wre), in1=Sb, op=ALU.mult), reads=[Wr, "E", Xr], writes=[M0], size=512)
                    s.op("vector", lambda e, wim=wim, Cb=Cb, m1=m1: e.tensor_tensor(out=v4(m1), in0=v4(wim), in1=Cb, op=ALU.mult), reads=[Wi, "E", Xr], writes=[M1], size=512)
                    s.op("vector", lambda e, nxi=nxi, m0=m0, m1=m1: e.scalar_tensor_tensor(out=nxi, in0=m0, scalar=-1.0, in1=m1, op0=ALU.mult, op1=ALU.subtract),
                         reads=[M0, M1], writes=[Xi], size=512)
                    s.op("tensor", lambda e, p=p, xre=xre, ybank=ybank, pp=pp: e.matmul(c.ps[:, ybank, :], c.Cz[:, 0, p, :], xre, start=(pp == 0), stop=False),
                         reads=[Xr, "Cz"], writes=[("ps", ybank)])
                    s.op("tensor", lambda e, p=p, nxi=nxi, ybank=ybank, pp=pp: e.matmul(c.ps[:, ybank, :], c.Cz[:, 1, p, :], nxi, start=False, stop=(pp == 3)),
                         reads=[Xi, "Cz"], writes=[("ps", ybank)])
            if full:
                ysl = cc % 2
                s.op("vector", lambda e, cc=cc, tsl=tsl, ybank=ybank, ysl=ysl: e.scalar_tensor_tensor(
                    out=c.ypre[:, ysl, :], in0=c.ua[:, cc, tsl], scalar=c.dq[:, cc:cc + 1], in1=c.ps[:, ybank, :], op0=ALU.mult, op1=ALU.add),
                    reads=[("ua", cc), "dq", ("ps", ybank)], writes=[("ypre", ysl)])
                if pregelu:
                    ids.append(s.dma("sync", lambda e, cc=cc, tsl=tsl, ysl=ysl: e.dma_start(out=yag_out[cc * 128:(cc + 1) * 128, tsl], in_=c.ypre[:, ysl, :]),
                                     reads=[("ypre", ysl)], sem_key=("ypre", ysl)))
                else:
                    emit_gelu_src(c, c.ypre[:, ysl, :], [("ypre", ysl)], c.yst[:, ysl, :], c.ga[:, ysl, :], c.gb[:, ysl, :],
                                  writes=[("yst", ysl)], tmpkeys=(("ga", ysl), ("gb", ysl)))
                    ids.append(s.dma("sync", lambda e, cc=cc, tsl=tsl, ysl=ysl: e.dma_start(out=yag_out[cc * 128:(cc + 1) * 128, tsl], in_=c.yst[:, ysl, :]),
                                     reads=[("yst", ysl)], sem_key=("yst", ysl)))
    if (not full) or zero_init:
        cq = c.qs[:, 5, :]; sq_ = c.qs[:, 4, :]
        a0 = c.qs[:, 9, :]; a1 = c.qs[:, 10, :]
        allini = [("ini", p) for p in range(NP)] + ["ini_all"]
        s.op("vector", lambda e: e.tensor_tensor(out=a0, in0=sq_, in1=c.ini[:, :, 1], op=ALU.mult), reads=[QS] + allini, writes=["a0"])
        s.op("vector", lambda e: e.tensor_tensor(out=a1, in0=cq, in1=c.ini[:, :, 0], op=ALU.mult), reads=[QC] + allini, writes=["a1"])
        s.op("vector", lambda e: e.tensor_tensor(out=c.xend[:, 0, :], in0=a1, in1=a0, op=ALU.add), reads=["a0", "a1"], writes=["xend"])
        s.op("vector", lambda e: e.tensor_tensor(out=a0, in0=sq_, in1=c.ini[:, :, 0], op=ALU.mult), reads=[QS, "xend"] + allini, writes=["a0"])
        s.op("vector", lambda e: e.tensor_tensor(out=a1, in0=cq, in1=c.ini[:, :, 1], op=ALU.mult), reads=[QC, "xend"] + allini, writes=["a1"])
        s.op("vector", lambda e: e.tensor_tensor(out=c.xend[:, 1, :], in0=a1, in1=a0, op=ALU.subtract), reads=["a0", "a1", "xend"], writes=["xend"])
    return ids


def s5_combine(c, xall_ap, oneh_ap):
    s = c.s
    s.dma("sync", lambda e: e.dma_start(out=c.xall[:], in_=xall_ap), writes=["xall"], sem_key="xall")
    s.dma("sync", lambda e: e.dma_start(out=c.oneh[:], in_=oneh_ap), writes=["oneh"], sem_key="oneh")
    s.op("vector", lambda e: e.memset(c.X[:], 0.0), writes=["X"])
    s.op("vector", lambda e: e.memset(c.xinit[:], 0.0), writes=["xinit"])
    a0 = c.qs[:, 9, :]; a1 = c.qs[:, 10, :]
    cur = 0
    for cidx in range(1, 8):
        nxt = 1 - cur
        xr = c.X[:, cur, 0, :]; xi = c.X[:, cur, 1, :]
        nr = c.X[:, nxt, 0, :]; ni = c.X[:, nxt, 1, :]
        er = c.xall[:, cidx - 1, 0, :]; ei = c.xall[:, cidx - 1, 1, :]
        Ar = c.A[:, 0, :]; Ai = c.A[:, 1, :]
        s.op("vector", lambda e, xr=xr, Ar=Ar: e.tensor_tensor(out=a0, in0=xr, in1=Ar, op=ALU.mult), reads=["X", "A"], writes=["a0"])
        s.op("vector", lambda e, xi=xi, Ai=Ai: e.tensor_tensor(out=a1, in0=xi, in1=Ai, op=ALU.mult), reads=["X", "A"], writes=["a1"])
        s.op("vector", lambda e: e.tensor_tensor(out=a0, in0=a0, in1=a1, op=ALU.subtract), reads=["a0", "a1"], writes=["a0"])
        s.op("vector", lambda e, nr=nr, er=er: e.tensor_tensor(out=nr, in0=a0, in1=er, op=ALU.add), reads=["a0", "xall", "X"], writes=["X"])
        s.op("vector", lambda e, xr=xr, Ai=Ai: e.tensor_tensor(out=a0, in0=xr, in1=Ai, op=ALU.mult), reads=["X", "A"], writes=["a0"])
        s.op("vector", lambda e, xi=xi, Ar=Ar: e.tensor_tensor(out=a1, in0=xi, in1=Ar, op=ALU.mult), reads=["X", "A"], writes=["a1"])
        s.op("vector", lambda e: e.tensor_tensor(out=a0, in0=a0, in1=a1, op=ALU.add), reads=["a0", "a1"], writes=["a0"])
        s.op("vector", lambda e, ni=ni, ei=ei: e.tensor_tensor(out=ni, in0=a0, in1=ei, op=ALU.add), reads=["a0", "xall", "X"], writes=["X"])
        for ri, src in ((0, nr), (1, ni)):
            s.op("vector", lambda e, ri=ri, src=src, cidx=cidx: e.scalar_tensor_tensor(
                out=c.xinit[:, ri, :], in0=src, scalar=c.oneh[:, cidx:cidx + 1], in1=c.xinit[:, ri, :], op0=ALU.mult, op1=ALU.add),
                reads=["X", "oneh", "xinit"], writes=["xinit"])
        cur = nxt


def wload_half(c, view_shape, src_ap, half):
    raise NotImplementedError


def emit_glu_sgu(c, V, projUV, w_glu, b_glu, ln_g, ln_b, wsT, b_s, ident):
    s = c.s
    T = V
    yag = c.yag
    ug = T("ug", [128, 8, TOK], BF16)
    vg = T("vg", [128, 8, TOK], BF16)
    vn = T("vn", [128, 8, TOK], BF16)
    sq = T("sq2", [128, 2, 512], BF16)
    ones = T("ones2", [128, 128], BF16)
    idb = T("idb", [128, 128], BF16)
    par = T("par", [128, 3, 8])
    epsb = T("epsb2", [128, 1])
    mean = T("mean", [128, 512]); msq = T("msq", [128, 512]); rstd = T("rstd2", [128, 512]); t1 = T("t1", [128, 2, 512])
    sig = T("sig", [128, 2, 512])
    wsb = T("wsb", [128, 8, 128], BF16)
    bsf = T("bsf", [1, 1024], F32, 1); bsh = T("bsh", [1, 1024], BF16, 1); bsl = T("bsl", [1, 1024], BF16, 1); bst = T("bst", [1, 1024], F32, 1)
    vT = T("vT", [128, 2, 128], BF16)
    ps = c.ps
    psT = c.ps[:, 7, 0:128].bitcast(BF16).rearrange("p (a b) -> p a b", a=2)
    ids = []
    s.op("gpsimd", lambda e: e.memset(ones[:], 1.0), writes=["ones"])
    s.op("gpsimd", lambda e: e.memset(epsb[:], EPS), writes=["epsb"])
    for i, ap in enumerate((b_glu, ln_g, ln_b)):
        s.dma("sync", lambda e, i=i, ap=ap: e.dma_start(out=par[:, i, :], in_=ap.rearrange("(k p) -> p k", p=128), allow_slow_non_contiguous=True),
              writes=[("par", i)], sem_key=("par", i))
    s.dma("gpsimd", lambda e: e.dma_start(out=idb[:], in_=ident), writes=["idb"], sem_key="idb")
    s.dma("gpsimd", lambda e: e.dma_start(out=wsb[:], in_=wsT.rearrange("h s t -> s h t")), writes=["wsb"], sem_key="wsb")
    s.op("vector", lambda e: e.memset(wsb[64:128, :, 0:64], 0.0), reads=["wsb"], writes=["wsb"])
    s.dma("sync", lambda e: e.dma_start(out=bsf[:], in_=b_s.rearrange("(o h) t -> o (h t)", o=1)), writes=["bsf"], sem_key="bsf")
    s.op("vector", lambda e: e.tensor_copy(out=bsh[:], in_=bsf[:]), reads=["bsf"], writes=["bsh"])
    s.op("vector", lambda e: e.tensor_copy(out=bst[:], in_=bsh[:]), reads=["bsh"], writes=["bst"])
    s.op("vector", lambda e: e.tensor_tensor(out=bsl[:], in0=bsf[:], in1=bst[:], op=ALU.subtract), reads=["bsf", "bst"], writes=["bsl"])
    for cc in range(8):
        s.dma("gpsimd", lambda e, cc=cc: e.dma_start(out=ug[:, cc, :], in_=projUV[cc * 128:(cc + 1) * 128, :]), writes=[("ug", cc)], sem_key=("ug", cc))
        s.dma("gpsimd", lambda e, cc=cc: e.dma_start(out=vg[:, cc, :], in_=projUV[1024 + cc * 128:1024 + (cc + 1) * 128, :]), writes=[("vg", cc)], sem_key=("vg", cc))
    wv = w_glu.rearrange("(k p) f -> p k f", p=128)
    gi = 0
    for u in range(2):
        ws, wview = wload(c, [128, 8, 512], wv[:, :, u * 512:(u + 1) * 512], "glu")
        for m in range(4):
            mc = u * 4 + m
            for t in range(NT):
                tsl = slice(t * 512, (t + 1) * 512)
                bank = gi % 4; sl = gi % 2; gi += 1
                for k in range(8):
                    s.op("tensor", lambda e, wview=wview, k=k, m=m, tsl=tsl, bank=bank: e.matmul(
                        ps[:, bank, :], wview[:, k, m * 128:(m + 1) * 128], yag[:, k, tsl], start=(k == 0), stop=(k == 7)),
                        reads=[("w", ws), ("yag", k)], writes=[("ps", bank)])
                s.op("scalar", lambda e, bank=bank, sl=sl, mc=mc: e.activation(out=sig[:, sl, :], in_=ps[:, bank, :], func=AF.Sigmoid,
                                                                            bias=par[:, 0, mc:mc + 1], scale=1.0),
                     reads=[("ps", bank), ("par", 0)], writes=[("sig", sl)])
                s.op("vector", lambda e, sl=sl, mc=mc, tsl=tsl: e.tensor_tensor(out=c.ya[:, mc, tsl], in0=yag[:, mc, tsl], in1=sig[:, sl, :], op=ALU.mult),
                     reads=[("yag", mc), ("sig", sl)], writes=[("ya", mc)])
    sqs = 0
    for t in range(NT):
        tsl = slice(t * 512, (t + 1) * 512)
        for cc in range(8):
            s.op("tensor", lambda e, cc=cc, tsl=tsl: e.matmul(ps[:, 4, :], ones[:], vg[:, cc, tsl], start=(cc == 0), stop=(cc == 7)),
                 reads=["ones", ("vg", cc)], writes=[("ps", 4)])
        for cc in range(8):
            sl = sqs; sqs ^= 1
            s.op("scalar", lambda e, cc=cc, tsl=tsl, sl=sl: e.activation(out=sq[:, sl, :], in_=vg[:, cc, tsl], func=AF.Square),
                 reads=[("vg", cc)], writes=[("sq", sl)])
            s.op("tensor", lambda e, cc=cc, sl=sl: e.matmul(ps[:, 5, :], ones[:], sq[:, sl, :], start=(cc == 0), stop=(cc == 7)),
                 reads=["ones", ("sq", sl)], writes=[("ps", 5)])
        s.op("scalar", lambda e: e.activation(out=mean[:], in_=ps[:, 4, :], func=AF.Copy, scale=1.0 / 1024), reads=[("ps", 4)], writes=["mean"])
        s.op("vector", lambda e: e.tensor_tensor(out=msq[:], in0=mean[:], in1=mean[:], op=ALU.mult), reads=["mean"], writes=["msq"])
        s.op("vector", lambda e: e.scalar_tensor_tensor(out=msq[:], in0=ps[:, 5, :], scalar=1.0 / 1024, in1=msq[:], op0=ALU.mult, op1=ALU.subtract),
             reads=[("ps", 5), "msq"], writes=["msq"])
        s.op("scalar", lambda e: e.activation(out=rstd[:], in_=msq[:], func=AF.Sqrt, bias=epsb[:, 0:1], scale=1.0), reads=["msq", "epsb"], writes=["rstd"])
        s.op("vector", lambda e: e.reciprocal(out=rstd[:], in_=rstd[:]), reads=["rstd"], writes=["rstd"])
        for cc in range(8):
            sl = cc % 2
            s.op("vector", lambda e, cc=cc, tsl=tsl, sl=sl: e.tensor_tensor(out=t1[:, sl, :], in0=vg[:, cc, tsl], in1=mean[:], op=ALU.subtract),
                 reads=[("vg", cc), "mean"], writes=[("t1", sl)])
            s.op("vector", lambda e, sl=sl: e.tensor_tensor(out=t1[:, sl, :], in0=t1[:, sl, :], in1=rstd[:], op=ALU.mult),
                 reads=[("t1", sl), "rstd"], writes=[("t1", sl)])
            s.op("vector", lambda e, cc=cc, tsl=tsl, sl=sl: e.tensor_scalar(out=vn[:, cc, tsl], in0=t1[:, sl, :], scalar1=par[:, 1, cc:cc + 1],
                                                                           scalar2=par[:, 2, cc:cc + 1], op0=ALU.mult, op1=ALU.add),
                 reads=[("t1", sl), ("par", 1), ("par", 2)], writes=[("vn", cc, t)])
        for hh in range(8):
            bank = 4 + 2 + (hh % 1)
            bank = 6
            for j in range(4):
                tok = slice(t * 512 + j * TC, t * 512 + (j + 1) * TC)
                vs = (hh * 4 + j) % 2
                s.op("tensor", lambda e, hh=hh, tok=tok, vs=vs: e.transpose(psT[:, vs, :], vn[:, hh, tok], idb[:]),
                     reads=[("vn", hh, t), "idb"], writes=[("psT", vs)])
                s.op("scalar", lambda e, vs=vs: e.activation(out=vT[:, vs, :], in_=psT[:, vs, :], func=AF.Copy), reads=[("psT", vs)], writes=[("vT", vs)])
                osl = slice(j * TC, (j + 1) * TC)
                s.op("tensor", lambda e, hh=hh, vs=vs, osl=osl: e.matmul(ps[:, 6, osl], vT[:, vs, :], wsb[:, hh, :], start=True, stop=False, skip_group_check=True),
                     reads=[("vT", vs), "wsb"], writes=[("ps", 6)])
                s.op("tensor", lambda e, hh=hh, osl=osl: e.matmul(ps[:, 6, osl], ones[0:1, :], bsh[0:1, hh * 128:(hh + 1) * 128], start=False, stop=False, skip_group_check=True),
                     reads=["ones", "bsh"], writes=[("ps", 6)])
                s.op("tensor", lambda e, hh=hh, osl=osl: e.matmul(ps[:, 6, osl], ones[0:1, :], bsl[0:1, hh * 128:(hh + 1) * 128], start=False, stop=True, skip_group_check=True),
                     reads=["ones", "bsl"], writes=[("ps", 6)])
            s.op("vector", lambda e, hh=hh, tsl=tsl: e.tensor_tensor(out=c.yb[:, hh, tsl], in0=ps[:, 6, :], in1=ug[:, hh, tsl], op=ALU.mult),
                 reads=[("ps", 6), ("ug", hh)], writes=[("yb", hh)])
    return ids


def emit_merge(c, V, x1T, g_mix, w_a, w_b, w_gate, b_gate, w_out):
    s = c.s
    T = V
    hm = T("hmix", [128, KC, TOK], BF16)
    ya = c.ya; yb = c.yb
    mg = T("mg", [128, KC, TOK], BF16)
    xst = T("xst", [128, 4, 512]); sq = T("sq3", [128, 2, 512], BF16)
    ones = T("ones3", [128, 128], BF16); epsb = T("epsb3", [128, 1]); rstd = T("rstd3", [128, TOK])
    gain = T("gain3", [128, KC]); bg = T("bg", [128, 2 * KC])
    sga = T("sga", [128, 2, 512]); tmp = T("tmp3", [128, 2, 512])
    ps = c.ps
    ids = []
    s.op("gpsimd", lambda e: e.memset(ones[:], 1.0), writes=["ones"])
    s.op("gpsimd", lambda e: e.memset(epsb[:], EPS), writes=["epsb"])
    s.dma("sync", lambda e: e.dma_start(out=gain[:], in_=g_mix.rearrange("(k p) -> p k", p=128), allow_slow_non_contiguous=True), writes=["gain"], sem_key="gain")
    s.dma("sync", lambda e: e.dma_start(out=bg[:], in_=b_gate.rearrange("(k p) -> p k", p=128), allow_slow_non_contiguous=True), writes=["bg"], sem_key="bg")
    xv = x1T.rearrange("(k p) t -> p k t", p=128)
    xs = 0
    for t in range(NT):
        tsl = slice(t * 512, (t + 1) * 512)
        for k in range(KC):
            sl = xs % 4; xs += 1
            s.dma("sync", lambda e, k=k, tsl=tsl, sl=sl: e.dma_start(out=xst[:, sl, :], in_=xv[:, k, tsl]), writes=[("xst", sl)], sem_key=("xst", sl))
            s.op("scalar", lambda e, sl=sl: e.activation(out=sq[:, sl % 2, :], in_=xst[:, sl, :], func=AF.Square), reads=[("xst", sl)], writes=[("sq", sl % 2)])
            s.op("tensor", lambda e, sl=sl, k=k: e.matmul(ps[:, 7, :], ones[:], sq[:, sl % 2, :], start=(k == 0), stop=(k == KC - 1)),
                 reads=["ones", ("sq", sl % 2)], writes=[("ps", 7)])
        s.op("scalar", lambda e, tsl=tsl: e.activation(out=rstd[:, tsl], in_=ps[:, 7, :], func=AF.Sqrt, bias=epsb[:, 0:1], scale=1.0 / D),
             reads=[("ps", 7), "epsb"], writes=[("rstd", t)])
        s.op("vector", lambda e, tsl=tsl: e.reciprocal(out=rstd[:, tsl], in_=rstd[:, tsl]), reads=[("rstd", t)], writes=[("rstd", t)])
        for k in range(KC):
            sl = xs % 4; xs += 1
            s.dma("sync", lambda e, k=k, tsl=tsl, sl=sl: e.dma_start(out=xst[:, sl, :], in_=xv[:, k, tsl]), writes=[("xst", sl)], sem_key=("xst", sl))
            s.op("vector", lambda e, k=k, tsl=tsl, sl=sl: e.scalar_tensor_tensor(out=hm[:, k, tsl], in0=xst[:, sl, :], scalar=gain[:, k:k + 1], in1=rstd[:, tsl],
                                                                                 op0=ALU.mult, op1=ALU.mult),
                 reads=[("xst", sl), "gain", ("rstd", t)], writes=[("hm", k, t)])
    wa_v = w_a.rearrange("(k p) f -> p k f", p=128); wb_v = w_b.rearrange("(k p) f -> p k f", p=128)
    wg_v = w_gate.rearrange("(k p) f -> p k f", p=128)
    gi = 0
    for n4 in range(4):
        cs = slice(n4 * 512, (n4 + 1) * 512)
        sa, va = wload(c, [128, 8, 512], wa_v[:, :, cs], "a")
        sga_s, vga = wload(c, [128, KC, 512], wg_v[:, :, cs], "ga")
        sb, vb = wload(c, [128, 8, 512], wb_v[:, :, cs], "b")
        sgb_s, vgb = wload(c, [128, KC, 512], wg_v[:, :, D + n4 * 512:D + (n4 + 1) * 512], "gb")
        for m in range(4):
            n = n4 * 4 + m
            msl = slice(m * 128, (m + 1) * 128)
            for t in range(NT):
                tsl = slice(t * 512, (t + 1) * 512)
                pb = (gi % 2) * 4; sl = gi % 2; gi += 1
                for (bank, wsl, wvw, src, nk, skey) in ((pb, sa, va, ya, 8, "ya"), (pb + 1, sga_s, vga, hm, KC, "hm"), (pb + 2, sb, vb, yb, 8, "yb"), (pb + 3, sgb_s, vgb, hm, KC, "hm")):
                    for k in range(nk):
                        rk = (skey, k, t) if skey == "hm" else (skey, k)
                        s.op("tensor", lambda e, bank=bank, wvw=wvw, src=src, k=k, nk=nk, msl=msl, tsl=tsl: e.matmul(
                            ps[:, bank, :], wvw[:, k, msl], src[:, k, tsl], start=(k == 0), stop=(k == nk - 1)),
                            reads=[("w", wsl), rk], writes=[("ps", bank)])
                s.op("scalar", lambda e, pb=pb, sl=sl, n=n: e.activation(out=sga[:, sl, :], in_=ps[:, pb + 1, :], func=AF.Sigmoid, bias=bg[:, n:n + 1], scale=1.0),
                     reads=[("ps", pb + 1), "bg"], writes=[("sga", sl)])
                s.op("vector", lambda e, pb=pb, sl=sl: e.tensor_tensor(out=tmp[:, sl, :], in0=ps[:, pb, :], in1=sga[:, sl, :], op=ALU.mult),
                     reads=[("ps", pb), ("sga", sl)], writes=[("tmp", sl)])
                s.op("scalar", lambda e, pb=pb, sl=sl, n=n: e.activation(out=sga[:, sl, :], in_=ps[:, pb + 3, :], func=AF.Sigmoid, bias=bg[:, KC + n:KC + n + 1], scale=1.0),
                     reads=[("ps", pb + 3), "bg", ("tmp", sl)], writes=[("sga", sl)])
                s.op("vector", lambda e, pb=pb, sl=sl: e.tensor_tensor(out=sga[:, sl, :], in0=ps[:, pb + 2, :], in1=sga[:, sl, :], op=ALU.mult),
                     reads=[("ps", pb + 2), ("sga", sl)], writes=[("sga", sl)])
                s.op("vector", lambda e, sl=sl, n=n, tsl=tsl: e.tensor_tensor(out=mg[:, n, tsl], in0=tmp[:, sl, :], in1=sga[:, sl, :], op=ALU.add),
                     reads=[("tmp", sl), ("sga", sl)], writes=[("mg", n, t)])
    s.barrier()
    wo_v = w_out.rearrange("(k p) f -> p k f", p=128)
    gi = 0
    for n4 in range(4):
        so, vo = wload(c, [128, KC, 512], wo_v[:, :, n4 * 512:(n4 + 1) * 512], "o")
        for m in range(4):
            n = n4 * 4 + m
            msl = slice(m * 128, (m + 1) * 128)
            for t in range(NT):
                tsl = slice(t * 512, (t + 1) * 512)
                bank = gi % 4; sl = gi % 2; xsl = gi % 4; gi += 1
                for k in range(KC):
                    s.op("tensor", lambda e, bank=bank, vo=vo, k=k, msl=msl, tsl=tsl: e.matmul(ps[:, bank, :], vo[:, k, msl], mg[:, k, tsl], start=(k == 0), stop=(k == KC - 1)),
                         reads=[("w", so), ("mg", k, t)], writes=[("ps", bank)])
                s.dma("sync", lambda e, n=n, tsl=tsl, xsl=xsl: e.dma_start(out=xst[:, xsl, :], in_=xv[:, n, tsl]), writes=[("xst", xsl)], sem_key=("xst", xsl))
                s.op("vector", lambda e, bank=bank, n=n, tsl=tsl, xsl=xsl: e.tensor_tensor(out=c.x[:, n, tsl], in0=ps[:, bank, :], in1=xst[:, xsl, :], op=ALU.add),
                     reads=[("ps", bank), ("xst", xsl)], writes=[("x", n, t)])
    return ids


def emit_final_norm(c, gidx, outT, stage):
    s = c.s
    ids = []
    gi = 0
    for t in range(NT):
        tsl = slice(t * 512, (t + 1) * 512)
        bank = 7
        for k in range(KC):
            sl = c.sqslot; c.sqslot ^= 1
            s.op("scalar", lambda e, k=k, sl=sl, tsl=tsl: e.activation(out=c.sq[:, sl, :], in_=c.x[:, k, tsl], func=AF.Square),
                 reads=[("x", k, t)], writes=[("sq", sl)])
            s.op("tensor", lambda e, k=k, sl=sl, bank=bank: e.matmul(c.ps[:, bank, :], c.ones[:], c.sq[:, sl, :], start=(k == 0), stop=(k == KC - 1)),
                 reads=[("sq", sl), ("ones",)], writes=[("ps", bank)])
        s.op("scalar", lambda e, tsl=tsl, bank=bank: e.activation(out=c.rstd[:, tsl], in_=c.ps[:, bank, :], func=AF.Sqrt, bias=c.epsb[:, 0:1], scale=1.0 / D),
             reads=[("ps", bank), ("epsb",)], writes=[("rstd", t)])
        s.op("vector", lambda e, tsl=tsl: e.reciprocal(out=c.rstd[:, tsl], in_=c.rstd[:, tsl]), reads=[("rstd", t)], writes=[("rstd", t)])
        for k in range(KC):
            sl = gi % 2; gi += 1
            s.op("vector", lambda e, k=k, tsl=tsl, sl=sl: e.scalar_tensor_tensor(out=stage[:, sl, :], in0=c.x[:, k, tsl], scalar=c.gains[:, gidx, k:k + 1],
                                                                                 in1=c.rstd[:, tsl], op0=ALU.mult, op1=ALU.mult),
                 reads=[("x", k, t), ("gain", gidx), ("rstd", t)], writes=[("fstage", sl)])
            ids.append(s.dma("sync", lambda e, k=k, tsl=tsl, sl=sl: e.dma_start(out=outT[k * 128:(k + 1) * 128, tsl], in_=stage[:, sl, :]),
                             reads=[("fstage", sl)], sem_key=("fstage", sl)))
    return ids


def emit_cpow(s, E, Gp, etmp, base_keys, k_end, first_is_base):
    t0 = etmp[:, 0]; t1 = etmp[:, 1]
    cur = 0; k = 1
    while k < k_end:
        gr = Gp[:, cur, 0, :]; gi = Gp[:, cur, 1, :]
        grb = bc_last(gr.unsqueeze(2), k); gib = bc_last(gi.unsqueeze(2), k)

        def mk(k=k, grb=grb, gib=gib):
            s.op("vector", lambda e: e.tensor_tensor(out=t0[:, :, 0:k], in0=E[:, 1, :, 0:k], in1=gib, op=ALU.mult), reads=["E", "G"], writes=["t0"])
            s.op("vector", lambda e: e.tensor_tensor(out=t1[:, :, 0:k], in0=E[:, 0, :, 0:k], in1=grb, op=ALU.mult), reads=["E", "G"], writes=["t1"])
            s.op("vector", lambda e: e.tensor_tensor(out=E[:, 0, :, k:2 * k], in0=t1[:, :, 0:k], in1=t0[:, :, 0:k], op=ALU.subtract), reads=["t0", "t1", "E"], writes=["E"])
            s.op("vector", lambda e: e.tensor_tensor(out=t0[:, :, 0:k], in0=E[:, 0, :, 0:k], in1=gib, op=ALU.mult), reads=["E", "G"], writes=["t0"])
            s.op("vector", lambda e: e.tensor_tensor(out=t1[:, :, 0:k], in0=E[:, 1, :, 0:k], in1=grb, op=ALU.mult), reads=["E", "G"], writes=["t1"])
            s.op("vector", lambda e: e.tensor_tensor(out=E[:, 1, :, k:2 * k], in0=t1[:, :, 0:k], in1=t0[:, :, 0:k], op=ALU.add), reads=["t0", "t1", "E"], writes=["E"])
        mk()
        nxt = 1 - cur
        sr, si = gr, gi
        dr, di = Gp[:, nxt, 0, :], Gp[:, nxt, 1, :]

        def sqr(sr=sr, si=si, dr=dr, di=di):
            s.op("vector", lambda e: e.tensor_tensor(out=t0[:, :, 0], in0=si, in1=si, op=ALU.mult), reads=["G"], writes=["t0"])
            s.op("vector", lambda e: e.tensor_tensor(out=t1[:, :, 0], in0=sr, in1=sr, op=ALU.mult), reads=["G"], writes=["t1"])
            s.op("vector", lambda e: e.scalar_tensor_tensor(out=di, in0=sr, scalar=2.0, in1=si, op0=ALU.mult, op1=ALU.mult), reads=["G"], writes=["G"])
            s.op("vector", lambda e: e.tensor_tensor(out=dr, in0=t1[:, :, 0], in1=t0[:, :, 0], op=ALU.subtract), reads=["t0", "t1", "G"], writes=["G"])
        sqr()
        cur = nxt; k *= 2
    return cur


def emit_s5_correct(c, V, aq_ap, Cz_ap, xall_ap, oneh_ap, ypreT):
    s = c.s
    T = V
    c.aq = T("aq", [128, 3, NP]); c.qs = T("qs", [128, 12, NP]); c.qi = T("qi", [128, NP], I32)
    P = T("P", [128, 2, NP, TC])
    Pb = T("Pb", [128, 2, NP, TC], BF16)
    Gp = T("Gp", [128, 2, 2, NP]); etmp = T("etmp", [128, 2, NP, 64])
    c.A = T("A", [128, 2, NP]); c.X = T("X", [128, 2, 2, NP]); c.xinit = T("xinit", [128, 2, NP])
    c.xall = T("xall", [128, 8, 2, NP]); c.oneh = T("oneh", [128, 8])
    Czf = T("Czf", [128, 2, NP, 32])
    V_ = T("V", [128, 2, 2, NP])
    Wt = T("Wt", [128, 2, 2, NP, 128], BF16)
    wa = T("wa", [128, NP, 32]); wb = T("wb", [128, NP, 32])
    yl = T("yl", [128, 2, 512]); ga = T("ga", [128, 2, 512]); gb = T("gb", [128, 2, 512])
    G128 = T("G128", [128, 2, NP])
    ps = c.ps
    ids = []
    q = lambda i: c.qs[:, i, :]
    s.dma("sync", lambda e: e.dma_start(out=c.aq[:], in_=aq_ap), writes=["aq"], sem_key="aq")
    Cz4 = Cz_ap.rearrange("q r (a b) c -> q r a b c", b=4)
    CZ = [("Czf", b) for b in range(4)]
    s.op("scalar", lambda e: e.activation(out=q(0), in_=c.aq[:, 2, :], func=AF.Exp), reads=["aq"], writes=["q0"])
    s.op("vector", lambda e: e.tensor_tensor(out=q(1), in0=c.aq[:, 0, :], in1=q(0), op=ALU.mult), reads=["aq", "q0"], writes=["q1"])
    s.op("vector", lambda e: e.scalar_tensor_tensor(out=q(2), in0=c.aq[:, 1, :], scalar=INV_2PI, in1=q(0), op0=ALU.mult, op1=ALU.mult),
         reads=["aq", "q0"], writes=["q2"])
    s.op("scalar", lambda e: e.activation(out=q(3), in_=q(1), func=AF.Exp), reads=["q1"], writes=["q3"])
    emit_sincos(c, q(2), q(4), q(5), {"i32": c.qi[:], "a": q(6), "b": q(7)}, "qsc", reads=["q2"])
    s.op("vector", lambda e: e.tensor_tensor(out=Gp[:, 0, 0, :], in0=q(5), in1=q(3), op=ALU.mult), reads=[("qsc", "c"), "q3"], writes=["G"])
    s.op("vector", lambda e: e.tensor_tensor(out=Gp[:, 0, 1, :], in0=q(4), in1=q(3), op=ALU.mult), reads=[("qsc", "s"), "q3", "G"], writes=["G"])
    s.op("vector", lambda e: e.tensor_copy(out=P[:, 0, :, 0], in_=Gp[:, 0, 0, :]), reads=["G"], writes=["E"])
    s.op("vector", lambda e: e.tensor_copy(out=P[:, 1, :, 0], in_=Gp[:, 0, 1, :]), reads=["G", "E"], writes=["E"])
    cur = emit_cpow(s, P, Gp, etmp, None, TC, True)
    s.op("vector", lambda e, cur=cur: e.tensor_copy(out=G128[:], in_=Gp[:, cur]), reads=["G"], writes=["G128"])
    s.op("vector", lambda e: e.tensor_copy(out=Pb[:], in_=P[:]), reads=["E"], writes=["Pb"])
    t0 = etmp[:, 0]; t1 = etmp[:, 1]
    for _ in range(3):
        nxt = 1 - cur
        sr, si = Gp[:, cur, 0, :], Gp[:, cur, 1, :]
        dr, di = Gp[:, nxt, 0, :], Gp[:, nxt, 1, :]
        s.op("vector", lambda e, si=si: e.tensor_tensor(out=t0[:, :, 0], in0=si, in1=si, op=ALU.mult), reads=["G"], writes=["t0"])
        s.op("vector", lambda e, sr=sr: e.tensor_tensor(out=t1[:, :, 0], in0=sr, in1=sr, op=ALU.mult), reads=["G"], writes=["t1"])
        s.op("vector", lambda e, sr=sr, si=si, di=di: e.scalar_tensor_tensor(out=di, in0=sr, scalar=2.0, in1=si, op0=ALU.mult, op1=ALU.mult), reads=["G"], writes=["G"])
        s.op("vector", lambda e, dr=dr: e.tensor_tensor(out=dr, in0=t1[:, :, 0], in1=t0[:, :, 0], op=ALU.subtract), reads=["t0", "t1", "G"], writes=["G"])
        cur = nxt
    s.op("vector", lambda e, cur=cur: e.tensor_copy(out=c.A[:], in_=Gp[:, cur]), reads=["G"], writes=["A"])
    s5_combine(c, xall_ap, oneh_ap)
    s.op("vector", lambda e: e.tensor_copy(out=V_[:, 0], in_=c.xinit[:]), reads=["xinit"], writes=["V"])
    s.barrier()
    for b in range(4):
        s.dma("sync", lambda e, b=b: e.dma_start(out=Czf.rearrange("q r (a b) c -> q r a b c", b=4)[:, :, :, b, :],
                                                 in_=Cz4[:, :, :, b, 32 * b:32 * b + 32]), writes=[("Czf", b)], sem_key=("Czf", b))
    s.op("gpsimd", lambda e: e.memset(Wt[:], 0.0), writes=[("Wt", 0), ("Wt", 1)])
    vcur = 0
    gi = 0
    for j in range(8):
        ws_ = j % 2
        vr = bc_last(V_[:, vcur, 0, :].unsqueeze(2), 32); vi = bc_last(V_[:, vcur, 1, :].unsqueeze(2), 32)
        cre = Czf[:, 0]; cim = Czf[:, 1]
        def blk(ri, ws_=ws_):
            return [Wt[:, ws_, ri].rearrange("q (a b) c -> q a b c", b=4)[:, :, b, 32 * b:32 * b + 32] for b in range(4)]
        s.op("vector", lambda e, vr=vr: e.tensor_tensor(out=wa[:], in0=cre, in1=vr, op=ALU.mult), reads=CZ + ["V"], writes=["wa"])
        s.op("vector", lambda e, vi=vi: e.tensor_tensor(out=wb[:], in0=cim, in1=vi, op=ALU.mult), reads=CZ + ["V"], writes=["wb"])
        s.op("vector", lambda e: e.tensor_tensor(out=wa[:], in0=wa[:], in1=wb[:], op=ALU.subtract), reads=["wa", "wb"], writes=["wa"])
        for b, dst in enumerate(blk(0)):
            s.op("vector", lambda e, b=b, dst=dst: e.tensor_copy(out=dst, in_=wa[:].rearrange("q (a b) c -> q a b c", b=4)[:, :, b, :]),
                 reads=["wa"], writes=[("Wt", ws_)])
        s.op("vector", lambda e, vi=vi: e.tensor_tensor(out=wa[:], in0=cre, in1=vi, op=ALU.mult), reads=CZ + ["V", ("Wt", ws_)], writes=["wa"])
        s.op("vector", lambda e, vr=vr: e.tensor_tensor(out=wb[:], in0=cim, in1=vr, op=ALU.mult), reads=CZ + ["V"], writes=["wb"])
        s.op("vector", lambda e: e.scalar_tensor_tensor(out=wa[:], in0=wa[:], scalar=-1.0, in1=wb[:], op0=ALU.mult, op1=ALU.subtract), reads=["wa", "wb"], writes=["wa"])
        for b, dst in enumerate(blk(1)):
            s.op("vector", lambda e, b=b, dst=dst: e.tensor_copy(out=dst, in_=wa[:].rearrange("q (a b) c -> q a b c", b=4)[:, :, b, :]),
                 reads=["wa"], writes=[("Wt", ws_)])
        osl = slice((j % 4) * TC, (j % 4 + 1) * TC)
        for cc in range(8):
            for pp in range(4):
                p = cc * 4 + pp
                s.op("tensor", lambda e, cc=cc, p=p, pp=pp, ws_=ws_, osl=osl: e.matmul(ps[:, cc, osl], Wt[:, ws_, 0, p, :], Pb[:, 0, p, :],
                                                                                    start=(pp == 0), stop=False, skip_group_check=True),
                     reads=[("Wt", ws_), "Pb"], writes=[("ps", cc)])
                s.op("tensor", lambda e, cc=cc, p=p, pp=pp, ws_=ws_, osl=osl: e.matmul(ps[:, cc, osl], Wt[:, ws_, 1, p, :], Pb[:, 1, p, :],
                                                                                    start=False, stop=(pp == 3), skip_group_check=True),
                     reads=[("Wt", ws_), "Pb"], writes=[("ps", cc)])
        nv = 1 - vcur
        a0 = c.qs[:, 9, :]; a1 = c.qs[:, 10, :]
        s.op("vector", lambda e, vcur=vcur: e.tensor_tensor(out=a0, in0=V_[:, vcur, 0, :], in1=G128[:, 0, :], op=ALU.mult), reads=["V", "G128"], writes=["a0"])
        s.op("vector", lambda e, vcur=vcur: e.tensor_tensor(out=a1, in0=V_[:, vcur, 1, :], in1=G128[:, 1, :], op=ALU.mult), reads=["V", "G128"], writes=["a1"])
        s.op("vector", lambda e, nv=nv: e.tensor_tensor(out=V_[:, nv, 0, :], in0=a0, in1=a1, op=ALU.subtract), reads=["a0", "a1", "V"], writes=["V"])
        s.op("vector", lambda e, vcur=vcur: e.tensor_tensor(out=a0, in0=V_[:, vcur, 0, :], in1=G128[:, 1, :], op=ALU.mult), reads=["V", "G128"], writes=["a0"])
        s.op("vector", lambda e, vcur=vcur: e.tensor_tensor(out=a1, in0=V_[:, vcur, 1, :], in1=G128[:, 0, :], op=ALU.mult), reads=["V", "G128"], writes=["a1"])
        s.op("vector", lambda e, nv=nv: e.tensor_tensor(out=V_[:, nv, 1, :], in0=a0, in1=a1, op=ALU.add), reads=["a0", "a1", "V"], writes=["V"])
        vcur = nv
        if j % 4 == 3:
            t = j // 4
            tsl = slice(t * 512, (t + 1) * 512)
            for cc in range(8):
                sl = gi % 2; gi += 1
                s.dma("sync", lambda e, cc=cc, tsl=tsl, sl=sl: e.dma_start(out=yl[:, sl, :], in_=ypreT[cc * 128:(cc + 1) * 128, tsl]), writes=[("yl", sl)], sem_key=("yl", sl))
                s.op("vector", lambda e, cc=cc, sl=sl: e.tensor_tensor(out=yl[:, sl, :], in0=ps[:, cc, :], in1=yl[:, sl, :], op=ALU.add),
                     reads=[("ps", cc), ("yl", sl)], writes=[("yl", sl)])
                emit_gelu_src(c, yl[:, sl, :], [("yl", sl)], c.yag[:, cc, tsl], ga[:, sl, :], gb[:, sl, :], writes=[("yag", cc)], tmpkeys=(("ga", sl), ("gb", sl)))
    return ids


def s5_layouts(a_re, a_im, log_dt, b_re, b_im, c_re, c_im, d_skip):
    NP = 32
    def qlay(v):
        return v.reshape(NP, 2, 64).transpose(1, 2, 0).reshape(128, NP)
    ldt2 = np.repeat(log_dt[:, None], 64, axis=1)
    aq = np.stack([qlay(a_re), qlay(a_im), qlay(ldt2)], axis=1).astype(np.float32)
    def rlay(v):
        return v.reshape(NP, 128).reshape(-1)
    arow1 = np.stack([rlay(a_re), rlay(a_im), rlay(ldt2)], axis=0)
    arow = np.ascontiguousarray(np.broadcast_to(arow1[None], (128, 3, NP * 128))).astype(np.float32)
    BT = np.zeros((128, 2, NP, 128), np.float32)
    Cz = np.zeros((128, 2, NP, 128), np.float32)
    for p in range(NP):
        for g2 in range(2):
            g = 2 * p + g2
            r0 = 32 * (p % 4) + 16 * g2
            for ri, (bm, cm) in enumerate(((b_re, c_re), (b_im, c_im))):
                BT[r0:r0 + 16, ri, p, g2 * 64:(g2 + 1) * 64] = bm[g].T
                Cz[g2 * 64:(g2 + 1) * 64, ri, p, r0:r0 + 16] = cm[g].T
    dq = np.ascontiguousarray(d_skip.reshape(8, 128).T).astype(np.float32)
    return dict(aq=aq, arow=arow, BT=BT.reshape(128, 2, NP * 128), Cz=Cz, dq=dq)


def _dram_in(nc, name, shape):
    return nc.dram_tensor(name, list(shape), F32, kind="ExternalInput").ap()


def _dram_out(nc, name, shape):
    return nc.dram_tensor(name, list(shape), F32, kind="ExternalOutput").ap()


def _carver(arena, layout):
    def V(name, shape, dt=F32, parts=128):
        return arena.view(layout[name], shape, dt, parts)
    return V


LAY_A_FFN = dict(x=0, h=64, wring=96, hid=160, sq=176, rstd=178, silu=182, ones=186, gains=186.25, epsb=186.5,
                 stage=187, tmpa=191, tmpb=195)
LAY_A_S5 = dict(BtT=0, Cz=16, E=32, z=64, mt=72, w=76, xs=84, ypre=88, etmp=92, arow=108, btp=120, rs=128, ua=160,
                ri=176, aq=180, qs=180.5, qi=182, Gp=182.25, G128=182.75, ini=183, itmp=183.25, xend=183.5, dq=183.75)


def build_A():
    nc = bass.Bass("TRN2", target_bir_lowering=False)
    xT = _dram_in(nc, "xT", [D, TOK]); g1 = _dram_in(nc, "g1", [D]); g2 = _dram_in(nc, "g2", [D])
    wg = _dram_in(nc, "wg", [D, DFF]); wu = _dram_in(nc, "wu", [D, DFF]); wd = _dram_in(nc, "wd", [DFF, D])
    win = _dram_in(nc, "win", [D, MIXIN])
    aq = _dram_in(nc, "aq_in", [128, 3, NP]); arow = _dram_in(nc, "arow_in", [128, 3, NP * 128]); BT = _dram_in(nc, "BT_in", [128, 2, NP * 128])
    Cz = _dram_in(nc, "Cz_in", [128, 2, NP, 128]); dq = _dram_in(nc, "dq_in", [128, 8])
    x1T = _dram_out(nc, "x1T", [D, TOK]); projUV = _dram_out(nc, "projUV", [2048, TOK])
    ypre = _dram_out(nc, "ypreT", [1024, TOK]); xend = _dram_out(nc, "xend_out", [128, 2, NP])
    c = Ctx(); c.nc = nc; c.s = Sched(); s = c.s
    with contextlib.ExitStack() as st:
        ar = Arena(nc, st, 200)
        c.ps = st.enter_context(nc.psum_tensor("ps", [128, 8, 512], F32))
        V1 = _carver(ar, LAY_A_FFN)
        alloc_ffn(c, V1)
        c.wring = V1("wring", [128, 4, 8192], BF16); c.wslot = 0
        stage = V1("stage", [128, 2, 512]); tmpa = V1("tmpa", [128, 2, 512]); tmpb = V1("tmpb", [128, 2, 512])
        V2 = _carver(ar, LAY_A_S5)
        c.ua = V2("ua", [128, 8, TOK], BF16)
        emit_consts(c)
        load_gain(c, 0, g1); load_gain(c, 1, g2)
        load_xT(c, xT)
        emit_ffn(c, 0, wg, wu, wd)
        ids = store_T(c, c.x, "x", x1T)
        emit_rmsnorm(c, 1)
        ids += emit_inproj(c, win, projUV, stage, tmpa, tmpb)
        s.barrier()
        s5_alloc_v(c, V2)
        s5_setup(c, aq, arow, BT, False)
        s.dma("gpsimd", lambda e: e.dma_start(out=c.Cz, in_=Cz), writes=["Cz"], sem_key="Cz")
        s.dma("sync", lambda e: e.dma_start(out=c.dq, in_=dq), writes=["dq"], sem_key="dq")
        ids += s5_main(c, True, ypre, zero_init=True, pregelu=True)
        ids.append(s.dma("sync", lambda e: e.dma_start(out=xend, in_=c.xend), reads=["xend"], sem_key="xe"))
        s.emit(nc, final_wait_ops=ids)
    return nc


LAY_B1 = dict(P=64, etmp=144, Pb=128, Wt=96, Czf=64, wa=72, wb=76, yl=80, ga=84, gb=88, yag=160,
              aq=176, qs=176.5, qi=178, Gp=178.25, A=178.75, X=179, xinit=179.5, xall=179.75, oneh=181.75, V=182, G128=182.5)
LAY_B2 = dict(yag=160, ug=128, vg=144, vn=64, ya=96, yb=112, bsf=80, bst=84, bsh=88, bsl=90,
              sq2=176, ones2=178, idb=178.25, par=178.5, epsb2=178.75, mean=179, msq=181, rstd2=183, t1=185, sig=189, wsb=193, vT=195)
LAY_B3 = dict(hmix=64, ya=96, yb=112, mg=128, x=64, xst=176, sq3=184, ones3=186, epsb3=186.25, rstd3=186.5, gain3=190.5, bg=190.75,
              sga=191, tmp3=195)
LAY_B5 = dict(x=64, h=128, hid=160, sq=176, rstd=178, silu=182, ones=186, gains=186.25, epsb=186.5, stage=187)


def build_B():
    nc = bass.Bass("TRN2", target_bir_lowering=False)
    x1T = _dram_in(nc, "x1T_in", [D, TOK]); projUV = _dram_in(nc, "projUV_in", [2048, TOK]); ypre = _dram_in(nc, "ypreT_in", [1024, TOK])
    aq = _dram_in(nc, "aq_in", [128, 3, NP]); Cz = _dram_in(nc, "Cz_in", [128, 2, NP, 128])
    xall = _dram_in(nc, "xall_in", [128, 8, 2, NP]); oneh = _dram_in(nc, "oneh_in", [128, 8])
    w_glu = _dram_in(nc, "w_glu", [1024, 1024]); b_glu = _dram_in(nc, "b_glu", [1024])
    ln_g = _dram_in(nc, "ln_g", [1024]); ln_b = _dram_in(nc, "ln_b", [1024])
    wsT = _dram_in(nc, "wsT", [8, 128, 128]); b_s = _dram_in(nc, "b_s", [8, 128]); ident = _dram_in(nc, "ident", [128, 128])
    g_mix = _dram_in(nc, "g_mix", [D]); w_a = _dram_in(nc, "w_a", [1024, D]); w_b = _dram_in(nc, "w_b", [1024, D])
    w_gate = _dram_in(nc, "w_gate", [D, 2 * D]); b_gate = _dram_in(nc, "b_gate", [2 * D]); w_out = _dram_in(nc, "w_out", [D, D])
    g1 = _dram_in(nc, "g1", [D]); g2 = _dram_in(nc, "g2", [D])
    wg = _dram_in(nc, "wg", [D, DFF]); wu = _dram_in(nc, "wu", [D, DFF]); wd = _dram_in(nc, "wd", [DFF, D])
    outT = _dram_out(nc, "outT", [D, TOK])
    c = Ctx(); c.nc = nc; c.s = Sched(); s = c.s
    with contextlib.ExitStack() as st:
        ar = Arena(nc, st, 200)
        c.ps = st.enter_context(nc.psum_tensor("ps", [128, 8, 512], F32))
        c.wring = ar.view(0, [128, 4, 8192], BF16); c.wslot = 0
        V1 = _carver(ar, LAY_B1); V2 = _carver(ar, LAY_B2); V3 = _carver(ar, LAY_B3); V5 = _carver(ar, LAY_B5)
        c.yag = V1("yag", [128, 8, TOK], BF16)
        emit_s5_correct(c, V1, aq, Cz, xall, oneh, ypre)
        s.barrier()
        c.ya = V2("ya", [128, 8, TOK], BF16); c.yb = V2("yb", [128, 8, TOK], BF16)
        emit_glu_sgu(c, V2, projUV, w_glu, b_glu, ln_g, ln_b, wsT, b_s, ident)
        s.barrier()
        c.x = V3("x", [128, KC, TOK], F32)
        emit_merge(c, V3, x1T, g_mix, w_a, w_b, w_gate, b_gate, w_out)
        s.barrier()
        alloc_ffn(c, V5)
        stage = V5("stage", [128, 2, 512])
        emit_consts(c)
        load_gain(c, 0, g1); load_gain(c, 1, g2)
        emit_ffn(c, 0, wg, wu, wd)
        ids = emit_final_norm(c, 1, outT, stage)
        s.emit(nc, final_wait_ops=ids)
    return nc


NCORES = 8


def _run(nc, maps):
    return run_bass_kernel_spmd(nc, maps, core_ids=list(range(NCORES))).results


def kernel(x, ffn1_norm, ffn1_w_gate, ffn1_w_up, ffn1_w_down, mix_norm, w_in,
           s5_a_re, s5_a_im, s5_log_dt, s5_b_re, s5_b_im, s5_c_re, s5_c_im, s5_d,
           s5_w_glu, s5_b_glu, sgu_ln_g, sgu_ln_b, sgu_w_s, sgu_b_s,
           w_branch_a, w_branch_b, w_gate, b_gate, w_out,
           ffn2_norm, ffn2_w_gate, ffn2_w_up, ffn2_w_down, final_norm):
    f = lambda a: np.ascontiguousarray(np.asarray(a, dtype=np.float32))
    x = f(x)[0]
    n = NCORES
    lay = s5_layouts(f(s5_a_re)[0], f(s5_a_im)[0], f(s5_log_dt)[0], f(s5_b_re)[0], f(s5_b_im)[0], f(s5_c_re)[0], f(s5_c_im)[0], f(s5_d)[0])
    wA = dict(g1=f(ffn1_norm)[0], g2=f(mix_norm)[0], wg=f(ffn1_w_gate)[0], wu=f(ffn1_w_up)[0], wd=f(ffn1_w_down)[0], win=f(w_in)[0],
              aq_in=lay["aq"], arow_in=lay["arow"], BT_in=lay["BT"], Cz_in=lay["Cz"], dq_in=lay["dq"])
    rA = _run(build_A(), [dict(wA, xT=np.ascontiguousarray(x[i * TOK:(i + 1) * TOK].T)) for i in range(n)])
    xall = np.ascontiguousarray(np.stack([rA[i]["xend_out"] for i in range(n)], axis=1))
    wsT = np.ascontiguousarray(np.transpose(f(sgu_w_s)[0], (0, 2, 1)))
    wB = dict(aq_in=lay["aq"], Cz_in=lay["Cz"], xall_in=xall,
              w_glu=f(s5_w_glu)[0], b_glu=f(s5_b_glu)[0], ln_g=f(sgu_ln_g)[0], ln_b=f(sgu_ln_b)[0], wsT=wsT, b_s=f(sgu_b_s)[0],
              ident=np.eye(128, dtype=np.float32),
              g_mix=f(mix_norm)[0], w_a=f(w_branch_a)[0], w_b=f(w_branch_b)[0], w_gate=f(w_gate)[0], b_gate=f(b_gate)[0], w_out=f(w_out)[0],
              g1=f(ffn2_norm)[0], g2=f(final_norm), wg=f(ffn2_w_gate)[0], wu=f(ffn2_w_up)[0], wd=f(ffn2_w_down)[0])
    maps = []
    for i in range(n):
        oh = np.zeros((128, 8), np.float32); oh[:, i] = 1.0
        maps.append(dict(wB, oneh_in=oh, x1T_in=rA[i]["x1T"], projUV_in=rA[i]["projUV"], ypreT_in=rA[i]["ypreT"]))
    rB = _run(build_B(), maps)
    out = np.concatenate([rB[i]["outT"].T for i in range(n)], axis=0)
    return np.ascontiguousarray(out[None].astype(np.float32))
```

```python
import contextlib
import numpy as np
import concourse.bass as bass
import concourse.mybir as mybir
from concourse.bass_utils import run_bass_kernel_spmd

ENGINES = ("tensor", "vector", "scalar", "gpsimd", "sync")
RELAX_BULK = False


class Sched:
    def __init__(self, self_edges=True):
        self.ops = []
        self.last_w = {}
        self.readers = {}
        self.self_edges = self_edges
        self.fence = []
        self.fenced = set()

    def _add(self, eng, emit, reads, writes, dma_key=None, size=0, strict=False):
        i = len(self.ops)
        deps = set()
        for r in reads:
            if r in self.last_w:
                deps.add(self.last_w[r])
        for w in writes:
            if w in self.last_w:
                deps.add(self.last_w[w])
            deps.update(self.readers.get(w, ()))
        if self.fence and eng not in self.fenced:
            deps.update(self.fence)
            self.fenced.add(eng)
        deps.discard(i)
        self.ops.append(dict(eng=eng, emit=emit, deps=sorted(deps), dma_key=dma_key, size=size, strict=strict))
        for r in reads:
            self.readers.setdefault(r, []).append(i)
        for w in writes:
            self.last_w[w] = i
            self.readers[w] = []
        return i

    def barrier(self):
        last = {}
        for i, o in enumerate(self.ops):
            k = ("d", o["dma_key"]) if o["dma_key"] is not None else ("e", o["eng"])
            last[k] = i
        self.fence = sorted(last.values())
        self.fenced = set()

    def op(self, eng, emit, reads=(), writes=(), size=0, strict=False):
        return self._add(eng, emit, list(reads), list(writes), size=size, strict=strict)

    def dma(self, eng, emit, reads=(), writes=(), sem_key=None):
        assert sem_key is not None
        return self._add(eng, emit, list(reads), list(writes), dma_key=sem_key)

    def _self_skip(self, p, o):
        if p["eng"] != o["eng"]:
            return False
        if p["eng"] == "tensor" or not self.self_edges:
            return True
        return RELAX_BULK and p["size"] >= 256 and not o["strict"]

    def emit(self, nc, final_wait_ops=()):
        ops = self.ops
        need_inc = [False] * len(ops)
        for i, o in enumerate(ops):
            for d in o["deps"]:
                p = ops[d]
                if p["dma_key"] is not None:
                    continue
                if self._self_skip(p, o):
                    continue
                need_inc[d] = True
        eng_cnt = {e: 0 for e in ENGINES}
        inc_val = [None] * len(ops)
        dma_cnt = {}
        for i, o in enumerate(ops):
            if o["dma_key"] is not None:
                k = o["dma_key"]
                dma_cnt[k] = dma_cnt.get(k, 0) + 16
                inc_val[i] = dma_cnt[k]
            elif need_inc[i]:
                eng_cnt[o["eng"]] += 1
                inc_val[i] = eng_cnt[o["eng"]]
        import contextlib
        with contextlib.ExitStack() as st:
            esem = {e: st.enter_context(nc.semaphore("e_" + e)) for e in ENGINES}
            dsem = {k: st.enter_context(nc.semaphore("d_%d" % j)) for j, k in enumerate(dma_cnt)}
            block = st.enter_context(nc.Block())
            per_eng = {e: [i for i, o in enumerate(ops) if o["eng"] == e] for e in ENGINES}

            def make(e):
                def body(eng):
                    waited = {}
                    for i in per_eng[e]:
                        o = ops[i]
                        for d in o["deps"]:
                            p = ops[d]
                            if p["dma_key"] is not None:
                                key = ("d", p["dma_key"])
                                sem = dsem[p["dma_key"]]
                            else:
                                if self._self_skip(p, o):
                                    continue
                                key = ("e", p["eng"])
                                sem = esem[p["eng"]]
                            v = inc_val[d]
                            if waited.get(key, 0) >= v:
                                continue
                            eng.wait_ge(sem, v)
                            waited[key] = v
                        ins = o["emit"](eng)
                        if o["dma_key"] is not None:
                            ins.then_inc(dsem[o["dma_key"]], 16)
                        elif need_inc[i]:
                            ins.then_inc(esem[e], 1)
                    if e == "sync":
                        for i in final_wait_ops:
                            p = ops[i]
                            sem = dsem[p["dma_key"]] if p["dma_key"] is not None else esem[p["eng"]]
                            eng.wait_ge(sem, inc_val[i])
                return body
            for e in ENGINES:
                if per_eng[e] or e == "sync":
                    getattr(block, e)(make(e))
        return nc


F32 = mybir.dt.float32
BF16 = mybir.dt.bfloat16
AF = mybir.ActivationFunctionType
ALU = mybir.AluOpType

D = 2048
KC = D // 128
TOK = 1024
NT = TOK // 512
DFF = 5632
NFB = DFF // 512
EPS = 1e-6


class Ctx:
    pass


_DSZ = {F32: 4, BF16: 2}


class Arena:
    def __init__(self, nc, st, kb):
        self.words = kb * 256
        self.t = st.enter_context(nc.sbuf_tensor("arena", [128, self.words], F32))

    def view(self, off_kb, shape, dt=F32, parts=128):
        n = 1
        for d in shape[1:]:
            n *= d
        esz = 2 if dt == BF16 else 4
        words = (n * esz + 3) // 4
        lo = int(round(off_kb * 256))
        assert lo + words <= self.words, (off_kb, shape)
        ap = self.t[0:parts, lo:lo + words]
        if dt != F32:
            ap = ap.bitcast(dt)
        ap = ap[:, 0:n]
        if len(shape) == 2:
            return ap
        names = " ".join("d%d" % i for i in range(len(shape) - 1))
        kw = {"d%d" % i: shape[i + 1] for i in range(len(shape) - 2)}
        return ap.rearrange("p (%s) -> p %s" % (names, names), **kw)


def alloc_ffn(c, V):
    c.x = V("x", [128, KC, TOK], F32)
    c.h = V("h", [128, KC, TOK], BF16)
    c.hid = V("hid", [128, 2, 4, TOK], BF16)
    c.sq = V("sq", [128, 2, 512], BF16)
    c.rstd = V("rstd", [128, TOK], F32)
    c.silu = V("silu", [128, 2, 512], F32)
    c.ones = V("ones", [128, 128], BF16)
    c.gains = V("gains", [128, 4, KC], F32)
    c.epsb = V("epsb", [128, 1], F32)
    c.sqslot = 0
    c.silslot = 0


def emit_consts(c):
    s = c.s
    s.op("gpsimd", lambda e: e.memset(c.ones[:], 1.0), writes=[("ones",)])
    s.op("gpsimd", lambda e: e.memset(c.epsb[:], EPS), writes=[("epsb",)])


def load_gain(c, idx, g_ap):
    c.s.dma("sync", lambda e: e.dma_start(out=c.gains[:, idx, :], in_=g_ap.rearrange("(k p) -> p k", p=128),
                                          allow_slow_non_contiguous=True),
            writes=[("gain", idx)], sem_key=("gain", idx))


def emit_rmsnorm(c, gidx, out_key="h"):
    s = c.s
    for t in range(NT):
        tsl = slice(t * 512, (t + 1) * 512)
        bank = 7
        for k in range(KC):
            sl = c.sqslot; c.sqslot ^= 1
            s.op("scalar", lambda e, k=k, sl=sl, tsl=tsl: e.activation(out=c.sq[:, sl, :], in_=c.x[:, k, tsl], func=AF.Square),
                 reads=[("x", k, t)], writes=[("sq", sl)])
            s.op("tensor", lambda e, k=k, sl=sl, bank=bank: e.matmul(c.ps[:, bank, :], c.ones[:], c.sq[:, sl, :],
                                                          start=(k == 0), stop=(k == KC - 1)),
                 reads=[("sq", sl), ("ones",)], writes=[("ps", bank)])
        s.op("scalar", lambda e, tsl=tsl, bank=bank: e.activation(out=c.rstd[:, tsl], in_=c.ps[:, bank, :], func=AF.Sqrt,
                                              bias=c.epsb[:, 0:1], scale=1.0 / D),
             reads=[("ps", bank), ("epsb",)], writes=[("rstd", t)])
        s.op("vector", lambda e, tsl=tsl: e.reciprocal(out=c.rstd[:, tsl], in_=c.rstd[:, tsl]),
             reads=[("rstd", t)], writes=[("rstd", t)])
        for k in range(KC):
            s.op("vector", lambda e, k=k, tsl=tsl: e.scalar_tensor_tensor(
                out=c.h[:, k, tsl], in0=c.x[:, k, tsl], scalar=c.gains[:, gidx, k:k + 1],
                in1=c.rstd[:, tsl], op0=ALU.mult, op1=ALU.mult),
                 reads=[("x", k, t), ("gain", gidx), ("rstd", t)], writes=[(out_key, k, t)])


def wload(c, view_shape, src_ap, tag):
    slot = c.wslot; c.wslot = (c.wslot + 1) % 4
    n = 1
    for d in view_shape[1:]:
        n *= d
    assert n <= 8192
    flat = c.wring[:, slot, 0:n]
    if len(view_shape) == 3:
        view = flat.rearrange("p (a b) -> p a b", a=view_shape[1])
    else:
        view = flat
    c.s.dma("gpsimd", lambda e: e.dma_start(out=view, in_=src_ap), writes=[("w", slot)], sem_key=("w", slot))
    return slot, view


def emit_ffn(c, gidx, wg, wu, wd):
    s = c.s
    emit_rmsnorm(c, gidx)
    wg_v = wg.rearrange("(k p) f -> p k f", p=128)
    wu_v = wu.rearrange("(k p) f -> p k f", p=128)
    wd_v = wd.rearrange("(m p) d -> p m d", p=128)
    for b in range(NFB):
        hb = b % 2
        gs, gv = wload(c, [128, KC, 512], wg_v[:, :, b * 512:(b + 1) * 512], "g")
        us, uv = wload(c, [128, KC, 512], wu_v[:, :, b * 512:(b + 1) * 512], "u")
        ds_, dv = wload(c, [128, 4, D], wd_v[:, b * 4:(b + 1) * 4, :], "d")
        for m in range(4):
            for kind, (ws, wv) in enumerate(((gs, gv), (us, uv))):
                for t in range(NT):
                    bank = kind * 2 + t
                    for k in range(KC):
                        s.op("tensor", lambda e, wv=wv, k=k, m=m, t=t, bank=bank: e.matmul(
                            c.ps[:, bank, :], wv[:, k, m * 128:(m + 1) * 128], c.h[:, k, t * 512:(t + 1) * 512],
                            start=(k == 0), stop=(k == KC - 1)),
                            reads=[("w", ws), ("h", k, t)], writes=[("ps", bank)])
            for t in range(NT):
                sl = c.silslot; c.silslot ^= 1
                s.op("scalar", lambda e, t=t, sl=sl: e.activation(out=c.silu[:, sl, :], in_=c.ps[:, t, :], func=AF.Silu),
                     reads=[("ps", t)], writes=[("silu", sl)])
                s.op("vector", lambda e, t=t, sl=sl, m=m, hb=hb: e.tensor_tensor(
                    out=c.hid[:, hb, m, t * 512:(t + 1) * 512], in0=c.ps[:, 2 + t, :], in1=c.silu[:, sl, :], op=ALU.mult),
                     reads=[("ps", 2 + t), ("silu", sl)], writes=[("hid", hb, m, t)])
        gi = 0
        for n in range(KC):
            for t in range(NT):
                bank = 4 + (gi % 4); gi += 1
                for m in range(4):
                    s.op("tensor", lambda e, n=n, t=t, m=m, bank=bank, hb=hb, dv=dv: e.matmul(
                        c.ps[:, bank, :], dv[:, m, n * 128:(n + 1) * 128], c.hid[:, hb, m, t * 512:(t + 1) * 512],
                        start=(m == 0), stop=(m == 3)),
                        reads=[("w", ds_), ("hid", hb, m, t)], writes=[("ps", bank)])
                s.op("vector", lambda e, n=n, t=t, bank=bank: e.scalar_tensor_tensor(
                    out=c.x[:, n, t * 512:(t + 1) * 512], in0=c.ps[:, bank, :], scalar=0.5,
                    in1=c.x[:, n, t * 512:(t + 1) * 512], op0=ALU.mult, op1=ALU.add),
                     reads=[("ps", bank), ("x", n, t)], writes=[("x", n, t)])


def load_x(c, x_ap):
    raise NotImplementedError


def load_xT(c, xT_ap):
    v = xT_ap.rearrange("(k p) t -> p k t", p=128)
    for k in range(KC):
        c.s.dma("sync", lambda e, k=k: e.dma_start(out=c.x[:, k, :], in_=v[:, k, :]),
                writes=[("x", k, t) for t in range(NT)], sem_key=("xld", k))


def store_T(c, src, key, outT_ap):
    v = outT_ap.rearrange("(k p) t -> p k t", p=128)
    ids = []
    for k in range(KC):
        ids.append(c.s.dma("sync", lambda e, k=k: e.dma_start(out=v[:, k, :], in_=src[:, k, :]),
                           reads=[(key, k, t) for t in range(NT)], sem_key=("st", k)))
    return ids


GELU_C1 = 0.044715
GELU_C2 = 1.5957691216057308


def emit_gelu_from_psum(c, bank, out_ap, tmp_a, tmp_b, reads, writes, tmpkeys):
    s = c.s
    ka, kb = tmpkeys
    s.op("scalar", lambda e: e.activation(out=tmp_a, in_=c.ps[:, bank, :], func=AF.Square),
         reads=reads, writes=[ka])
    s.op("vector", lambda e: e.tensor_scalar(out=tmp_a, in0=tmp_a, scalar1=GELU_C1, scalar2=1.0, op0=ALU.mult, op1=ALU.add),
         reads=[ka], writes=[ka])
    s.op("vector", lambda e: e.tensor_tensor(out=tmp_a, in0=c.ps[:, bank, :], in1=tmp_a, op=ALU.mult),
         reads=reads + [ka], writes=[ka])
    s.op("scalar", lambda e: e.activation(out=tmp_b, in_=tmp_a, func=AF.Sigmoid, scale=GELU_C2),
         reads=[ka], writes=[kb])
    s.op("vector", lambda e: e.tensor_tensor(out=out_ap, in0=c.ps[:, bank, :], in1=tmp_b, op=ALU.mult),
         reads=reads + [kb], writes=writes)


MIXIN = 3072


def emit_inproj(c, w_in, projT_ap, stage, tmpa, tmpb):
    s = c.s
    wv = w_in.rearrange("(k p) f -> p k f", p=128)
    ids = []
    gi = 0
    for u in range(MIXIN // 512):
        ws, wview = wload(c, [128, KC, 512], wv[:, :, u * 512:(u + 1) * 512], "in")
        for m in range(4):
            cc = u * 4 + m
            for t in range(NT):
                bank = gi % 4
                sl = gi % 2
                gi += 1
                for k in range(KC):
                    s.op("tensor", lambda e, wview=wview, k=k, m=m, t=t, bank=bank: e.matmul(
                        c.ps[:, bank, :], wview[:, k, m * 128:(m + 1) * 128], c.h[:, k, t * 512:(t + 1) * 512],
                        start=(k == 0), stop=(k == KC - 1)),
                        reads=[("w", ws), ("h", k, t)], writes=[("ps", bank)])
                if cc < 8:
                    s.op("scalar", lambda e, bank=bank, cc=cc, t=t: e.activation(out=c.ua[:, cc, t * 512:(t + 1) * 512], in_=c.ps[:, bank, :], func=AF.Copy),
                         reads=[("ps", bank)], writes=[("ua", cc)])
                    continue
                else:
                    emit_gelu_from_psum(c, bank, stage[:, sl, :], tmpa[:, sl, :], tmpb[:, sl, :],
                                        reads=[("ps", bank)], writes=[("stage", sl)], tmpkeys=(("tmpa", sl), ("tmpb", sl)))
                ids.append(s.dma("sync", lambda e, cc=cc, t=t, sl=sl: e.dma_start(
                    out=projT_ap[(cc - 8) * 128:(cc - 7) * 128, t * 512:(t + 1) * 512], in_=stage[:, sl, :]),
                    reads=[("stage", sl)], sem_key=("stg", sl)))
    return ids


TWO_PI = 6.283185
INV_2PI = 0.15915494309189535
I32 = mybir.dt.int32


def bc_last(ap, n):
    shp = list(ap.shape)
    shp[-1] = n
    return ap.to_broadcast(shp)


def emit_sincos(c, f_ap, sin_ap, cos_ap, scr, keyp, reads):
    s = c.s
    K = lambda n: (keyp, n)
    s.op("vector", lambda e: e.tensor_copy(out=scr["i32"], in_=f_ap), reads=reads, writes=[K("i32")])
    s.op("vector", lambda e: e.tensor_copy(out=scr["a"], in_=scr["i32"]), reads=[K("i32")], writes=[K("a")])
    s.op("vector", lambda e: e.tensor_tensor(out=scr["a"], in0=f_ap, in1=scr["a"], op=ALU.subtract), reads=reads + [K("a")], writes=[K("a")])
    for which, out_ap, off in (("s", sin_ap, 0.0), ("c", cos_ap, 0.25)):
        if off != 0.0:
            s.op("vector", lambda e, off=off: e.tensor_scalar(out=scr["b"], in0=scr["a"], scalar1=off, scalar2=None, op0=ALU.add),
                 reads=[K("a")], writes=[K("b")])
            src = scr["b"]; srck = K("b")
        else:
            src = scr["a"]; srck = K("a")
        s.op("vector", lambda e, src=src: e.scalar_tensor_tensor(out=scr["b"], in0=src, scalar=0.5, in1=src, op0=ALU.is_gt, op1=ALU.subtract),
             reads=[srck], writes=[K("b")])
        s.op("vector", lambda e: e.scalar_tensor_tensor(out=scr["b"], in0=scr["b"], scalar=0.5, in1=scr["b"], op0=ALU.is_gt, op1=ALU.subtract),
             reads=[K("b")], writes=[K("b")])
        s.op("scalar", lambda e, out_ap=out_ap: e.activation(out=out_ap, in_=scr["b"], func=AF.Sin, scale=TWO_PI),
             reads=[K("b")], writes=[K(which)])


NP = 32
TC = 128
TCM = 256
NJ = 512 // TCM


def emit_gelu_src(c, src, src_keys, out_ap, tmp_a, tmp_b, writes, tmpkeys):
    s = c.s
    ka, kb = tmpkeys
    s.op("scalar", lambda e: e.activation(out=tmp_a, in_=src, func=AF.Square), reads=src_keys, writes=[ka])
    s.op("vector", lambda e: e.tensor_scalar(out=tmp_a, in0=tmp_a, scalar1=GELU_C1, scalar2=1.0, op0=ALU.mult, op1=ALU.add),
         reads=[ka], writes=[ka])
    s.op("vector", lambda e: e.tensor_tensor(out=tmp_a, in0=src, in1=tmp_a, op=ALU.mult), reads=src_keys + [ka], writes=[ka])
    s.op("scalar", lambda e: e.activation(out=tmp_b, in_=tmp_a, func=AF.Sigmoid, scale=GELU_C2), reads=[ka], writes=[kb])
    s.op("vector", lambda e: e.tensor_tensor(out=out_ap, in0=src, in1=tmp_b, op=ALU.mult), reads=src_keys + [kb], writes=writes)


def s5_alloc_v(c, V):
    T = V
    c.BtT = T("BtT", [128, 2, NP, 128], BF16)
    c.Cz = T("Cz", [128, 2, NP, 128], BF16)
    c.E = T("E", [128, 2, NP, TCM])
    c.z = T("z", [128, 2, 2, 512])
    c.mt = T("mt", [128, 2, 512])
    c.w = T("w", [128, 2, 2, 512])
    c.xs = T("xs", [128, 2, 4, 512], BF16)
    c.ypre = T("ypre", [128, 2, 512])
    c.etmp = T("etmp", [128, 2, NP, TCM // 2])
    c.arow = T("arow", [128, 3, 1024])
    c.btp = T("btp", [128, 2, 1024])
    c.rs = T("rs", [128, 8, 1024])
    c.ri = T("ri", [128, 1024], I32)
    c.aq = T("aq", [128, 3, NP])
    c.qs = T("qs", [128, 12, NP])
    c.qi = T("qi", [128, NP], I32)
    c.Gp = T("Gp", [128, 2, 2, NP])
    c.G128 = T("G128", [128, 2, NP])
    c.ini = T("ini", [128, NP, 2])
    c.itmp = T("itmp", [128, 2])
    c.xend = T("xend", [128, 2, NP])
    c.dq = T("dq", [128, 8])


def s5_setup(c, aq_ap, arow_ap, BT_ap, full):
    s = c.s
    s.dma("sync", lambda e: e.dma_start(out=c.aq[:], in_=aq_ap), writes=["aq"], sem_key="aq")
    q = lambda i: c.qs[:, i, :]
    s.op("scalar", lambda e: e.activation(out=q(0), in_=c.aq[:, 2, :], func=AF.Exp), reads=["aq"], writes=["q0"])
    s.op("vector", lambda e: e.tensor_tensor(out=q(1), in0=c.aq[:, 0, :], in1=q(0), op=ALU.mult), reads=["aq", "q0"], writes=["q1"])
    s.op("vector", lambda e: e.scalar_tensor_tensor(out=q(2), in0=c.aq[:, 1, :], scalar=INV_2PI, in1=q(0), op0=ALU.mult, op1=ALU.mult),
         reads=["aq", "q0"], writes=["q2"])
    s.op("scalar", lambda e: e.activation(out=q(3), in_=q(1), func=AF.Exp), reads=["q1"], writes=["q3"])
    emit_sincos(c, q(2), q(4), q(5), {"i32": c.qi[:], "a": q(6), "b": q(7)}, "qsc", reads=["q2"])
    QS, QC = ("qsc", "s"), ("qsc", "c")
    R = lambda i: c.rs[:, i, :]
    for pc in range(4):
        sl = slice(pc * 1024, (pc + 1) * 1024)
        s.dma("sync", lambda e, sl=sl: e.dma_start(out=c.arow[:], in_=arow_ap[:, :, sl]), writes=["arow"], sem_key="arow")
        s.dma("sync", lambda e, sl=sl: e.dma_start(out=c.btp[:], in_=BT_ap[:, :, sl]), writes=["btp"], sem_key="btp")
        are = c.arow[:, 0, :]; aim = c.arow[:, 1, :]
        s.op("scalar", lambda e: e.activation(out=R(0), in_=c.arow[:, 2, :], func=AF.Exp), reads=["arow"], writes=["r0"])
        s.op("vector", lambda e: e.tensor_tensor(out=R(1), in0=are, in1=R(0), op=ALU.mult), reads=["arow", "r0"], writes=["r1"])
        s.op("vector", lambda e: e.scalar_tensor_tensor(out=R(2), in0=aim, scalar=INV_2PI, in1=R(0), op0=ALU.mult, op1=ALU.mult),
             reads=["arow", "r0"], writes=["r2"])
        s.op("scalar", lambda e: e.activation(out=R(3), in_=R(1), func=AF.Exp), reads=["r1"], writes=["r3"])
        emit_sincos(c, R(2), R(4), R(5), {"i32": c.ri[:], "a": R(6), "b": R(7)}, "rsc", reads=["r2"])
        RS, RC = ("rsc", "s"), ("rsc", "c")
        s.op("vector", lambda e: e.tensor_tensor(out=R(5), in0=R(5), in1=R(3), op=ALU.mult), reads=[RC, "r3"], writes=[RC])
        s.op("vector", lambda e: e.tensor_scalar(out=R(5), in0=R(5), scalar1=-1.0, scalar2=None, op0=ALU.add), reads=[RC], writes=[RC])
        s.op("vector", lambda e: e.tensor_tensor(out=R(4), in0=R(4), in1=R(3), op=ALU.mult), reads=[RS, "r3"], writes=[RS])
        s.op("vector", lambda e: e.tensor_tensor(out=R(0), in0=are, in1=are, op=ALU.mult), reads=["arow", "r1", "r2"], writes=["r0"])
        s.op("vector", lambda e: e.tensor_tensor(out=R(1), in0=aim, in1=aim, op=ALU.mult), reads=["arow", "r3"], writes=["r1"])
        s.op("vector", lambda e: e.tensor_tensor(out=R(0), in0=R(0), in1=R(1), op=ALU.add), reads=["r0", "r1"], writes=["r0"])
        s.op("vector", lambda e: e.reciprocal(out=R(0), in_=R(0)), reads=["r0"], writes=["r0"])
        s.op("vector", lambda e: e.tensor_tensor(out=R(1), in0=R(5), in1=are, op=ALU.mult), reads=[RC, "arow", "r1"], writes=["r1"])
        s.op("vector", lambda e: e.tensor_tensor(out=R(2), in0=R(4), in1=aim, op=ALU.mult), reads=[RS, "arow", "r2", ("rsc", "a"), ("rsc", "b")], writes=["r2"])
        s.op("vector", lambda e: e.tensor_tensor(out=R(1), in0=R(1), in1=R(2), op=ALU.add), reads=["r1", "r2"], writes=["r1"])
        s.op("vector", lambda e: e.tensor_tensor(out=R(6), in0=R(1), in1=R(0), op=ALU.mult), reads=["r1", "r0", ("rsc", "a"), ("rsc", "b")], writes=["r6"])
        s.op("vector", lambda e: e.tensor_tensor(out=R(1), in0=R(4), in1=are, op=ALU.mult), reads=[RS, "arow", "r1", "r6"], writes=["r1"])
        s.op("vector", lambda e: e.tensor_tensor(out=R(2), in0=R(5), in1=aim, op=ALU.mult), reads=[RC, "arow", "r2"], writes=["r2"])
        s.op("vector", lambda e: e.tensor_tensor(out=R(1), in0=R(1), in1=R(2), op=ALU.subtract), reads=["r1", "r2"], writes=["r1"])
        s.op("vector", lambda e: e.tensor_tensor(out=R(7), in0=R(1), in1=R(0), op=ALU.mult), reads=["r1", "r0", "r6"], writes=["r7"])
        bre = c.btp[:, 0, :]; bim = c.btp[:, 1, :]
        ore = c.BtT[:, 0, pc * 8:(pc + 1) * 8, :].rearrange("p a b -> p (a b)")
        oim = c.BtT[:, 1, pc * 8:(pc + 1) * 8, :].rearrange("p a b -> p (a b)")
        s.op("vector", lambda e: e.tensor_tensor(out=R(1), in0=bre, in1=R(6), op=ALU.mult), reads=["btp", "r6", "r1"], writes=["r1"])
        s.op("vector", lambda e: e.tensor_tensor(out=R(2), in0=bim, in1=R(7), op=ALU.mult), reads=["btp", "r7", "r2"], writes=["r2"])
        s.op("vector", lambda e, ore=ore: e.tensor_tensor(out=ore, in0=R(1), in1=R(2), op=ALU.subtract), reads=["r1", "r2"], writes=[("BtT", pc)])
        s.op("vector", lambda e: e.tensor_tensor(out=R(1), in0=bre, in1=R(7), op=ALU.mult), reads=["btp", "r7", "r1", ("BtT", pc)], writes=["r1"])
        s.op("vector", lambda e: e.tensor_tensor(out=R(2), in0=bim, in1=R(6), op=ALU.mult), reads=["btp", "r6", "r2", ("BtT", pc)], writes=["r2"])
        s.op("vector", lambda e, oim=oim: e.tensor_tensor(out=oim, in0=R(1), in1=R(2), op=ALU.add), reads=["r1", "r2", ("BtT", pc)], writes=[("BtT", pc)])
    s.barrier()
    s.op("vector", lambda e: e.memset(c.E[:, 0, :, 0:1], 1.0), writes=["E"])
    s.op("vector", lambda e: e.memset(c.E[:, 1, :, 0:1], 0.0), reads=["E"], writes=["E"])
    s.op("vector", lambda e: e.tensor_copy(out=c.Gp[:, 0, 0, :], in_=q(5)), reads=[QC], writes=["G"])
    s.op("vector", lambda e: e.tensor_copy(out=c.Gp[:, 0, 1, :], in_=q(4)), reads=[QS, "G"], writes=["G"])
    cur = 0
    k = 1
    t0 = c.etmp[:, 0]; t1 = c.etmp[:, 1]

    def square(src, dst, dst_is_pp=True):
        sr, si = src
        dr, di = dst
        s.op("vector", lambda e: e.tensor_tensor(out=t0[:, :, 0], in0=si, in1=si, op=ALU.mult), reads=["G"], writes=["t0"])
        s.op("vector", lambda e: e.tensor_tensor(out=t1[:, :, 0], in0=sr, in1=sr, op=ALU.mult), reads=["G"], writes=["t1"])
        s.op("vector", lambda e: e.scalar_tensor_tensor(out=di, in0=sr, scalar=2.0, in1=si, op0=ALU.mult, op1=ALU.mult), reads=["G"], writes=["G"])
        s.op("vector", lambda e: e.tensor_tensor(out=dr, in0=t1[:, :, 0], in1=t0[:, :, 0], op=ALU.subtract), reads=["t0", "t1", "G"], writes=["G"])

    while k < TCM:
        gr = c.Gp[:, cur, 0, :]; gi = c.Gp[:, cur, 1, :]
        grb = bc_last(gr.unsqueeze(2), k); gib = bc_last(gi.unsqueeze(2), k)

        def mk(k=k, grb=grb, gib=gib):
            s.op("vector", lambda e: e.tensor_tensor(out=t0[:, :, 0:k], in0=c.E[:, 1, :, 0:k], in1=gib, op=ALU.mult), reads=["E", "G"], writes=["t0"])
            s.op("vector", lambda e: e.tensor_tensor(out=t1[:, :, 0:k], in0=c.E[:, 0, :, 0:k], in1=grb, op=ALU.mult), reads=["E", "G"], writes=["t1"])
            s.op("vector", lambda e: e.tensor_tensor(out=c.E[:, 0, :, k:2 * k], in0=t1[:, :, 0:k], in1=t0[:, :, 0:k], op=ALU.subtract), reads=["t0", "t1", "E"], writes=["E"])
            s.op("vector", lambda e: e.tensor_tensor(out=t0[:, :, 0:k], in0=c.E[:, 0, :, 0:k], in1=gib, op=ALU.mult), reads=["E", "G"], writes=["t0"])
            s.op("vector", lambda e: e.tensor_tensor(out=t1[:, :, 0:k], in0=c.E[:, 1, :, 0:k], in1=grb, op=ALU.mult), reads=["E", "G"], writes=["t1"])
            s.op("vector", lambda e: e.tensor_tensor(out=c.E[:, 1, :, k:2 * k], in0=t1[:, :, 0:k], in1=t0[:, :, 0:k], op=ALU.add), reads=["t0", "t1", "E"], writes=["E"])
        mk()
        nxt = 1 - cur
        square((gr, gi), (c.Gp[:, nxt, 0, :], c.Gp[:, nxt, 1, :]))
        cur = nxt
        k *= 2
    s.op("vector", lambda e, cur=cur: e.tensor_copy(out=c.G128[:], in_=c.Gp[:, cur]), reads=["G"], writes=["G128"])
    if full:
        for _ in range(3):
            nxt = 1 - cur
            square((c.Gp[:, cur, 0, :], c.Gp[:, cur, 1, :]), (c.Gp[:, nxt, 0, :], c.Gp[:, nxt, 1, :]))
            cur = nxt
        s.op("scalar", lambda e: e.activation(out=q(8), in_=q(1), func=AF.Exp, scale=1024.0), reads=["q1"], writes=["q8"])
        s.op("vector", lambda e, cur=cur: e.tensor_tensor(out=c.A[:, 0, :], in0=c.Gp[:, cur, 0, :], in1=q(8), op=ALU.mult), reads=["G", "q8"], writes=["A"])
        s.op("vector", lambda e, cur=cur: e.tensor_tensor(out=c.A[:, 1, :], in0=c.Gp[:, cur, 1, :], in1=q(8), op=ALU.mult), reads=["G", "q8", "A"], writes=["A"])
    s.barrier()


def s5_setup_v2(c, V, aq_ap, Bq_ap, ident_ap):
    s = c.s
    Bq = V("Bq", [128, 2, NP, 16]); Btq = V("Btq", [128, 2, NP, 16]); Zp = V("Zp", [128, 2, 8, 128])
    idf = V("identf", [128, 128]); kq = V("kq", [128, 8, NP])
    s.dma("sync", lambda e: e.dma_start(out=c.aq, in_=aq_ap), writes=["aq"], sem_key="aq")
    s.dma("sync", lambda e: e.dma_start(out=Bq, in_=Bq_ap), writes=["Bq"], sem_key="Bq")
    s.dma("sync", lambda e: e.dma_start(out=idf, in_=ident_ap), writes=["idf"], sem_key="idf")
    q = lambda i: c.qs[:, i, :]
    K = lambda i: kq[:, i, :]
    s.op("scalar", lambda e: e.activation(out=q(0), in_=c.aq[:, 2, :], func=AF.Exp), reads=["aq"], writes=["q0"])
    s.op("vector", lambda e: e.tensor_tensor(out=q(1), in0=c.aq[:, 0, :], in1=q(0), op=ALU.mult), reads=["aq", "q0"], writes=["q1"])
    s.op("vector", lambda e: e.scalar_tensor_tensor(out=q(2), in0=c.aq[:, 1, :], scalar=INV_2PI, in1=q(0), op0=ALU.mult, op1=ALU.mult),
         reads=["aq", "q0"], writes=["q2"])
    s.op("scalar", lambda e: e.activation(out=q(3), in_=q(1), func=AF.Exp), reads=["q1"], writes=["q3"])
    emit_sincos(c, q(2), q(4), q(5), {"i32": c.qi, "a": q(6), "b": q(7)}, "qsc", reads=["q2"])
    QS, QC = ("qsc", "s"), ("qsc", "c")
    are = c.aq[:, 0, :]; aim = c.aq[:, 1, :]
    s.op("vector", lambda e: e.tensor_tensor(out=K(0), in0=q(5), in1=q(3), op=ALU.mult), reads=[QC, "q3"], writes=["k0"])
    s.op("vector", lambda e: e.tensor_scalar(out=K(0), in0=K(0), scalar1=-1.0, scalar2=None, op0=ALU.add), reads=["k0"], writes=["k0"])
    s.op("vector", lambda e: e.tensor_tensor(out=K(1), in0=q(4), in1=q(3), op=ALU.mult), reads=[QS, "q3"], writes=["k1"])
    s.op("vector", lambda e: e.tensor_tensor(out=K(2), in0=are, in1=are, op=ALU.mult), reads=["aq"], writes=["k2"])
    s.op("vector", lambda e: e.tensor_tensor(out=K(3), in0=aim, in1=aim, op=ALU.mult), reads=["aq"], writes=["k3"])
    s.op("vector", lambda e: e.tensor_tensor(out=K(2), in0=K(2), in1=K(3), op=ALU.add), reads=["k2", "k3"], writes=["k2"])
    s.op("vector", lambda e: e.reciprocal(out=K(2), in_=K(2)), reads=["k2"], writes=["k2"])
    s.op("vector", lambda e: e.tensor_tensor(out=K(3), in0=K(0), in1=are, op=ALU.mult), reads=["k0", "aq", "k2"], writes=["k3"])
    s.op("vector", lambda e: e.tensor_tensor(out=K(4), in0=K(1), in1=aim, op=ALU.mult), reads=["k1", "aq"], writes=["k4"])
    s.op("vector", lambda e: e.tensor_tensor(out=K(3), in0=K(3), in1=K(4), op=ALU.add), reads=["k3", "k4"], writes=["k3"])
    s.op("vector", lambda e: e.tensor_tensor(out=K(5), in0=K(3), in1=K(2), op=ALU.mult), reads=["k3", "k2"], writes=["k5"])
    s.op("vector", lambda e: e.tensor_tensor(out=K(3), in0=K(1), in1=are, op=ALU.mult), reads=["k1", "aq", "k5"], writes=["k3"])
    s.op("vector", lambda e: e.tensor_tensor(out=K(4), in0=K(0), in1=aim, op=ALU.mult), reads=["k0", "aq", "k3"], writes=["k4"])
    s.op("vector", lambda e: e.tensor_tensor(out=K(3), in0=K(3), in1=K(4), op=ALU.subtract), reads=["k3", "k4"], writes=["k3"])
    s.op("vector", lambda e: e.tensor_tensor(out=K(6), in0=K(3), in1=K(2), op=ALU.mult), reads=["k3", "k2"], writes=["k6"])
    krb = bc_last(K(5).unsqueeze(2), 16); kib = bc_last(K(6).unsqueeze(2), 16)
    bre = Bq[:, 0]; bim = Bq[:, 1]
    tA = c.etmp[:, 0, :, 0:16]; tB = c.etmp[:, 1, :, 0:16]
    s.op("vector", lambda e: e.tensor_tensor(out=tA, in0=bre, in1=krb, op=ALU.mult), reads=["Bq", "k5"], writes=["t0"])
    s.op("vector", lambda e: e.tensor_tensor(out=tB, in0=bim, in1=kib, op=ALU.mult), reads=["Bq", "k6"], writes=["t1"])
    s.op("vector", lambda e: e.tensor_tensor(out=Btq[:, 0], in0=tA, in1=tB, op=ALU.subtract), reads=["t0", "t1"], writes=["Btq"])
    s.op("vector", lambda e: e.tensor_tensor(out=tA, in0=bre, in1=kib, op=ALU.mult), reads=["Bq", "k6", "Btq"], writes=["t0"])
    s.op("vector", lambda e: e.tensor_tensor(out=tB, in0=bim, in1=krb, op=ALU.mult), reads=["Bq", "k5", "Btq"], writes=["t1"])
    s.op("vector", lambda e: e.tensor_tensor(out=Btq[:, 1], in0=tA, in1=tB, op=ALU.add), reads=["t0", "t1", "Btq"], writes=["Btq"])
    s.op("vector", lambda e: e.memset(Zp, 0.0), writes=["Zp"])
    for pc in range(4):
        for ri in range(2):
            for g2 in range(2):
                for u in range(2):
                    for b in range(4):
                        a_ = 4 * u + b
                        s.op("vector", lambda e, g2=g2, ri=ri, a_=a_, b=b, pc=pc: e.tensor_copy(
                            out=Zp[g2 * 64:(g2 + 1) * 64, ri, a_, 32 * b + 16 * g2:32 * b + 16 * g2 + 16],
                            in_=Btq[g2 * 64:(g2 + 1) * 64, ri, pc * 8 + a_, :]),
                            reads=["Btq", "Zp"], writes=["Zp"])
        for ri in range(2):
            for hb in range(2):
                bank = (pc * 4 + ri * 2 + hb) % 4
                for i4 in range(4):
                    a_ = hb * 4 + i4
                    s.op("tensor", lambda e, ri=ri, a_=a_, bank=bank, i4=i4: e.transpose(c.ps[:, bank, i4 * 128:(i4 + 1) * 128], Zp[:, ri, a_, :], idf),
                         reads=["Zp", "idf"], writes=[("ps", bank)])
                p0 = pc * 8 + hb * 4
                s.op("scalar", lambda e, ri=ri, bank=bank, p0=p0: e.activation(out=c.BtT[:, ri, p0:p0 + 4, :].rearrange("r a q -> r (a q)"), in_=c.ps[:, bank, :], func=AF.Copy),
                     reads=[("ps", bank)], writes=[("BtT", pc)])
    s.op("vector", lambda e: e.memset(c.E[:, 0, :, 0:1], 1.0), writes=["E"])
    s.op("vector", lambda e: e.memset(c.E[:, 1, :, 0:1], 0.0), reads=["E"], writes=["E"])
    s.op("vector", lambda e: e.tensor_copy(out=c.Gp[:, 0, 0, :], in_=q(5)), reads=[QC], writes=["G"])
    s.op("vector", lambda e: e.tensor_copy(out=c.Gp[:, 0, 1, :], in_=q(4)), reads=[QS, "G"], writes=["G"])
    cur = emit_cpow(s, c.E, c.Gp, c.etmp, None, TCM, False)
    s.op("vector", lambda e, cur=cur: e.tensor_copy(out=c.G128, in_=c.Gp[:, cur]), reads=["G"], writes=["G128"])
    s.barrier()


def s5_main(c, full, yag_out=None, zero_init=False, pregelu=False):
    s = c.s
    ids = []
    rq = lambda p: c.qs[:, 3, p:p + 1]
    QC, QS = ("qsc", "c"), ("qsc", "s")
    if full and not zero_init:
        cq = c.qs[:, 5, :]; sq_ = c.qs[:, 4, :]
        a0 = c.qs[:, 9, :]; a1 = c.qs[:, 10, :]
        s.op("vector", lambda e: e.tensor_tensor(out=a0, in0=sq_, in1=c.xinit[:, 1, :], op=ALU.mult), reads=[QS, "xinit"], writes=["a0"])
        s.op("vector", lambda e: e.tensor_tensor(out=a1, in0=cq, in1=c.xinit[:, 0, :], op=ALU.mult), reads=[QC, "xinit"], writes=["a1"])
        s.op("vector", lambda e: e.tensor_tensor(out=c.ini[:, :, 0], in0=a1, in1=a0, op=ALU.subtract), reads=["a0", "a1"], writes=["ini_all"])
        s.op("vector", lambda e: e.tensor_tensor(out=a0, in0=sq_, in1=c.xinit[:, 0, :], op=ALU.mult), reads=[QS, "xinit", "ini_all"], writes=["a0"])
        s.op("vector", lambda e: e.tensor_tensor(out=a1, in0=cq, in1=c.xinit[:, 1, :], op=ALU.mult), reads=[QC, "xinit", "ini_all"], writes=["a1"])
        s.op("vector", lambda e: e.tensor_tensor(out=c.ini[:, :, 1], in0=a1, in1=a0, op=ALU.add), reads=["a0", "a1", "ini_all"], writes=["ini_all"])
    else:
        s.op("vector", lambda e: e.memset(c.ini[:], 0.0), writes=["ini_all"])
    gi = 0
    for t in range(NT):
        tsl = slice(t * 512, (t + 1) * 512)
        for cc in range(8):
            ybank = 4 + (cc % 2)
            for pp in range(4):
                p = cc * 4 + pp
                sl = gi % 2; gi += 1
                b_re, b_im = 2 * sl, 2 * sl + 1
                for ri, bank in ((0, b_re), (1, b_im)):
                    s.op("tensor", lambda e, ri=ri, bank=bank, p=p, cc=cc, tsl=tsl: e.matmul(
                        c.ps[:, bank, :], c.BtT[:, ri, p, :], c.ua[:, cc, tsl], start=True, stop=True),
                        reads=[("BtT", p // 8), ("ua", cc)], writes=[("ps", bank)])
                Cb = c.E[:, 0, p, :].unsqueeze(1).to_broadcast([128, NJ, TCM])
                Sb = c.E[:, 1, p, :].unsqueeze(1).to_broadcast([128, NJ, TCM])
                v4 = lambda ap: ap.rearrange("p (a b) -> p a b", a=NJ)
                pre = v4(c.ps[:, b_re, :]); pim = v4(c.ps[:, b_im, :])
                zre = c.z[:, sl, 0, :]; zim = c.z[:, sl, 1, :]
                m0 = c.mt[:, 0, :]; m1 = c.mt[:, 1, :]
                Zr, Zi, M0, M1 = ("z", sl, 0), ("z", sl, 1), "m0", "m1"
                s.op("vector", lambda e, pre=pre, Cb=Cb, m0=m0: e.tensor_tensor(out=v4(m0), in0=pre, in1=Cb, op=ALU.mult), reads=[("ps", b_re), "E"], writes=[M0], size=512)
                s.op("vector", lambda e, pim=pim, Sb=Sb, m1=m1: e.tensor_tensor(out=v4(m1), in0=pim, in1=Sb, op=ALU.mult), reads=[("ps", b_im), "E"], writes=[M1], size=512)
                s.op("vector", lambda e, zre=zre, m0=m0, m1=m1: e.tensor_tensor(out=zre, in0=m0, in1=m1, op=ALU.add), reads=[M0, M1], writes=[Zr], size=512)
                s.op("vector", lambda e, pim=pim, Cb=Cb, m0=m0: e.tensor_tensor(out=v4(m0), in0=pim, in1=Cb, op=ALU.mult), reads=[("ps", b_im), "E", Zr], writes=[M0], size=512)
                s.op("vector", lambda e, pre=pre, Sb=Sb, m1=m1: e.tensor_tensor(out=v4(m1), in0=pre, in1=Sb, op=ALU.mult), reads=[("ps", b_re), "E", Zr], writes=[M1], size=512)
                s.op("vector", lambda e, zim=zim, m0=m0, m1=m1: e.tensor_tensor(out=zim, in0=m0, in1=m1, op=ALU.subtract), reads=[M0, M1], writes=[Zi], size=512)
                wre = c.w[:, sl, 0, :]; wim = c.w[:, sl, 1, :]
                Wr, Wi, INI = ("w", sl, 0), ("w", sl, 1), ("ini", p)
                rb = rq(p).to_broadcast([128, TCM])
                g_re = c.G128[:, 0, p:p + 1]; g_im = c.G128[:, 1, p:p + 1]
                for j in range(NJ):
                    js = slice(j * TCM, (j + 1) * TCM)
                    s.op("vector", lambda e, js=js, wre=wre, zre=zre, rb=rb, p=p: e.tensor_tensor_scan(
                        out=wre[:, js], data0=rb, data1=zre[:, js], initial=c.ini[:, p, 0:1], op0=ALU.mult, op1=ALU.add),
                        reads=[Zr, INI, "ini_all", "q3"], writes=[Wr], strict=True)
                    s.op("vector", lambda e, js=js, wim=wim, zim=zim, rb=rb, p=p: e.tensor_tensor_scan(
                        out=wim[:, js], data0=rb, data1=zim[:, js], initial=c.ini[:, p, 1:2], op0=ALU.mult, op1=ALU.add),
                        reads=[Zi, INI, "ini_all", "q3"], writes=[Wi], strict=True)
                    er = wre[:, j * TCM + TCM - 1:j * TCM + TCM]; ei = wim[:, j * TCM + TCM - 1:j * TCM + TCM]
                    s.op("vector", lambda e, ei=ei, g_im=g_im: e.tensor_scalar(out=c.itmp[:, 0:1], in0=ei, scalar1=g_im, scalar2=None, op0=ALU.mult),
                         reads=[Wi, "G128"], writes=["it0"])
                    s.op("vector", lambda e, ei=ei, g_re=g_re: e.tensor_scalar(out=c.itmp[:, 1:2], in0=ei, scalar1=g_re, scalar2=None, op0=ALU.mult),
                         reads=[Wi, "G128"], writes=["it1"])
                    s.op("vector", lambda e, er=er, g_re=g_re, p=p: e.scalar_tensor_tensor(out=c.ini[:, p, 0:1], in0=er, scalar=g_re, in1=c.itmp[:, 0:1],
                                                                                       op0=ALU.mult, op1=ALU.subtract),
                         reads=[Wr, "it0", "G128"], writes=[INI])
                    s.op("vector", lambda e, er=er, g_im=g_im, p=p: e.scalar_tensor_tensor(out=c.ini[:, p, 1:2], in0=er, scalar=g_im, in1=c.itmp[:, 1:2],
                                                                                       op0=ALU.mult, op1=ALU.add),
                         reads=[Wr, "it1", "G128", INI], writes=[INI])
                if full:
                    P1 = c.xs[:, sl, 0, :]; P2 = c.xs[:, sl, 1, :]; P3 = c.xs[:, sl, 2, :]; P4 = c.xs[:, sl, 3, :]
                    K1, K2, K3, K4 = ("xs", sl, 0), ("xs", sl, 1), ("xs", sl, 2), ("xs", sl, 3)
                    s.op("vector", lambda e, wre=wre, Cb=Cb, P1=P1: e.tensor_tensor(out=v4(P1), in0=v4(wre), in1=Cb, op=ALU.mult), reads=[Wr, "E"], writes=[K1])
                    s.op("vector", lambda e, wim=wim, Sb=Sb, P2=P2: e.scalar_tensor_tensor(out=v4(P2), in0=v4(wim), scalar=-1.0, in1=Sb, op0=ALU.mult, op1=ALU.mult), reads=[Wi, "E"], writes=[K2])
                    s.op("vector", lambda e, wre=wre, Sb=Sb, P3=P3: e.scalar_tensor_tensor(out=v4(P3), in0=v4(wre), scalar=-1.0, in1=Sb, op0=ALU.mult, op1=ALU.mult), reads=[Wr, "E"], writes=[K3])
                    s.op("vector", lambda e, wim=wim, Cb=Cb, P4=P4: e.scalar_tensor_tensor(out=v4(P4), in0=v4(wim), scalar=-1.0, in1=Cb, op0=ALU.mult, op1=ALU.mult), reads=[Wi, "E"], writes=[K4])
                    for qi_, (ri, Pq, Kq) in enumerate(((0, P1, K1), (0, P2, K2), (1, P3, K3), (1, P4, K4))):
                        s.op("tensor", lambda e, p=p, ri=ri, Pq=Pq, ybank=ybank, pp=pp, qi_=qi_: e.matmul(
                            c.ps[:, ybank, :], c.Cz[:, ri, p, :], Pq, start=(pp == 0 and qi_ == 0), stop=(pp == 3 and qi_ == 3)),
                            reads=[Kq, "Cz"], writes=[("ps", ybank)])
            if full:
                ysl = cc % 2
                s.op("vector", lambda e, cc=cc, tsl=tsl, ybank=ybank, ysl=ysl: e.scalar_tensor_tensor(
                    out=c.ypre[:, ysl, :], in0=c.ua[:, cc, tsl], scalar=c.dq[:, cc:cc + 1], in1=c.ps[:, ybank, :], op0=ALU.mult, op1=ALU.add),
                    reads=[("ua", cc), "dq", ("ps", ybank)], writes=[("ypre", ysl)])
                if pregelu:
                    ids.append(s.dma("sync", lambda e, cc=cc, tsl=tsl, ysl=ysl: e.dma_start(out=yag_out[cc * 128:(cc + 1) * 128, tsl], in_=c.ypre[:, ysl, :]),
                                     reads=[("ypre", ysl)], sem_key=("ypre", ysl)))
                else:
                    emit_gelu_src(c, c.ypre[:, ysl, :], [("ypre", ysl)], c.yst[:, ysl, :], c.ga[:, ysl, :], c.gb[:, ysl, :],
                                  writes=[("yst", ysl)], tmpkeys=(("ga", ysl), ("gb", ysl)))
                    ids.append(s.dma("sync", lambda e, cc=cc, tsl=tsl, ysl=ysl: e.dma_start(out=yag_out[cc * 128:(cc + 1) * 128, tsl], in_=c.yst[:, ysl, :]),
                                     reads=[("yst", ysl)], sem_key=("yst", ysl)))
    if (not full) or zero_init:
        cq = c.qs[:, 5, :]; sq_ = c.qs[:, 4, :]
        a0 = c.qs[:, 9, :]; a1 = c.qs[:, 10, :]
        allini = [("ini", p) for p in range(NP)] + ["ini_all"]
        s.op("vector", lambda e: e.tensor_tensor(out=a0, in0=sq_, in1=c.ini[:, :, 1], op=ALU.mult), reads=[QS] + allini, writes=["a0"])
        s.op("vector", lambda e: e.tensor_tensor(out=a1, in0=cq, in1=c.ini[:, :, 0], op=ALU.mult), reads=[QC] + allini, writes=["a1"])
        s.op("vector", lambda e: e.tensor_tensor(out=c.xend[:, 0, :], in0=a1, in1=a0, op=ALU.add), reads=["a0", "a1"], writes=["xend"])
        s.op("vector", lambda e: e.tensor_tensor(out=a0, in0=sq_, in1=c.ini[:, :, 0], op=ALU.mult), reads=[QS, "xend"] + allini, writes=["a0"])
        s.op("vector", lambda e: e.tensor_tensor(out=a1, in0=cq, in1=c.ini[:, :, 1], op=ALU.mult), reads=[QC, "xend"] + allini, writes=["a1"])
        s.op("vector", lambda e: e.tensor_tensor(out=c.xend[:, 1, :], in0=a1, in1=a0, op=ALU.subtract), reads=["a0", "a1", "xend"], writes=["xend"])
    return ids


def s5_combine(c, xall_ap, oneh_ap):
    s = c.s
    s.dma("sync", lambda e: e.dma_start(out=c.xall[:], in_=xall_ap), writes=["xall"], sem_key="xall")
    s.dma("sync", lambda e: e.dma_start(out=c.oneh[:], in_=oneh_ap), writes=["oneh"], sem_key="oneh")
    s.op("vector", lambda e: e.memset(c.X[:], 0.0), writes=["X"])
    s.op("vector", lambda e: e.memset(c.xinit[:], 0.0), writes=["xinit"])
    a0 = c.qs[:, 9, :]; a1 = c.qs[:, 10, :]
    cur = 0
    for cidx in range(1, 8):
        nxt = 1 - cur
        xr = c.X[:, cur, 0, :]; xi = c.X[:, cur, 1, :]
        nr = c.X[:, nxt, 0, :]; ni = c.X[:, nxt, 1, :]
        er = c.xall[:, cidx - 1, 0, :]; ei = c.xall[:, cidx - 1, 1, :]
        Ar = c.A[:, 0, :]; Ai = c.A[:, 1, :]
        s.op("vector", lambda e, xr=xr, Ar=Ar: e.tensor_tensor(out=a0, in0=xr, in1=Ar, op=ALU.mult), reads=["X", "A"], writes=["a0"])
        s.op("vector", lambda e, xi=xi, Ai=Ai: e.tensor_tensor(out=a1, in0=xi, in1=Ai, op=ALU.mult), reads=["X", "A"], writes=["a1"])
        s.op("vector", lambda e: e.tensor_tensor(out=a0, in0=a0, in1=a1, op=ALU.subtract), reads=["a0", "a1"], writes=["a0"])
        s.op("vector", lambda e, nr=nr, er=er: e.tensor_tensor(out=nr, in0=a0, in1=er, op=ALU.add), reads=["a0", "xall", "X"], writes=["X"])
        s.op("vector", lambda e, xr=xr, Ai=Ai: e.tensor_tensor(out=a0, in0=xr, in1=Ai, op=ALU.mult), reads=["X", "A"], writes=["a0"])
        s.op("vector", lambda e, xi=xi, Ar=Ar: e.tensor_tensor(out=a1, in0=xi, in1=Ar, op=ALU.mult), reads=["X", "A"], writes=["a1"])
        s.op("vector", lambda e: e.tensor_tensor(out=a0, in0=a0, in1=a1, op=ALU.add), reads=["a0", "a1"], writes=["a0"])
        s.op("vector", lambda e, ni=ni, ei=ei: e.tensor_tensor(out=ni, in0=a0, in1=ei, op=ALU.add), reads=["a0", "xall", "X"], writes=["X"])
        for ri, src in ((0, nr), (1, ni)):
            s.op("vector", lambda e, ri=ri, src=src, cidx=cidx: e.scalar_tensor_tensor(
                out=c.xinit[:, ri, :], in0=src, scalar=c.oneh[:, cidx:cidx + 1], in1=c.xinit[:, ri, :], op0=ALU.mult, op1=ALU.add),
                reads=["X", "oneh", "xinit"], writes=["xinit"])
        cur = nxt


def wload_half(c, view_shape, src_ap, half):
    raise NotImplementedError


def emit_glu_sgu(c, V, projUV, w_glu, b_glu, ln_g, ln_b, wsT, b_s, ident):
    s = c.s
    T = V
    yag = c.yag
    ug = T("ug", [128, 8, TOK], BF16)
    vg = T("vg", [128, 8, TOK], BF16)
    vn = T("vn", [128, 8, TOK], BF16)
    sq = T("sq2", [128, 2, 512], BF16)
    ones = T("ones2", [128, 128], BF16)
    idb = T("idb", [128, 128], BF16)
    par = T("par", [128, 3, 8])
    epsb = T("epsb2", [128, 1])
    mean = T("mean", [128, 512]); msq = T("msq", [128, 512]); rstd = T("rstd2", [128, 512]); t1 = T("t1", [128, 2, 512])
    sig = T("sig", [128, 2, 512])
    wsb = T("wsb", [128, 8, 128], BF16)
    bsf = T("bsf", [1, 1024], F32, 1); bsh = T("bsh", [1, 1024], BF16, 1); bsl = T("bsl", [1, 1024], BF16, 1); bst = T("bst", [1, 1024], F32, 1)
    vT = T("vT", [128, 2, 128], BF16)
    ps = c.ps
    psT = c.ps[:, 7, 0:128].bitcast(BF16).rearrange("p (a b) -> p a b", a=2)
    ids = []
    s.op("gpsimd", lambda e: e.memset(ones[:], 1.0), writes=["ones"])
    s.op("gpsimd", lambda e: e.memset(epsb[:], EPS), writes=["epsb"])
    for i, ap in enumerate((b_glu, ln_g, ln_b)):
        s.dma("sync", lambda e, i=i, ap=ap: e.dma_start(out=par[:, i, :], in_=ap.rearrange("(k p) -> p k", p=128), allow_slow_non_contiguous=True),
              writes=[("par", i)], sem_key=("par", i))
    s.dma("gpsimd", lambda e: e.dma_start(out=idb[:], in_=ident), writes=["idb"], sem_key="idb")
    s.dma("gpsimd", lambda e: e.dma_start(out=wsb[:], in_=wsT.rearrange("h s t -> s h t")), writes=["wsb"], sem_key="wsb")
    s.op("vector", lambda e: e.memset(wsb[64:128, :, 0:64], 0.0), reads=["wsb"], writes=["wsb"])
    s.dma("sync", lambda e: e.dma_start(out=bsf[:], in_=b_s.rearrange("(o h) t -> o (h t)", o=1)), writes=["bsf"], sem_key="bsf")
    s.op("vector", lambda e: e.tensor_copy(out=bsh[:], in_=bsf[:]), reads=["bsf"], writes=["bsh"])
    s.op("vector", lambda e: e.tensor_copy(out=bst[:], in_=bsh[:]), reads=["bsh"], writes=["bst"])
    s.op("vector", lambda e: e.tensor_tensor(out=bsl[:], in0=bsf[:], in1=bst[:], op=ALU.subtract), reads=["bsf", "bst"], writes=["bsl"])
    for cc in range(8):
        s.dma("gpsimd", lambda e, cc=cc: e.dma_start(out=ug[:, cc, :], in_=projUV[cc * 128:(cc + 1) * 128, :]), writes=[("ug", cc)], sem_key=("ug", cc))
        s.dma("gpsimd", lambda e, cc=cc: e.dma_start(out=vg[:, cc, :], in_=projUV[1024 + cc * 128:1024 + (cc + 1) * 128, :]), writes=[("vg", cc)], sem_key=("vg", cc))
    wv = w_glu.rearrange("(k p) f -> p k f", p=128)
    gi = 0
    for u in range(2):
        ws, wview = wload(c, [128, 8, 512], wv[:, :, u * 512:(u + 1) * 512], "glu")
        for m in range(4):
            mc = u * 4 + m
            for t in range(NT):
                tsl = slice(t * 512, (t + 1) * 512)
                bank = gi % 4; sl = gi % 2; gi += 1
                for k in range(8):
                    s.op("tensor", lambda e, wview=wview, k=k, m=m, tsl=tsl, bank=bank: e.matmul(
                        ps[:, bank, :], wview[:, k, m * 128:(m + 1) * 128], yag[:, k, tsl], start=(k == 0), stop=(k == 7)),
                        reads=[("w", ws), ("yag", k)], writes=[("ps", bank)])
                s.op("scalar", lambda e, bank=bank, sl=sl, mc=mc: e.activation(out=sig[:, sl, :], in_=ps[:, bank, :], func=AF.Sigmoid,
                                                                            bias=par[:, 0, mc:mc + 1], scale=1.0),
                     reads=[("ps", bank), ("par", 0)], writes=[("sig", sl)])
                s.op("vector", lambda e, sl=sl, mc=mc, tsl=tsl: e.tensor_tensor(out=c.ya[:, mc, tsl], in0=yag[:, mc, tsl], in1=sig[:, sl, :], op=ALU.mult),
                     reads=[("yag", mc), ("sig", sl)], writes=[("ya", mc)])
    sqs = 0
    for t in range(NT):
        tsl = slice(t * 512, (t + 1) * 512)
        for cc in range(8):
            s.op("tensor", lambda e, cc=cc, tsl=tsl: e.matmul(ps[:, 4, :], ones[:], vg[:, cc, tsl], start=(cc == 0), stop=(cc == 7)),
                 reads=["ones", ("vg", cc)], writes=[("ps", 4)])
        for cc in range(8):
            sl = sqs; sqs ^= 1
            s.op("scalar", lambda e, cc=cc, tsl=tsl, sl=sl: e.activation(out=sq[:, sl, :], in_=vg[:, cc, tsl], func=AF.Square),
                 reads=[("vg", cc)], writes=[("sq", sl)])
            s.op("tensor", lambda e, cc=cc, sl=sl: e.matmul(ps[:, 5, :], ones[:], sq[:, sl, :], start=(cc == 0), stop=(cc == 7)),
                 reads=["ones", ("sq", sl)], writes=[("ps", 5)])
        s.op("scalar", lambda e: e.activation(out=mean[:], in_=ps[:, 4, :], func=AF.Copy, scale=1.0 / 1024), reads=[("ps", 4)], writes=["mean"])
        s.op("vector", lambda e: e.tensor_tensor(out=msq[:], in0=mean[:], in1=mean[:], op=ALU.mult), reads=["mean"], writes=["msq"])
        s.op("vector", lambda e: e.scalar_tensor_tensor(out=msq[:], in0=ps[:, 5, :], scalar=1.0 / 1024, in1=msq[:], op0=ALU.mult, op1=ALU.subtract),
             reads=[("ps", 5), "msq"], writes=["msq"])
        s.op("scalar", lambda e: e.activation(out=rstd[:], in_=msq[:], func=AF.Sqrt, bias=epsb[:, 0:1], scale=1.0), reads=["msq", "epsb"], writes=["rstd"])
        s.op("vector", lambda e: e.reciprocal(out=rstd[:], in_=rstd[:]), reads=["rstd"], writes=["rstd"])
        for cc in range(8):
            sl = cc % 2
            s.op("vector", lambda e, cc=cc, tsl=tsl, sl=sl: e.tensor_tensor(out=t1[:, sl, :], in0=vg[:, cc, tsl], in1=mean[:], op=ALU.subtract),
                 reads=[("vg", cc), "mean"], writes=[("t1", sl)])
            s.op("vector", lambda e, sl=sl: e.tensor_tensor(out=t1[:, sl, :], in0=t1[:, sl, :], in1=rstd[:], op=ALU.mult),
                 reads=[("t1", sl), "rstd"], writes=[("t1", sl)])
            s.op("vector", lambda e, cc=cc, tsl=tsl, sl=sl: e.tensor_scalar(out=vn[:, cc, tsl], in0=t1[:, sl, :], scalar1=par[:, 1, cc:cc + 1],
                                                                           scalar2=par[:, 2, cc:cc + 1], op0=ALU.mult, op1=ALU.add),
                 reads=[("t1", sl), ("par", 1), ("par", 2)], writes=[("vn", cc, t)])
        for hh in range(8):
            bank = 4 + 2 + (hh % 1)
            bank = 6
            for j in range(4):
                tok = slice(t * 512 + j * TC, t * 512 + (j + 1) * TC)
                vs = (hh * 4 + j) % 2
                s.op("tensor", lambda e, hh=hh, tok=tok, vs=vs: e.transpose(psT[:, vs, :], vn[:, hh, tok], idb[:]),
                     reads=[("vn", hh, t), "idb"], writes=[("psT", vs)])
                s.op("scalar", lambda e, vs=vs: e.activation(out=vT[:, vs, :], in_=psT[:, vs, :], func=AF.Copy), reads=[("psT", vs)], writes=[("vT", vs)])
                osl = slice(j * TC, (j + 1) * TC)
                s.op("tensor", lambda e, hh=hh, vs=vs, osl=osl: e.matmul(ps[:, 6, osl], vT[:, vs, :], wsb[:, hh, :], start=True, stop=False, skip_group_check=True),
                     reads=[("vT", vs), "wsb"], writes=[("ps", 6)])
                s.op("tensor", lambda e, hh=hh, osl=osl: e.matmul(ps[:, 6, osl], ones[0:1, :], bsh[0:1, hh * 128:(hh + 1) * 128], start=False, stop=False, skip_group_check=True),
                     reads=["ones", "bsh"], writes=[("ps", 6)])
                s.op("tensor", lambda e, hh=hh, osl=osl: e.matmul(ps[:, 6, osl], ones[0:1, :], bsl[0:1, hh * 128:(hh + 1) * 128], start=False, stop=True, skip_group_check=True),
                     reads=["ones", "bsl"], writes=[("ps", 6)])
            s.op("vector", lambda e, hh=hh, tsl=tsl: e.tensor_tensor(out=c.yb[:, hh, tsl], in0=ps[:, 6, :], in1=ug[:, hh, tsl], op=ALU.mult),
                 reads=[("ps", 6), ("ug", hh)], writes=[("yb", hh)])
    return ids


def emit_merge(c, V, x1T, g_mix, w_a, w_b, w_gate, b_gate, w_out):
    s = c.s
    T = V
    hm = T("hmix", [128, KC, TOK], BF16)
    ya = c.ya; yb = c.yb
    mg = T("mg", [128, KC, TOK], BF16)
    xst = T("xst", [128, 4, 512]); sq = T("sq3", [128, 2, 512], BF16)
    ones = T("ones3", [128, 128], BF16); epsb = T("epsb3", [128, 1]); rstd = T("rstd3", [128, TOK])
    gain = T("gain3", [128, KC]); bg = T("bg", [128, 2 * KC])
    sga = T("sga", [128, 2, 512]); tmp = T("tmp3", [128, 2, 512])
    ps = c.ps
    ids = []
    s.op("gpsimd", lambda e: e.memset(ones[:], 1.0), writes=["ones"])
    s.op("gpsimd", lambda e: e.memset(epsb[:], EPS), writes=["epsb"])
    s.dma("sync", lambda e: e.dma_start(out=gain[:], in_=g_mix.rearrange("(k p) -> p k", p=128), allow_slow_non_contiguous=True), writes=["gain"], sem_key="gain")
    s.dma("sync", lambda e: e.dma_start(out=bg[:], in_=b_gate.rearrange("(k p) -> p k", p=128), allow_slow_non_contiguous=True), writes=["bg"], sem_key="bg")
    xv = x1T.rearrange("(k p) t -> p k t", p=128)
    xs = 0
    for t in range(NT):
        tsl = slice(t * 512, (t + 1) * 512)
        for k in range(KC):
            sl = xs % 4; xs += 1
            s.dma("sync", lambda e, k=k, tsl=tsl, sl=sl: e.dma_start(out=xst[:, sl, :], in_=xv[:, k, tsl]), writes=[("xst", sl)], sem_key=("xst", sl))
            s.op("scalar", lambda e, sl=sl: e.activation(out=sq[:, sl % 2, :], in_=xst[:, sl, :], func=AF.Square), reads=[("xst", sl)], writes=[("sq", sl % 2)])
            s.op("tensor", lambda e, sl=sl, k=k: e.matmul(ps[:, 7, :], ones[:], sq[:, sl % 2, :], start=(k == 0), stop=(k == KC - 1)),
                 reads=["ones", ("sq", sl % 2)], writes=[("ps", 7)])
        s.op("scalar", lambda e, tsl=tsl: e.activation(out=rstd[:, tsl], in_=ps[:, 7, :], func=AF.Sqrt, bias=epsb[:, 0:1], scale=1.0 / D),
             reads=[("ps", 7), "epsb"], writes=[("rstd", t)])
        s.op("vector", lambda e, tsl=tsl: e.reciprocal(out=rstd[:, tsl], in_=rstd[:, tsl]), reads=[("rstd", t)], writes=[("rstd", t)])
        for k in range(KC):
            sl = xs % 4; xs += 1
            s.dma("sync", lambda e, k=k, tsl=tsl, sl=sl: e.dma_start(out=xst[:, sl, :], in_=xv[:, k, tsl]), writes=[("xst", sl)], sem_key=("xst", sl))
            s.op("vector", lambda e, k=k, tsl=tsl, sl=sl: e.scalar_tensor_tensor(out=hm[:, k, tsl], in0=xst[:, sl, :], scalar=gain[:, k:k + 1], in1=rstd[:, tsl],
                                                                                 op0=ALU.mult, op1=ALU.mult),
                 reads=[("xst", sl), "gain", ("rstd", t)], writes=[("hm", k, t)])
    wa_v = w_a.rearrange("(k p) f -> p k f", p=128); wb_v = w_b.rearrange("(k p) f -> p k f", p=128)
    wg_v = w_gate.rearrange("(k p) f -> p k f", p=128)
    gi = 0
    for n4 in range(4):
        cs = slice(n4 * 512, (n4 + 1) * 512)
        sa, va = wload(c, [128, 8, 512], wa_v[:, :, cs], "a")
        sga_s, vga = wload(c, [128, KC, 512], wg_v[:, :, cs], "ga")
        sb, vb = wload(c, [128, 8, 512], wb_v[:, :, cs], "b")
        sgb_s, vgb = wload(c, [128, KC, 512], wg_v[:, :, D + n4 * 512:D + (n4 + 1) * 512], "gb")
        for m in range(4):
            n = n4 * 4 + m
            msl = slice(m * 128, (m + 1) * 128)
            for t in range(NT):
                tsl = slice(t * 512, (t + 1) * 512)
                pb = (gi % 2) * 4; sl = gi % 2; gi += 1
                for (bank, wsl, wvw, src, nk, skey) in ((pb, sa, va, ya, 8, "ya"), (pb + 1, sga_s, vga, hm, KC, "hm"), (pb + 2, sb, vb, yb, 8, "yb"), (pb + 3, sgb_s, vgb, hm, KC, "hm")):
                    for k in range(nk):
                        rk = (skey, k, t) if skey == "hm" else (skey, k)
                        s.op("tensor", lambda e, bank=bank, wvw=wvw, src=src, k=k, nk=nk, msl=msl, tsl=tsl: e.matmul(
                            ps[:, bank, :], wvw[:, k, msl], src[:, k, tsl], start=(k == 0), stop=(k == nk - 1)),
                            reads=[("w", wsl), rk], writes=[("ps", bank)])
                s.op("scalar", lambda e, pb=pb, sl=sl, n=n: e.activation(out=sga[:, sl, :], in_=ps[:, pb + 1, :], func=AF.Sigmoid, bias=bg[:, n:n + 1], scale=1.0),
                     reads=[("ps", pb + 1), "bg"], writes=[("sga", sl)])
                s.op("vector", lambda e, pb=pb, sl=sl: e.tensor_tensor(out=tmp[:, sl, :], in0=ps[:, pb, :], in1=sga[:, sl, :], op=ALU.mult),
                     reads=[("ps", pb), ("sga", sl)], writes=[("tmp", sl)])
                s.op("scalar", lambda e, pb=pb, sl=sl, n=n: e.activation(out=sga[:, sl, :], in_=ps[:, pb + 3, :], func=AF.Sigmoid, bias=bg[:, KC + n:KC + n + 1], scale=1.0),
                     reads=[("ps", pb + 3), "bg", ("tmp", sl)], writes=[("sga", sl)])
                s.op("vector", lambda e, pb=pb, sl=sl: e.tensor_tensor(out=sga[:, sl, :], in0=ps[:, pb + 2, :], in1=sga[:, sl, :], op=ALU.mult),
                     reads=[("ps", pb + 2), ("sga", sl)], writes=[("sga", sl)])
                s.op("vector", lambda e, sl=sl, n=n, tsl=tsl: e.tensor_tensor(out=mg[:, n, tsl], in0=tmp[:, sl, :], in1=sga[:, sl, :], op=ALU.add),
                     reads=[("tmp", sl), ("sga", sl)], writes=[("mg", n, t)])
    s.barrier()
    wo_v = w_out.rearrange("(k p) f -> p k f", p=128)
    gi = 0
    for n4 in range(4):
        so, vo = wload(c, [128, KC, 512], wo_v[:, :, n4 * 512:(n4 + 1) * 512], "o")
        for m in range(4):
            n = n4 * 4 + m
            msl = slice(m * 128, (m + 1) * 128)
            for t in range(NT):
                tsl = slice(t * 512, (t + 1) * 512)
                bank = gi % 4; sl = gi % 2; xsl = gi % 4; gi += 1
                for k in range(KC):
                    s.op("tensor", lambda e, bank=bank, vo=vo, k=k, msl=msl, tsl=tsl: e.matmul(ps[:, bank, :], vo[:, k, msl], mg[:, k, tsl], start=(k == 0), stop=(k == KC - 1)),
                         reads=[("w", so), ("mg", k, t)], writes=[("ps", bank)])
                s.dma("sync", lambda e, n=n, tsl=tsl, xsl=xsl: e.dma_start(out=xst[:, xsl, :], in_=xv[:, n, tsl]), writes=[("xst", xsl)], sem_key=("xst", xsl))
                s.op("vector", lambda e, bank=bank, n=n, tsl=tsl, xsl=xsl: e.tensor_tensor(out=c.x[:, n, tsl], in0=ps[:, bank, :], in1=xst[:, xsl, :], op=ALU.add),
                     reads=[("ps", bank), ("xst", xsl)], writes=[("x", n, t)])
    return ids


def emit_final_norm(c, gidx, outT, stage):
    s = c.s
    ids = []
    gi = 0
    for t in range(NT):
        tsl = slice(t * 512, (t + 1) * 512)
        bank = 7
        for k in range(KC):
            sl = c.sqslot; c.sqslot ^= 1
            s.op("scalar", lambda e, k=k, sl=sl, tsl=tsl: e.activation(out=c.sq[:, sl, :], in_=c.x[:, k, tsl], func=AF.Square),
                 reads=[("x", k, t)], writes=[("sq", sl)])
            s.op("tensor", lambda e, k=k, sl=sl, bank=bank: e.matmul(c.ps[:, bank, :], c.ones[:], c.sq[:, sl, :], start=(k == 0), stop=(k == KC - 1)),
                 reads=[("sq", sl), ("ones",)], writes=[("ps", bank)])
        s.op("scalar", lambda e, tsl=tsl, bank=bank: e.activation(out=c.rstd[:, tsl], in_=c.ps[:, bank, :], func=AF.Sqrt, bias=c.epsb[:, 0:1], scale=1.0 / D),
             reads=[("ps", bank), ("epsb",)], writes=[("rstd", t)])
        s.op("vector", lambda e, tsl=tsl: e.reciprocal(out=c.rstd[:, tsl], in_=c.rstd[:, tsl]), reads=[("rstd", t)], writes=[("rstd", t)])
        for k in range(KC):
            sl = gi % 2; gi += 1
            s.op("vector", lambda e, k=k, tsl=tsl, sl=sl: e.scalar_tensor_tensor(out=stage[:, sl, :], in0=c.x[:, k, tsl], scalar=c.gains[:, gidx, k:k + 1],
                                                                                 in1=c.rstd[:, tsl], op0=ALU.mult, op1=ALU.mult),
                 reads=[("x", k, t), ("gain", gidx), ("rstd", t)], writes=[("fstage", sl)])
            ids.append(s.dma("sync", lambda e, k=k, tsl=tsl, sl=sl: e.dma_start(out=outT[k * 128:(k + 1) * 128, tsl], in_=stage[:, sl, :]),
                             reads=[("fstage", sl)], sem_key=("fstage", sl)))
    return ids


def emit_cpow(s, E, Gp, etmp, base_keys, k_end, first_is_base):
    t0 = etmp[:, 0]; t1 = etmp[:, 1]
    cur = 0; k = 1
    while k < k_end:
        gr = Gp[:, cur, 0, :]; gi = Gp[:, cur, 1, :]
        grb = bc_last(gr.unsqueeze(2), k); gib = bc_last(gi.unsqueeze(2), k)

        def mk(k=k, grb=grb, gib=gib):
            s.op("vector", lambda e: e.tensor_tensor(out=t0[:, :, 0:k], in0=E[:, 1, :, 0:k], in1=gib, op=ALU.mult), reads=["E", "G"], writes=["t0"])
            s.op("vector", lambda e: e.tensor_tensor(out=t1[:, :, 0:k], in0=E[:, 0, :, 0:k], in1=grb, op=ALU.mult), reads=["E", "G"], writes=["t1"])
            s.op("vector", lambda e: e.tensor_tensor(out=E[:, 0, :, k:2 * k], in0=t1[:, :, 0:k], in1=t0[:, :, 0:k], op=ALU.subtract), reads=["t0", "t1", "E"], writes=["E"])
            s.op("vector", lambda e: e.tensor_tensor(out=t0[:, :, 0:k], in0=E[:, 0, :, 0:k], in1=gib, op=ALU.mult), reads=["E", "G"], writes=["t0"])
            s.op("vector", lambda e: e.tensor_tensor(out=t1[:, :, 0:k], in0=E[:, 1, :, 0:k], in1=grb, op=ALU.mult), reads=["E", "G"], writes=["t1"])
            s.op("vector", lambda e: e.tensor_tensor(out=E[:, 1, :, k:2 * k], in0=t1[:, :, 0:k], in1=t0[:, :, 0:k], op=ALU.add), reads=["t0", "t1", "E"], writes=["E"])
        mk()
        nxt = 1 - cur
        sr, si = gr, gi
        dr, di = Gp[:, nxt, 0, :], Gp[:, nxt, 1, :]

        def sqr(sr=sr, si=si, dr=dr, di=di):
            s.op("vector", lambda e: e.tensor_tensor(out=t0[:, :, 0], in0=si, in1=si, op=ALU.mult), reads=["G"], writes=["t0"])
            s.op("vector", lambda e: e.tensor_tensor(out=t1[:, :, 0], in0=sr, in1=sr, op=ALU.mult), reads=["G"], writes=["t1"])
            s.op("vector", lambda e: e.scalar_tensor_tensor(out=di, in0=sr, scalar=2.0, in1=si, op0=ALU.mult, op1=ALU.mult), reads=["G"], writes=["G"])
            s.op("vector", lambda e: e.tensor_tensor(out=dr, in0=t1[:, :, 0], in1=t0[:, :, 0], op=ALU.subtract), reads=["t0", "t1", "G"], writes=["G"])
        sqr()
        cur = nxt; k *= 2
    return cur


def emit_s5_correct(c, V, aq_ap, Cz_ap, xall_ap, oneh_ap, ypreT):
    s = c.s
    T = V
    c.aq = T("aq", [128, 3, NP]); c.qs = T("qs", [128, 12, NP]); c.qi = T("qi", [128, NP], I32)
    P = T("P", [128, 2, NP, TC])
    Pb = T("Pb", [128, 2, NP, TC], BF16)
    Gp = T("Gp", [128, 2, 2, NP]); etmp = T("etmp", [128, 2, NP, 64])
    c.A = T("A", [128, 2, NP]); c.X = T("X", [128, 2, 2, NP]); c.xinit = T("xinit", [128, 2, NP])
    c.xall = T("xall", [128, 8, 2, NP]); c.oneh = T("oneh", [128, 8])
    Czf = T("Czf", [128, 2, NP, 32])
    V_ = T("V", [128, 2, 2, NP])
    Wt = T("Wt", [128, 2, 2, NP, 128], BF16)
    wa = T("wa", [128, NP, 32]); wb = T("wb", [128, NP, 32])
    yl = T("yl", [128, 2, 512]); ga = T("ga", [128, 2, 512]); gb = T("gb", [128, 2, 512])
    G128 = T("G128", [128, 2, NP])
    ps = c.ps
    ids = []
    q = lambda i: c.qs[:, i, :]
    s.dma("sync", lambda e: e.dma_start(out=c.aq[:], in_=aq_ap), writes=["aq"], sem_key="aq")
    Cz4 = Cz_ap.rearrange("q r (a b) c -> q r a b c", b=4)
    CZ = [("Czf", b) for b in range(4)]
    s.op("scalar", lambda e: e.activation(out=q(0), in_=c.aq[:, 2, :], func=AF.Exp), reads=["aq"], writes=["q0"])
    s.op("vector", lambda e: e.tensor_tensor(out=q(1), in0=c.aq[:, 0, :], in1=q(0), op=ALU.mult), reads=["aq", "q0"], writes=["q1"])
    s.op("vector", lambda e: e.scalar_tensor_tensor(out=q(2), in0=c.aq[:, 1, :], scalar=INV_2PI, in1=q(0), op0=ALU.mult, op1=ALU.mult),
         reads=["aq", "q0"], writes=["q2"])
    s.op("scalar", lambda e: e.activation(out=q(3), in_=q(1), func=AF.Exp), reads=["q1"], writes=["q3"])
    emit_sincos(c, q(2), q(4), q(5), {"i32": c.qi[:], "a": q(6), "b": q(7)}, "qsc", reads=["q2"])
    s.op("vector", lambda e: e.tensor_tensor(out=Gp[:, 0, 0, :], in0=q(5), in1=q(3), op=ALU.mult), reads=[("qsc", "c"), "q3"], writes=["G"])
    s.op("vector", lambda e: e.tensor_tensor(out=Gp[:, 0, 1, :], in0=q(4), in1=q(3), op=ALU.mult), reads=[("qsc", "s"), "q3", "G"], writes=["G"])
    s.op("vector", lambda e: e.tensor_copy(out=P[:, 0, :, 0], in_=Gp[:, 0, 0, :]), reads=["G"], writes=["E"])
    s.op("vector", lambda e: e.tensor_copy(out=P[:, 1, :, 0], in_=Gp[:, 0, 1, :]), reads=["G", "E"], writes=["E"])
    cur = emit_cpow(s, P, Gp, etmp, None, TC, True)
    s.op("vector", lambda e, cur=cur: e.tensor_copy(out=G128[:], in_=Gp[:, cur]), reads=["G"], writes=["G128"])
    s.op("vector", lambda e: e.tensor_copy(out=Pb[:], in_=P[:]), reads=["E"], writes=["Pb"])
    t0 = etmp[:, 0]; t1 = etmp[:, 1]
    for _ in range(3):
        nxt = 1 - cur
        sr, si = Gp[:, cur, 0, :], Gp[:, cur, 1, :]
        dr, di = Gp[:, nxt, 0, :], Gp[:, nxt, 1, :]
        s.op("vector", lambda e, si=si: e.tensor_tensor(out=t0[:, :, 0], in0=si, in1=si, op=ALU.mult), reads=["G"], writes=["t0"])
        s.op("vector", lambda e, sr=sr: e.tensor_tensor(out=t1[:, :, 0], in0=sr, in1=sr, op=ALU.mult), reads=["G"], writes=["t1"])
        s.op("vector", lambda e, sr=sr, si=si, di=di: e.scalar_tensor_tensor(out=di, in0=sr, scalar=2.0, in1=si, op0=ALU.mult, op1=ALU.mult), reads=["G"], writes=["G"])
        s.op("vector", lambda e, dr=dr: e.tensor_tensor(out=dr, in0=t1[:, :, 0], in1=t0[:, :, 0], op=ALU.subtract), reads=["t0", "t1", "G"], writes=["G"])
        cur = nxt
    s.op("vector", lambda e, cur=cur: e.tensor_copy(out=c.A[:], in_=Gp[:, cur]), reads=["G"], writes=["A"])
    s5_combine(c, xall_ap, oneh_ap)
    s.op("vector", lambda e: e.tensor_copy(out=V_[:, 0], in_=c.xinit[:]), reads=["xinit"], writes=["V"])
    s.barrier()
    for b in range(4):
        s.dma("sync", lambda e, b=b: e.dma_start(out=Czf.rearrange("q r (a b) c -> q r a b c", b=4)[:, :, :, b, :],
                                                 in_=Cz4[:, :, :, b, 32 * b:32 * b + 32]), writes=[("Czf", b)], sem_key=("Czf", b))
    s.op("gpsimd", lambda e: e.memset(Wt[:], 0.0), writes=[("Wt", 0), ("Wt", 1)])
    vcur = 0
    gi = 0
    for j in range(8):
        ws_ = j % 2
        vr = bc_last(V_[:, vcur, 0, :].unsqueeze(2), 32); vi = bc_last(V_[:, vcur, 1, :].unsqueeze(2), 32)
        cre = Czf[:, 0]; cim = Czf[:, 1]
        def blk(ri, ws_=ws_):
            return [Wt[:, ws_, ri].rearrange("q (a b) c -> q a b c", b=4)[:, :, b, 32 * b:32 * b + 32] for b in range(4)]
        s.op("vector", lambda e, vr=vr: e.tensor_tensor(out=wa[:], in0=cre, in1=vr, op=ALU.mult), reads=CZ + ["V"], writes=["wa"])
        s.op("vector", lambda e, vi=vi: e.tensor_tensor(out=wb[:], in0=cim, in1=vi, op=ALU.mult), reads=CZ + ["V"], writes=["wb"])
        s.op("vector", lambda e: e.tensor_tensor(out=wa[:], in0=wa[:], in1=wb[:], op=ALU.subtract), reads=["wa", "wb"], writes=["wa"])
        for b, dst in enumerate(blk(0)):
            s.op("vector", lambda e, b=b, dst=dst: e.tensor_copy(out=dst, in_=wa[:].rearrange("q (a b) c -> q a b c", b=4)[:, :, b, :]),
                 reads=["wa"], writes=[("Wt", ws_)])
        s.op("vector", lambda e, vi=vi: e.tensor_tensor(out=wa[:], in0=cre, in1=vi, op=ALU.mult), reads=CZ + ["V", ("Wt", ws_)], writes=["wa"])
        s.op("vector", lambda e, vr=vr: e.tensor_tensor(out=wb[:], in0=cim, in1=vr, op=ALU.mult), reads=CZ + ["V"], writes=["wb"])
        s.op("vector", lambda e: e.scalar_tensor_tensor(out=wa[:], in0=wa[:], scalar=-1.0, in1=wb[:], op0=ALU.mult, op1=ALU.subtract), reads=["wa", "wb"], writes=["wa"])
        for b, dst in enumerate(blk(1)):
            s.op("vector", lambda e, b=b, dst=dst: e.tensor_copy(out=dst, in_=wa[:].rearrange("q (a b) c -> q a b c", b=4)[:, :, b, :]),
                 reads=["wa"], writes=[("Wt", ws_)])
        osl = slice((j % 4) * TC, (j % 4 + 1) * TC)
        for cc in range(8):
            for pp in range(4):
                p = cc * 4 + pp
                s.op("tensor", lambda e, cc=cc, p=p, pp=pp, ws_=ws_, osl=osl: e.matmul(ps[:, cc, osl], Wt[:, ws_, 0, p, :], Pb[:, 0, p, :],
                                                                                    start=(pp == 0), stop=False, skip_group_check=True),
                     reads=[("Wt", ws_), "Pb"], writes=[("ps", cc)])
                s.op("tensor", lambda e, cc=cc, p=p, pp=pp, ws_=ws_, osl=osl: e.matmul(ps[:, cc, osl], Wt[:, ws_, 1, p, :], Pb[:, 1, p, :],
                                                                                    start=False, stop=(pp == 3), skip_group_check=True),
                     reads=[("Wt", ws_), "Pb"], writes=[("ps", cc)])
        nv = 1 - vcur
        a0 = c.qs[:, 9, :]; a1 = c.qs[:, 10, :]
        s.op("vector", lambda e, vcur=vcur: e.tensor_tensor(out=a0, in0=V_[:, vcur, 0, :], in1=G128[:, 0, :], op=ALU.mult), reads=["V", "G128"], writes=["a0"])
        s.op("vector", lambda e, vcur=vcur: e.tensor_tensor(out=a1, in0=V_[:, vcur, 1, :], in1=G128[:, 1, :], op=ALU.mult), reads=["V", "G128"], writes=["a1"])
        s.op("vector", lambda e, nv=nv: e.tensor_tensor(out=V_[:, nv, 0, :], in0=a0, in1=a1, op=ALU.subtract), reads=["a0", "a1", "V"], writes=["V"])
        s.op("vector", lambda e, vcur=vcur: e.tensor_tensor(out=a0, in0=V_[:, vcur, 0, :], in1=G128[:, 1, :], op=ALU.mult), reads=["V", "G128"], writes=["a0"])
        s.op("vector", lambda e, vcur=vcur: e.tensor_tensor(out=a1, in0=V_[:, vcur, 1, :], in1=G128[:, 0, :], op=ALU.mult), reads=["V", "G128"], writes=["a1"])
        s.op("vector", lambda e, nv=nv: e.tensor_tensor(out=V_[:, nv, 1, :], in0=a0, in1=a1, op=ALU.add), reads=["a0", "a1", "V"], writes=["V"])
        vcur = nv
        if j % 4 == 3:
            t = j // 4
            tsl = slice(t * 512, (t + 1) * 512)
            for cc in range(8):
                sl = gi % 2; gi += 1
                s.dma("sync", lambda e, cc=cc, tsl=tsl, sl=sl: e.dma_start(out=yl[:, sl, :], in_=ypreT[cc * 128:(cc + 1) * 128, tsl]), writes=[("yl", sl)], sem_key=("yl", sl))
                s.op("vector", lambda e, cc=cc, sl=sl: e.tensor_tensor(out=yl[:, sl, :], in0=ps[:, cc, :], in1=yl[:, sl, :], op=ALU.add),
                     reads=[("ps", cc), ("yl", sl)], writes=[("yl", sl)])
                emit_gelu_src(c, yl[:, sl, :], [("yl", sl)], c.yag[:, cc, tsl], ga[:, sl, :], gb[:, sl, :], writes=[("yag", cc)], tmpkeys=(("ga", sl), ("gb", sl)))
    return ids


def s5_layouts(a_re, a_im, log_dt, b_re, b_im, c_re, c_im, d_skip):
    NP = 32
    def qlay(v):
        return v.reshape(NP, 2, 64).transpose(1, 2, 0).reshape(128, NP)
    ldt2 = np.repeat(log_dt[:, None], 64, axis=1)
    aq = np.stack([qlay(a_re), qlay(a_im), qlay(ldt2)], axis=1).astype(np.float32)
    Cz = np.zeros((128, 2, NP, 128), np.float32)
    for p in range(NP):
        for g2 in range(2):
            g = 2 * p + g2
            r0 = 32 * (p % 4) + 16 * g2
            for ri, (bm, cm) in enumerate(((b_re, c_re), (b_im, c_im))):
                Cz[g2 * 64:(g2 + 1) * 64, ri, p, r0:r0 + 16] = cm[g].T
    Bq = np.zeros((128, 2, NP, 16), np.float32)
    for p in range(NP):
        for g2 in range(2):
            Bq[g2 * 64:(g2 + 1) * 64, 0, p, :] = b_re[2 * p + g2]
            Bq[g2 * 64:(g2 + 1) * 64, 1, p, :] = b_im[2 * p + g2]
    dq = np.ascontiguousarray(d_skip.reshape(8, 128).T).astype(np.float32)
    return dict(aq=aq, Bq=Bq, Cz=Cz, dq=dq)


def _dram_in(nc, name, shape):
    return nc.dram_tensor(name, list(shape), F32, kind="ExternalInput").ap()


def _dram_out(nc, name, shape):
    return nc.dram_tensor(name, list(shape), F32, kind="ExternalOutput").ap()


def _carver(arena, layout):
    def V(name, shape, dt=F32, parts=128):
        return arena.view(layout[name], shape, dt, parts)
    return V


LAY_A_FFN = dict(x=0, h=64, wring=96, hid=160, sq=176, rstd=178, silu=182, ones=186, gains=186.25, epsb=186.5,
                 stage=187, tmpa=191, tmpb=195)
LAY_A_S5 = dict(BtT=0, Cz=16, E=32, z=96, mt=104, w=108, xs=116, ypre=124, etmp=96, rs=32, arow=64, btp=76, ua=160,
                Bq=128, Btq=132, Zp=136, identf=144, kq=144.5,
                ri=176, aq=180, qs=180.5, qi=182, Gp=182.25, G128=182.75, ini=183, itmp=183.25, xend=183.5, dq=183.75)


def build_A():
    nc = bass.Bass("TRN2", target_bir_lowering=False)
    xT = _dram_in(nc, "xT", [D, TOK]); g1 = _dram_in(nc, "g1", [D]); g2 = _dram_in(nc, "g2", [D])
    wg = _dram_in(nc, "wg", [D, DFF]); wu = _dram_in(nc, "wu", [D, DFF]); wd = _dram_in(nc, "wd", [DFF, D])
    win = _dram_in(nc, "win", [D, MIXIN])
    aq = _dram_in(nc, "aq_in", [128, 3, NP]); Bq = _dram_in(nc, "Bq_in", [128, 2, NP, 16]); identA = _dram_in(nc, "identA", [128, 128])
    Cz = _dram_in(nc, "Cz_in", [128, 2, NP, 128]); dq = _dram_in(nc, "dq_in", [128, 8])
    x1T = _dram_out(nc, "x1T", [D, TOK]); projUV = _dram_out(nc, "projUV", [2048, TOK])
    ypre = _dram_out(nc, "ypreT", [1024, TOK]); xend = _dram_out(nc, "xend_out", [128, 2, NP])
    c = Ctx(); c.nc = nc; c.s = Sched(); s = c.s
    with contextlib.ExitStack() as st:
        ar = Arena(nc, st, 200)
        c.ps = st.enter_context(nc.psum_tensor("ps", [128, 8, 512], F32))
        V1 = _carver(ar, LAY_A_FFN)
        alloc_ffn(c, V1)
        c.wring = V1("wring", [128, 4, 8192], BF16); c.wslot = 0
        stage = V1("stage", [128, 2, 512]); tmpa = V1("tmpa", [128, 2, 512]); tmpb = V1("tmpb", [128, 2, 512])
        V2 = _carver(ar, LAY_A_S5)
        c.ua = V2("ua", [128, 8, TOK], BF16)
        emit_consts(c)
        load_gain(c, 0, g1); load_gain(c, 1, g2)
        load_xT(c, xT)
        emit_ffn(c, 0, wg, wu, wd)
        ids = store_T(c, c.x, "x", x1T)
        emit_rmsnorm(c, 1)
        ids += emit_inproj(c, win, projUV, stage, tmpa, tmpb)
        s.barrier()
        s5_alloc_v(c, V2)
        s5_setup_v2(c, V2, aq, Bq, identA)
        s.dma("gpsimd", lambda e: e.dma_start(out=c.Cz, in_=Cz), writes=["Cz"], sem_key="Cz")
        s.dma("sync", lambda e: e.dma_start(out=c.dq, in_=dq), writes=["dq"], sem_key="dq")
        ids += s5_main(c, True, ypre, zero_init=True, pregelu=True)
        ids.append(s.dma("sync", lambda e: e.dma_start(out=xend, in_=c.xend), reads=["xend"], sem_key="xe"))
        s.emit(nc, final_wait_ops=ids)
    return nc


LAY_B1 = dict(P=64, etmp=144, Pb=128, Wt=96, Czf=64, wa=72, wb=76, yl=80, ga=84, gb=88, yag=160,
              aq=176, qs=176.5, qi=178, Gp=178.25, A=178.75, X=179, xinit=179.5, xall=179.75, oneh=181.75, V=182, G128=182.5)
LAY_B2 = dict(yag=160, ug=128, vg=144, vn=64, ya=96, yb=112, bsf=80, bst=84, bsh=88, bsl=90,
              sq2=176, ones2=178, idb=178.25, par=178.5, epsb2=178.75, mean=179, msq=181, rstd2=183, t1=185, sig=189, wsb=193, vT=195)
LAY_B3 = dict(hmix=64, ya=96, yb=112, mg=128, x=64, xst=176, sq3=184, ones3=186, epsb3=186.25, rstd3=186.5, gain3=190.5, bg=190.75,
              sga=191, tmp3=195)
LAY_B5 = dict(x=64, h=128, hid=160, sq=176, rstd=178, silu=182, ones=186, gains=186.25, epsb=186.5, stage=187)


def build_B():
    nc = bass.Bass("TRN2", target_bir_lowering=False)
    x1T = _dram_in(nc, "x1T_in", [D, TOK]); projUV = _dram_in(nc, "projUV_in", [2048, TOK]); ypre = _dram_in(nc, "ypreT_in", [1024, TOK])
    aq = _dram_in(nc, "aq_in", [128, 3, NP]); Cz = _dram_in(nc, "Cz_in", [128, 2, NP, 128])
    xall = _dram_in(nc, "xall_in", [128, 8, 2, NP]); oneh = _dram_in(nc, "oneh_in", [128, 8])
    w_glu = _dram_in(nc, "w_glu", [1024, 1024]); b_glu = _dram_in(nc, "b_glu", [1024])
    ln_g = _dram_in(nc, "ln_g", [1024]); ln_b = _dram_in(nc, "ln_b", [1024])
    wsT = _dram_in(nc, "wsT", [8, 128, 128]); b_s = _dram_in(nc, "b_s", [8, 128]); ident = _dram_in(nc, "ident", [128, 128])
    g_mix = _dram_in(nc, "g_mix", [D]); w_a = _dram_in(nc, "w_a", [1024, D]); w_b = _dram_in(nc, "w_b", [1024, D])
    w_gate = _dram_in(nc, "w_gate", [D, 2 * D]); b_gate = _dram_in(nc, "b_gate", [2 * D]); w_out = _dram_in(nc, "w_out", [D, D])
    g1 = _dram_in(nc, "g1", [D]); g2 = _dram_in(nc, "g2", [D])
    wg = _dram_in(nc, "wg", [D, DFF]); wu = _dram_in(nc, "wu", [D, DFF]); wd = _dram_in(nc, "wd", [DFF, D])
    outT = _dram_out(nc, "outT", [D, TOK])
    c = Ctx(); c.nc = nc; c.s = Sched(); s = c.s
    with contextlib.ExitStack() as st:
        ar = Arena(nc, st, 200)
        c.ps = st.enter_context(nc.psum_tensor("ps", [128, 8, 512], F32))
        c.wring = ar.view(0, [128, 4, 8192], BF16); c.wslot = 0
        V1 = _carver(ar, LAY_B1); V2 = _carver(ar, LAY_B2); V3 = _carver(ar, LAY_B3); V5 = _carver(ar, LAY_B5)
        c.yag = V1("yag", [128, 8, TOK], BF16)
        emit_s5_correct(c, V1, aq, Cz, xall, oneh, ypre)
        s.barrier()
        c.ya = V2("ya", [128, 8, TOK], BF16); c.yb = V2("yb", [128, 8, TOK], BF16)
        emit_glu_sgu(c, V2, projUV, w_glu, b_glu, ln_g, ln_b, wsT, b_s, ident)
        s.barrier()
        c.x = V3("x", [128, KC, TOK], F32)
        emit_merge(c, V3, x1T, g_mix, w_a, w_b, w_gate, b_gate, w_out)
        s.barrier()
        alloc_ffn(c, V5)
        stage = V5("stage", [128, 2, 512])
        emit_consts(c)
        load_gain(c, 0, g1); load_gain(c, 1, g2)
        emit_ffn(c, 0, wg, wu, wd)
        ids = emit_final_norm(c, 1, outT, stage)
        s.emit(nc, final_wait_ops=ids)
    return nc


NCORES = 8


def _run(nc, maps):
    return run_bass_kernel_spmd(nc, maps, core_ids=list(range(NCORES))).results


def kernel(x, ffn1_norm, ffn1_w_gate, ffn1_w_up, ffn1_w_down, mix_norm, w_in,
           s5_a_re, s5_a_im, s5_log_dt, s5_b_re, s5_b_im, s5_c_re, s5_c_im, s5_d,
           s5_w_glu, s5_b_glu, sgu_ln_g, sgu_ln_b, sgu_w_s, sgu_b_s,
           w_branch_a, w_branch_b, w_gate, b_gate, w_out,
           ffn2_norm, ffn2_w_gate, ffn2_w_up, ffn2_w_down, final_norm):
    f = lambda a: np.ascontiguousarray(np.asarray(a, dtype=np.float32))
    x = f(x)[0]
    n = NCORES
    lay = s5_layouts(f(s5_a_re)[0], f(s5_a_im)[0], f(s5_log_dt)[0], f(s5_b_re)[0], f(s5_b_im)[0], f(s5_c_re)[0], f(s5_c_im)[0], f(s5_d)[0])
    wA = dict(g1=f(ffn1_norm)[0], g2=f(mix_norm)[0], wg=f(ffn1_w_gate)[0], wu=f(ffn1_w_up)[0], wd=f(ffn1_w_down)[0], win=f(w_in)[0],
              aq_in=lay["aq"], Bq_in=lay["Bq"], identA=np.eye(128, dtype=np.float32), Cz_in=lay["Cz"], dq_in=lay["dq"])
    rA = _run(build_A(), [dict(wA, xT=np.ascontiguousarray(x[i * TOK:(i + 1) * TOK].T)) for i in range(n)])
    xall = np.ascontiguousarray(np.stack([rA[i]["xend_out"] for i in range(n)], axis=1))
    wsT = np.ascontiguousarray(np.transpose(f(sgu_w_s)[0], (0, 2, 1)))
    wB = dict(aq_in=lay["aq"], Cz_in=lay["Cz"], xall_in=xall,
              w_glu=f(s5_w_glu)[0], b_glu=f(s5_b_glu)[0], ln_g=f(sgu_ln_g)[0], ln_b=f(sgu_ln_b)[0], wsT=wsT, b_s=f(sgu_b_s)[0],
              ident=np.eye(128, dtype=np.float32),
              g_mix=f(mix_norm)[0], w_a=f(w_branch_a)[0], w_b=f(w_branch_b)[0], w_gate=f(w_gate)[0], b_gate=f(b_gate)[0], w_out=f(w_out)[0],
              g1=f(ffn2_norm)[0], g2=f(final_norm), wg=f(ffn2_w_gate)[0], wu=f(ffn2_w_up)[0], wd=f(ffn2_w_down)[0])
    maps = []
    for i in range(n):
        oh = np.zeros((128, 8), np.float32); oh[:, i] = 1.0
        maps.append(dict(wB, oneh_in=oh, x1T_in=rA[i]["x1T"], projUV_in=rA[i]["projUV"], ypreT_in=rA[i]["ypreT"]))
    rB = _run(build_B(), maps)
    out = np.concatenate([rB[i]["outT"].T for i in range(n)], axis=0)
    return np.ascontiguousarray(out[None].astype(np.float32))
```

```python
import contextlib
import numpy as np
import concourse.bass as bass
import concourse.mybir as mybir
from concourse.bass_utils import run_bass_kernel_spmd

ENGINES = ("tensor", "vector", "scalar", "gpsimd", "sync")
RELAX_BULK = False


class Sched:
    def __init__(self, self_edges=True):
        self.ops = []
        self.last_w = {}
        self.readers = {}
        self.self_edges = self_edges
        self.fence = []
        self.fenced = set()

    def _add(self, eng, emit, reads, writes, dma_key=None, size=0, strict=False):
        i = len(self.ops)
        deps = set()
        for r in reads:
            if r in self.last_w:
                deps.add(self.last_w[r])
        for w in writes:
            if w in self.last_w:
                deps.add(self.last_w[w])
            deps.update(self.readers.get(w, ()))
        if self.fence and eng not in self.fenced:
            deps.update(self.fence)
            self.fenced.add(eng)
        deps.discard(i)
        self.ops.append(dict(eng=eng, emit=emit, deps=sorted(deps), dma_key=dma_key, size=size, strict=strict))
        for r in reads:
            self.readers.setdefault(r, []).append(i)
        for w in writes:
            self.last_w[w] = i
            self.readers[w] = []
        return i

    def barrier(self):
        last = {}
        for i, o in enumerate(self.ops):
            k = ("d", o["dma_key"]) if o["dma_key"] is not None else ("e", o["eng"])
            last[k] = i
        self.fence = sorted(last.values())
        self.fenced = set()

    def op(self, eng, emit, reads=(), writes=(), size=0, strict=False):
        return self._add(eng, emit, list(reads), list(writes), size=size, strict=strict)

    def dma(self, eng, emit, reads=(), writes=(), sem_key=None):
        assert sem_key is not None
        return self._add(eng, emit, list(reads), list(writes), dma_key=sem_key)

    def _self_skip(self, p, o):
        if p["eng"] != o["eng"]:
            return False
        if p["eng"] == "tensor" or not self.self_edges:
            return True
        return RELAX_BULK and p["size"] >= 256 and not o["strict"]

    def emit(self, nc, final_wait_ops=()):
        ops = self.ops
        need_inc = [False] * len(ops)
        for i, o in enumerate(ops):
            for d in o["deps"]:
                p = ops[d]
                if p["dma_key"] is not None:
                    continue
                if self._self_skip(p, o):
                    continue
                need_inc[d] = True
        eng_cnt = {e: 0 for e in ENGINES}
        inc_val = [None] * len(ops)
        dma_cnt = {}
        for i, o in enumerate(ops):
            if o["dma_key"] is not None:
                k = o["dma_key"]
                dma_cnt[k] = dma_cnt.get(k, 0) + 16
                inc_val[i] = dma_cnt[k]
            elif need_inc[i]:
                eng_cnt[o["eng"]] += 1
                inc_val[i] = eng_cnt[o["eng"]]
        import contextlib
        with contextlib.ExitStack() as st:
            esem = {e: st.enter_context(nc.semaphore("e_" + e)) for e in ENGINES}
            dsem = {k: st.enter_context(nc.semaphore("d_%d" % j)) for j, k in enumerate(dma_cnt)}
            block = st.enter_context(nc.Block())
            per_eng = {e: [i for i, o in enumerate(ops) if o["eng"] == e] for e in ENGINES}

            def make(e):
                def body(eng):
                    waited = {}
                    for i in per_eng[e]:
                        o = ops[i]
                        for d in o["deps"]:
                            p = ops[d]
                            if p["dma_key"] is not None:
                                key = ("d", p["dma_key"])
                                sem = dsem[p["dma_key"]]
                            else:
                                if self._self_skip(p, o):
                                    continue
                                key = ("e", p["eng"])
                                sem = esem[p["eng"]]
                            v = inc_val[d]
                            if waited.get(key, 0) >= v:
                                continue
                            eng.wait_ge(sem, v)
                            waited[key] = v
                        ins = o["emit"](eng)
                        if o["dma_key"] is not None:
                            ins.then_inc(dsem[o["dma_key"]], 16)
                        elif need_inc[i]:
                            ins.then_inc(esem[e], 1)
                    if e == "sync":
                        for i in final_wait_ops:
                            p = ops[i]
                            sem = dsem[p["dma_key"]] if p["dma_key"] is not None else esem[p["eng"]]
                            eng.wait_ge(sem, inc_val[i])
                return body
            for e in ENGINES:
                if per_eng[e] or e == "sync":
                    getattr(block, e)(make(e))
        return nc


F32 = mybir.dt.float32
BF16 = mybir.dt.bfloat16
AF = mybir.ActivationFunctionType
ALU = mybir.AluOpType

D = 2048
KC = D // 128
TOK = 1024
NT = TOK // 512
DFF = 5632
NFB = DFF // 512
EPS = 1e-6


class Ctx:
    pass


_DSZ = {F32: 4, BF16: 2}


class Arena:
    def __init__(self, nc, st, kb):
        self.words = kb * 256
        self.t = st.enter_context(nc.sbuf_tensor("arena", [128, self.words], F32))

    def view(self, off_kb, shape, dt=F32, parts=128):
        n = 1
        for d in shape[1:]:
            n *= d
        esz = 2 if dt == BF16 else 4
        words = (n * esz + 3) // 4
        lo = int(round(off_kb * 256))
        assert lo + words <= self.words, (off_kb, shape)
        ap = self.t[0:parts, lo:lo + words]
        if dt != F32:
            ap = ap.bitcast(dt)
        ap = ap[:, 0:n]
        if len(shape) == 2:
            return ap
        names = " ".join("d%d" % i for i in range(len(shape) - 1))
        kw = {"d%d" % i: shape[i + 1] for i in range(len(shape) - 2)}
        return ap.rearrange("p (%s) -> p %s" % (names, names), **kw)


def alloc_ffn(c, V):
    c.x = V("x", [128, KC, TOK], F32)
    c.h = V("h", [128, KC, TOK], BF16)
    c.hid = V("hid", [128, 2, 4, TOK], BF16)
    c.sq = V("sq", [128, 2, 512], BF16)
    c.rstd = V("rstd", [128, TOK], F32)
    c.silu = V("silu", [128, 2, 512], F32)
    c.ones = V("ones", [128, 128], BF16)
    c.gains = V("gains", [128, 4, KC], F32)
    c.epsb = V("epsb", [128, 1], F32)
    c.sqslot = 0
    c.silslot = 0


def emit_consts(c):
    s = c.s
    s.op("gpsimd", lambda e: e.memset(c.ones[:], 1.0), writes=[("ones",)])
    s.op("gpsimd", lambda e: e.memset(c.epsb[:], EPS), writes=[("epsb",)])


def load_gain(c, idx, g_ap):
    c.s.dma("sync", lambda e: e.dma_start(out=c.gains[:, idx, :], in_=g_ap.rearrange("(k p) -> p k", p=128),
                                          allow_slow_non_contiguous=True),
            writes=[("gain", idx)], sem_key=("gain", idx))


def emit_rmsnorm(c, gidx, out_key="h"):
    s = c.s
    for t in range(NT):
        tsl = slice(t * 512, (t + 1) * 512)
        bank = 7
        for k in range(KC):
            sl = c.sqslot; c.sqslot ^= 1
            s.op("scalar", lambda e, k=k, sl=sl, tsl=tsl: e.activation(out=c.sq[:, sl, :], in_=c.x[:, k, tsl], func=AF.Square),
                 reads=[("x", k, t)], writes=[("sq", sl)])
            s.op("tensor", lambda e, k=k, sl=sl, bank=bank: e.matmul(c.ps[:, bank, :], c.ones[:], c.sq[:, sl, :],
                                                          start=(k == 0), stop=(k == KC - 1)),
                 reads=[("sq", sl), ("ones",)], writes=[("ps", bank)])
        s.op("scalar", lambda e, tsl=tsl, bank=bank: e.activation(out=c.rstd[:, tsl], in_=c.ps[:, bank, :], func=AF.Sqrt,
                                              bias=c.epsb[:, 0:1], scale=1.0 / D),
             reads=[("ps", bank), ("epsb",)], writes=[("rstd", t)])
        s.op("vector", lambda e, tsl=tsl: e.reciprocal(out=c.rstd[:, tsl], in_=c.rstd[:, tsl]),
             reads=[("rstd", t)], writes=[("rstd", t)])
        for k in range(KC):
            s.op("vector", lambda e, k=k, tsl=tsl: e.scalar_tensor_tensor(
                out=c.h[:, k, tsl], in0=c.x[:, k, tsl], scalar=c.gains[:, gidx, k:k + 1],
                in1=c.rstd[:, tsl], op0=ALU.mult, op1=ALU.mult),
                 reads=[("x", k, t), ("gain", gidx), ("rstd", t)], writes=[(out_key, k, t)])


def wload(c, view_shape, src_ap, tag):
    slot = c.wslot; c.wslot = (c.wslot + 1) % 4
    n = 1
    for d in view_shape[1:]:
        n *= d
    assert n <= 8192
    flat = c.wring[:, slot, 0:n]
    if len(view_shape) == 3:
        view = flat.rearrange("p (a b) -> p a b", a=view_shape[1])
    else:
        view = flat
    c.s.dma("gpsimd", lambda e: e.dma_start(out=view, in_=src_ap), writes=[("w", slot)], sem_key=("w", slot))
    return slot, view


def emit_ffn(c, gidx, wg, wu, wd):
    s = c.s
    emit_rmsnorm(c, gidx)
    wg_v = wg.rearrange("(k p) f -> p k f", p=128)
    wu_v = wu.rearrange("(k p) f -> p k f", p=128)
    wd_v = wd.rearrange("(m p) d -> p m d", p=128)
    for b in range(NFB):
        hb = b % 2
        gs, gv = wload(c, [128, KC, 512], wg_v[:, :, b * 512:(b + 1) * 512], "g")
        us, uv = wload(c, [128, KC, 512], wu_v[:, :, b * 512:(b + 1) * 512], "u")
        ds_, dv = wload(c, [128, 4, D], wd_v[:, b * 4:(b + 1) * 4, :], "d")
        for m in range(4):
            for kind, (ws, wv) in enumerate(((gs, gv), (us, uv))):
                for t in range(NT):
                    bank = kind * 2 + t
                    for k in range(KC):
                        s.op("tensor", lambda e, wv=wv, k=k, m=m, t=t, bank=bank: e.matmul(
                            c.ps[:, bank, :], wv[:, k, m * 128:(m + 1) * 128], c.h[:, k, t * 512:(t + 1) * 512],
                            start=(k == 0), stop=(k == KC - 1)),
                            reads=[("w", ws), ("h", k, t)], writes=[("ps", bank)])
            for t in range(NT):
                sl = c.silslot; c.silslot ^= 1
                s.op("scalar", lambda e, t=t, sl=sl: e.activation(out=c.silu[:, sl, :], in_=c.ps[:, t, :], func=AF.Silu),
                     reads=[("ps", t)], writes=[("silu", sl)])
                s.op("vector", lambda e, t=t, sl=sl, m=m, hb=hb: e.tensor_tensor(
                    out=c.hid[:, hb, m, t * 512:(t + 1) * 512], in0=c.ps[:, 2 + t, :], in1=c.silu[:, sl, :], op=ALU.mult),
                     reads=[("ps", 2 + t), ("silu", sl)], writes=[("hid", hb, m, t)])
        gi = 0
        for n in range(KC):
            for t in range(NT):
                bank = 4 + (gi % 4); gi += 1
                for m in range(4):
                    s.op("tensor", lambda e, n=n, t=t, m=m, bank=bank, hb=hb, dv=dv: e.matmul(
                        c.ps[:, bank, :], dv[:, m, n * 128:(n + 1) * 128], c.hid[:, hb, m, t * 512:(t + 1) * 512],
                        start=(m == 0), stop=(m == 3)),
                        reads=[("w", ds_), ("hid", hb, m, t)], writes=[("ps", bank)])
                s.op("vector", lambda e, n=n, t=t, bank=bank: e.scalar_tensor_tensor(
                    out=c.x[:, n, t * 512:(t + 1) * 512], in0=c.ps[:, bank, :], scalar=0.5,
                    in1=c.x[:, n, t * 512:(t + 1) * 512], op0=ALU.mult, op1=ALU.add),
                     reads=[("ps", bank), ("x", n, t)], writes=[("x", n, t)])


def load_x(c, x_ap):
    raise NotImplementedError


def load_xT(c, xT_ap):
    v = xT_ap.rearrange("(k p) t -> p k t", p=128)
    for k in range(KC):
        c.s.dma("sync", lambda e, k=k: e.dma_start(out=c.x[:, k, :], in_=v[:, k, :]),
                writes=[("x", k, t) for t in range(NT)], sem_key=("xld", k))


def store_T(c, src, key, outT_ap):
    v = outT_ap.rearrange("(k p) t -> p k t", p=128)
    ids = []
    for k in range(KC):
        ids.append(c.s.dma("sync", lambda e, k=k: e.dma_start(out=v[:, k, :], in_=src[:, k, :]),
                           reads=[(key, k, t) for t in range(NT)], sem_key=("st", k)))
    return ids


GELU_C1 = 0.044715
GELU_C2 = 1.5957691216057308


def emit_gelu_from_psum(c, bank, out_ap, tmp_a, tmp_b, reads, writes, tmpkeys):
    s = c.s
    ka, kb = tmpkeys
    s.op("scalar", lambda e: e.activation(out=tmp_a, in_=c.ps[:, bank, :], func=AF.Square),
         reads=reads, writes=[ka])
    s.op("vector", lambda e: e.tensor_scalar(out=tmp_a, in0=tmp_a, scalar1=GELU_C1, scalar2=1.0, op0=ALU.mult, op1=ALU.add),
         reads=[ka], writes=[ka])
    s.op("vector", lambda e: e.tensor_tensor(out=tmp_a, in0=c.ps[:, bank, :], in1=tmp_a, op=ALU.mult),
         reads=reads + [ka], writes=[ka])
    s.op("scalar", lambda e: e.activation(out=tmp_b, in_=tmp_a, func=AF.Sigmoid, scale=GELU_C2),
         reads=[ka], writes=[kb])
    s.op("vector", lambda e: e.tensor_tensor(out=out_ap, in0=c.ps[:, bank, :], in1=tmp_b, op=ALU.mult),
         reads=reads + [kb], writes=writes)


MIXIN = 3072


def emit_inproj(c, w_in, projT_ap, stage, tmpa, tmpb):
    s = c.s
    wv = w_in.rearrange("(k p) f -> p k f", p=128)
    ids = []
    gi = 0
    for u in range(MIXIN // 512):
        ws, wview = wload(c, [128, KC, 512], wv[:, :, u * 512:(u + 1) * 512], "in")
        for m in range(4):
            cc = u * 4 + m
            for t in range(NT):
                bank = gi % 4
                sl = gi % 2
                gi += 1
                for k in range(KC):
                    s.op("tensor", lambda e, wview=wview, k=k, m=m, t=t, bank=bank: e.matmul(
                        c.ps[:, bank, :], wview[:, k, m * 128:(m + 1) * 128], c.h[:, k, t * 512:(t + 1) * 512],
                        start=(k == 0), stop=(k == KC - 1)),
                        reads=[("w", ws), ("h", k, t)], writes=[("ps", bank)])
                if cc < 8:
                    s.op("scalar", lambda e, bank=bank, cc=cc, t=t: e.activation(out=c.ua[:, cc, t * 512:(t + 1) * 512], in_=c.ps[:, bank, :], func=AF.Copy),
                         reads=[("ps", bank)], writes=[("ua", cc)])
                    continue
                else:
                    emit_gelu_from_psum(c, bank, stage[:, sl, :], tmpa[:, sl, :], tmpb[:, sl, :],
                                        reads=[("ps", bank)], writes=[("stage", sl)], tmpkeys=(("tmpa", sl), ("tmpb", sl)))
                ids.append(s.dma("sync", lambda e, cc=cc, t=t, sl=sl: e.dma_start(
                    out=projT_ap[(cc - 8) * 128:(cc - 7) * 128, t * 512:(t + 1) * 512], in_=stage[:, sl, :]),
                    reads=[("stage", sl)], sem_key=("stg", sl)))
    return ids


TWO_PI = 6.283185
INV_2PI = 0.15915494309189535
I32 = mybir.dt.int32


def bc_last(ap, n):
    shp = list(ap.shape)
    shp[-1] = n
    return ap.to_broadcast(shp)


def emit_sincos(c, f_ap, sin_ap, cos_ap, scr, keyp, reads):
    s = c.s
    K = lambda n: (keyp, n)
    s.op("vector", lambda e: e.tensor_copy(out=scr["i32"], in_=f_ap), reads=reads, writes=[K("i32")])
    s.op("vector", lambda e: e.tensor_copy(out=scr["a"], in_=scr["i32"]), reads=[K("i32")], writes=[K("a")])
    s.op("vector", lambda e: e.tensor_tensor(out=scr["a"], in0=f_ap, in1=scr["a"], op=ALU.subtract), reads=reads + [K("a")], writes=[K("a")])
    for which, out_ap, off in (("s", sin_ap, 0.0), ("c", cos_ap, 0.25)):
        if off != 0.0:
            s.op("vector", lambda e, off=off: e.tensor_scalar(out=scr["b"], in0=scr["a"], scalar1=off, scalar2=None, op0=ALU.add),
                 reads=[K("a")], writes=[K("b")])
            src = scr["b"]; srck = K("b")
        else:
            src = scr["a"]; srck = K("a")
        s.op("vector", lambda e, src=src: e.scalar_tensor_tensor(out=scr["b"], in0=src, scalar=0.5, in1=src, op0=ALU.is_gt, op1=ALU.subtract),
             reads=[srck], writes=[K("b")])
        s.op("vector", lambda e: e.scalar_tensor_tensor(out=scr["b"], in0=scr["b"], scalar=0.5, in1=scr["b"], op0=ALU.is_gt, op1=ALU.subtract),
             reads=[K("b")], writes=[K("b")])
        s.op("scalar", lambda e, out_ap=out_ap: e.activation(out=out_ap, in_=scr["b"], func=AF.Sin, scale=TWO_PI),
             reads=[K("b")], writes=[K(which)])


NP = 32
TC = 128
TCM = 256
NJ = 512 // TCM


def emit_gelu_src(c, src, src_keys, out_ap, tmp_a, tmp_b, writes, tmpkeys):
    s = c.s
    ka, kb = tmpkeys
    s.op("scalar", lambda e: e.activation(out=tmp_a, in_=src, func=AF.Square), reads=src_keys, writes=[ka])
    s.op("vector", lambda e: e.tensor_scalar(out=tmp_a, in0=tmp_a, scalar1=GELU_C1, scalar2=1.0, op0=ALU.mult, op1=ALU.add),
         reads=[ka], writes=[ka])
    s.op("vector", lambda e: e.tensor_tensor(out=tmp_a, in0=src, in1=tmp_a, op=ALU.mult), reads=src_keys + [ka], writes=[ka])
    s.op("scalar", lambda e: e.activation(out=tmp_b, in_=tmp_a, func=AF.Sigmoid, scale=GELU_C2), reads=[ka], writes=[kb])
    s.op("vector", lambda e: e.tensor_tensor(out=out_ap, in0=src, in1=tmp_b, op=ALU.mult), reads=src_keys + [kb], writes=writes)


def s5_alloc_v(c, V):
    T = V
    c.BtT = T("BtT", [128, 2, NP, 128], BF16)
    c.Cz = T("Cz", [128, 2, NP, 128], BF16)
    c.E = T("E", [128, 2, NP, TCM])
    c.z = T("z", [128, 2, 2, 512])
    c.mt = T("mt", [128, 2, 512])
    c.w = T("w", [128, 2, 2, 512])
    c.xs = T("xs", [128, 2, 4, 512], BF16)
    c.ypre = T("ypre", [128, 2, 512])
    c.etmp = T("etmp", [128, 2, NP, TCM // 2])
    c.arow = T("arow", [128, 3, 1024])
    c.btp = T("btp", [128, 2, 1024])
    c.rs = T("rs", [128, 8, 1024])
    c.ri = T("ri", [128, 1024], I32)
    c.aq = T("aq", [128, 3, NP])
    c.qs = T("qs", [128, 12, NP])
    c.qi = T("qi", [128, NP], I32)
    c.Gp = T("Gp", [128, 2, 2, NP])
    c.G128 = T("G128", [128, 2, NP])
    c.ini = T("ini", [128, NP, 2])
    c.itmp = T("itmp", [128, 2])
    c.xend = T("xend", [128, 2, NP])
    c.dq = T("dq", [128, 8])


def s5_setup(c, aq_ap, arow_ap, BT_ap, full):
    s = c.s
    s.dma("sync", lambda e: e.dma_start(out=c.aq[:], in_=aq_ap), writes=["aq"], sem_key="aq")
    q = lambda i: c.qs[:, i, :]
    s.op("scalar", lambda e: e.activation(out=q(0), in_=c.aq[:, 2, :], func=AF.Exp), reads=["aq"], writes=["q0"])
    s.op("vector", lambda e: e.tensor_tensor(out=q(1), in0=c.aq[:, 0, :], in1=q(0), op=ALU.mult), reads=["aq", "q0"], writes=["q1"])
    s.op("vector", lambda e: e.scalar_tensor_tensor(out=q(2), in0=c.aq[:, 1, :], scalar=INV_2PI, in1=q(0), op0=ALU.mult, op1=ALU.mult),
         reads=["aq", "q0"], writes=["q2"])
    s.op("scalar", lambda e: e.activation(out=q(3), in_=q(1), func=AF.Exp), reads=["q1"], writes=["q3"])
    emit_sincos(c, q(2), q(4), q(5), {"i32": c.qi[:], "a": q(6), "b": q(7)}, "qsc", reads=["q2"])
    QS, QC = ("qsc", "s"), ("qsc", "c")
    R = lambda i: c.rs[:, i, :]
    for pc in range(4):
        sl = slice(pc * 1024, (pc + 1) * 1024)
        s.dma("sync", lambda e, sl=sl: e.dma_start(out=c.arow[:], in_=arow_ap[:, :, sl]), writes=["arow"], sem_key="arow")
        s.dma("sync", lambda e, sl=sl: e.dma_start(out=c.btp[:], in_=BT_ap[:, :, sl]), writes=["btp"], sem_key="btp")
        are = c.arow[:, 0, :]; aim = c.arow[:, 1, :]
        s.op("scalar", lambda e: e.activation(out=R(0), in_=c.arow[:, 2, :], func=AF.Exp), reads=["arow"], writes=["r0"])
        s.op("vector", lambda e: e.tensor_tensor(out=R(1), in0=are, in1=R(0), op=ALU.mult), reads=["arow", "r0"], writes=["r1"])
        s.op("vector", lambda e: e.scalar_tensor_tensor(out=R(2), in0=aim, scalar=INV_2PI, in1=R(0), op0=ALU.mult, op1=ALU.mult),
             reads=["arow", "r0"], writes=["r2"])
        s.op("scalar", lambda e: e.activation(out=R(3), in_=R(1), func=AF.Exp), reads=["r1"], writes=["r3"])
        emit_sincos(c, R(2), R(4), R(5), {"i32": c.ri[:], "a": R(6), "b": R(7)}, "rsc", reads=["r2"])
        RS, RC = ("rsc", "s"), ("rsc", "c")
        s.op("vector", lambda e: e.tensor_tensor(out=R(5), in0=R(5), in1=R(3), op=ALU.mult), reads=[RC, "r3"], writes=[RC])
        s.op("vector", lambda e: e.tensor_scalar(out=R(5), in0=R(5), scalar1=-1.0, scalar2=None, op0=ALU.add), reads=[RC], writes=[RC])
        s.op("vector", lambda e: e.tensor_tensor(out=R(4), in0=R(4), in1=R(3), op=ALU.mult), reads=[RS, "r3"], writes=[RS])
        s.op("vector", lambda e: e.tensor_tensor(out=R(0), in0=are, in1=are, op=ALU.mult), reads=["arow", "r1", "r2"], writes=["r0"])
        s.op("vector", lambda e: e.tensor_tensor(out=R(1), in0=aim, in1=aim, op=ALU.mult), reads=["arow", "r3"], writes=["r1"])
        s.op("vector", lambda e: e.tensor_tensor(out=R(0), in0=R(0), in1=R(1), op=ALU.add), reads=["r0", "r1"], writes=["r0"])
        s.op("vector", lambda e: e.reciprocal(out=R(0), in_=R(0)), reads=["r0"], writes=["r0"])
        s.op("vector", lambda e: e.tensor_tensor(out=R(1), in0=R(5), in1=are, op=ALU.mult), reads=[RC, "arow", "r1"], writes=["r1"])
        s.op("vector", lambda e: e.tensor_tensor(out=R(2), in0=R(4), in1=aim, op=ALU.mult), reads=[RS, "arow", "r2", ("rsc", "a"), ("rsc", "b")], writes=["r2"])
        s.op("vector", lambda e: e.tensor_tensor(out=R(1), in0=R(1), in1=R(2), op=ALU.add), reads=["r1", "r2"], writes=["r1"])
        s.op("vector", lambda e: e.tensor_tensor(out=R(6), in0=R(1), in1=R(0), op=ALU.mult), reads=["r1", "r0", ("rsc", "a"), ("rsc", "b")], writes=["r6"])
        s.op("vector", lambda e: e.tensor_tensor(out=R(1), in0=R(4), in1=are, op=ALU.mult), reads=[RS, "arow", "r1", "r6"], writes=["r1"])
        s.op("vector", lambda e: e.tensor_tensor(out=R(2), in0=R(5), in1=aim, op=ALU.mult), reads=[RC, "arow", "r2"], writes=["r2"])
        s.op("vector", lambda e: e.tensor_tensor(out=R(1), in0=R(1), in1=R(2), op=ALU.subtract), reads=["r1", "r2"], writes=["r1"])
        s.op("vector", lambda e: e.tensor_tensor(out=R(7), in0=R(1), in1=R(0), op=ALU.mult), reads=["r1", "r0", "r6"], writes=["r7"])
        bre = c.btp[:, 0, :]; bim = c.btp[:, 1, :]
        ore = c.BtT[:, 0, pc * 8:(pc + 1) * 8, :].rearrange("p a b -> p (a b)")
        oim = c.BtT[:, 1, pc * 8:(pc + 1) * 8, :].rearrange("p a b -> p (a b)")
        s.op("vector", lambda e: e.tensor_tensor(out=R(1), in0=bre, in1=R(6), op=ALU.mult), reads=["btp", "r6", "r1"], writes=["r1"])
        s.op("vector", lambda e: e.tensor_tensor(out=R(2), in0=bim, in1=R(7), op=ALU.mult), reads=["btp", "r7", "r2"], writes=["r2"])
        s.op("vector", lambda e, ore=ore: e.tensor_tensor(out=ore, in0=R(1), in1=R(2), op=ALU.subtract), reads=["r1", "r2"], writes=[("BtT", pc)])
        s.op("vector", lambda e: e.tensor_tensor(out=R(1), in0=bre, in1=R(7), op=ALU.mult), reads=["btp", "r7", "r1", ("BtT", pc)], writes=["r1"])
        s.op("vector", lambda e: e.tensor_tensor(out=R(2), in0=bim, in1=R(6), op=ALU.mult), reads=["btp", "r6", "r2", ("BtT", pc)], writes=["r2"])
        s.op("vector", lambda e, oim=oim: e.tensor_tensor(out=oim, in0=R(1), in1=R(2), op=ALU.add), reads=["r1", "r2", ("BtT", pc)], writes=[("BtT", pc)])
    s.barrier()
    s.op("vector", lambda e: e.memset(c.E[:, 0, :, 0:1], 1.0), writes=["E"])
    s.op("vector", lambda e: e.memset(c.E[:, 1, :, 0:1], 0.0), reads=["E"], writes=["E"])
    s.op("vector", lambda e: e.tensor_copy(out=c.Gp[:, 0, 0, :], in_=q(5)), reads=[QC], writes=["G"])
    s.op("vector", lambda e: e.tensor_copy(out=c.Gp[:, 0, 1, :], in_=q(4)), reads=[QS, "G"], writes=["G"])
    cur = 0
    k = 1
    t0 = c.etmp[:, 0]; t1 = c.etmp[:, 1]

    def square(src, dst, dst_is_pp=True):
        sr, si = src
        dr, di = dst
        s.op("vector", lambda e: e.tensor_tensor(out=t0[:, :, 0], in0=si, in1=si, op=ALU.mult), reads=["G"], writes=["t0"])
        s.op("vector", lambda e: e.tensor_tensor(out=t1[:, :, 0], in0=sr, in1=sr, op=ALU.mult), reads=["G"], writes=["t1"])
        s.op("vector", lambda e: e.scalar_tensor_tensor(out=di, in0=sr, scalar=2.0, in1=si, op0=ALU.mult, op1=ALU.mult), reads=["G"], writes=["G"])
        s.op("vector", lambda e: e.tensor_tensor(out=dr, in0=t1[:, :, 0], in1=t0[:, :, 0], op=ALU.subtract), reads=["t0", "t1", "G"], writes=["G"])

    while k < TCM:
        gr = c.Gp[:, cur, 0, :]; gi = c.Gp[:, cur, 1, :]
        grb = bc_last(gr.unsqueeze(2), k); gib = bc_last(gi.unsqueeze(2), k)

        def mk(k=k, grb=grb, gib=gib):
            s.op("vector", lambda e: e.tensor_tensor(out=t0[:, :, 0:k], in0=c.E[:, 1, :, 0:k], in1=gib, op=ALU.mult), reads=["E", "G"], writes=["t0"])
            s.op("vector", lambda e: e.tensor_tensor(out=t1[:, :, 0:k], in0=c.E[:, 0, :, 0:k], in1=grb, op=ALU.mult), reads=["E", "G"], writes=["t1"])
            s.op("vector", lambda e: e.tensor_tensor(out=c.E[:, 0, :, k:2 * k], in0=t1[:, :, 0:k], in1=t0[:, :, 0:k], op=ALU.subtract), reads=["t0", "t1", "E"], writes=["E"])
            s.op("vector", lambda e: e.tensor_tensor(out=t0[:, :, 0:k], in0=c.E[:, 0, :, 0:k], in1=gib, op=ALU.mult), reads=["E", "G"], writes=["t0"])
            s.op("vector", lambda e: e.tensor_tensor(out=t1[:, :, 0:k], in0=c.E[:, 1, :, 0:k], in1=grb, op=ALU.mult), reads=["E", "G"], writes=["t1"])
            s.op("vector", lambda e: e.tensor_tensor(out=c.E[:, 1, :, k:2 * k], in0=t1[:, :, 0:k], in1=t0[:, :, 0:k], op=ALU.add), reads=["t0", "t1", "E"], writes=["E"])
        mk()
        nxt = 1 - cur
        square((gr, gi), (c.Gp[:, nxt, 0, :], c.Gp[:, nxt, 1, :]))
        cur = nxt
        k *= 2
    s.op("vector", lambda e, cur=cur: e.tensor_copy(out=c.G128[:], in_=c.Gp[:, cur]), reads=["G"], writes=["G128"])
    if full:
        for _ in range(3):
            nxt = 1 - cur
            square((c.Gp[:, cur, 0, :], c.Gp[:, cur, 1, :]), (c.Gp[:, nxt, 0, :], c.Gp[:, nxt, 1, :]))
            cur = nxt
        s.op("scalar", lambda e: e.activation(out=q(8), in_=q(1), func=AF.Exp, scale=1024.0), reads=["q1"], writes=["q8"])
        s.op("vector", lambda e, cur=cur: e.tensor_tensor(out=c.A[:, 0, :], in0=c.Gp[:, cur, 0, :], in1=q(8), op=ALU.mult), reads=["G", "q8"], writes=["A"])
        s.op("vector", lambda e, cur=cur: e.tensor_tensor(out=c.A[:, 1, :], in0=c.Gp[:, cur, 1, :], in1=q(8), op=ALU.mult), reads=["G", "q8", "A"], writes=["A"])
    s.barrier()


def s5_setup_v2(c, V, aq_ap, Bq_ap, ident_ap):
    s = c.s
    Bq = V("Bq", [128, 2, NP, 16]); Btq = V("Btq", [128, 2, NP, 16]); Zp = V("Zp", [128, 2, 8, 128])
    idf = V("identf", [128, 128]); kq = V("kq", [128, 8, NP])
    s.dma("sync", lambda e: e.dma_start(out=c.aq, in_=aq_ap), writes=["aq"], sem_key="aq")
    s.dma("sync", lambda e: e.dma_start(out=Bq, in_=Bq_ap), writes=["Bq"], sem_key="Bq")
    s.dma("sync", lambda e: e.dma_start(out=idf, in_=ident_ap), writes=["idf"], sem_key="idf")
    q = lambda i: c.qs[:, i, :]
    K = lambda i: kq[:, i, :]
    s.op("scalar", lambda e: e.activation(out=q(0), in_=c.aq[:, 2, :], func=AF.Exp), reads=["aq"], writes=["q0"])
    s.op("vector", lambda e: e.tensor_tensor(out=q(1), in0=c.aq[:, 0, :], in1=q(0), op=ALU.mult), reads=["aq", "q0"], writes=["q1"])
    s.op("vector", lambda e: e.scalar_tensor_tensor(out=q(2), in0=c.aq[:, 1, :], scalar=INV_2PI, in1=q(0), op0=ALU.mult, op1=ALU.mult),
         reads=["aq", "q0"], writes=["q2"])
    s.op("scalar", lambda e: e.activation(out=q(3), in_=q(1), func=AF.Exp), reads=["q1"], writes=["q3"])
    emit_sincos(c, q(2), q(4), q(5), {"i32": c.qi, "a": q(6), "b": q(7)}, "qsc", reads=["q2"])
    QS, QC = ("qsc", "s"), ("qsc", "c")
    are = c.aq[:, 0, :]; aim = c.aq[:, 1, :]
    s.op("vector", lambda e: e.tensor_tensor(out=K(0), in0=q(5), in1=q(3), op=ALU.mult), reads=[QC, "q3"], writes=["k0"])
    s.op("vector", lambda e: e.tensor_scalar(out=K(0), in0=K(0), scalar1=-1.0, scalar2=None, op0=ALU.add), reads=["k0"], writes=["k0"])
    s.op("vector", lambda e: e.tensor_tensor(out=K(1), in0=q(4), in1=q(3), op=ALU.mult), reads=[QS, "q3"], writes=["k1"])
    s.op("vector", lambda e: e.tensor_tensor(out=K(2), in0=are, in1=are, op=ALU.mult), reads=["aq"], writes=["k2"])
    s.op("vector", lambda e: e.tensor_tensor(out=K(3), in0=aim, in1=aim, op=ALU.mult), reads=["aq"], writes=["k3"])
    s.op("vector", lambda e: e.tensor_tensor(out=K(2), in0=K(2), in1=K(3), op=ALU.add), reads=["k2", "k3"], writes=["k2"])
    s.op("vector", lambda e: e.reciprocal(out=K(2), in_=K(2)), reads=["k2"], writes=["k2"])
    s.op("vector", lambda e: e.tensor_tensor(out=K(3), in0=K(0), in1=are, op=ALU.mult), reads=["k0", "aq", "k2"], writes=["k3"])
    s.op("vector", lambda e: e.tensor_tensor(out=K(4), in0=K(1), in1=aim, op=ALU.mult), reads=["k1", "aq"], writes=["k4"])
    s.op("vector", lambda e: e.tensor_tensor(out=K(3), in0=K(3), in1=K(4), op=ALU.add), reads=["k3", "k4"], writes=["k3"])
    s.op("vector", lambda e: e.tensor_tensor(out=K(5), in0=K(3), in1=K(2), op=ALU.mult), reads=["k3", "k2"], writes=["k5"])
    s.op("vector", lambda e: e.tensor_tensor(out=K(3), in0=K(1), in1=are, op=ALU.mult), reads=["k1", "aq", "k5"], writes=["k3"])
    s.op("vector", lambda e: e.tensor_tensor(out=K(4), in0=K(0), in1=aim, op=ALU.mult), reads=["k0", "aq", "k3"], writes=["k4"])
    s.op("vector", lambda e: e.tensor_tensor(out=K(3), in0=K(3), in1=K(4), op=ALU.subtract), reads=["k3", "k4"], writes=["k3"])
    s.op("vector", lambda e: e.tensor_tensor(out=K(6), in0=K(3), in1=K(2), op=ALU.mult), reads=["k3", "k2"], writes=["k6"])
    krb = bc_last(K(5).unsqueeze(2), 16); kib = bc_last(K(6).unsqueeze(2), 16)
    bre = Bq[:, 0]; bim = Bq[:, 1]
    tA = c.etmp[:, 0, :, 0:16]; tB = c.etmp[:, 1, :, 0:16]
    s.op("vector", lambda e: e.tensor_tensor(out=tA, in0=bre, in1=krb, op=ALU.mult), reads=["Bq", "k5"], writes=["t0"])
    s.op("vector", lambda e: e.tensor_tensor(out=tB, in0=bim, in1=kib, op=ALU.mult), reads=["Bq", "k6"], writes=["t1"])
    s.op("vector", lambda e: e.tensor_tensor(out=Btq[:, 0], in0=tA, in1=tB, op=ALU.subtract), reads=["t0", "t1"], writes=["Btq"])
    s.op("vector", lambda e: e.tensor_tensor(out=tA, in0=bre, in1=kib, op=ALU.mult), reads=["Bq", "k6", "Btq"], writes=["t0"])
    s.op("vector", lambda e: e.tensor_tensor(out=tB, in0=bim, in1=krb, op=ALU.mult), reads=["Bq", "k5", "Btq"], writes=["t1"])
    s.op("vector", lambda e: e.tensor_tensor(out=Btq[:, 1], in0=tA, in1=tB, op=ALU.add), reads=["t0", "t1", "Btq"], writes=["Btq"])
    s.op("vector", lambda e: e.memset(Zp, 0.0), writes=["Zp"])
    for pc in range(4):
        for ri in range(2):
            for g2 in range(2):
                for u in range(2):
                    for b in range(4):
                        a_ = 4 * u + b
                        s.op("vector", lambda e, g2=g2, ri=ri, a_=a_, b=b, pc=pc: e.tensor_copy(
                            out=Zp[g2 * 64:(g2 + 1) * 64, ri, a_, 32 * b + 16 * g2:32 * b + 16 * g2 + 16],
                            in_=Btq[g2 * 64:(g2 + 1) * 64, ri, pc * 8 + a_, :]),
                            reads=["Btq", "Zp"], writes=["Zp"])
        for ri in range(2):
            for hb in range(2):
                bank = (pc * 4 + ri * 2 + hb) % 4
                for i4 in range(4):
                    a_ = hb * 4 + i4
                    s.op("tensor", lambda e, ri=ri, a_=a_, bank=bank, i4=i4: e.transpose(c.ps[:, bank, i4 * 128:(i4 + 1) * 128], Zp[:, ri, a_, :], idf),
                         reads=["Zp", "idf"], writes=[("ps", bank)])
                p0 = pc * 8 + hb * 4
                s.op("scalar", lambda e, ri=ri, bank=bank, p0=p0: e.activation(out=c.BtT[:, ri, p0:p0 + 4, :].rearrange("r a q -> r (a q)"), in_=c.ps[:, bank, :], func=AF.Copy),
                     reads=[("ps", bank)], writes=[("BtT", pc)])
    s.op("vector", lambda e: e.memset(c.E[:, 0, :, 0:1], 1.0), writes=["E"])
    s.op("vector", lambda e: e.memset(c.E[:, 1, :, 0:1], 0.0), reads=["E"], writes=["E"])
    s.op("vector", lambda e: e.tensor_copy(out=c.Gp[:, 0, 0, :], in_=q(5)), reads=[QC], writes=["G"])
    s.op("vector", lambda e: e.tensor_copy(out=c.Gp[:, 0, 1, :], in_=q(4)), reads=[QS, "G"], writes=["G"])
    cur = emit_cpow(s, c.E, c.Gp, c.etmp, None, TCM, False)
    s.op("vector", lambda e, cur=cur: e.tensor_copy(out=c.G128, in_=c.Gp[:, cur]), reads=["G"], writes=["G128"])
    s.barrier()


def s5_main(c, full, yag_out=None, zero_init=False, pregelu=False):
    s = c.s
    ids = []
    rq = lambda p: c.qs[:, 3, p:p + 1]
    QC, QS = ("qsc", "c"), ("qsc", "s")
    if full and not zero_init:
        cq = c.qs[:, 5, :]; sq_ = c.qs[:, 4, :]
        a0 = c.qs[:, 9, :]; a1 = c.qs[:, 10, :]
        s.op("vector", lambda e: e.tensor_tensor(out=a0, in0=sq_, in1=c.xinit[:, 1, :], op=ALU.mult), reads=[QS, "xinit"], writes=["a0"])
        s.op("vector", lambda e: e.tensor_tensor(out=a1, in0=cq, in1=c.xinit[:, 0, :], op=ALU.mult), reads=[QC, "xinit"], writes=["a1"])
        s.op("vector", lambda e: e.tensor_tensor(out=c.ini[:, :, 0], in0=a1, in1=a0, op=ALU.subtract), reads=["a0", "a1"], writes=["ini_all"])
        s.op("vector", lambda e: e.tensor_tensor(out=a0, in0=sq_, in1=c.xinit[:, 0, :], op=ALU.mult), reads=[QS, "xinit", "ini_all"], writes=["a0"])
        s.op("vector", lambda e: e.tensor_tensor(out=a1, in0=cq, in1=c.xinit[:, 1, :], op=ALU.mult), reads=[QC, "xinit", "ini_all"], writes=["a1"])
        s.op("vector", lambda e: e.tensor_tensor(out=c.ini[:, :, 1], in0=a1, in1=a0, op=ALU.add), reads=["a0", "a1", "ini_all"], writes=["ini_all"])
    else:
        s.op("vector", lambda e: e.memset(c.ini[:], 0.0), writes=["ini_all"])
    gi = 0
    for t in range(NT):
        tsl = slice(t * 512, (t + 1) * 512)
        for cc in range(8):
            ybank = 4
            for pp in range(4):
                p = cc * 4 + pp
                sl = gi % 2; gi += 1
                b_re, b_im = 2 * sl, 2 * sl + 1
                for ri, bank in ((0, b_re), (1, b_im)):
                    s.op("tensor", lambda e, ri=ri, bank=bank, p=p, cc=cc, tsl=tsl: e.matmul(
                        c.ps[:, bank, :], c.BtT[:, ri, p, :], c.ua[:, cc, tsl], start=True, stop=True),
                        reads=[("BtT", p // 8), ("ua", cc)], writes=[("ps", bank)])
                Cb = c.E[:, 0, p, :].unsqueeze(1).to_broadcast([128, NJ, TCM])
                Sb = c.E[:, 1, p, :].unsqueeze(1).to_broadcast([128, NJ, TCM])
                v4 = lambda ap: ap.rearrange("p (a b) -> p a b", a=NJ)
                pre = v4(c.ps[:, b_re, :]); pim = v4(c.ps[:, b_im, :])
                zre = c.z[:, sl, 0, :]; zim = c.z[:, sl, 1, :]
                m0 = c.ps[:, 7, :]; m1 = c.mt[:, 1, :]
                Zr, Zi, M0, M1 = ("z", sl, 0), ("z", sl, 1), ("ps", 7), "m1"
                s.op("vector", lambda e, pre=pre, Cb=Cb, m0=m0: e.tensor_tensor(out=v4(m0), in0=pre, in1=Cb, op=ALU.mult), reads=[("ps", b_re), "E"], writes=[M0], size=512)
                s.op("vector", lambda e, pim=pim, Sb=Sb, m1=m1: e.tensor_tensor(out=v4(m1), in0=pim, in1=Sb, op=ALU.mult), reads=[("ps", b_im), "E"], writes=[M1], size=512)
                s.op("vector", lambda e, zre=zre, m0=m0, m1=m1: e.tensor_tensor(out=zre, in0=m0, in1=m1, op=ALU.add), reads=[M0, M1], writes=[Zr], size=512)
                s.op("vector", lambda e, pim=pim, Cb=Cb, m0=m0: e.tensor_tensor(out=v4(m0), in0=pim, in1=Cb, op=ALU.mult), reads=[("ps", b_im), "E", Zr], writes=[M0], size=512)
                s.op("vector", lambda e, pre=pre, Sb=Sb, m1=m1: e.tensor_tensor(out=v4(m1), in0=pre, in1=Sb, op=ALU.mult), reads=[("ps", b_re), "E", Zr], writes=[M1], size=512)
                s.op("vector", lambda e, zim=zim, m0=m0, m1=m1: e.tensor_tensor(out=zim, in0=m0, in1=m1, op=ALU.subtract), reads=[M0, M1], writes=[Zi], size=512)
                wre = c.ps[:, 5, :]; wim = c.ps[:, 6, :]
                Wr, Wi, INI = ("ps", 5), ("ps", 6), ("ini", p)
                rb = rq(p).to_broadcast([128, TCM])
                g_re = c.G128[:, 0, p:p + 1]; g_im = c.G128[:, 1, p:p + 1]
                for j in range(NJ):
                    js = slice(j * TCM, (j + 1) * TCM)
                    s.op("vector", lambda e, js=js, wre=wre, zre=zre, rb=rb, p=p: e.tensor_tensor_scan(
                        out=wre[:, js], data0=rb, data1=zre[:, js], initial=c.ini[:, p, 0:1], op0=ALU.mult, op1=ALU.add),
                        reads=[Zr, INI, "ini_all", "q3"], writes=[Wr], strict=True)
                    s.op("vector", lambda e, js=js, wim=wim, zim=zim, rb=rb, p=p: e.tensor_tensor_scan(
                        out=wim[:, js], data0=rb, data1=zim[:, js], initial=c.ini[:, p, 1:2], op0=ALU.mult, op1=ALU.add),
                        reads=[Zi, INI, "ini_all", "q3"], writes=[Wi], strict=True)
                    er = wre[:, j * TCM + TCM - 1:j * TCM + TCM]; ei = wim[:, j * TCM + TCM - 1:j * TCM + TCM]
                    s.op("vector", lambda e, ei=ei, g_im=g_im: e.tensor_scalar(out=c.itmp[:, 0:1], in0=ei, scalar1=g_im, scalar2=None, op0=ALU.mult),
                         reads=[Wi, "G128"], writes=["it0"])
                    s.op("vector", lambda e, ei=ei, g_re=g_re: e.tensor_scalar(out=c.itmp[:, 1:2], in0=ei, scalar1=g_re, scalar2=None, op0=ALU.mult),
                         reads=[Wi, "G128"], writes=["it1"])
                    s.op("vector", lambda e, er=er, g_re=g_re, p=p: e.scalar_tensor_tensor(out=c.ini[:, p, 0:1], in0=er, scalar=g_re, in1=c.itmp[:, 0:1],
                                                                                       op0=ALU.mult, op1=ALU.subtract),
                         reads=[Wr, "it0", "G128"], writes=[INI])
                    s.op("vector", lambda e, er=er, g_im=g_im, p=p: e.scalar_tensor_tensor(out=c.ini[:, p, 1:2], in0=er, scalar=g_im, in1=c.itmp[:, 1:2],
                                                                                       op0=ALU.mult, op1=ALU.add),
                         reads=[Wr, "it1", "G128", INI], writes=[INI])
                if full:
                    P1 = c.xs[:, sl, 0, :]; P2 = c.xs[:, sl, 1, :]; P3 = c.xs[:, sl, 2, :]; P4 = c.xs[:, sl, 3, :]
                    K1, K2, K3, K4 = ("xs", sl, 0), ("xs", sl, 1), ("xs", sl, 2), ("xs", sl, 3)
                    s.op("vector", lambda e, wre=wre, Cb=Cb, P1=P1: e.tensor_tensor(out=v4(P1), in0=v4(wre), in1=Cb, op=ALU.mult), reads=[Wr, "E"], writes=[K1])
                    s.op("vector", lambda e, wim=wim, Sb=Sb, P2=P2: e.scalar_tensor_tensor(out=v4(P2), in0=v4(wim), scalar=-1.0, in1=Sb, op0=ALU.mult, op1=ALU.mult), reads=[Wi, "E"], writes=[K2])
                    s.op("vector", lambda e, wre=wre, Sb=Sb, P3=P3: e.scalar_tensor_tensor(out=v4(P3), in0=v4(wre), scalar=-1.0, in1=Sb, op0=ALU.mult, op1=ALU.mult), reads=[Wr, "E"], writes=[K3])
                    s.op("vector", lambda e, wim=wim, Cb=Cb, P4=P4: e.scalar_tensor_tensor(out=v4(P4), in0=v4(wim), scalar=-1.0, in1=Cb, op0=ALU.mult, op1=ALU.mult), reads=[Wi, "E"], writes=[K4])
                    for qi_, (ri, Pq, Kq) in enumerate(((0, P1, K1), (0, P2, K2), (1, P3, K3), (1, P4, K4))):
                        s.op("tensor", lambda e, p=p, ri=ri, Pq=Pq, ybank=ybank, pp=pp, qi_=qi_: e.matmul(
                            c.ps[:, ybank, :], c.Cz[:, ri, p, :], Pq, start=(pp == 0 and qi_ == 0), stop=(pp == 3 and qi_ == 3)),
                            reads=[Kq, "Cz"], writes=[("ps", ybank)])
            if full:
                ysl = cc % 2
                s.op("vector", lambda e, cc=cc, tsl=tsl, ybank=ybank, ysl=ysl: e.scalar_tensor_tensor(
                    out=c.ypre[:, ysl, :], in0=c.ua[:, cc, tsl], scalar=c.dq[:, cc:cc + 1], in1=c.ps[:, ybank, :], op0=ALU.mult, op1=ALU.add),
                    reads=[("ua", cc), "dq", ("ps", ybank)], writes=[("ypre", ysl)])
                if pregelu:
                    ids.append(s.dma("sync", lambda e, cc=cc, tsl=tsl, ysl=ysl: e.dma_start(out=yag_out[cc * 128:(cc + 1) * 128, tsl], in_=c.ypre[:, ysl, :]),
                                     reads=[("ypre", ysl)], sem_key=("ypre", ysl)))
                else:
                    emit_gelu_src(c, c.ypre[:, ysl, :], [("ypre", ysl)], c.yst[:, ysl, :], c.ga[:, ysl, :], c.gb[:, ysl, :],
                                  writes=[("yst", ysl)], tmpkeys=(("ga", ysl), ("gb", ysl)))
                    ids.append(s.dma("sync", lambda e, cc=cc, tsl=tsl, ysl=ysl: e.dma_start(out=yag_out[cc * 128:(cc + 1) * 128, tsl], in_=c.yst[:, ysl, :]),
                                     reads=[("yst", ysl)], sem_key=("yst", ysl)))
    if (not full) or zero_init:
        cq = c.qs[:, 5, :]; sq_ = c.qs[:, 4, :]
        a0 = c.qs[:, 9, :]; a1 = c.qs[:, 10, :]
        allini = [("ini", p) for p in range(NP)] + ["ini_all"]
        s.op("vector", lambda e: e.tensor_tensor(out=a0, in0=sq_, in1=c.ini[:, :, 1], op=ALU.mult), reads=[QS] + allini, writes=["a0"])
        s.op("vector", lambda e: e.tensor_tensor(out=a1, in0=cq, in1=c.ini[:, :, 0], op=ALU.mult), reads=[QC] + allini, writes=["a1"])
        s.op("vector", lambda e: e.tensor_tensor(out=c.xend[:, 0, :], in0=a1, in1=a0, op=ALU.add), reads=["a0", "a1"], writes=["xend"])
        s.op("vector", lambda e: e.tensor_tensor(out=a0, in0=sq_, in1=c.ini[:, :, 0], op=ALU.mult), reads=[QS, "xend"] + allini, writes=["a0"])
        s.op("vector", lambda e: e.tensor_tensor(out=a1, in0=cq, in1=c.ini[:, :, 1], op=ALU.mult), reads=[QC, "xend"] + allini, writes=["a1"])
        s.op("vector", lambda e: e.tensor_tensor(out=c.xend[:, 1, :], in0=a1, in1=a0, op=ALU.subtract), reads=["a0", "a1", "xend"], writes=["xend"])
    return ids


def s5_combine(c, xall_ap, oneh_ap):
    s = c.s
    s.dma("sync", lambda e: e.dma_start(out=c.xall[:], in_=xall_ap), writes=["xall"], sem_key="xall")
    s.dma("sync", lambda e: e.dma_start(out=c.oneh[:], in_=oneh_ap), writes=["oneh"], sem_key="oneh")
    s.op("vector", lambda e: e.memset(c.X[:], 0.0), writes=["X"])
    s.op("vector", lambda e: e.memset(c.xinit[:], 0.0), writes=["xinit"])
    a0 = c.qs[:, 9, :]; a1 = c.qs[:, 10, :]
    cur = 0
    for cidx in range(1, 8):
        nxt = 1 - cur
        xr = c.X[:, cur, 0, :]; xi = c.X[:, cur, 1, :]
        nr = c.X[:, nxt, 0, :]; ni = c.X[:, nxt, 1, :]
        er = c.xall[:, cidx - 1, 0, :]; ei = c.xall[:, cidx - 1, 1, :]
        Ar = c.A[:, 0, :]; Ai = c.A[:, 1, :]
        s.op("vector", lambda e, xr=xr, Ar=Ar: e.tensor_tensor(out=a0, in0=xr, in1=Ar, op=ALU.mult), reads=["X", "A"], writes=["a0"])
        s.op("vector", lambda e, xi=xi, Ai=Ai: e.tensor_tensor(out=a1, in0=xi, in1=Ai, op=ALU.mult), reads=["X", "A"], writes=["a1"])
        s.op("vector", lambda e: e.tensor_tensor(out=a0, in0=a0, in1=a1, op=ALU.subtract), reads=["a0", "a1"], writes=["a0"])
        s.op("vector", lambda e, nr=nr, er=er: e.tensor_tensor(out=nr, in0=a0, in1=er, op=ALU.add), reads=["a0", "xall", "X"], writes=["X"])
        s.op("vector", lambda e, xr=xr, Ai=Ai: e.tensor_tensor(out=a0, in0=xr, in1=Ai, op=ALU.mult), reads=["X", "A"], writes=["a0"])
        s.op("vector", lambda e, xi=xi, Ar=Ar: e.tensor_tensor(out=a1, in0=xi, in1=Ar, op=ALU.mult), reads=["X", "A"], writes=["a1"])
        s.op("vector", lambda e: e.tensor_tensor(out=a0, in0=a0, in1=a1, op=ALU.add), reads=["a0", "a1"], writes=["a0"])
        s.op("vector", lambda e, ni=ni, ei=ei: e.tensor_tensor(out=ni, in0=a0, in1=ei, op=ALU.add), reads=["a0", "xall", "X"], writes=["X"])
        for ri, src in ((0, nr), (1, ni)):
            s.op("vector", lambda e, ri=ri, src=src, cidx=cidx: e.scalar_tensor_tensor(
                out=c.xinit[:, ri, :], in0=src, scalar=c.oneh[:, cidx:cidx + 1], in1=c.xinit[:, ri, :], op0=ALU.mult, op1=ALU.add),
                reads=["X", "oneh", "xinit"], writes=["xinit"])
        cur = nxt


def wload_half(c, view_shape, src_ap, half):
    raise NotImplementedError


def emit_glu_sgu(c, V, projUV, w_glu, b_glu, ln_g, ln_b, wsT, b_s, ident):
    s = c.s
    T = V
    yag = c.yag
    ug = T("ug", [128, 8, TOK], BF16)
    vg = T("vg", [128, 8, TOK], BF16)
    vn = T("vn", [128, 8, TOK], BF16)
    sq = T("sq2", [128, 2, 512], BF16)
    ones = T("ones2", [128, 128], BF16)
    idb = T("idb", [128, 128], BF16)
    par = T("par", [128, 3, 8])
    epsb = T("epsb2", [128, 1])
    mean = T("mean", [128, 512]); msq = T("msq", [128, 512]); rstd = T("rstd2", [128, 512]); t1 = T("t1", [128, 2, 512])
    sig = T("sig", [128, 2, 512])
    wsb = T("wsb", [128, 8, 128], BF16)
    bsf = T("bsf", [1, 1024], F32, 1); bsh = T("bsh", [1, 1024], BF16, 1); bsl = T("bsl", [1, 1024], BF16, 1); bst = T("bst", [1, 1024], F32, 1)
    vT = T("vT", [128, 2, 128], BF16)
    ps = c.ps
    psT = c.ps[:, 7, 0:128].bitcast(BF16).rearrange("p (a b) -> p a b", a=2)
    ids = []
    s.op("gpsimd", lambda e: e.memset(ones[:], 1.0), writes=["ones"])
    s.op("gpsimd", lambda e: e.memset(epsb[:], EPS), writes=["epsb"])
    for i, ap in enumerate((b_glu, ln_g, ln_b)):
        s.dma("sync", lambda e, i=i, ap=ap: e.dma_start(out=par[:, i, :], in_=ap.rearrange("(k p) -> p k", p=128), allow_slow_non_contiguous=True),
              writes=[("par", i)], sem_key=("par", i))
    s.dma("gpsimd", lambda e: e.dma_start(out=idb[:], in_=ident), writes=["idb"], sem_key="idb")
    s.dma("gpsimd", lambda e: e.dma_start(out=wsb[:], in_=wsT.rearrange("h s t -> s h t")), writes=["wsb"], sem_key="wsb")
    s.op("vector", lambda e: e.memset(wsb[64:128, :, 0:64], 0.0), reads=["wsb"], writes=["wsb"])
    s.dma("sync", lambda e: e.dma_start(out=bsf[:], in_=b_s.rearrange("(o h) t -> o (h t)", o=1)), writes=["bsf"], sem_key="bsf")
    s.op("vector", lambda e: e.tensor_copy(out=bsh[:], in_=bsf[:]), reads=["bsf"], writes=["bsh"])
    s.op("vector", lambda e: e.tensor_copy(out=bst[:], in_=bsh[:]), reads=["bsh"], writes=["bst"])
    s.op("vector", lambda e: e.tensor_tensor(out=bsl[:], in0=bsf[:], in1=bst[:], op=ALU.subtract), reads=["bsf", "bst"], writes=["bsl"])
    for cc in range(8):
        s.dma("gpsimd", lambda e, cc=cc: e.dma_start(out=ug[:, cc, :], in_=projUV[cc * 128:(cc + 1) * 128, :]), writes=[("ug", cc)], sem_key=("ug", cc))
        s.dma("gpsimd", lambda e, cc=cc: e.dma_start(out=vg[:, cc, :], in_=projUV[1024 + cc * 128:1024 + (cc + 1) * 128, :]), writes=[("vg", cc)], sem_key=("vg", cc))
    wv = w_glu.rearrange("(k p) f -> p k f", p=128)
    gi = 0
    for u in range(2):
        ws, wview = wload(c, [128, 8, 512], wv[:, :, u * 512:(u + 1) * 512], "glu")
        for m in range(4):
            mc = u * 4 + m
            for t in range(NT):
                tsl = slice(t * 512, (t + 1) * 512)
                bank = gi % 4; sl = gi % 2; gi += 1
                for k in range(8):
                    s.op("tensor", lambda e, wview=wview, k=k, m=m, tsl=tsl, bank=bank: e.matmul(
                        ps[:, bank, :], wview[:, k, m * 128:(m + 1) * 128], yag[:, k, tsl], start=(k == 0), stop=(k == 7)),
                        reads=[("w", ws), ("yag", k)], writes=[("ps", bank)])
                s.op("scalar", lambda e, bank=bank, sl=sl, mc=mc: e.activation(out=sig[:, sl, :], in_=ps[:, bank, :], func=AF.Sigmoid,
                                                                            bias=par[:, 0, mc:mc + 1], scale=1.0),
                     reads=[("ps", bank), ("par", 0)], writes=[("sig", sl)])
                s.op("vector", lambda e, sl=sl, mc=mc, tsl=tsl: e.tensor_tensor(out=c.ya[:, mc, tsl], in0=yag[:, mc, tsl], in1=sig[:, sl, :], op=ALU.mult),
                     reads=[("yag", mc), ("sig", sl)], writes=[("ya", mc)])
    sqs = 0
    for t in range(NT):
        tsl = slice(t * 512, (t + 1) * 512)
        for cc in range(8):
            s.op("tensor", lambda e, cc=cc, tsl=tsl: e.matmul(ps[:, 4, :], ones[:], vg[:, cc, tsl], start=(cc == 0), stop=(cc == 7)),
                 reads=["ones", ("vg", cc)], writes=[("ps", 4)])
        for cc in range(8):
            sl = sqs; sqs ^= 1
            s.op("scalar", lambda e, cc=cc, tsl=tsl, sl=sl: e.activation(out=sq[:, sl, :], in_=vg[:, cc, tsl], func=AF.Square),
                 reads=[("vg", cc)], writes=[("sq", sl)])
            s.op("tensor", lambda e, cc=cc, sl=sl: e.matmul(ps[:, 5, :], ones[:], sq[:, sl, :], start=(cc == 0), stop=(cc == 7)),
                 reads=["ones", ("sq", sl)], writes=[("ps", 5)])
        s.op("scalar", lambda e: e.activation(out=mean[:], in_=ps[:, 4, :], func=AF.Copy, scale=1.0 / 1024), reads=[("ps", 4)], writes=["mean"])
        s.op("vector", lambda e: e.tensor_tensor(out=msq[:], in0=mean[:], in1=mean[:], op=ALU.mult), reads=["mean"], writes=["msq"])
        s.op("vector", lambda e: e.scalar_tensor_tensor(out=msq[:], in0=ps[:, 5, :], scalar=1.0 / 1024, in1=msq[:], op0=ALU.mult, op1=ALU.subtract),
             reads=[("ps", 5), "msq"], writes=["msq"])
        s.op("scalar", lambda e: e.activation(out=rstd[:], in_=msq[:], func=AF.Sqrt, bias=epsb[:, 0:1], scale=1.0), reads=["msq", "epsb"], writes=["rstd"])
        s.op("vector", lambda e: e.reciprocal(out=rstd[:], in_=rstd[:]), reads=["rstd"], writes=["rstd"])
        for cc in range(8):
            sl = cc % 2
            s.op("vector", lambda e, cc=cc, tsl=tsl, sl=sl: e.tensor_tensor(out=t1[:, sl, :], in0=vg[:, cc, tsl], in1=mean[:], op=ALU.subtract),
                 reads=[("vg", cc), "mean"], writes=[("t1", sl)])
            s.op("vector", lambda e, sl=sl: e.tensor_tensor(out=t1[:, sl, :], in0=t1[:, sl, :], in1=rstd[:], op=ALU.mult),
                 reads=[("t1", sl), "rstd"], writes=[("t1", sl)])
            s.op("vector", lambda e, cc=cc, tsl=tsl, sl=sl: e.tensor_scalar(out=vn[:, cc, tsl], in0=t1[:, sl, :], scalar1=par[:, 1, cc:cc + 1],
                                                                           scalar2=par[:, 2, cc:cc + 1], op0=ALU.mult, op1=ALU.add),
                 reads=[("t1", sl), ("par", 1), ("par", 2)], writes=[("vn", cc, t)])
        for hh in range(8):
            bank = 4 + 2 + (hh % 1)
            bank = 6
            for j in range(4):
                tok = slice(t * 512 + j * TC, t * 512 + (j + 1) * TC)
                vs = (hh * 4 + j) % 2
                s.op("tensor", lambda e, hh=hh, tok=tok, vs=vs: e.transpose(psT[:, vs, :], vn[:, hh, tok], idb[:]),
                     reads=[("vn", hh, t), "idb"], writes=[("psT", vs)])
                s.op("scalar", lambda e, vs=vs: e.activation(out=vT[:, vs, :], in_=psT[:, vs, :], func=AF.Copy), reads=[("psT", vs)], writes=[("vT", vs)])
                osl = slice(j * TC, (j + 1) * TC)
                s.op("tensor", lambda e, hh=hh, vs=vs, osl=osl: e.matmul(ps[:, 6, osl], vT[:, vs, :], wsb[:, hh, :], start=True, stop=False, skip_group_check=True),
                     reads=[("vT", vs), "wsb"], writes=[("ps", 6)])
                s.op("tensor", lambda e, hh=hh, osl=osl: e.matmul(ps[:, 6, osl], ones[0:1, :], bsh[0:1, hh * 128:(hh + 1) * 128], start=False, stop=False, skip_group_check=True),
                     reads=["ones", "bsh"], writes=[("ps", 6)])
                s.op("tensor", lambda e, hh=hh, osl=osl: e.matmul(ps[:, 6, osl], ones[0:1, :], bsl[0:1, hh * 128:(hh + 1) * 128], start=False, stop=True, skip_group_check=True),
                     reads=["ones", "bsl"], writes=[("ps", 6)])
            s.op("vector", lambda e, hh=hh, tsl=tsl: e.tensor_tensor(out=c.yb[:, hh, tsl], in0=ps[:, 6, :], in1=ug[:, hh, tsl], op=ALU.mult),
                 reads=[("ps", 6), ("ug", hh)], writes=[("yb", hh)])
    return ids


def emit_merge(c, V, x1T, g_mix, w_a, w_b, w_gate, b_gate, w_out):
    s = c.s
    T = V
    hm = T("hmix", [128, KC, TOK], BF16)
    ya = c.ya; yb = c.yb
    mg = T("mg", [128, KC, TOK], BF16)
    xst = T("xst", [128, 4, 512]); sq = T("sq3", [128, 2, 512], BF16)
    ones = T("ones3", [128, 128], BF16); epsb = T("epsb3", [128, 1]); rstd = T("rstd3", [128, TOK])
    gain = T("gain3", [128, KC]); bg = T("bg", [128, 2 * KC])
    sga = T("sga", [128, 2, 512]); tmp = T("tmp3", [128, 2, 512])
    ps = c.ps
    ids = []
    s.op("gpsimd", lambda e: e.memset(ones[:], 1.0), writes=["ones"])
    s.op("gpsimd", lambda e: e.memset(epsb[:], EPS), writes=["epsb"])
    s.dma("sync", lambda e: e.dma_start(out=gain[:], in_=g_mix.rearrange("(k p) -> p k", p=128), allow_slow_non_contiguous=True), writes=["gain"], sem_key="gain")
    s.dma("sync", lambda e: e.dma_start(out=bg[:], in_=b_gate.rearrange("(k p) -> p k", p=128), allow_slow_non_contiguous=True), writes=["bg"], sem_key="bg")
    xv = x1T.rearrange("(k p) t -> p k t", p=128)
    xs = 0
    for t in range(NT):
        tsl = slice(t * 512, (t + 1) * 512)
        for k in range(KC):
            sl = xs % 4; xs += 1
            s.dma("sync", lambda e, k=k, tsl=tsl, sl=sl: e.dma_start(out=xst[:, sl, :], in_=xv[:, k, tsl]), writes=[("xst", sl)], sem_key=("xst", sl))
            s.op("scalar", lambda e, sl=sl: e.activation(out=sq[:, sl % 2, :], in_=xst[:, sl, :], func=AF.Square), reads=[("xst", sl)], writes=[("sq", sl % 2)])
            s.op("tensor", lambda e, sl=sl, k=k: e.matmul(ps[:, 7, :], ones[:], sq[:, sl % 2, :], start=(k == 0), stop=(k == KC - 1)),
                 reads=["ones", ("sq", sl % 2)], writes=[("ps", 7)])
        s.op("scalar", lambda e, tsl=tsl: e.activation(out=rstd[:, tsl], in_=ps[:, 7, :], func=AF.Sqrt, bias=epsb[:, 0:1], scale=1.0 / D),
             reads=[("ps", 7), "epsb"], writes=[("rstd", t)])
        s.op("vector", lambda e, tsl=tsl: e.reciprocal(out=rstd[:, tsl], in_=rstd[:, tsl]), reads=[("rstd", t)], writes=[("rstd", t)])
        for k in range(KC):
            sl = xs % 4; xs += 1
            s.dma("sync", lambda e, k=k, tsl=tsl, sl=sl: e.dma_start(out=xst[:, sl, :], in_=xv[:, k, tsl]), writes=[("xst", sl)], sem_key=("xst", sl))
            s.op("vector", lambda e, k=k, tsl=tsl, sl=sl: e.scalar_tensor_tensor(out=hm[:, k, tsl], in0=xst[:, sl, :], scalar=gain[:, k:k + 1], in1=rstd[:, tsl],
                                                                                 op0=ALU.mult, op1=ALU.mult),
                 reads=[("xst", sl), "gain", ("rstd", t)], writes=[("hm", k, t)])
    wa_v = w_a.rearrange("(k p) f -> p k f", p=128); wb_v = w_b.rearrange("(k p) f -> p k f", p=128)
    wg_v = w_gate.rearrange("(k p) f -> p k f", p=128)
    gi = 0
    for n4 in range(4):
        cs = slice(n4 * 512, (n4 + 1) * 512)
        sa, va = wload(c, [128, 8, 512], wa_v[:, :, cs], "a")
        sga_s, vga = wload(c, [128, KC, 512], wg_v[:, :, cs], "ga")
        sb, vb = wload(c, [128, 8, 512], wb_v[:, :, cs], "b")
        sgb_s, vgb = wload(c, [128, KC, 512], wg_v[:, :, D + n4 * 512:D + (n4 + 1) * 512], "gb")
        for m in range(4):
            n = n4 * 4 + m
            msl = slice(m * 128, (m + 1) * 128)
            for t in range(NT):
                tsl = slice(t * 512, (t + 1) * 512)
                pb = (gi % 2) * 4; sl = gi % 2; gi += 1
                for (bank, wsl, wvw, src, nk, skey) in ((pb, sa, va, ya, 8, "ya"), (pb + 1, sga_s, vga, hm, KC, "hm"), (pb + 2, sb, vb, yb, 8, "yb"), (pb + 3, sgb_s, vgb, hm, KC, "hm")):
                    for k in range(nk):
                        rk = (skey, k, t) if skey == "hm" else (skey, k)
                        s.op("tensor", lambda e, bank=bank, wvw=wvw, src=src, k=k, nk=nk, msl=msl, tsl=tsl: e.matmul(
                            ps[:, bank, :], wvw[:, k, msl], src[:, k, tsl], start=(k == 0), stop=(k == nk - 1)),
                            reads=[("w", wsl), rk], writes=[("ps", bank)])
                s.op("scalar", lambda e, pb=pb, sl=sl, n=n: e.activation(out=sga[:, sl, :], in_=ps[:, pb + 1, :], func=AF.Sigmoid, bias=bg[:, n:n + 1], scale=1.0),
                     reads=[("ps", pb + 1), "bg"], writes=[("sga", sl)])
                s.op("vector", lambda e, pb=pb, sl=sl: e.tensor_tensor(out=tmp[:, sl, :], in0=ps[:, pb, :], in1=sga[:, sl, :], op=ALU.mult),
                     reads=[("ps", pb), ("sga", sl)], writes=[("tmp", sl)])
                s.op("scalar", lambda e, pb=pb, sl=sl, n=n: e.activation(out=sga[:, sl, :], in_=ps[:, pb + 3, :], func=AF.Sigmoid, bias=bg[:, KC + n:KC + n + 1], scale=1.0),
                     reads=[("ps", pb + 3), "bg", ("tmp", sl)], writes=[("sga", sl)])
                s.op("vector", lambda e, pb=pb, sl=sl: e.tensor_tensor(out=sga[:, sl, :], in0=ps[:, pb + 2, :], in1=sga[:, sl, :], op=ALU.mult),
                     reads=[("ps", pb + 2), ("sga", sl)], writes=[("sga", sl)])
                s.op("vector", lambda e, sl=sl, n=n, tsl=tsl: e.tensor_tensor(out=mg[:, n, tsl], in0=tmp[:, sl, :], in1=sga[:, sl, :], op=ALU.add),
                     reads=[("tmp", sl), ("sga", sl)], writes=[("mg", n, t)])
    s.barrier()
    wo_v = w_out.rearrange("(k p) f -> p k f", p=128)
    gi = 0
    for n4 in range(4):
        so, vo = wload(c, [128, KC, 512], wo_v[:, :, n4 * 512:(n4 + 1) * 512], "o")
        for m in range(4):
            n = n4 * 4 + m
            msl = slice(m * 128, (m + 1) * 128)
            for t in range(NT):
                tsl = slice(t * 512, (t + 1) * 512)
                bank = gi % 4; sl = gi % 2; xsl = gi % 4; gi += 1
                for k in range(KC):
                    s.op("tensor", lambda e, bank=bank, vo=vo, k=k, msl=msl, tsl=tsl: e.matmul(ps[:, bank, :], vo[:, k, msl], mg[:, k, tsl], start=(k == 0), stop=(k == KC - 1)),
                         reads=[("w", so), ("mg", k, t)], writes=[("ps", bank)])
                s.dma("sync", lambda e, n=n, tsl=tsl, xsl=xsl: e.dma_start(out=xst[:, xsl, :], in_=xv[:, n, tsl]), writes=[("xst", xsl)], sem_key=("xst", xsl))
                s.op("vector", lambda e, bank=bank, n=n, tsl=tsl, xsl=xsl: e.tensor_tensor(out=c.x[:, n, tsl], in0=ps[:, bank, :], in1=xst[:, xsl, :], op=ALU.add),
                     reads=[("ps", bank), ("xst", xsl)], writes=[("x", n, t)])
    return ids


def emit_final_norm(c, gidx, outT, stage):
    s = c.s
    ids = []
    gi = 0
    for t in range(NT):
        tsl = slice(t * 512, (t + 1) * 512)
        bank = 7
        for k in range(KC):
            sl = c.sqslot; c.sqslot ^= 1
            s.op("scalar", lambda e, k=k, sl=sl, tsl=tsl: e.activation(out=c.sq[:, sl, :], in_=c.x[:, k, tsl], func=AF.Square),
                 reads=[("x", k, t)], writes=[("sq", sl)])
            s.op("tensor", lambda e, k=k, sl=sl, bank=bank: e.matmul(c.ps[:, bank, :], c.ones[:], c.sq[:, sl, :], start=(k == 0), stop=(k == KC - 1)),
                 reads=[("sq", sl), ("ones",)], writes=[("ps", bank)])
        s.op("scalar", lambda e, tsl=tsl, bank=bank: e.activation(out=c.rstd[:, tsl], in_=c.ps[:, bank, :], func=AF.Sqrt, bias=c.epsb[:, 0:1], scale=1.0 / D),
             reads=[("ps", bank), ("epsb",)], writes=[("rstd", t)])
        s.op("vector", lambda e, tsl=tsl: e.reciprocal(out=c.rstd[:, tsl], in_=c.rstd[:, tsl]), reads=[("rstd", t)], writes=[("rstd", t)])
        for k in range(KC):
            sl = gi % 2; gi += 1
            s.op("vector", lambda e, k=k, tsl=tsl, sl=sl: e.scalar_tensor_tensor(out=stage[:, sl, :], in0=c.x[:, k, tsl], scalar=c.gains[:, gidx, k:k + 1],
                                                                                 in1=c.rstd[:, tsl], op0=ALU.mult, op1=ALU.mult),
                 reads=[("x", k, t), ("gain", gidx), ("rstd", t)], writes=[("fstage", sl)])
            ids.append(s.dma("sync", lambda e, k=k, tsl=tsl, sl=sl: e.dma_start(out=outT[k * 128:(k + 1) * 128, tsl], in_=stage[:, sl, :]),
                             reads=[("fstage", sl)], sem_key=("fstage", sl)))
    return ids


def emit_cpow(s, E, Gp, etmp, base_keys, k_end, first_is_base):
    t0 = etmp[:, 0]; t1 = etmp[:, 1]
    cur = 0; k = 1
    while k < k_end:
        gr = Gp[:, cur, 0, :]; gi = Gp[:, cur, 1, :]
        grb = bc_last(gr.unsqueeze(2), k); gib = bc_last(gi.unsqueeze(2), k)

        def mk(k=k, grb=grb, gib=gib):
            s.op("vector", lambda e: e.tensor_tensor(out=t0[:, :, 0:k], in0=E[:, 1, :, 0:k], in1=gib, op=ALU.mult), reads=["E", "G"], writes=["t0"])
            s.op("vector", lambda e: e.tensor_tensor(out=t1[:, :, 0:k], in0=E[:, 0, :, 0:k], in1=grb, op=ALU.mult), reads=["E", "G"], writes=["t1"])
            s.op("vector", lambda e: e.tensor_tensor(out=E[:, 0, :, k:2 * k], in0=t1[:, :, 0:k], in1=t0[:, :, 0:k], op=ALU.subtract), reads=["t0", "t1", "E"], writes=["E"])
            s.op("vector", lambda e: e.tensor_tensor(out=t0[:, :, 0:k], in0=E[:, 0, :, 0:k], in1=gib, op=ALU.mult), reads=["E", "G"], writes=["t0"])
            s.op("vector", lambda e: e.tensor_tensor(out=t1[:, :, 0:k], in0=E[:, 1, :, 0:k], in1=grb, op=ALU.mult), reads=["E", "G"], writes=["t1"])
            s.op("vector", lambda e: e.tensor_tensor(out=E[:, 1, :, k:2 * k], in0=t1[:, :, 0:k], in1=t0[:, :, 0:k], op=ALU.add), reads=["t0", "t1", "E"], writes=["E"])
        mk()
        nxt = 1 - cur
        sr, si = gr, gi
        dr, di = Gp[:, nxt, 0, :], Gp[:, nxt, 1, :]

        def sqr(sr=sr, si=si, dr=dr, di=di):
            s.op("vector", lambda e: e.tensor_tensor(out=t0[:, :, 0], in0=si, in1=si, op=ALU.mult), reads=["G"], writes=["t0"])
            s.op("vector", lambda e: e.tensor_tensor(out=t1[:, :, 0], in0=sr, in1=sr, op=ALU.mult), reads=["G"], writes=["t1"])
            s.op("vector", lambda e: e.scalar_tensor_tensor(out=di, in0=sr, scalar=2.0, in1=si, op0=ALU.mult, op1=ALU.mult), reads=["G"], writes=["G"])
            s.op("vector", lambda e: e.tensor_tensor(out=dr, in0=t1[:, :, 0], in1=t0[:, :, 0], op=ALU.subtract), reads=["t0", "t1", "G"], writes=["G"])
        sqr()
        cur = nxt; k *= 2
    return cur


def emit_s5_correct(c, V, aq_ap, Cz_ap, xall_ap, oneh_ap, ypreT):
    s = c.s
    T = V
    c.aq = T("aq", [128, 3, NP]); c.qs = T("qs", [128, 12, NP]); c.qi = T("qi", [128, NP], I32)
    P = T("P", [128, 2, NP, TC])
    Pb = T("Pb", [128, 2, NP, TC], BF16)
    Gp = T("Gp", [128, 2, 2, NP]); etmp = T("etmp", [128, 2, NP, 64])
    c.A = T("A", [128, 2, NP]); c.X = T("X", [128, 2, 2, NP]); c.xinit = T("xinit", [128, 2, NP])
    c.xall = T("xall", [128, 8, 2, NP]); c.oneh = T("oneh", [128, 8])
    Czf = T("Czf", [128, 2, NP, 32])
    V_ = T("V", [128, 2, 2, NP])
    Wt = T("Wt", [128, 2, 2, NP, 128], BF16)
    wa = T("wa", [128, NP, 32]); wb = T("wb", [128, NP, 32])
    yl = T("yl", [128, 2, 512]); ga = T("ga", [128, 2, 512]); gb = T("gb", [128, 2, 512])
    G128 = T("G128", [128, 2, NP])
    ps = c.ps
    ids = []
    q = lambda i: c.qs[:, i, :]
    s.dma("sync", lambda e: e.dma_start(out=c.aq[:], in_=aq_ap), writes=["aq"], sem_key="aq")
    Cz4 = Cz_ap.rearrange("q r (a b) c -> q r a b c", b=4)
    CZ = [("Czf", b) for b in range(4)]
    s.op("scalar", lambda e: e.activation(out=q(0), in_=c.aq[:, 2, :], func=AF.Exp), reads=["aq"], writes=["q0"])
    s.op("vector", lambda e: e.tensor_tensor(out=q(1), in0=c.aq[:, 0, :], in1=q(0), op=ALU.mult), reads=["aq", "q0"], writes=["q1"])
    s.op("vector", lambda e: e.scalar_tensor_tensor(out=q(2), in0=c.aq[:, 1, :], scalar=INV_2PI, in1=q(0), op0=ALU.mult, op1=ALU.mult),
         reads=["aq", "q0"], writes=["q2"])
    s.op("scalar", lambda e: e.activation(out=q(3), in_=q(1), func=AF.Exp), reads=["q1"], writes=["q3"])
    emit_sincos(c, q(2), q(4), q(5), {"i32": c.qi[:], "a": q(6), "b": q(7)}, "qsc", reads=["q2"])
    s.op("vector", lambda e: e.tensor_tensor(out=Gp[:, 0, 0, :], in0=q(5), in1=q(3), op=ALU.mult), reads=[("qsc", "c"), "q3"], writes=["G"])
    s.op("vector", lambda e: e.tensor_tensor(out=Gp[:, 0, 1, :], in0=q(4), in1=q(3), op=ALU.mult), reads=[("qsc", "s"), "q3", "G"], writes=["G"])
    s.op("vector", lambda e: e.tensor_copy(out=P[:, 0, :, 0], in_=Gp[:, 0, 0, :]), reads=["G"], writes=["E"])
    s.op("vector", lambda e: e.tensor_copy(out=P[:, 1, :, 0], in_=Gp[:, 0, 1, :]), reads=["G", "E"], writes=["E"])
    cur = emit_cpow(s, P, Gp, etmp, None, TC, True)
    s.op("vector", lambda e, cur=cur: e.tensor_copy(out=G128[:], in_=Gp[:, cur]), reads=["G"], writes=["G128"])
    s.op("vector", lambda e: e.tensor_copy(out=Pb[:], in_=P[:]), reads=["E"], writes=["Pb"])
    t0 = etmp[:, 0]; t1 = etmp[:, 1]
    for _ in range(3):
        nxt = 1 - cur
        sr, si = Gp[:, cur, 0, :], Gp[:, cur, 1, :]
        dr, di = Gp[:, nxt, 0, :], Gp[:, nxt, 1, :]
        s.op("vector", lambda e, si=si: e.tensor_tensor(out=t0[:, :, 0], in0=si, in1=si, op=ALU.mult), reads=["G"], writes=["t0"])
        s.op("vector", lambda e, sr=sr: e.tensor_tensor(out=t1[:, :, 0], in0=sr, in1=sr, op=ALU.mult), reads=["G"], writes=["t1"])
        s.op("vector", lambda e, sr=sr, si=si, di=di: e.scalar_tensor_tensor(out=di, in0=sr, scalar=2.0, in1=si, op0=ALU.mult, op1=ALU.mult), reads=["G"], writes=["G"])
        s.op("vector", lambda e, dr=dr: e.tensor_tensor(out=dr, in0=t1[:, :, 0], in1=t0[:, :, 0], op=ALU.subtract), reads=["t0", "t1", "G"], writes=["G"])
        cur = nxt
    s.op("vector", lambda e, cur=cur: e.tensor_copy(out=c.A[:], in_=Gp[:, cur]), reads=["G"], writes=["A"])
    s5_combine(c, xall_ap, oneh_ap)
    s.op("vector", lambda e: e.tensor_copy(out=V_[:, 0], in_=c.xinit[:]), reads=["xinit"], writes=["V"])
    s.barrier()
    for b in range(4):
        s.dma("sync", lambda e, b=b: e.dma_start(out=Czf.rearrange("q r (a b) c -> q r a b c", b=4)[:, :, :, b, :],
                                                 in_=Cz4[:, :, :, b, 32 * b:32 * b + 32]), writes=[("Czf", b)], sem_key=("Czf", b))
    s.op("gpsimd", lambda e: e.memset(Wt[:], 0.0), writes=[("Wt", 0), ("Wt", 1)])
    vcur = 0
    gi = 0
    for j in range(8):
        ws_ = j % 2
        vr = bc_last(V_[:, vcur, 0, :].unsqueeze(2), 32); vi = bc_last(V_[:, vcur, 1, :].unsqueeze(2), 32)
        cre = Czf[:, 0]; cim = Czf[:, 1]
        def blk(ri, ws_=ws_):
            return [Wt[:, ws_, ri].rearrange("q (a b) c -> q a b c", b=4)[:, :, b, 32 * b:32 * b + 32] for b in range(4)]
        s.op("vector", lambda e, vr=vr: e.tensor_tensor(out=wa[:], in0=cre, in1=vr, op=ALU.mult), reads=CZ + ["V"], writes=["wa"])
        s.op("vector", lambda e, vi=vi: e.tensor_tensor(out=wb[:], in0=cim, in1=vi, op=ALU.mult), reads=CZ + ["V"], writes=["wb"])
        s.op("vector", lambda e: e.tensor_tensor(out=wa[:], in0=wa[:], in1=wb[:], op=ALU.subtract), reads=["wa", "wb"], writes=["wa"])
        for b, dst in enumerate(blk(0)):
            s.op("vector", lambda e, b=b, dst=dst: e.tensor_copy(out=dst, in_=wa[:].rearrange("q (a b) c -> q a b c", b=4)[:, :, b, :]),
                 reads=["wa"], writes=[("Wt", ws_)])
        s.op("vector", lambda e, vi=vi: e.tensor_tensor(out=wa[:], in0=cre, in1=vi, op=ALU.mult), reads=CZ + ["V", ("Wt", ws_)], writes=["wa"])
        s.op("vector", lambda e, vr=vr: e.tensor_tensor(out=wb[:], in0=cim, in1=vr, op=ALU.mult), reads=CZ + ["V"], writes=["wb"])
        s.op("vector", lambda e: e.scalar_tensor_tensor(out=wa[:], in0=wa[:], scalar=-1.0, in1=wb[:], op0=ALU.mult, op1=ALU.subtract), reads=["wa", "wb"], writes=["wa"])
        for b, dst in enumerate(blk(1)):
            s.op("vector", lambda e, b=b, dst=dst: e.tensor_copy(out=dst, in_=wa[:].rearrange("q (a b) c -> q a b c", b=4)[:, :, b, :]),
                 reads=["wa"], writes=[("Wt", ws_)])
        osl = slice((j % 4) * TC, (j % 4 + 1) * TC)
        for cc in range(8):
            for pp in range(4):
                p = cc * 4 + pp
                s.op("tensor", lambda e, cc=cc, p=p, pp=pp, ws_=ws_, osl=osl: e.matmul(ps[:, cc, osl], Wt[:, ws_, 0, p, :], Pb[:, 0, p, :],
                                                                                    start=(pp == 0), stop=False, skip_group_check=True),
                     reads=[("Wt", ws_), "Pb"], writes=[("ps", cc)])
                s.op("tensor", lambda e, cc=cc, p=p, pp=pp, ws_=ws_, osl=osl: e.matmul(ps[:, cc, osl], Wt[:, ws_, 1, p, :], Pb[:, 1, p, :],
                                                                                    start=False, stop=(pp == 3), skip_group_check=True),
                     reads=[("Wt", ws_), "Pb"], writes=[("ps", cc)])
        nv = 1 - vcur
        a0 = c.qs[:, 9, :]; a1 = c.qs[:, 10, :]
        s.op("vector", lambda e, vcur=vcur: e.tensor_tensor(out=a0, in0=V_[:, vcur, 0, :], in1=G128[:, 0, :], op=ALU.mult), reads=["V", "G128"], writes=["a0"])
        s.op("vector", lambda e, vcur=vcur: e.tensor_tensor(out=a1, in0=V_[:, vcur, 1, :], in1=G128[:, 1, :], op=ALU.mult), reads=["V", "G128"], writes=["a1"])
        s.op("vector", lambda e, nv=nv: e.tensor_tensor(out=V_[:, nv, 0, :], in0=a0, in1=a1, op=ALU.subtract), reads=["a0", "a1", "V"], writes=["V"])
        s.op("vector", lambda e, vcur=vcur: e.tensor_tensor(out=a0, in0=V_[:, vcur, 0, :], in1=G128[:, 1, :], op=ALU.mult), reads=["V", "G128"], writes=["a0"])
        s.op("vector", lambda e, vcur=vcur: e.tensor_tensor(out=a1, in0=V_[:, vcur, 1, :], in1=G128[:, 0, :], op=ALU.mult), reads=["V", "G128"], writes=["a1"])
        s.op("vector", lambda e, nv=nv: e.tensor_tensor(out=V_[:, nv, 1, :], in0=a0, in1=a1, op=ALU.add), reads=["a0", "a1", "V"], writes=["V"])
        vcur = nv
        if j % 4 == 3:
            t = j // 4
            tsl = slice(t * 512, (t + 1) * 512)
            for cc in range(8):
                sl = gi % 2; gi += 1
                s.dma("sync", lambda e, cc=cc, tsl=tsl, sl=sl: e.dma_start(out=yl[:, sl, :], in_=ypreT[cc * 128:(cc + 1) * 128, tsl]), writes=[("yl", sl)], sem_key=("yl", sl))
                s.op("vector", lambda e, cc=cc, sl=sl: e.tensor_tensor(out=yl[:, sl, :], in0=ps[:, cc, :], in1=yl[:, sl, :], op=ALU.add),
                     reads=[("ps", cc), ("yl", sl)], writes=[("yl", sl)])
                emit_gelu_src(c, yl[:, sl, :], [("yl", sl)], c.yag[:, cc, tsl], ga[:, sl, :], gb[:, sl, :], writes=[("yag", cc)], tmpkeys=(("ga", sl), ("gb", sl)))
    return ids


def s5_layouts(a_re, a_im, log_dt, b_re, b_im, c_re, c_im, d_skip):
    NP = 32
    def qlay(v):
        return v.reshape(NP, 2, 64).transpose(1, 2, 0).reshape(128, NP)
    ldt2 = np.repeat(log_dt[:, None], 64, axis=1)
    aq = np.stack([qlay(a_re), qlay(a_im), qlay(ldt2)], axis=1).astype(np.float32)
    Cz = np.zeros((128, 2, NP, 128), np.float32)
    for p in range(NP):
        for g2 in range(2):
            g = 2 * p + g2
            r0 = 32 * (p % 4) + 16 * g2
            for ri, (bm, cm) in enumerate(((b_re, c_re), (b_im, c_im))):
                Cz[g2 * 64:(g2 + 1) * 64, ri, p, r0:r0 + 16] = cm[g].T
    Bq = np.zeros((128, 2, NP, 16), np.float32)
    for p in range(NP):
        for g2 in range(2):
            Bq[g2 * 64:(g2 + 1) * 64, 0, p, :] = b_re[2 * p + g2]
            Bq[g2 * 64:(g2 + 1) * 64, 1, p, :] = b_im[2 * p + g2]
    dq = np.ascontiguousarray(d_skip.reshape(8, 128).T).astype(np.float32)
    return dict(aq=aq, Bq=Bq, Cz=Cz, dq=dq)


def _dram_in(nc, name, shape):
    return nc.dram_tensor(name, list(shape), F32, kind="ExternalInput").ap()


def _dram_out(nc, name, shape):
    return nc.dram_tensor(name, list(shape), F32, kind="ExternalOutput").ap()


def _carver(arena, layout):
    def V(name, shape, dt=F32, parts=128):
        return arena.view(layout[name], shape, dt, parts)
    return V


LAY_A_FFN = dict(x=0, h=64, wring=96, hid=160, sq=176, rstd=178, silu=182, ones=186, gains=186.25, epsb=186.5,
                 stage=187, tmpa=191, tmpb=195)
LAY_A_S5 = dict(BtT=0, Cz=16, E=32, z=96, mt=104, w=108, xs=116, ypre=124, etmp=96, rs=32, arow=64, btp=76, ua=160,
                Bq=128, Btq=132, Zp=136, identf=144, kq=144.5,
                ri=176, aq=180, qs=180.5, qi=182, Gp=182.25, G128=182.75, ini=183, itmp=183.25, xend=183.5, dq=183.75)


def build_A():
    nc = bass.Bass("TRN2", target_bir_lowering=False)
    xT = _dram_in(nc, "xT", [D, TOK]); g1 = _dram_in(nc, "g1", [D]); g2 = _dram_in(nc, "g2", [D])
    wg = _dram_in(nc, "wg", [D, DFF]); wu = _dram_in(nc, "wu", [D, DFF]); wd = _dram_in(nc, "wd", [DFF, D])
    win = _dram_in(nc, "win", [D, MIXIN])
    aq = _dram_in(nc, "aq_in", [128, 3, NP]); Bq = _dram_in(nc, "Bq_in", [128, 2, NP, 16]); identA = _dram_in(nc, "identA", [128, 128])
    Cz = _dram_in(nc, "Cz_in", [128, 2, NP, 128]); dq = _dram_in(nc, "dq_in", [128, 8])
    x1T = _dram_out(nc, "x1T", [D, TOK]); projUV = _dram_out(nc, "projUV", [2048, TOK])
    ypre = _dram_out(nc, "ypreT", [1024, TOK]); xend = _dram_out(nc, "xend_out", [128, 2, NP])
    c = Ctx(); c.nc = nc; c.s = Sched(); s = c.s
    with contextlib.ExitStack() as st:
        ar = Arena(nc, st, 200)
        c.ps = st.enter_context(nc.psum_tensor("ps", [128, 8, 512], F32))
        V1 = _carver(ar, LAY_A_FFN)
        alloc_ffn(c, V1)
        c.wring = V1("wring", [128, 4, 8192], BF16); c.wslot = 0
        stage = V1("stage", [128, 2, 512]); tmpa = V1("tmpa", [128, 2, 512]); tmpb = V1("tmpb", [128, 2, 512])
        V2 = _carver(ar, LAY_A_S5)
        c.ua = V2("ua", [128, 8, TOK], BF16)
        emit_consts(c)
        load_gain(c, 0, g1); load_gain(c, 1, g2)
        load_xT(c, xT)
        emit_ffn(c, 0, wg, wu, wd)
        ids = store_T(c, c.x, "x", x1T)
        emit_rmsnorm(c, 1)
        ids += emit_inproj(c, win, projUV, stage, tmpa, tmpb)
        s.barrier()
        s5_alloc_v(c, V2)
        s5_setup_v2(c, V2, aq, Bq, identA)
        s.dma("gpsimd", lambda e: e.dma_start(out=c.Cz, in_=Cz), writes=["Cz"], sem_key="Cz")
        s.dma("sync", lambda e: e.dma_start(out=c.dq, in_=dq), writes=["dq"], sem_key="dq")
        ids += s5_main(c, True, ypre, zero_init=True, pregelu=True)
        ids.append(s.dma("sync", lambda e: e.dma_start(out=xend, in_=c.xend), reads=["xend"], sem_key="xe"))
        s.emit(nc, final_wait_ops=ids)
    return nc


LAY_B1 = dict(P=64, etmp=144, Pb=128, Wt=96, Czf=64, wa=72, wb=76, yl=80, ga=84, gb=88, yag=160,
              aq=176, qs=176.5, qi=178, Gp=178.25, A=178.75, X=179, xinit=179.5, xall=179.75, oneh=181.75, V=182, G128=182.5)
LAY_B2 = dict(yag=160, ug=128, vg=144, vn=64, ya=96, yb=112, bsf=80, bst=84, bsh=88, bsl=90,
              sq2=176, ones2=178, idb=178.25, par=178.5, epsb2=178.75, mean=179, msq=181, rstd2=183, t1=185, sig=189, wsb=193, vT=195)
LAY_B3 = dict(hmix=64, ya=96, yb=112, mg=128, x=64, xst=176, sq3=184, ones3=186, epsb3=186.25, rstd3=186.5, gain3=190.5, bg=190.75,
              sga=191, tmp3=195)
LAY_B5 = dict(x=64, h=128, hid=160, sq=176, rstd=178, silu=182, ones=186, gains=186.25, epsb=186.5, stage=187)


def build_B():
    nc = bass.Bass("TRN2", target_bir_lowering=False)
    x1T = _dram_in(nc, "x1T_in", [D, TOK]); projUV = _dram_in(nc, "projUV_in", [2048, TOK]); ypre = _dram_in(nc, "ypreT_in", [1024, TOK])
    aq = _dram_in(nc, "aq_in", [128, 3, NP]); Cz = _dram_in(nc, "Cz_in", [128, 2, NP, 128])
    xall = _dram_in(nc, "xall_in", [128, 8, 2, NP]); oneh = _dram_in(nc, "oneh_in", [128, 8])
    w_glu = _dram_in(nc, "w_glu", [1024, 1024]); b_glu = _dram_in(nc, "b_glu", [1024])
    ln_g = _dram_in(nc, "ln_g", [1024]); ln_b = _dram_in(nc, "ln_b", [1024])
    wsT = _dram_in(nc, "wsT", [8, 128, 128]); b_s = _dram_in(nc, "b_s", [8, 128]); ident = _dram_in(nc, "ident", [128, 128])
    g_mix = _dram_in(nc, "g_mix", [D]); w_a = _dram_in(nc, "w_a", [1024, D]); w_b = _dram_in(nc, "w_b", [1024, D])
    w_gate = _dram_in(nc, "w_gate", [D, 2 * D]); b_gate = _dram_in(nc, "b_gate", [2 * D]); w_out = _dram_in(nc, "w_out", [D, D])
    g1 = _dram_in(nc, "g1", [D]); g2 = _dram_in(nc, "g2", [D])
    wg = _dram_in(nc, "wg", [D, DFF]); wu = _dram_in(nc, "wu", [D, DFF]); wd = _dram_in(nc, "wd", [DFF, D])
    outT = _dram_out(nc, "outT", [D, TOK])
    c = Ctx(); c.nc = nc; c.s = Sched(); s = c.s
    with contextlib.ExitStack() as st:
        ar = Arena(nc, st, 200)
        c.ps = st.enter_context(nc.psum_tensor("ps", [128, 8, 512], F32))
        c.wring = ar.view(0, [128, 4, 8192], BF16); c.wslot = 0
        V1 = _carver(ar, LAY_B1); V2 = _carver(ar, LAY_B2); V3 = _carver(ar, LAY_B3); V5 = _carver(ar, LAY_B5)
        c.yag = V1("yag", [128, 8, TOK], BF16)
        emit_s5_correct(c, V1, aq, Cz, xall, oneh, ypre)
        s.barrier()
        c.ya = V2("ya", [128, 8, TOK], BF16); c.yb = V2("yb", [128, 8, TOK], BF16)
        emit_glu_sgu(c, V2, projUV, w_glu, b_glu, ln_g, ln_b, wsT, b_s, ident)
        s.barrier()
        c.x = V3("x", [128, KC, TOK], F32)
        emit_merge(c, V3, x1T, g_mix, w_a, w_b, w_gate, b_gate, w_out)
        s.barrier()
        alloc_ffn(c, V5)
        stage = V5("stage", [128, 2, 512])
        emit_consts(c)
        load_gain(c, 0, g1); load_gain(c, 1, g2)
        emit_ffn(c, 0, wg, wu, wd)
        ids = emit_final_norm(c, 1, outT, stage)
        s.emit(nc, final_wait_ops=ids)
    return nc


NCORES = 8


def _run(nc, maps):
    return run_bass_kernel_spmd(nc, maps, core_ids=list(range(NCORES))).results


def kernel(x, ffn1_norm, ffn1_w_gate, ffn1_w_up, ffn1_w_down, mix_norm, w_in,
           s5_a_re, s5_a_im, s5_log_dt, s5_b_re, s5_b_im, s5_c_re, s5_c_im, s5_d,
           s5_w_glu, s5_b_glu, sgu_ln_g, sgu_ln_b, sgu_w_s, sgu_b_s,
           w_branch_a, w_branch_b, w_gate, b_gate, w_out,
           ffn2_norm, ffn2_w_gate, ffn2_w_up, ffn2_w_down, final_norm):
    f = lambda a: np.ascontiguousarray(np.asarray(a, dtype=np.float32))
    x = f(x)[0]
    n = NCORES
    lay = s5_layouts(f(s5_a_re)[0], f(s5_a_im)[0], f(s5_log_dt)[0], f(s5_b_re)[0], f(s5_b_im)[0], f(s5_c_re)[0], f(s5_c_im)[0], f(s5_d)[0])
    wA = dict(g1=f(ffn1_norm)[0], g2=f(mix_norm)[0], wg=f(ffn1_w_gate)[0], wu=f(ffn1_w_up)[0], wd=f(ffn1_w_down)[0], win=f(w_in)[0],
              aq_in=lay["aq"], Bq_in=lay["Bq"], identA=np.eye(128, dtype=np.float32), Cz_in=lay["Cz"], dq_in=lay["dq"])
    rA = _run(build_A(), [dict(wA, xT=np.ascontiguousarray(x[i * TOK:(i + 1) * TOK].T)) for i in range(n)])
    xall = np.ascontiguousarray(np.stack([rA[i]["xend_out"] for i in range(n)], axis=1))
    wsT = np.ascontiguousarray(np.transpose(f(sgu_w_s)[0], (0, 2, 1)))
    wB = dict(aq_in=lay["aq"], Cz_in=lay["Cz"], xall_in=xall,
              w_glu=f(s5_w_glu)[0], b_glu=f(s5_b_glu)[0], ln_g=f(sgu_ln_g)[0], ln_b=f(sgu_ln_b)[0], wsT=wsT, b_s=f(sgu_b_s)[0],
              ident=np.eye(128, dtype=np.float32),
              g_mix=f(mix_norm)[0], w_a=f(w_branch_a)[0], w_b=f(w_branch_b)[0], w_gate=f(w_gate)[0], b_gate=f(b_gate)[0], w_out=f(w_out)[0],
              g1=f(ffn2_norm)[0], g2=f(final_norm), wg=f(ffn2_w_gate)[0], wu=f(ffn2_w_up)[0], wd=f(ffn2_w_down)[0])
    maps = []
    for i in range(n):
        oh = np.zeros((128, 8), np.float32); oh[:, i] = 1.0
        maps.append(dict(wB, oneh_in=oh, x1T_in=rA[i]["x1T"], projUV_in=rA[i]["projUV"], ypreT_in=rA[i]["ypreT"]))
    rB = _run(build_B(), maps)
    out = np.concatenate([rB[i]["outT"].T for i in range(n)], axis=0)
    return np.ascontiguousarray(out[None].astype(np.float32))
```
